# Optimizing a Trainium2 kernel written in Bass

```python
import jax
import jax.numpy as jnp
from jax import lax
import numpy as np

D_MODEL = 2048
BATCH = 16
SEQ = 256
DEPTH = 4
DEC_BATCH = 4
DEC_SEQ = 4096
PAST_LEN = 256

GRID_W = 64
N_MIXERS = 3
N_A_LAYERS = (DEPTH + 2) // 3
N_B_LAYERS = (DEPTH + 1) // 3
N_C_LAYERS = DEPTH // 3
D_FF = 4 * D_MODEL
N_MOD = 6
NORM_EPS = 1e-6
ROPE_BASE = 10000.0
Q_BLOCK = 128
NEG_INF = -1e30

HGRN_HEAD_K = 128
HGRN_HEADS = D_MODEL // HGRN_HEAD_K
HGRN_HEAD_V = D_MODEL // HGRN_HEADS
HGRN_KW = HGRN_HEADS * HGRN_HEAD_K
HGRN_VW = HGRN_HEADS * HGRN_HEAD_V
HGRN_IN = 3 * HGRN_KW + 2 * HGRN_VW
HGRN_CHUNK = 32

MLA_HEADS = 16
MLA_Q_LORA = 512
MLA_KV_LORA = 512
MLA_NOPE = 128
MLA_ROPE = 64
MLA_V = 128
MLA_QK = MLA_NOPE + MLA_ROPE
MLA_DOWN = MLA_Q_LORA + MLA_KV_LORA + MLA_ROPE

SWA_HEAD_DIM = 64
SWA_Q_HEADS = D_MODEL // SWA_HEAD_DIM
SWA_KV_HEADS = 8
SWA_WINDOW = 128
SWA_BLOCK = 128
SWA_QW = SWA_Q_HEADS * SWA_HEAD_DIM
SWA_KVW = SWA_KV_HEADS * SWA_HEAD_DIM

kernel_name = 'hybrid_diffusion_trunk_step'


def rmsnorm(x, g):
    xf = x.astype(jnp.float32)
    y = xf * lax.rsqrt(jnp.mean(xf * xf, axis=-1, keepdims=True) + NORM_EPS)
    return (y * g.astype(jnp.float32)).astype(x.dtype)


def grid_positions(n_tokens):
    rows = n_tokens // GRID_W
    row = jnp.repeat(jnp.arange(rows, dtype=jnp.int32), GRID_W)
    col = jnp.tile(jnp.arange(GRID_W, dtype=jnp.int32), rows)
    return row, col


def axial_rope(x, row, col):
    r = x.shape[-1]
    half = r // 2
    quarter = half // 2
    inv = ROPE_BASE ** (-jnp.arange(quarter, dtype=jnp.float32) / quarter)

    def rot(xa, pos):
        ang = pos.astype(jnp.float32)[:, None] * inv[None, :]
        cos = jnp.cos(ang)[None, :, None, :]
        sin = jnp.sin(ang)[None, :, None, :]
        xa = xa.astype(jnp.float32)
        x1, x2 = xa[..., :quarter], xa[..., quarter:]
        return jnp.concatenate([x1 * cos - x2 * sin, x2 * cos + x1 * sin], axis=-1)

    out = jnp.concatenate([rot(x[..., :half], row), rot(x[..., half:], col)], axis=-1)
    return out.astype(x.dtype)


def adaln_params(cond, w, b):
    m = jax.nn.silu(cond) @ w + b
    return m.reshape(cond.shape[0], 1, N_MOD, D_MODEL)


def pre_sublayer(y, g, m, k):
    return rmsnorm(y, g) * (1 + m[:, :, k + 1]) + m[:, :, k]


def post_sublayer(y, out, g, m, k):
    return y + m[:, :, k + 2] * rmsnorm(out, g)


def softmax_with_sink(s, sink):
    if sink is None:
        return jax.nn.softmax(s, axis=-1)
    col = jnp.broadcast_to(sink.astype(jnp.float32)[None, :, :, None, None], s.shape[:-1] + (1,))
    return jax.nn.softmax(jnp.concatenate([s, col], axis=-1), axis=-1)[..., :-1]


def dense_attention(q, k, v, scale, sink):
    B, Lq, Hq, Dk = q.shape
    Hkv, Dv = k.shape[2], v.shape[-1]
    G = Hq // Hkv
    nq = Lq // Q_BLOCK
    qb = q.reshape(B, nq, Q_BLOCK, Hkv, G, Dk).transpose(1, 0, 2, 3, 4, 5)
    sink_g = None if sink is None else sink.reshape(Hkv, G)

    def block(qblk):
        s = jnp.einsum('bqkgd,bskd->bkgqs', qblk, k).astype(jnp.float32) * scale
        p = softmax_with_sink(s, sink_g).astype(v.dtype)
        return jnp.einsum('bkgqs,bskd->bqkgd', p, v)

    o = lax.map(block, qb)
    return o.transpose(1, 0, 2, 3, 4, 5).reshape(B, Lq, Hq, Dv)


def window_attention(q, k, v, k_ctx, v_ctx, sink, scale):
    B, L, Hq, Dh = q.shape
    Hkv, Dv = k.shape[2], v.shape[-1]
    G = Hq // Hkv
    nb = L // SWA_BLOCK
    n_ctx = k_ctx.shape[1]
    pad = ((0, 0), (SWA_BLOCK, SWA_BLOCK), (0, 0), (0, 0))
    kp = jnp.pad(k, pad)
    vp = jnp.pad(v, pad)
    qb = q.reshape(B, nb, SWA_BLOCK, Hkv, G, Dh).transpose(1, 0, 2, 3, 4, 5)
    sink_g = sink.reshape(Hkv, G)

    def block(args):
        j, qblk = args
        start = j * SWA_BLOCK
        kb = lax.dynamic_slice_in_dim(kp, start, 3 * SWA_BLOCK, axis=1)
        vb = lax.dynamic_slice_in_dim(vp, start, 3 * SWA_BLOCK, axis=1)
        qi = start + jnp.arange(SWA_BLOCK)
        ki = start - SWA_BLOCK + jnp.arange(3 * SWA_BLOCK)
        keep = (jnp.abs(qi[:, None] - ki[None, :]) <= SWA_WINDOW) & (ki[None, :] >= 0) & (ki[None, :] < L)
        s_loc = jnp.einsum('bqkgd,bskd->bkgqs', qblk, kb).astype(jnp.float32) * scale
        s_loc = jnp.where(keep, s_loc, NEG_INF)
        s_ctx = jnp.einsum('bqkgd,bskd->bkgqs', qblk, k_ctx).astype(jnp.float32) * scale
        p = softmax_with_sink(jnp.concatenate([s_ctx, s_loc], axis=-1), sink_g).astype(v.dtype)
        return (jnp.einsum('bkgqs,bskd->bqkgd', p[..., :n_ctx], v_ctx)
                + jnp.einsum('bkgqs,bskd->bqkgd', p[..., n_ctx:], vb))

    o = lax.map(block, (jnp.arange(nb), qb))
    return o.transpose(1, 0, 2, 3, 4, 5).reshape(B, L, Hq, Dv)


def gla_scan(q, k, v, log_f, s0):
    B, L, H, K = q.shape
    V = v.shape[-1]
    n = L // HGRN_CHUNK

    def chunks(a):
        return a.reshape(B, n, HGRN_CHUNK, H, a.shape[-1]).transpose(1, 0, 3, 2, 4)

    causal = jnp.tril(jnp.ones((HGRN_CHUNK, HGRN_CHUNK), dtype=bool))[:, :, None]

    def step(s, blk):
        qc, kc, vc, gc = blk
        b = jnp.cumsum(gc, axis=2)
        o_inter = jnp.einsum('bhtk,bhkv->bhtv', qc * jnp.exp(b), s)
        decay = jnp.exp(jnp.where(causal, b[:, :, :, None, :] - b[:, :, None, :, :], -jnp.inf))
        attn = jnp.einsum('bhtk,bhsk,bhtsk->bhts', qc, kc, decay)
        o_intra = jnp.einsum('bhts,bhsv->bhtv', attn, vc)
        b_end = b[:, :, -1:, :]
        s_new = (jnp.exp(b_end[:, :, 0, :])[..., None] * s
                 + jnp.einsum('bhsk,bhsv->bhkv', kc * jnp.exp(b_end - b), vc))
        return s_new, o_inter + o_intra

    s_fin, o = lax.scan(step, s0, (chunks(q), chunks(k), chunks(v), chunks(log_f)))
    return o.transpose(1, 0, 3, 2, 4).reshape(B, L, H, V), s_fin


def hgrn_mix(h, s_init, w_in, lb, norm_g, w_out):
    B, L, _ = h.shape
    p = h @ w_in
    kw, vw = HGRN_KW, HGRN_VW
    q = jax.nn.silu(p[..., :kw].astype(jnp.float32)) * HGRN_HEAD_K ** -0.5
    q = q.reshape(B, L, HGRN_HEADS, HGRN_HEAD_K)
    i = p[..., 3 * kw:3 * kw + vw].astype(jnp.float32).reshape(B, L, HGRN_HEADS, HGRN_HEAD_V)
    g = p[..., 3 * kw + vw:].reshape(B, L, HGRN_HEADS, HGRN_HEAD_V)
    lb = lb.astype(jnp.float32)

    def direction(z, lb_d, s0, flip):
        f = lb_d + (1 - lb_d) * jax.nn.sigmoid(z.astype(jnp.float32))
        f = f.reshape(B, L, HGRN_HEADS, HGRN_HEAD_K)
        args = (q, 1 - f, i, jnp.log(f))
        if flip:
            args = tuple(jnp.flip(a, axis=1) for a in args)
        o, s = gla_scan(*args, s0.astype(jnp.float32))
        if flip:
            o = jnp.flip(o, axis=1)
        return o, s

    o_f, s_f = direction(p[..., kw:2 * kw], lb[0], s_init[:, 0], False)
    o_b, s_b = direction(p[..., 2 * kw:3 * kw], lb[1], s_init[:, 1], True)
    o = rmsnorm(o_f + o_b, norm_g) * jax.nn.silu(g.astype(jnp.float32))
    out = o.reshape(B, L, vw).astype(h.dtype) @ w_out
    return out, jnp.stack([s_f, s_b], axis=1)


def mla_down(h, w_down, q_norm_g, kv_norm_g):
    p = h @ w_down
    cq = rmsnorm(p[..., :MLA_Q_LORA], q_norm_g)
    ckv = rmsnorm(p[..., MLA_Q_LORA:MLA_Q_LORA + MLA_KV_LORA], kv_norm_g)
    kpe = p[..., MLA_Q_LORA + MLA_KV_LORA:]
    return cq, ckv, kpe


def mla_queries(cq, w_uq):
    B, L, _ = cq.shape
    return (cq @ w_uq).reshape(B, L, MLA_HEADS, MLA_QK)


def mla_keys_values(ckv, kpe, w_ukv):
    B, L, _ = ckv.shape
    kv = (ckv @ w_ukv).reshape(B, L, MLA_HEADS, MLA_NOPE + MLA_V)
    k_pe = jnp.broadcast_to(kpe[:, :, None, :], (B, L, MLA_HEADS, MLA_ROPE))
    return jnp.concatenate([kv[..., :MLA_NOPE], k_pe], axis=-1), kv[..., MLA_NOPE:]


def mla_context(h, w_down, q_norm_g, kv_norm_g, w_uq, w_ukv, w_out):
    B, L, _ = h.shape
    cq, ckv, kpe = mla_down(h, w_down, q_norm_g, kv_norm_g)
    q = mla_queries(cq, w_uq)
    k, v = mla_keys_values(ckv, kpe, w_ukv)
    o = dense_attention(q, k, v, MLA_QK ** -0.5, None)
    return o.reshape(B, L, MLA_HEADS * MLA_V) @ w_out, ckv, kpe


def mla_latent(h, ckv_ctx, kpe_ctx, row, col, w_down, q_norm_g, kv_norm_g, w_uq, w_ukv, w_out):
    B, L, _ = h.shape
    cq, ckv, kpe = mla_down(h, w_down, q_norm_g, kv_norm_g)
    q = mla_queries(cq, w_uq)
    q = jnp.concatenate([q[..., :MLA_NOPE], axial_rope(q[..., MLA_NOPE:], row, col)], axis=-1)
    kpe = axial_rope(kpe[:, :, None, :], row, col)[:, :, 0, :]
    k_lat, v_lat = mla_keys_values(ckv, kpe, w_ukv)
    k_ctx, v_ctx = mla_keys_values(ckv_ctx, kpe_ctx, w_ukv)
    k = jnp.concatenate([k_ctx, k_lat], axis=1)
    v = jnp.concatenate([v_ctx, v_lat], axis=1)
    o = dense_attention(q, k, v, MLA_QK ** -0.5, None)
    return o.reshape(B, L, MLA_HEADS * MLA_V) @ w_out


def swa_qkv(h, w_qkv):
    B, L, _ = h.shape
    p = h @ w_qkv
    q = p[..., :SWA_QW].reshape(B, L, SWA_Q_HEADS, SWA_HEAD_DIM)
    k = p[..., SWA_QW:SWA_QW + SWA_KVW].reshape(B, L, SWA_KV_HEADS, SWA_HEAD_DIM)
    v = p[..., SWA_QW + SWA_KVW:].reshape(B, L, SWA_KV_HEADS, SWA_HEAD_DIM)
    return q, k, v


def swa_context(h, w_qkv, sink, w_out):
    B, L, _ = h.shape
    q, k, v = swa_qkv(h, w_qkv)
    o = dense_attention(q, k, v, SWA_HEAD_DIM ** -0.5, sink)
    return o.reshape(B, L, SWA_QW) @ w_out, k, v


def swa_latent(h, k_ctx, v_ctx, row, col, w_qkv, sink, w_out):
    B, L, _ = h.shape
    q, k, v = swa_qkv(h, w_qkv)
    q = axial_rope(q, row, col)
    k = axial_rope(k, row, col)
    o = window_attention(q, k, v, k_ctx, v_ctx, sink, SWA_HEAD_DIM ** -0.5)
    return o.reshape(B, L, SWA_QW) @ w_out


def sq_relu_mlp(h, w_in, w_out):
    return jnp.square(jax.nn.relu(h @ w_in)) @ w_out


def setup_inputs(seed: int = 0) -> dict:
    key = jax.random.key(seed)
    ks = iter(jax.random.split(key, 32))

    def nrm(shape, scale):
        return jax.random.normal(next(ks), shape, jnp.float32) * scale

    def gain(shape):
        return 1.0 + nrm(shape, 0.02)

    return {
        'x_prompt': nrm((BATCH, SEQ, D_MODEL), 1.0),
        'x_sample': nrm((DEC_BATCH, DEC_SEQ, D_MODEL), 1.0),
        'c': nrm((DEC_BATCH, D_MODEL), 1.0),
        'state_hgrn': nrm((DEC_BATCH, N_A_LAYERS, 2, HGRN_HEADS, HGRN_HEAD_K, HGRN_HEAD_V), 0.5),
        'cache_mla_ckv': nrm((DEC_BATCH, N_B_LAYERS, PAST_LEN, MLA_KV_LORA), 1.0),
        'cache_mla_kpe': nrm((DEC_BATCH, N_B_LAYERS, PAST_LEN, MLA_ROPE), 1.0),
        'cache_swa_k': nrm((DEC_BATCH, N_C_LAYERS, PAST_LEN, SWA_KV_HEADS, SWA_HEAD_DIM), 1.0),
        'cache_swa_v': nrm((DEC_BATCH, N_C_LAYERS, PAST_LEN, SWA_KV_HEADS, SWA_HEAD_DIM), 1.0),
        'c_ctx': nrm((D_MODEL,), 1.0),
        'ada_w': nrm((DEPTH, D_MODEL, N_MOD * D_MODEL), 0.5 * D_MODEL ** -0.5),
        'ada_b': nrm((DEPTH, N_MOD * D_MODEL), 0.02),
        'norm_g': gain((DEPTH, 4, D_MODEL)),
        'mlp_w_in': nrm((DEPTH, D_MODEL, D_FF), D_MODEL ** -0.5),
        'mlp_w_out': nrm((DEPTH, D_FF, D_MODEL), D_FF ** -0.5),
        'hgrn_w_in': nrm((N_A_LAYERS, D_MODEL, HGRN_IN), D_MODEL ** -0.5),
        'hgrn_lb_logits': nrm((2, DEPTH, HGRN_KW), 0.5),
        'hgrn_norm_g': gain((N_A_LAYERS, HGRN_HEAD_V)),
        'hgrn_w_out': nrm((N_A_LAYERS, HGRN_VW, D_MODEL), HGRN_VW ** -0.5),
        'mla_w_down': nrm((N_B_LAYERS, D_MODEL, MLA_DOWN), D_MODEL ** -0.5),
        'mla_q_norm_g': gain((N_B_LAYERS, MLA_Q_LORA)),
        'mla_kv_norm_g': gain((N_B_LAYERS, MLA_KV_LORA)),
        'mla_w_uq': nrm((N_B_LAYERS, MLA_Q_LORA, MLA_HEADS * MLA_QK), MLA_Q_LORA ** -0.5),
        'mla_w_ukv': nrm((N_B_LAYERS, MLA_KV_LORA, MLA_HEADS * (MLA_NOPE + MLA_V)), MLA_KV_LORA ** -0.5),
        'mla_w_out': nrm((N_B_LAYERS, MLA_HEADS * MLA_V, D_MODEL), (MLA_HEADS * MLA_V) ** -0.5),
        'swa_w_qkv': nrm((N_C_LAYERS, D_MODEL, SWA_QW + 2 * SWA_KVW), D_MODEL ** -0.5),
        'swa_sink': nrm((N_C_LAYERS, SWA_Q_HEADS), 0.5),
        'swa_w_out': nrm((N_C_LAYERS, SWA_QW, D_MODEL), SWA_QW ** -0.5),
    }


def reference(x_prompt, x_sample, c, state_hgrn, cache_mla_ckv, cache_mla_kpe, cache_swa_k, cache_swa_v,
              c_ctx, ada_w, ada_b, norm_g, mlp_w_in, mlp_w_out,
              hgrn_w_in, hgrn_lb_logits, hgrn_norm_g, hgrn_w_out,
              mla_w_down, mla_q_norm_g, mla_kv_norm_g, mla_w_uq, mla_w_ukv, mla_w_out,
              swa_w_qkv, swa_sink, swa_w_out):
    row, col = grid_positions(x_sample.shape[1])
    lb_soft = jax.nn.softmax(hgrn_lb_logits.astype(jnp.float32), axis=1)
    lb_all = jnp.cumsum(lb_soft, axis=1) - lb_soft[:, :1]

    yp, ys = x_prompt, x_sample
    new_hgrn, new_ckv, new_kpe, new_k, new_v = [], [], [], [], []
    for layer in range(DEPTH):
        kind, j = layer % N_MIXERS, layer // N_MIXERS
        mp = adaln_params(c_ctx[None, :], ada_w[layer], ada_b[layer])
        ms = adaln_params(c, ada_w[layer], ada_b[layer])
        g = norm_g[layer]

        hp = pre_sublayer(yp, g[0], mp, 0)
        hs = pre_sublayer(ys, g[0], ms, 0)
        if kind == 0:
            zero_state = jnp.zeros((yp.shape[0], 2, HGRN_HEADS, HGRN_HEAD_K, HGRN_HEAD_V), jnp.float32)
            op, st = hgrn_mix(hp, zero_state, hgrn_w_in[j], lb_all[:, layer], hgrn_norm_g[j], hgrn_w_out[j])
            os, _ = hgrn_mix(hs, state_hgrn[:, j], hgrn_w_in[j], lb_all[:, layer], hgrn_norm_g[j], hgrn_w_out[j])
            new_hgrn.append(st.astype(state_hgrn.dtype))
        elif kind == 1:
            op, ckv, kpe = mla_context(hp, mla_w_down[j], mla_q_norm_g[j], mla_kv_norm_g[j],
                                       mla_w_uq[j], mla_w_ukv[j], mla_w_out[j])
            os = mla_latent(hs, cache_mla_ckv[:, j], cache_mla_kpe[:, j], row, col, mla_w_down[j],
                            mla_q_norm_g[j], mla_kv_norm_g[j], mla_w_uq[j], mla_w_ukv[j], mla_w_out[j])
            new_ckv.append(ckv)
            new_kpe.append(kpe)
        else:
            op, kc, vc = swa_context(hp, swa_w_qkv[j], swa_sink[j], swa_w_out[j])
            os = swa_latent(hs, cache_swa_k[:, j], cache_swa_v[:, j], row, col,
                            swa_w_qkv[j], swa_sink[j], swa_w_out[j])
            new_k.append(kc)
            new_v.append(vc)
        yp = post_sublayer(yp, op, g[1], mp, 0)
        ys = post_sublayer(ys, os, g[1], ms, 0)

        yp = post_sublayer(yp, sq_relu_mlp(pre_sublayer(yp, g[2], mp, 3), mlp_w_in[layer], mlp_w_out[layer]), g[3], mp, 3)
        ys = post_sublayer(ys, sq_relu_mlp(pre_sublayer(ys, g[2], ms, 3), mlp_w_in[layer], mlp_w_out[layer]), g[3], ms, 3)

    return (yp, ys, jnp.stack(new_hgrn, axis=1), jnp.stack(new_ckv, axis=1), jnp.stack(new_kpe, axis=1),
            jnp.stack(new_k, axis=1), jnp.stack(new_v, axis=1))
```

```python
import numpy as np
from contextlib import ExitStack
import concourse.bass as bass
import concourse.mybir as mybir
from concourse.bass_utils import run_bass_kernel_spmd

F32 = mybir.dt.float32
BF16 = mybir.dt.bfloat16
AF = mybir.ActivationFunctionType
ALU = mybir.AluOpType
AX = mybir.AxisListType

SEM_CAP = 20000
D = 2048
KC = 16
TT = 512
EPS = 1e-6


class Buf:
    __slots__ = ("name", "last_w", "readers", "excl")

    def __init__(self, name="", excl=False):
        self.name = name
        self.last_w = None
        self.readers = []
        self.excl = excl


class Op:
    __slots__ = ("eng", "fn", "deps", "signal", "val", "semi", "dma", "dsem", "dval", "idx")

    def __init__(self, eng, fn, dma):
        self.eng = eng
        self.fn = fn
        self.deps = []
        self.signal = False
        self.val = None
        self.semi = None
        self.dma = dma
        self.dsem = None
        self.dval = None


class Sched:
    ENGS = ("pe", "act", "dve", "pool", "sp")

    def __init__(self, nc):
        self.nc = nc
        self.ops = {e: [] for e in self.ENGS}
        self.dma_cnt = {}
        self.dma_last = {}
        self.pending = {e: [] for e in self.ENGS}
        self.n = 0

    def add(self, eng, fn, reads=(), writes=(), dma_key=None):
        op = Op(eng, fn, dma_key is not None)
        op.idx = self.n
        self.n += 1
        deps = set(self.pending[eng])
        self.pending[eng] = []
        if dma_key is not None and dma_key in self.dma_last:
            deps.add(self.dma_last[dma_key])
        for b in reads:
            if b.last_w is not None:
                deps.add(b.last_w)
            if b.excl:
                for r in b.readers:
                    if r.eng != eng:
                        deps.add(r)
        for b in writes:
            if b.last_w is not None:
                deps.add(b.last_w)
            deps.update(b.readers)
        last = {}
        for d in deps:
            if d.dma:
                op.deps.append(d)
            elif d.eng not in last or last[d.eng].idx < d.idx:
                last[d.eng] = d
        for d in last.values():
            if d.eng == "pe" and eng == "pe" and not op.dma:
                continue
            d.signal = True
            op.deps.append(d)
        for b in reads:
            b.readers.append(op)
        for b in writes:
            b.last_w = op
            b.readers = []
        if dma_key is not None:
            c = self.dma_cnt.get(dma_key, 0) + 16
            self.dma_cnt[dma_key] = c
            op.dsem = dma_key
            op.dval = c
            self.dma_last[dma_key] = op
        self.ops[eng].append(op)
        return op

    def barrier(self):
        snap = []
        for e in self.ENGS:
            for op in reversed(self.ops[e]):
                if not op.dma:
                    op.signal = True
                    snap.append(op)
                    break
        snap.extend(self.dma_last.values())
        for e in self.ENGS:
            self.pending[e] = list(snap)

    def emit(self):
        nc = self.nc
        nsem = {}
        for e in self.ENGS:
            c = 0
            for op in self.ops[e]:
                if op.dma or not op.signal:
                    continue
                op.semi = c // SEM_CAP
                op.val = c % SEM_CAP + 1
                c += 1
            nsem[e] = max(1, (c + SEM_CAP - 1) // SEM_CAP)
        with ExitStack() as st:
            esem = {e: [st.enter_context(nc.semaphore(f"s_{e}_{i}")) for i in range(nsem[e])] for e in self.ENGS}
            dsem = {k: st.enter_context(nc.semaphore(f"d_{i}")) for i, k in enumerate(self.dma_cnt)}
            block = st.enter_context(nc.Block())
            handles = {"pe": block.tensor, "act": block.scalar, "dve": block.vector, "pool": block.gpsimd,
                       "sp": block.sync}

            def make(e):
                def body(eng):
                    waited = {}
                    for op in self.ops[e]:
                        need = {}
                        for d in op.deps:
                            if d.dma:
                                k, v = ("d", d.dsem), d.dval
                            else:
                                k, v = ("e", d.eng, d.semi), d.val
                            if need.get(k, 0) < v:
                                need[k] = v
                        for k, v in need.items():
                            if waited.get(k, 0) >= v:
                                continue
                            waited[k] = v
                            eng.wait_ge(dsem[k[1]] if k[0] == "d" else esem[k[1]][k[2]], v)
                        ins = op.fn(eng)
                        if op.dma:
                            ins.then_inc(dsem[op.dsem], 16)
                        elif op.signal:
                            ins.then_inc(esem[e][op.semi], 1)
                    if e == "sp":
                        for k, tot in self.dma_cnt.items():
                            if waited.get(("d", k), 0) < tot:
                                eng.wait_ge(dsem[k], tot)
                return body

            for e in self.ENGS:
                if self.ops[e] or e == "sp":
                    handles[e](make(e))


def tile_w(W, chunks, spc, kp=128, cw=128):
    K = W.shape[0]
    kc = K // kp
    ns = (len(chunks) + spc - 1) // spc
    out = np.zeros((ns, kp, kc, spc * cw), np.float32)
    Wr = W.reshape(kc, kp, W.shape[1])
    for i, cols in enumerate(chunks):
        s, j = divmod(i, spc)
        out[s, :, :, j * cw:j * cw + len(cols)] = Wr[:, :, cols].transpose(1, 0, 2)
    return out.reshape(ns, kp, kc * spc * cw)


def plain_chunks(n, w=128, start=0):
    return [np.arange(start + i * w, start + (i + 1) * w) for i in range(n)]


def fm(vec):
    v = np.asarray(vec, np.float32)
    v = v.reshape(v.shape[:-1] + (v.shape[-1] // 128, 128))
    return np.ascontiguousarray(np.moveaxis(v, -1, 0))


class Prog:
    def __init__(self, NS, NP, SEQ, DEPTH, GRID_W):
        self.NS, self.NP, self.SEQ, self.L, self.GW = NS, NP, SEQ, DEPTH, GRID_W
        self.NT = NS + NP
        self.tiles = [(t0, TT, 0) for t0 in range(0, NS, TT)] + [(NS + t0, TT, 1) for t0 in range(0, NP, TT)]
        self.nc = bass.Bass("TRN2", target_bir_lowering=False)
        self.S = Sched(self.nc)
        self.din = {}
        self.dout = {}
        self.uid = 0
        self.bYd = {}
        self.rope_t0 = None

    def inp(self, name, shape):
        self.din[name] = self.nc.dram_tensor(name, list(shape), F32, kind="ExternalInput").ap()
        return self.din[name]

    def outp(self, name, shape):
        self.dout[name] = self.nc.dram_tensor(name, list(shape), F32, kind="ExternalOutput").ap()
        return self.dout[name]

    def scratch(self, name, shape, dt=F32):
        return self.nc.dram_tensor(name, list(shape), dt).ap()

    def sb(self, st, shape, dt, name=None):
        self.uid += 1
        return st.enter_context(self.nc.sbuf_tensor(f"{name or 't'}_{self.uid}", list(shape), dt))

    def bY(self, c, t0):
        k = (c, t0)
        if k not in self.bYd:
            self.bYd[k] = Buf(f"Y{c}_{t0}")
        return self.bYd[k]

    def key(self, p="k"):
        self.uid += 1
        return f"{p}{self.uid}"

    def linear_fm(self, wd, KCn, nchunks, spc, x, T, consume, widths=None, kp=128, cw=128):
        S = self.S
        ns = (nchunks + spc - 1) // spc
        sc = spc * cw
        wb = self.wbufs

        def load(s):
            t, b = wb[self.wrot % len(wb)]
            self.wrot += 1
            S.add("pool", lambda e, t=t, s=s: e.dma_start(out=t[:kp, :KCn * sc], in_=wd[s]), writes=[b], dma_key=b.name)
            return t, b

        nxt = load(0)
        for s in range(ns):
            t, b = nxt
            if s + 1 < ns:
                nxt = load(s + 1)
            for j in range(min(spc, nchunks - s * spc)):
                ci = s * spc + j
                w = cw if widths is None else widths[ci]
                ps, pb = self.linps[self.linrot % len(self.linps)]
                self.linrot += 1
                for kc in range(KCn):
                    xa, xb = x(kc)
                    S.add("pe", lambda e, ps=ps, t=t, kc=kc, j=j, w=w, xa=xa: e.matmul(
                        ps[:w, :T], lhsT=t[:kp, kc * sc + j * cw: kc * sc + j * cw + w], rhs=xa,
                        start=(kc == 0), stop=(kc == KCn - 1)), reads=[b] + xb, writes=[pb])
                consume(ci, ps[:w, :T], pb)

    def linear_tm(self, wd, KCn, nslabs, x, T, blk, consume):
        S = self.S
        wb = self.wbufs

        def load(s):
            t, b = wb[self.wrot % len(wb)]
            self.wrot += 1
            S.add("pool", lambda e, t=t, s=s: e.dma_start(out=t[:, :KCn * 512], in_=wd[s]), writes=[b], dma_key=b.name)
            return t, b

        nxt = load(0)
        for s in range(nslabs):
            t, b = nxt
            if s + 1 < nslabs:
                nxt = load(s + 1)
            for tb in range(T // blk):
                ps, pb = self.linps[self.linrot % len(self.linps)]
                self.linrot += 1
                for kc in range(KCn):
                    xa, xb = x(kc, tb)
                    S.add("pe", lambda e, ps=ps, t=t, kc=kc, xa=xa: e.matmul(
                        ps[:blk, :512], lhsT=xa, rhs=t[:, kc * 512:(kc + 1) * 512],
                        start=(kc == 0), stop=(kc == KCn - 1)), reads=[b] + xb, writes=[pb])
                consume(s, tb, ps[:blk, :512], pb)

    def rstd_from(self, ps_ap, pb, n, out_t, out_b, parts=128):
        S = self.S
        T = ps_ap.shape[-1]
        S.add("act", lambda e: e.activation(out=out_t, in_=ps_ap, func=AF.Ln, scale=1.0 / n, bias=self.c_eps[:parts, :]),
              reads=[pb, self.b_const], writes=[out_b])
        S.add("act", lambda e: e.activation(out=out_t, in_=out_t, func=AF.Exp, scale=-0.5), reads=[out_b], writes=[out_b])

    def sumsq_acc(self, src_ap, src_bufs, first, last, T, parts=128):
        S = self.S
        sq, sqb = self.sqs[self.sqrot % len(self.sqs)]
        self.sqrot += 1
        S.add("act", lambda e: e.activation(out=sq[:parts, :T], in_=src_ap, func=AF.Square), reads=src_bufs, writes=[sqb])
        st, stb = self.statps
        S.add("pe", lambda e: e.matmul(st[:, :T], lhsT=self.ones_bf[:parts, :], rhs=sq[:parts, :T], start=first, stop=last),
              reads=[sqb, self.b_const], writes=[stb])

    def pre(self, l, k, tile):
        S = self.S
        t0, T, cond = tile
        big, hT, YT = self.big, self.hT, self.YT
        for c in range(KC):
            S.add("sp", lambda e, c=c: e.dma_start(out=big[:, c, :T], in_=YT[c, :, t0:t0 + T]),
                  reads=[self.bY(c, t0)], writes=[self.b_big[c]], dma_key=f"big{c % 4}")
            self.sumsq_acc(big[:, c, :T], [self.b_big[c]], c == 0, c == KC - 1, T)
        self.rstd_from(self.statps[0][:, :T], self.statps[1], D, self.rstd[:, :T], self.b_rstd)
        for c in range(KC):
            tmp, tb = self.tmps[self.tmprot % len(self.tmps)]
            self.tmprot += 1
            S.add("dve", lambda e, c=c, tmp=tmp: e.tensor_tensor(out=tmp[:, :T], in0=big[:, c, :T], in1=self.rstd[:, :T], op=ALU.mult),
                  reads=[self.b_big[c], self.b_rstd], writes=[tb])
            S.add("act", lambda e, c=c, tmp=tmp: e.activation(out=hT[:, c, :T], in_=tmp[:, :T], func=AF.Identity,
                                                               scale=self.modA[:, k, c, cond:cond + 1], bias=self.modS[:, k, c, cond:cond + 1]),
                  reads=[tb, self.b_mod], writes=[self.b_hT[c]])

    def post_consume(self, T):
        S = self.S
        big = self.big

        def consume(ci, ps, pb):
            S.add("act", lambda e: e.activation(out=big[:, ci, :T], in_=ps, func=AF.Copy), reads=[pb], writes=[self.b_big[ci]])
            self.sumsq_acc(big[:, ci, :T], [self.b_big[ci]], ci == 0, ci == KC - 1, T)
        return consume

    def post(self, l, k, tile, final):
        S = self.S
        t0, T, cond = tile
        big, YT = self.big, self.YT
        self.rstd_from(self.statps[0][:, :T], self.statps[1], D, self.rstd[:, :T], self.b_rstd)
        for c in range(KC):
            tmp, tb = self.tmps[self.tmprot % len(self.tmps)]
            self.tmprot += 1
            S.add("sp", lambda e, c=c, tmp=tmp: e.dma_start(out=tmp[:, :T], in_=YT[c, :, t0:t0 + T]), reads=[self.bY(c, t0)], writes=[tb], dma_key=tb.name)
            S.add("dve", lambda e, c=c: e.scalar_tensor_tensor(out=big[:, c, :T], in0=big[:, c, :T], scalar=self.modG[:, k, c, cond:cond + 1],
                                                                in1=self.rstd[:, :T], op0=ALU.mult, op1=ALU.mult),
                  reads=[self.b_big[c], self.b_rstd, self.b_mod], writes=[self.b_big[c]])
            S.add("pool", lambda e, c=c, tmp=tmp: e.tensor_tensor(out=tmp[:, :T], in0=tmp[:, :T], in1=big[:, c, :T], op=ALU.add),
                  reads=[tb, self.b_big[c]], writes=[tb])
            dst = self.OUTT if final else self.YS
            S.add("sp", lambda e, c=c, tmp=tmp, dst=dst: e.dma_start(out=dst[c, :, t0:t0 + T], in_=tmp[:, :T]), reads=[tb], writes=[self.bY(c, t0)],
                  dma_key=tb.name + "s")

    def adaln(self, l):
        S = self.S
        mr = self.modraw

        def x(kc):
            return self.scT[:, kc, :], [self.b_const]

        def consume(ci, ps, pb):
            S.add("dve", lambda e: e.tensor_scalar(out=mr[:, ci, :], in0=ps, scalar1=self.adab[:, l, ci:ci + 1], scalar2=None, op0=ALU.add),
                  reads=[pb, self.b_const], writes=[self.b_mod])
        self.linear_fm(self.din["ada_w"][l], KC, 96, 4, x, 2, consume)
        for k in range(2):
            sh = mr[:, (3 * k) * 16:(3 * k + 1) * 16, :]
            scl = mr[:, (3 * k + 1) * 16:(3 * k + 2) * 16, :]
            gt = mr[:, (3 * k + 2) * 16:(3 * k + 3) * 16, :]
            gpre = self.ngT[:, l, 2 * k, :].unsqueeze(2).broadcast_to([128, 16, 2])
            gpost = self.ngT[:, l, 2 * k + 1, :].unsqueeze(2).broadcast_to([128, 16, 2])
            S.add("dve", lambda e, k=k, scl=scl, gpre=gpre: e.scalar_tensor_tensor(out=self.modA[:, k, :, :], in0=scl, scalar=1.0, in1=gpre, op0=ALU.add, op1=ALU.mult),
                  reads=[self.b_mod, self.b_const], writes=[self.b_mod])
            S.add("dve", lambda e, k=k, gt=gt, gpost=gpost: e.tensor_tensor(out=self.modG[:, k, :, :], in0=gt, in1=gpost, op=ALU.mult),
                  reads=[self.b_mod, self.b_const], writes=[self.b_mod])
            S.add("dve", lambda e, k=k, sh=sh: e.tensor_copy(out=self.modS[:, k, :, :], in_=sh), reads=[self.b_mod], writes=[self.b_mod])

    def mlp(self, l, tile, final):
        S = self.S
        t0, T, cond = tile
        self.pre(l, 1, tile)
        hT, aT = self.hT, self.aT

        def x1(kc):
            return hT[:, kc, :T], [self.b_hT[kc]]

        def c1(ci, ps, pb):
            tmp, tb = self.tmps[self.tmprot % len(self.tmps)]
            self.tmprot += 1
            S.add("act", lambda e: e.activation(out=tmp[:, :T], in_=ps, func=AF.Relu), reads=[pb], writes=[tb])
            eng = "dve" if ci % 2 == 0 else "pool"
            S.add(eng, lambda e: e.tensor_tensor(out=aT[:, ci, :T], in0=tmp[:, :T], in1=tmp[:, :T], op=ALU.mult), reads=[tb], writes=[self.b_aT[ci]])
        self.linear_fm(self.din["mlp_w_in"][l], KC, 64, 4, x1, T, c1)

        def x2(kc):
            return aT[:, kc, :T], [self.b_aT[kc]]
        self.linear_fm(self.din["mlp_w_out"][l], 64, 16, 1, x2, T, self.post_consume(T))
        self.post(l, 1, tile, final)


    def attend(self, pieces_q, nkb, kpieces, vblk, Tq, dv, scale, maskf, finish):
        S = self.S
        (o_ps, o_pb), (d_ps, d_pb) = self.accps[self.accrot % 2]
        self.accrot += 1
        for kb in range(nkb):
            ps, pb = self.linps[self.linrot % len(self.linps)]
            self.linrot += 1
            kp = kpieces(kb)
            for i, ((qa, qb), (ka, kbufs)) in enumerate(zip(pieces_q, kp)):
                S.add("pe", lambda e, ps=ps, ka=ka, qa=qa, i=i: e.matmul(ps[:, :Tq], lhsT=ka, rhs=qa, start=(i == 0), stop=(i == len(kp) - 1)),
                      reads=qb + kbufs, writes=[pb])
            pt, ptb = self.pts[self.ptrot % len(self.pts)]
            self.ptrot += 1
            S.add("act", lambda e, ps=ps, pt=pt: e.activation(out=pt[:, :Tq], in_=ps[:, :Tq], func=AF.Exp, scale=scale), reads=[pb], writes=[ptb])
            m = maskf(kb) if maskf is not None else None
            if m is not None:
                ma, mb = m
                S.add("pool", lambda e, pt=pt, ma=ma: e.tensor_tensor(out=pt[:, :Tq], in0=pt[:, :Tq], in1=ma, op=ALU.mult), reads=[ptb] + mb, writes=[ptb])
            va, vb = vblk(kb)
            S.add("pe", lambda e, pt=pt, va=va, kb=kb: e.matmul(o_ps[:dv, :Tq], lhsT=va, rhs=pt[:, :Tq], start=(kb == 0), stop=(kb == nkb - 1)),
                  reads=[ptb] + vb, writes=[o_pb])
            S.add("pe", lambda e, pt=pt, kb=kb: e.matmul(d_ps[:dv, :Tq], lhsT=self.ones_bf[:, :dv], rhs=pt[:, :Tq], start=(kb == 0), stop=(kb == nkb - 1)),
                  reads=[ptb, self.b_const], writes=[d_pb])
        finish(o_ps[:dv, :Tq], o_pb, d_ps[:dv, :Tq], d_pb)

    def attn_finish(self, out_ap, out_bufs, sink_ap=None, then=None):
        S = self

        def fin(o_ps, o_pb, d_ps, d_pb):
            rd, rdb = self.tmps[self.tmprot % len(self.tmps)]
            self.tmprot += 1
            p, T = o_ps.shape[0], o_ps.shape[-1]
            rda = rd[:p, :T]
            if sink_ap is not None:
                self.S.add("dve", lambda e: e.tensor_tensor(out=rda.rearrange("p (g t) -> p g t", g=4), in0=d_ps.rearrange("p (g t) -> p g t", g=4), in1=sink_ap, op=ALU.add),
                           reads=[d_pb, self.b_const], writes=[rdb])
                self.S.add("dve", lambda e: e.reciprocal(out=rda, in_=rda), reads=[rdb], writes=[rdb])
            else:
                self.S.add("dve", lambda e: e.reciprocal(out=rda, in_=d_ps), reads=[d_pb], writes=[rdb])
            self.S.add("dve", lambda e: e.tensor_tensor(out=out_ap, in0=o_ps, in1=rda, op=ALU.mult), reads=[o_pb, rdb], writes=out_bufs)
            if then is not None:
                then()
        return fin

    def rope_pair(self, t0, T, dst_fn):
        S = self.S
        st = {}
        if self.rope_t0 != t0:
            self.rope_t0 = t0
            S.add("sp", lambda e: e.dma_start(out=self.ropeC[:, :T], in_=self.din["ropeC"][:, t0:t0 + T]), writes=[self.b_rope], dma_key="c3")
            S.add("sp", lambda e: e.dma_start(out=self.ropeS[:, :T], in_=self.din["ropeS"][:, t0:t0 + T]), writes=[self.b_rope], dma_key="c4")

        def consume(ci, ps, pb):
            if ci % 2 == 0:
                tmp, tb = self.tmps[self.tmprot % len(self.tmps)]
                self.tmprot += 1
                st["a"] = (tmp, tb)
                S.add("dve", lambda e: e.tensor_tensor(out=tmp[:64, :T], in0=ps, in1=self.ropeC[:, :T], op=ALU.mult), reads=[pb, self.b_rope], writes=[tb])
            else:
                tmp, tb = st["a"]
                t2, tb2 = self.tmps[self.tmprot % len(self.tmps)]
                self.tmprot += 1
                S.add("dve", lambda e: e.tensor_tensor(out=t2[:64, :T], in0=ps, in1=self.ropeS[:, :T], op=ALU.mult), reads=[pb, self.b_rope], writes=[tb2])
                da, db = dst_fn(ci // 2)
                S.add("pool", lambda e: e.tensor_tensor(out=da, in0=tmp[:64, :T], in1=t2[:64, :T], op=ALU.add), reads=[tb, tb2], writes=db)
        return consume

    def swa_layer(self, l, j):
        nc, S = self.nc, self.S
        NT, NS, NP, SEQ = self.NT, self.NS, self.NP, self.SEQ
        QT = self.scratch(f"swaQ{l}", [64, 32, NT], BF16)
        KT = self.scratch(f"swaK{l}", [64, 8, NT], BF16)
        VV = self.scratch(f"swaV{l}", [NT, 512], BF16)
        OT = self.scratch(f"swaO{l}", [64, 32, NT], BF16)
        bQ, bK, bV, bO = {}, {}, {}, {}

        def gb(d, k):
            if k not in d:
                d[k] = Buf()
            return d[k]
        wq = self.din["swa_w_qk"][j]
        wv = self.din["swa_w_v"][j]
        wo = self.din["swa_w_out"][j]
        with ExitStack() as ph:
            self.big = self.sb(ph, [128, KC, TT], F32, "big")
            self.b_big = [Buf() for c in range(KC)]
            self.hT = self.sb(ph, [128, KC, TT], BF16, "hT")
            self.b_hT = [Buf() for c in range(KC)]
            hT = self.hT
            qst = self.sb(ph, [64, 40, TT], BF16, "qst")
            b_qst = [Buf() for _ in range(40)]
            vst = self.sb(ph, [128, 4, 512], BF16, "vst")
            b_vst = [Buf() for _ in range(4)]
            kf = self.sb(ph, [64, 8, TT], F32, "kf")
            vf = self.sb(ph, [128, 4, 512], F32, "vf")
            b_kf, b_vf = Buf(), Buf()
            for tile in self.tiles:
                t0, T, cond = tile
                self.pre(l, 0, tile)

                def x(kc):
                    return hT[:, kc, :T], [self.b_hT[kc]]
                rp = self.rope_pair(t0, T, lambda h: (qst[:, h, :T], [b_qst[h]]))

                def cons(ci, ps, pb, rp=rp, cond=cond, T=T):
                    rp(ci, ps, pb)
                    if cond == 1 and ci >= 64 and ci % 2 == 0:
                        S.add("act", lambda e: e.activation(out=kf[:, (ci - 64) // 2, :T], in_=ps, func=AF.Copy), reads=[pb], writes=[b_kf])
                self.linear_fm(wq, KC, 80, 8, x, T, cons, kp=128, cw=64)
                S.add("sp", lambda e, t0=t0, T=T: e.dma_start(out=QT[:, :, t0:t0 + T], in_=qst[:, 0:32, :T]), reads=b_qst[:32], writes=[gb(bQ, t0)], dma_key="swq")
                S.add("sp", lambda e, t0=t0, T=T: e.dma_start(out=KT[:, :, t0:t0 + T], in_=qst[:, 32:40, :T]), reads=b_qst[32:], writes=[gb(bK, t0)], dma_key="swk")

                def xv(kc, tb):
                    return hT[:, kc, tb * 128:(tb + 1) * 128], [self.b_hT[kc]]

                def consv(s_, tb, ps, pb, cond=cond):
                    S.add("act", lambda e: e.activation(out=vst[:, tb, :], in_=ps, func=AF.Copy), reads=[pb], writes=[b_vst[tb]])
                    if cond == 1:
                        S.add("dve", lambda e: e.tensor_copy(out=vf[:, tb, :], in_=ps), reads=[pb], writes=[b_vf])
                self.linear_tm(wv, KC, 1, xv, T, 128, consv)
                S.add("sp", lambda e, t0=t0, T=T: e.dma_start(out=VV[t0:t0 + T, :].rearrange("(b p) n -> p b n", p=128), in_=vst[:, :, :]), reads=b_vst, writes=[gb(bV, t0)], dma_key="swv")
                if cond == 1:
                    p0 = t0 - NS
                    S.add("sp", lambda e, p0=p0, T=T: e.dma_start(out=self.dout["swa_kout"][:, :, p0:p0 + T], in_=kf[:, :, :T]), reads=[b_kf], writes=[Buf()], dma_key="swko")
                    S.add("sp", lambda e, p0=p0, T=T: e.dma_start(out=self.dout["swa_vout"][p0:p0 + T, :].rearrange("(b p) n -> p b n", p=128), in_=vf[:, :, :]), reads=[b_vf], writes=[Buf()], dma_key="swvo")
            S.barrier()
        with ExitStack() as ph:
            self.pts = [(self.sb(ph, [128, TT], BF16, "pt"), Buf()) for _ in range(3)]
            self.ptrot = 0
            kctx = self.sb(ph, [64, 8, 256], BF16, "kctx")
            vctx = self.sb(ph, [128, 2, 512], BF16, "vctx")
            sinkr = self.sb(ph, [64, 32], F32, "sinkr")
            sinke = self.sb(ph, [64, 32], F32, "sinke")
            msk = self.sb(ph, [128, 2, 512], BF16, "msk")
            b_ctx = Buf()
            S.add("pool", lambda e: e.dma_start(out=kctx[:], in_=self.din["swa_kctxT"]), writes=[b_ctx], dma_key="sk1")
            S.add("pool", lambda e: e.dma_start(out=vctx[:], in_=self.din["swa_vctx"].rearrange("(b p) n -> p b n", p=128)), writes=[b_ctx], dma_key="sk2")
            S.add("pool", lambda e: e.dma_start(out=msk[:], in_=self.din["swa_masks"]), writes=[b_ctx], dma_key="sk3")
            S.add("sp", lambda e: e.dma_start(out=sinkr[:], in_=self.din["swa_sink_bc"][j]), writes=[b_ctx], dma_key="sk4")
            S.add("act", lambda e: e.activation(out=sinke[:], in_=sinkr[:], func=AF.Exp), reads=[b_ctx], writes=[b_ctx])
            NB = 2
            qb_ = [(self.sb(ph, [64, 32, 128], BF16, "qb"), Buf()) for _ in range(NB)]
            kl_ = [(self.sb(ph, [64, 8, 384], BF16, "kl"), Buf()) for _ in range(NB)]
            vl_ = [(self.sb(ph, [128, 3, 512], BF16, "vl"), Buf()) for _ in range(NB)]
            ob_ = [(self.sb(ph, [64, 32, 128], BF16, "ob"), Buf()) for _ in range(NB)]
            blocks = [(jb * 128, 0, NS, True) for jb in range(NS // 128)]
            for s_ in range(NP // SEQ):
                blocks += [(NS + s_ * SEQ + jb * 128, NS + s_ * SEQ, NS + (s_ + 1) * SEQ, False) for jb in range(SEQ // 128)]
            scale = 64 ** -0.5
            for bi, (q0, lo, hi, is_s) in enumerate(blocks):
                (qb, qbb), (kl, klb), (vl, vlb), (ob, obb) = qb_[bi % NB], kl_[bi % NB], vl_[bi % NB], ob_[bi % NB]
                tq = (q0 // TT) * TT
                S.add("sp", lambda e, qb=qb, q0=q0: e.dma_start(out=qb[:], in_=QT[:, :, q0:q0 + 128]), reads=[gb(bQ, tq)], writes=[qbb], dma_key="lq%d" % (bi % NB))
                if is_s:
                    kbs = [x_ for x_ in (q0 - 128, q0, q0 + 128) if lo <= x_ < hi]
                else:
                    kbs = list(range(lo, hi, 128))
                for i, k0 in enumerate(kbs):
                    tk = (k0 // TT) * TT
                    S.add("sp", lambda e, kl=kl, i=i, k0=k0: e.dma_start(out=kl[:, :, i * 128:(i + 1) * 128], in_=KT[:, :, k0:k0 + 128]), reads=[gb(bK, tk)], writes=[klb], dma_key="lk%d" % (bi % NB))
                    S.add("sp", lambda e, vl=vl, i=i, k0=k0: e.dma_start(out=vl[:, i, :], in_=VV[k0:k0 + 128, :]), reads=[gb(bV, tk)], writes=[vlb], dma_key="lv%d" % (bi % NB))
                nctx = 2 if is_s else 0
                for g in range(8):
                    def kpieces(kb, g=g, kl=kl, klb=klb):
                        if kb < nctx:
                            return [(kctx[:, g, kb * 128:(kb + 1) * 128], [b_ctx])]
                        i = kb - nctx
                        return [(kl[:, g, i * 128:(i + 1) * 128], [klb])]

                    def vblk(kb, g=g, vl=vl, vlb=vlb):
                        if kb < nctx:
                            return vctx[:, kb, g * 64:(g + 1) * 64], [b_ctx]
                        return vl[:, kb - nctx, g * 64:(g + 1) * 64], [vlb]

                    def maskf(kb, kbs=kbs, q0=q0):
                        if kb < nctx or not is_s:
                            return None
                        k0 = kbs[kb - nctx]
                        if k0 < q0:
                            return msk[:, 0, :], [b_ctx]
                        if k0 > q0:
                            return msk[:, 1, :], [b_ctx]
                        return None
                    qa = qb[:, 4 * g:4 * g + 4, :]
                    oa = ob[:, 4 * g:4 * g + 4, :].rearrange("p g t -> p (g t)")
                    sk = sinke[:, 4 * g:4 * g + 4].unsqueeze(2).broadcast_to([64, 4, 128])
                    self.attend([(qa, [qbb])], nctx + len(kbs), kpieces, vblk, 512, 64, scale, maskf, self.attn_finish(oa, [obb], sink_ap=sk))
                S.add("sp", lambda e, ob=ob, q0=q0: e.dma_start(out=OT[:, :, q0:q0 + 128], in_=ob[:]), reads=[obb], writes=[gb(bO, tq)], dma_key="so%d" % (bi % NB))
            S.barrier()
        with ExitStack() as ph:
            self.big = self.sb(ph, [128, KC, TT], F32, "big")
            self.b_big = [Buf() for c in range(KC)]
            oT = self.sb(ph, [64, 32, TT], BF16, "oTt")
            b_oT = Buf()
            for tile in self.tiles:
                t0, T, cond = tile
                S.add("sp", lambda e, t0=t0, T=T: e.dma_start(out=oT[:, :, :T], in_=OT[:, :, t0:t0 + T]), reads=[gb(bO, t0)], writes=[b_oT], dma_key="swo")

                def x(kc):
                    return oT[:, kc, :T], [b_oT]
                self.linear_fm(wo, 32, 16, 2, x, T, self.post_consume(T), kp=64)
                self.post(l, 0, tile, False)
            S.barrier()


    def mla_layer(self, l, j):
        nc, S = self.nc, self.S
        NT, NS, NP, SEQ = self.NT, self.NS, self.NP, self.SEQ
        NKEY = 256 + NT
        QN = self.scratch(f"mlaQN{l}", [16, 128, NT], BF16)
        QR = self.scratch(f"mlaQR{l}", [16, 64, NT], BF16)
        OT = self.scratch(f"mlaO{l}", [128, 16, NT], BF16)
        bQ, bO = {}, {}

        def gb(d, k):
            if k not in d:
                d[k] = Buf()
            return d[k]
        wdn = self.din["mla_w_down"][j]
        wuq = self.din["mla_w_uq"][j]
        wo = self.din["mla_w_out"][j]
        with ExitStack() as allph:
            ckv_all = self.sb(allph, [128, 4, NKEY], BF16, "ckvall")
            kpe_all = self.sb(allph, [64, NKEY], BF16, "kpeall")
            b_ckv = [Buf() for _ in range((NKEY + TT - 1) // TT + 1)]
            gq = self.sb(allph, [128, 2, 4], F32, "gq")
            b_g = Buf()
            S.add("sp", lambda e: e.dma_start(out=gq[:], in_=self.din["mla_gT"][j]), writes=[b_g], dma_key="mg")
            S.add("pool", lambda e: e.dma_start(out=ckv_all[:, :, 0:256], in_=self.din["mla_ckv_ctxT"]), writes=[b_ckv[0]], dma_key="mc1")
            S.add("pool", lambda e: e.dma_start(out=kpe_all[:, 0:256], in_=self.din["mla_kpe_ctxT"]), writes=[b_ckv[0]], dma_key="mc2")
            with ExitStack() as ph:
                self.big = self.sb(ph, [128, KC, TT], F32, "big")
                self.b_big = [Buf() for c in range(KC)]
                self.hT = self.sb(ph, [128, KC, TT], BF16, "hT")
                self.b_hT = [Buf() for c in range(KC)]
                hT = self.hT
                cf = self.sb(ph, [128, 8, TT], F32, "cf")
                b_cf = [Buf() for _ in range(8)]
                cqn = self.sb(ph, [128, 4, TT], BF16, "cqn")
                b_cqn = [Buf() for _ in range(4)]
                kpf = self.sb(ph, [64, TT], F32, "kpf")
                b_kpf = Buf()
                qn = self.sb(ph, [128, 16, TT], BF16, "qn")
                qr = self.sb(ph, [64, 16, TT], BF16, "qr")
                b_qn = [Buf() for _ in range(16)]
                b_qr = [Buf() for _ in range(16)]
                r2 = self.sb(ph, [128, TT], F32, "r2")
                b_r2 = Buf()
                for ti, tile in enumerate(self.tiles):
                    t0, T, cond = tile
                    k0 = 256 + t0
                    bk = b_ckv[1 + ti]
                    self.pre(l, 0, tile)

                    def x(kc):
                        return hT[:, kc, :T], [self.b_hT[kc]]
                    rp = self.rope_pair(t0, T, lambda h: (kpe_all[:, k0:k0 + T], [bk]))
                    st2 = self.miscps[0]

                    def cons(ci, ps, pb, T=T, rp=rp, cond=cond):
                        if ci < 8:
                            S.add("act", lambda e: e.activation(out=cf[:, ci, :T], in_=ps, func=AF.Copy), reads=[pb], writes=[b_cf[ci]])
                            sq, sqb = self.sqs[self.sqrot % len(self.sqs)]
                            self.sqrot += 1
                            S.add("act", lambda e: e.activation(out=sq[:, :T], in_=cf[:, ci, :T], func=AF.Square), reads=[b_cf[ci]], writes=[sqb])
                            st, stb = self.statps if ci < 4 else st2
                            S.add("pe", lambda e: e.matmul(st[:, :T], lhsT=self.ones_bf[:, :], rhs=sq[:, :T], start=(ci % 4 == 0), stop=(ci % 4 == 3)),
                                  reads=[sqb, self.b_const], writes=[stb])
                        else:
                            rp(ci - 8, ps, pb)
                            if ci == 8 and cond == 1:
                                S.add("act", lambda e: e.activation(out=kpf[:, :T], in_=ps, func=AF.Copy), reads=[pb], writes=[b_kpf])
                    self.linear_fm(wdn, KC, 10, 4, x, T, cons, widths=[128] * 8 + [64, 64])
                    self.rstd_from(self.statps[0][:, :T], self.statps[1], 512, self.rstd[:, :T], self.b_rstd)
                    self.rstd_from(st2[0][:, :T], st2[1], 512, r2[:, :T], b_r2)
                    for c in range(4):
                        S.add("dve", lambda e, c=c, T=T: e.scalar_tensor_tensor(out=cqn[:, c, :T], in0=cf[:, c, :T], scalar=gq[:, 0, c:c + 1], in1=self.rstd[:, :T], op0=ALU.mult, op1=ALU.mult),
                              reads=[b_cf[c], b_g, self.b_rstd], writes=[b_cqn[c]])
                        S.add("dve", lambda e, c=c, T=T: e.scalar_tensor_tensor(out=cf[:, 4 + c, :T], in0=cf[:, 4 + c, :T], scalar=gq[:, 1, c:c + 1], in1=r2[:, :T], op0=ALU.mult, op1=ALU.mult),
                              reads=[b_cf[4 + c], b_g, b_r2], writes=[b_cf[4 + c]])
                        S.add("pool", lambda e, c=c, k0=k0, T=T: e.tensor_copy(out=ckv_all[:, c, k0:k0 + T], in_=cf[:, 4 + c, :T]), reads=[b_cf[4 + c]], writes=[bk])
                    if cond == 1:
                        p0 = t0 - NS
                        S.add("sp", lambda e, p0=p0, T=T: e.dma_start(out=self.dout["mla_ckvout"][:, :, p0:p0 + T].rearrange("c p t -> p c t"), in_=cf[:, 4:8, :T]),
                              reads=b_cf[4:8], writes=[Buf()], dma_key="mco")
                        S.add("sp", lambda e, p0=p0, T=T: e.dma_start(out=self.dout["mla_kpeout"][:, p0:p0 + T], in_=kpf[:, :T]), reads=[b_kpf], writes=[Buf()], dma_key="mko")

                    def xq(kc):
                        return cqn[:, kc, :T], [b_cqn[kc]]
                    rq = self.rope_pair(t0, T, lambda h: (qr[:, h, :T], [b_qr[h]]))

                    def consq(ci, ps, pb, T=T, rq=rq):
                        if ci < 16:
                            S.add("act", lambda e: e.activation(out=qn[:, ci, :T], in_=ps, func=AF.Copy), reads=[pb], writes=[b_qn[ci]])
                        else:
                            rq(ci - 16, ps, pb)
                    self.linear_fm(wuq, 4, 48, 16, xq, T, consq, widths=[128] * 16 + [64] * 32)
                    S.add("sp", lambda e, t0=t0, T=T: e.dma_start(out=QN[:, :, t0:t0 + T].rearrange("h p t -> p h t"), in_=qn[:, :, :T]), reads=b_qn, writes=[gb(bQ, t0)], dma_key="mqn")
                    S.add("sp", lambda e, t0=t0, T=T: e.dma_start(out=QR[:, :, t0:t0 + T].rearrange("h p t -> p h t"), in_=qr[:, :, :T]), reads=b_qr, writes=[gb(bQ, t0)], dma_key="mqr")
                S.barrier()
            with ExitStack() as ph:
                self.pts = [(self.sb(ph, [128, TT], BF16, "pt"), Buf()) for _ in range(3)]
                self.ptrot = 0
                wkv = self.sb(ph, [128, 4, 4096], BF16, "wkv")
                b_wkv = Buf()
                S.add("pool", lambda e: e.dma_start(out=wkv[:], in_=self.din["mla_w_ukvT"][j]), writes=[b_wkv], dma_key="mwkv")
                NB = 2
                kth_ = [(self.sb(ph, [128, NKEY], BF16, "kth"), Buf()) for _ in range(NB)]
                vh_ = [(self.sb(ph, [128, NKEY // 128, 128], BF16, "vh"), Buf()) for _ in range(NB)]
                qnh_ = [(self.sb(ph, [128, NT], BF16, "qnh"), Buf())] * NB
                qrh_ = [(self.sb(ph, [64, NT], BF16, "qrh"), Buf())] * NB
                oh_ = [(self.sb(ph, [128, TT], BF16, "oh"), Buf()) for _ in range(3)]
                orot = 0
                allck = b_ckv
                scale = 192 ** -0.5
                for h in range(16):
                    (kth, kthb), (vh, vhb), (qnh, qnhb), (qrh, qrhb) = kth_[h % NB], vh_[h % NB], qnh_[h % NB], qrh_[h % NB]
                    S.add("sp", lambda e, qnh=qnh, h=h: e.dma_start(out=qnh[:], in_=QN[h]), reads=list(bQ.values()), writes=[qnhb], dma_key="mlq")
                    S.add("sp", lambda e, qrh=qrh, h=h: e.dma_start(out=qrh[:], in_=QR[h]), reads=list(bQ.values()), writes=[qrhb], dma_key="mlr")
                    for kt in range(0, NKEY, TT):
                        w = min(TT, NKEY - kt)
                        ps, pb = self.linps[self.linrot % len(self.linps)]
                        self.linrot += 1
                        for kc in range(4):
                            S.add("pe", lambda e, ps=ps, kc=kc, kt=kt, w=w, h=h: e.matmul(ps[:, :w], lhsT=wkv[:, kc, h * 256:h * 256 + 128], rhs=ckv_all[:, kc, kt:kt + w], start=(kc == 0), stop=(kc == 3)),
                                  reads=[b_wkv] + allck, writes=[pb])
                        S.add("act", lambda e, ps=ps, kt=kt, w=w, kth=kth: e.activation(out=kth[:, kt:kt + w], in_=ps[:, :w], func=AF.Copy), reads=[pb], writes=[kthb])
                    for kb4 in range(0, NKEY // 128, 4):
                        nb4 = min(4, NKEY // 128 - kb4)
                        ps, pb = self.linps[self.linrot % len(self.linps)]
                        self.linrot += 1
                        for i in range(nb4):
                            kb = kb4 + i
                            for kc in range(4):
                                S.add("pe", lambda e, ps=ps, kc=kc, kb=kb, i=i, h=h: e.matmul(ps[:, i * 128:(i + 1) * 128], lhsT=ckv_all[:, kc, kb * 128:(kb + 1) * 128], rhs=wkv[:, kc, h * 256 + 128:h * 256 + 256], start=(kc == 0), stop=(kc == 3)),
                                      reads=[b_wkv] + allck, writes=[pb])
                        S.add("dve", lambda e, ps=ps, kb4=kb4, nb4=nb4, vh=vh: e.tensor_copy(out=vh[:, kb4:kb4 + nb4, :].rearrange("p b d -> p (b d)"), in_=ps[:, :nb4 * 128]), reads=[pb], writes=[vhb])
                    qts = [(t0, TT, 0, (256 + NS) // 128) for t0 in range(0, NS, TT)]
                    for s_ in range(NP // SEQ):
                        qts.append((NS + s_ * SEQ, SEQ, (256 + NS + s_ * SEQ) // 128, SEQ // 128))
                    for (q0, Tq, kb0, nkb) in qts:
                        oh, ohb = oh_[orot % 3]
                        orot += 1

                        def kpieces(kb, kb0=kb0, kth=kth, kthb=kthb):
                            a = (kb0 + kb) * 128
                            return [(kth[:, a:a + 128], [kthb]), (kpe_all[:, a:a + 128], allck)]

                        def vblk(kb, kb0=kb0, vh=vh, vhb=vhb):
                            return vh[:, kb0 + kb, :], [vhb]
                        tq = (q0 // TT) * TT
                        self.attend([(qnh[:, q0:q0 + Tq], [qnhb]), (qrh[:, q0:q0 + Tq], [qrhb])], nkb, kpieces, vblk, Tq, 128, scale, None,
                                    self.attn_finish(oh[:, :Tq], [ohb]))
                        S.add("sp", lambda e, oh=oh, h=h, q0=q0, Tq=Tq: e.dma_start(out=OT[:, h, q0:q0 + Tq], in_=oh[:, :Tq]), reads=[ohb], writes=[gb(bO, (tq, h, q0))], dma_key="mo%d" % (orot % 3))
                S.barrier()
        with ExitStack() as ph:
            self.big = self.sb(ph, [128, KC, TT], F32, "big")
            self.b_big = [Buf() for c in range(KC)]
            oT = self.sb(ph, [128, 16, TT], BF16, "oTt")
            b_oT = Buf()
            for tile in self.tiles:
                t0, T, cond = tile
                S.add("sp", lambda e, t0=t0, T=T: e.dma_start(out=oT[:, :, :T], in_=OT[:, :, t0:t0 + T]), reads=[b for k_, b in bO.items() if k_[0] == t0], writes=[b_oT], dma_key="mlo")

                def x(kc):
                    return oT[:, kc, :T], [b_oT]
                self.linear_fm(wo, KC, 16, 4, x, T, self.post_consume(T))
                self.post(l, 0, tile, False)
            S.barrier()


    def hgrn_setup(self, g):
        S, L = self.S, self.L
        lg = self.sb(g, [128, 2, L, 16], F32, "lbl")
        self.lb = self.sb(g, [128, 2, L, 16], F32, "lb")
        self.oml = self.sb(g, [128, 2, L, 16], F32, "oml")
        sm = self.sb(g, [128, 2, 16], F32, "lbs")
        self.b_lb = Buf()
        b = self.b_lb
        S.add("sp", lambda e: e.dma_start(out=lg[:], in_=self.din["hg_lbT"]), writes=[b], dma_key="hlb")
        S.add("act", lambda e: e.activation(out=lg[:], in_=lg[:], func=AF.Exp), reads=[b], writes=[b])
        S.add("dve", lambda e: e.tensor_copy(out=sm[:], in_=lg[:, :, 0, :]), reads=[b], writes=[b])
        for i in range(1, L):
            S.add("dve", lambda e, i=i: e.tensor_tensor(out=sm[:], in0=sm[:], in1=lg[:, :, i, :], op=ALU.add), reads=[b], writes=[b])
        S.add("dve", lambda e: e.reciprocal(out=sm[:], in_=sm[:]), reads=[b], writes=[b])
        S.add("dve", lambda e: e.memset(self.lb[:, :, 0, :], 0.0), writes=[b])
        for i in range(1, L):
            S.add("dve", lambda e, i=i: e.tensor_tensor(out=lg[:, :, i, :], in0=lg[:, :, i, :], in1=sm[:], op=ALU.mult), reads=[b], writes=[b])
            S.add("dve", lambda e, i=i: e.tensor_tensor(out=self.lb[:, :, i, :], in0=self.lb[:, :, i - 1, :], in1=lg[:, :, i, :], op=ALU.add), reads=[b], writes=[b])
        S.add("dve", lambda e: e.tensor_scalar(out=self.oml[:], in0=self.lb[:], scalar1=-1.0, scalar2=1.0, op0=ALU.mult, op1=ALU.add), reads=[b], writes=[b])

    def hgrn_layer(self, l, j):
        nc, S = self.nc, self.S
        NT, NS, NP, SEQ = self.NT, self.NS, self.NP, self.SEQ
        NCH = NT // 64
        Q2 = self.scratch(f"hgQ2{l}", [16, 128, NT], BF16)
        K2 = self.scratch(f"hgK2{l}", [16, 128, NT], BF16)
        D2 = self.scratch(f"hgD2{l}", [16, 128, NCH, 3], F32)
        V64 = self.scratch(f"hgV{l}", [NT, 2048], BF16)
        GS = self.scratch(f"hgG{l}", [NT, 2048], BF16)
        O1 = self.scratch(f"hgO1{l}", [NT, 2048], F32)
        bsc = {}

        def gb(k):
            if k not in bsc:
                bsc[k] = Buf()
            return bsc[k]
        wqf = self.din["hg_w_qf"][j]
        wig = self.din["hg_w_ig"][j]
        wo = self.din["hg_w_out"][j]
        lb, oml = self.lb, self.oml
        with ExitStack() as allph:
            Sst = [self.sb(allph, [128, 16, 128], F32, "Sst") for _ in range(2)]
            b_S = [[Buf() for _ in range(16)] for _ in range(2)]
            hmask = self.sb(allph, [64, 2, 64], BF16, "hmask")
            ident = self.sb(allph, [128, 128], BF16, "ident")
            onesf = self.sb(allph, [128, TT], F32, "onesf")
            hgbc = self.sb(allph, [64, 2048], F32, "hgbc")
            b_hc = Buf()
            S.add("pool", lambda e: e.dma_start(out=hmask[:], in_=self.din["hg_masks"]), writes=[b_hc], dma_key="hm1")
            S.add("pool", lambda e: e.dma_start(out=ident[:], in_=self.din["ident"]), writes=[b_hc], dma_key="hm2")
            S.add("sp", lambda e: e.dma_start(out=hgbc[:], in_=self.din["hg_gbc"][j]), writes=[b_hc], dma_key="hm3")
            S.add("dve", lambda e: e.memset(onesf[:], 1.0), writes=[b_hc])
            sbf_ = [(self.sb(allph, [128, 128], BF16, "sbf"), Buf()) for _ in range(3)]
            t1_ = [(self.sb(allph, [128, 128], F32, "t1"), Buf()) for _ in range(3)]
            am_ = [(self.sb(allph, [64, 64], BF16, "am"), Buf()) for _ in range(3)]
            amf_ = [(self.sb(allph, [64, 64], F32, "amf"), Buf()) for _ in range(3)]
            hmaskf = self.sb(allph, [64, 2, 64], F32, "hmaskf")
            S.add("sp", lambda e: e.dma_start(out=hmaskf[:], in_=self.din["hg_masks"]), writes=[b_hc], dma_key="hm4")
            kt_ = [(self.sb(allph, [64, 128], BF16, "ktok"), Buf()) for _ in range(3)]
            rot = {"sbf": 0, "t1": 0, "am": 0, "amf": 0, "kt": 0, "ps": 0, "tr": 0}
            slots = [(self.miscps[i][0], 0, self.miscps[i][1]) for i in range(3)]
            trslots = [(q4, self.b_pstr) for q4 in range(8)]

            def slot():
                r = slots[rot["ps"] % len(slots)]
                rot["ps"] += 1
                return r

            def nxt(lst, k):
                r = lst[rot[k] % len(lst)]
                rot[k] += 1
                return r

            def chunk_step(d, h, qt, qtb, kt, ktb, dv, dvb, v_ap, v_b, o_sink):
                St = Sst[d]
                bS = b_S[d][h]
                pa, ca, ba = slot()
                S.add("pe", lambda e: e.matmul(pa[:64, ca * 128:ca * 128 + 64], lhsT=kt, rhs=qt, start=True, stop=True), reads=qtb + ktb, writes=[ba])
                am, amb = nxt(am_, "am")
                amf, amfb = nxt(amf_, "amf")
                S.add("dve", lambda e: e.tensor_scalar(out=amf[:], in0=pa[:64, ca * 128:ca * 128 + 64], scalar1=1e30, scalar2=-1e30, op0=ALU.min, op1=ALU.max), reads=[ba], writes=[amfb])
                S.add("pool", lambda e: e.tensor_tensor(out=am[:], in0=amf[:], in1=hmaskf[:, d, :], op=ALU.mult), reads=[amfb, b_hc], writes=[amb])
                q4, trb = trslots[rot["tr"] % 8]
                rot["tr"] += 1
                S.add("pe", lambda e: e.transpose(self.pstr[:64, q4 * 128:(q4 + 1) * 128], kt, ident[:, :]), reads=ktb + [b_hc], writes=[trb])
                ktok, ktokb = nxt(kt_, "kt")
                S.add("act", lambda e: e.activation(out=ktok[:], in_=self.pstr[:64, q4 * 128:(q4 + 1) * 128], func=AF.Copy), reads=[trb], writes=[ktokb])
                sbf, sbfb = nxt(sbf_, "sbf")
                S.add("pool", lambda e: e.tensor_scalar(out=sbf[:], in0=St[:, h, :], scalar1=dv[:, 0:1], scalar2=None, op0=ALU.mult), reads=[bS] + dvb, writes=[sbfb])
                po, co, bo = slot()
                S.add("pe", lambda e: e.matmul(po[:64, co * 128:(co + 1) * 128], lhsT=qt, rhs=sbf[:], start=True, stop=False), reads=qtb + [sbfb], writes=[bo])
                S.add("pe", lambda e: e.matmul(po[:64, co * 128:(co + 1) * 128], lhsT=am[:], rhs=v_ap, start=False, stop=True), reads=[amb] + v_b, writes=[bo])
                o_sink(po[:64, co * 128:(co + 1) * 128], bo)
                pk, ck, bk = slot()
                S.add("pe", lambda e: e.matmul(pk[:, ck * 128:(ck + 1) * 128], lhsT=ktok[:], rhs=v_ap, start=True, stop=True), reads=[ktokb] + v_b, writes=[bk])
                t1, t1b = nxt(t1_, "t1")
                S.add("dve", lambda e: e.tensor_scalar(out=t1[:], in0=St[:, h, :], scalar1=dv[:, 2:3], scalar2=None, op0=ALU.mult), reads=[bS] + dvb, writes=[t1b])
                S.add("dve", lambda e: e.scalar_tensor_tensor(out=St[:, h, :], in0=pk[:, ck * 128:(ck + 1) * 128], scalar=dv[:, 1:2], in1=t1[:], op0=ALU.mult, op1=ALU.add),
                      reads=[bk, t1b] + dvb, writes=[bS])

            with ExitStack() as ph:
                self.big = self.sb(ph, [128, KC, TT], F32, "big")
                self.b_big = [Buf() for c in range(KC)]
                self.hT = self.sb(ph, [128, KC, TT], BF16, "hT")
                self.b_hT = [Buf() for c in range(KC)]
                hT = self.hT
                v64 = self.sb(ph, [64, 8, 2048], BF16, "v64")
                b_v64 = [Buf() for _ in range(8)]
                gst_ = [(self.sb(ph, [64, 512], BF16, "gst"), Buf()) for _ in range(2)]
                gtmp = self.sb(ph, [64, 512], F32, "gtmp")
                b_gtmp = Buf()
                qs_ = [(self.sb(ph, [128, TT], F32, "qs"), Buf()) for _ in range(2)]
                ft = {n: (self.sb(ph, [128, TT], F32, n), Buf()) for n in ["f", "g", "B", "X", "E", "eq", "ek"]}
                qk_ = [[(self.sb(ph, [128, TT], BF16, "qkt"), Buf()) for _ in range(2)] for _ in range(4)]
                dvt_ = [(self.sb(ph, [128, 8, 3], F32, "dvt"), Buf()) for _ in range(4)]
                o1h_ = [(self.sb(ph, [64, 8, 128], F32, "o1h"), Buf()) for _ in range(2)]
                S.add("sp", lambda e: e.dma_start(out=Sst[0][:], in_=self.din["hg_s0"][j, 0]), writes=b_S[0], dma_key="hs0")
                hcount = 0
                for ti, tile in enumerate(self.tiles):
                    t0, T, cond = tile
                    self.pre(l, 0, tile)

                    def xv(kc, tb):
                        return hT[:, kc, tb * 64:(tb + 1) * 64], [self.b_hT[kc]]

                    def consv(s_, tb, ps, pb, t0=t0):
                        if s_ < 4:
                            S.add("act", lambda e: e.activation(out=v64[:, tb, s_ * 512:(s_ + 1) * 512], in_=ps, func=AF.Copy), reads=[pb], writes=[b_v64[tb]])
                        else:
                            gst, gstb = gst_[(s_ * 8 + tb) % 2]
                            S.add("act", lambda e: e.activation(out=gtmp[:], in_=ps, func=AF.Silu), reads=[pb], writes=[b_gtmp])
                            S.add("dve", lambda e: e.tensor_tensor(out=gst[:], in0=gtmp[:], in1=hgbc[:, (s_ - 4) * 512:(s_ - 3) * 512], op=ALU.mult), reads=[b_gtmp, b_hc], writes=[gstb])
                            r0 = t0 + tb * 64
                            S.add("sp", lambda e: e.dma_start(out=GS[r0:r0 + 64, (s_ - 4) * 512:(s_ - 3) * 512], in_=gst[:]), reads=[gstb], writes=[gb(("G", t0))], dma_key="hg%d" % ((s_ * 8 + tb) % 2))
                    self.linear_tm(wig, KC, 8, xv, T, 64, consv)
                    S.add("sp", lambda e, t0=t0: e.dma_start(out=V64[t0:t0 + TT, :].rearrange("(c p) n -> p c n", p=64), in_=v64[:]), reads=b_v64, writes=[gb(("V", t0))], dma_key="hv")

                    def x(kc):
                        return hT[:, kc, :T], [self.b_hT[kc]]
                    stq = {}

                    def cons(ci, ps, pb, t0=t0, cond=cond, ti=ti):
                        h, kind = divmod(ci, 3)
                        if kind == 0:
                            qs, qsb = qs_[h % 2]
                            stq["qs"] = (qs, qsb)
                            S.add("act", lambda e: e.activation(out=qs[:], in_=ps, func=AF.Silu), reads=[pb], writes=[qsb])
                            return
                        d = kind - 1
                        qs, qsb = stq["qs"]
                        (f, fb), (g_, gb_), (B, Bb), (X, Xb), (E, Eb), (eq, eqb), (ek, ekb) = [ft[n] for n in ["f", "g", "B", "X", "E", "eq", "ek"]]
                        S.add("act", lambda e: e.activation(out=f[:], in_=ps, func=AF.Sigmoid), reads=[pb], writes=[fb])
                        S.add("dve", lambda e: e.tensor_scalar(out=f[:], in0=f[:], scalar1=oml[:, d, l, h:h + 1], scalar2=lb[:, d, l, h:h + 1], op0=ALU.mult, op1=ALU.add), reads=[fb, self.b_lb], writes=[fb])
                        S.add("act", lambda e: e.activation(out=g_[:], in_=f[:], func=AF.Ln), reads=[fb], writes=[gb_])
                        S.add("dve", lambda e: e.tensor_tensor_scan(out=B[:], data0=onesf[:], data1=g_[:], initial=0.0, op0=ALU.mult, op1=ALU.add), reads=[gb_, b_hc], writes=[Bb])
                        S.add("pool", lambda e: e.tensor_tensor(out=X[:], in0=B[:], in1=g_[:], op=ALU.subtract), reads=[Bb, gb_], writes=[Xb])
                        S.add("pool", lambda e: e.tensor_scalar(out=f[:], in0=f[:], scalar1=-1.0, scalar2=1.0, op0=ALU.mult, op1=ALU.add), reads=[fb], writes=[fb])
                        B3 = B[:].rearrange("p (c t) -> p c t", t=64)
                        X3 = X[:].rearrange("p (c t) -> p c t", t=64)
                        E3 = E[:].rearrange("p (c t) -> p c t", t=64)
                        if d == 0:
                            S.add("dve", lambda e: e.tensor_tensor(out=E3, in0=B3, in1=B3[:, :, 32:33].broadcast_to([128, 8, 64]), op=ALU.subtract), reads=[Bb], writes=[Eb])
                        else:
                            S.add("dve", lambda e: e.tensor_tensor(out=E3, in0=X3[:, :, 32:33].broadcast_to([128, 8, 64]), in1=X3, op=ALU.subtract), reads=[Xb], writes=[Eb])
                        S.add("act", lambda e: e.activation(out=eq[:], in_=E[:], func=AF.Exp), reads=[Eb], writes=[eqb])
                        S.add("act", lambda e: e.activation(out=ek[:], in_=E[:], func=AF.Exp, scale=-1.0), reads=[Eb], writes=[ekb])
                        (qt, qtb) = qk_[2 * d][hcount_ref[0] % 2]
                        (kt, ktb) = qk_[2 * d + 1][hcount_ref[0] % 2]
                        (dvt, dvb) = dvt_[2 * d + hcount_ref[0] % 2]
                        S.add("dve", lambda e: e.scalar_tensor_tensor(out=qt[:], in0=qs[:], scalar=128 ** -0.5, in1=eq[:], op0=ALU.mult, op1=ALU.mult), reads=[qsb, eqb], writes=[qtb])
                        S.add("pool", lambda e: e.tensor_tensor(out=kt[:], in0=f[:], in1=ek[:], op=ALU.mult), reads=[fb, ekb], writes=[ktb])
                        mid = (B3 if d == 0 else X3)[:, :, 32:33]
                        if d == 0:
                            S.add("pool", lambda e: e.tensor_tensor(out=dvt[:, :, 0:1], in0=mid, in1=X3[:, :, 0:1], op=ALU.subtract), reads=[Bb, Xb], writes=[dvb])
                            S.add("pool", lambda e: e.tensor_tensor(out=dvt[:, :, 1:2], in0=B3[:, :, 63:64], in1=mid, op=ALU.subtract), reads=[Bb, Xb], writes=[dvb])
                        else:
                            S.add("pool", lambda e: e.tensor_tensor(out=dvt[:, :, 0:1], in0=B3[:, :, 63:64], in1=mid, op=ALU.subtract), reads=[Bb, Xb], writes=[dvb])
                            S.add("pool", lambda e: e.tensor_tensor(out=dvt[:, :, 1:2], in0=mid, in1=X3[:, :, 0:1], op=ALU.subtract), reads=[Bb, Xb], writes=[dvb])
                        S.add("pool", lambda e: e.tensor_tensor(out=dvt[:, :, 2:3], in0=B3[:, :, 63:64], in1=X3[:, :, 0:1], op=ALU.subtract), reads=[Bb, Xb], writes=[dvb])
                        S.add("act", lambda e: e.activation(out=dvt[:], in_=dvt[:], func=AF.Exp), reads=[dvb], writes=[dvb])
                        if d == 0:
                            o1h, o1hb = o1h_[h % 2]
                            for c in range(8):
                                if cond == 1 and c % (SEQ // 64) == 0:
                                    S.add("pool", lambda e: e.memset(Sst[0][:, h, :], 0.0), writes=[b_S[0][h]])

                                def sink(po, bo, c=c):
                                    S.add("act", lambda e: e.activation(out=o1h[:, c, :], in_=po, func=AF.Copy), reads=[bo], writes=[o1hb])
                                chunk_step(0, h, qt[:, c * 64:(c + 1) * 64], [qtb], kt[:, c * 64:(c + 1) * 64], [ktb], dvt[:, c, :], [dvb],
                                           v64[:, c, h * 128:(h + 1) * 128], [b_v64[c]], sink)
                                if cond == 1 and (c + 1) % (SEQ // 64) == 0:
                                    sq_ = c // (SEQ // 64)
                                    S.add("sp", lambda e, sq_=sq_: e.dma_start(out=self.dout["hg_stout"][j, sq_, 0, :, h, :], in_=Sst[0][:, h, :]), reads=[b_S[0][h]], writes=[Buf()], dma_key="hso%d" % (h % 4))
                            S.add("sp", lambda e: e.dma_start(out=O1[t0:t0 + TT, h * 128:(h + 1) * 128].rearrange("(c p) v -> p c v", p=64), in_=o1h[:]), reads=[o1hb], writes=[gb(("O", t0, h))], dma_key="ho%d" % (h % 2))
                        else:
                            S.add("sp", lambda e: e.dma_start(out=Q2[h, :, t0:t0 + TT], in_=qt[:]), reads=[qtb], writes=[gb(("Q", t0, h))], dma_key="hq%d" % (hcount_ref[0] % 2))
                            S.add("sp", lambda e: e.dma_start(out=K2[h, :, t0:t0 + TT], in_=kt[:]), reads=[ktb], writes=[gb(("K", t0, h))], dma_key="hk%d" % (hcount_ref[0] % 2))
                            S.add("sp", lambda e: e.dma_start(out=D2[h, :, ti * 8:(ti + 1) * 8, :], in_=dvt[:]), reads=[dvb], writes=[gb(("D", t0, h))], dma_key="hd%d" % (hcount_ref[0] % 2))
                            hcount_ref[0] += 1
                    hcount_ref = [hcount]
                    self.linear_fm(wqf, KC, 48, 4, x, T, cons)
                    hcount = hcount_ref[0]
                S.barrier()
            with ExitStack() as ph:
                self.big = self.sb(ph, [128, KC, TT], F32, "big")
                self.b_big = [Buf() for c in range(KC)]
                oT = self.sb(ph, [128, 16, TT], BF16, "oT")
                b_oT = [Buf() for _ in range(8)]
                NB = 2
                q2c_ = [(self.sb(ph, [128, 16, 64], BF16, "q2c"), Buf()) for _ in range(NB)]
                k2c_ = [(self.sb(ph, [128, 16, 64], BF16, "k2c"), Buf()) for _ in range(NB)]
                vc_ = [(self.sb(ph, [64, 2048], BF16, "vc"), Buf()) for _ in range(NB)]
                gc_ = [(self.sb(ph, [64, 2048], BF16, "gc"), Buf()) for _ in range(NB)]
                o1c_ = [(self.sb(ph, [64, 2048], F32, "o1c"), Buf()) for _ in range(NB)]
                d2t = self.sb(ph, [128, 16, 8, 3], F32, "d2t")
                b_d2t = Buf()
                osum = self.sb(ph, [64, 16, 128], F32, "osum")
                b_osum = [Buf() for _ in range(16)]
                sqt = self.sb(ph, [64, 2048], F32, "sqt")
                b_sqt = Buf()
                ssq = self.sb(ph, [64, 16], F32, "ssq")
                b_ssq = Buf()
                obf = self.sb(ph, [64, 2048], BF16, "obf")
                b_obf = Buf()
                order = [t for t in self.tiles if t[2] == 1] + [t for t in reversed(self.tiles) if t[2] == 0]
                first_sample = True
                ci_ = 0
                for tile in order:
                    t0, T, cond = tile
                    ti = t0 // TT
                    if cond == 0 and first_sample:
                        first_sample = False
                        S.add("sp", lambda e: e.dma_start(out=Sst[1][:], in_=self.din["hg_s0"][j, 1]), writes=b_S[1], dma_key="hs1")
                    S.add("sp", lambda e, ti=ti: e.dma_start(out=d2t[:], in_=D2[:, :, ti * 8:(ti + 1) * 8, :].rearrange("h p c k -> p h c k")), reads=[gb(("D", t0, h)) for h in range(16)], writes=[b_d2t], dma_key="hd2")
                    for c in reversed(range(8)):
                        r0 = t0 + c * 64
                        (q2c, q2b), (k2c, k2b), (vc, vcb), (gc, gcb), (o1c, o1b) = q2c_[ci_ % NB], k2c_[ci_ % NB], vc_[ci_ % NB], gc_[ci_ % NB], o1c_[ci_ % NB]
                        kk = ci_ % NB
                        ci_ += 1
                        S.add("sp", lambda e, q2c=q2c, r0=r0: e.dma_start(out=q2c[:], in_=Q2[:, :, r0:r0 + 64].rearrange("h p t -> p h t")), reads=[gb(("Q", t0, h)) for h in range(16)], writes=[q2b], dma_key="p2q%d" % kk)
                        S.add("sp", lambda e, k2c=k2c, r0=r0: e.dma_start(out=k2c[:], in_=K2[:, :, r0:r0 + 64].rearrange("h p t -> p h t")), reads=[gb(("K", t0, h)) for h in range(16)], writes=[k2b], dma_key="p2k%d" % kk)
                        S.add("sp", lambda e, vc=vc, r0=r0: e.dma_start(out=vc[:], in_=V64[r0:r0 + 64, :]), reads=[gb(("V", t0))], writes=[vcb], dma_key="p2v%d" % kk)
                        S.add("sp", lambda e, gc=gc, r0=r0: e.dma_start(out=gc[:], in_=GS[r0:r0 + 64, :]), reads=[gb(("G", t0))], writes=[gcb], dma_key="p2g%d" % kk)
                        S.add("sp", lambda e, o1c=o1c, r0=r0: e.dma_start(out=o1c[:], in_=O1[r0:r0 + 64, :]), reads=[gb(("O", t0, h)) for h in range(16)], writes=[o1b], dma_key="p2o%d" % kk)
                        for h in range(16):
                            if cond == 1 and (c + 1) % (SEQ // 64) == 0:
                                S.add("pool", lambda e, h=h: e.memset(Sst[1][:, h, :], 0.0), writes=[b_S[1][h]])

                            def sink(po, bo, h=h, o1c=o1c, o1b=o1b):
                                S.add("dve", lambda e: e.tensor_tensor(out=osum[:, h, :], in0=po, in1=o1c[:, h * 128:(h + 1) * 128], op=ALU.add), reads=[bo, o1b], writes=[b_osum[h]])
                            chunk_step(1, h, q2c[:, h, :], [q2b], k2c[:, h, :], [k2b], d2t[:, h, c, :], [b_d2t], vc[:, h * 128:(h + 1) * 128], [vcb], sink)
                            if cond == 1 and c % (SEQ // 64) == 0:
                                sq_ = c // (SEQ // 64)
                                S.add("sp", lambda e, h=h, sq_=sq_: e.dma_start(out=self.dout["hg_stout"][j, sq_, 1, :, h, :], in_=Sst[1][:, h, :]), reads=[b_S[1][h]], writes=[Buf()], dma_key="hso%d" % (h % 4))
                        of = osum[:].rearrange("p h v -> p (h v)")
                        S.add("act", lambda e: e.activation(out=sqt[:], in_=of, func=AF.Square), reads=b_osum, writes=[b_sqt])
                        S.add("dve", lambda e: e.tensor_reduce(out=ssq[:], in_=sqt[:].rearrange("p (h v) -> p h v", v=128), axis=AX.X, op=ALU.add), reads=[b_sqt], writes=[b_ssq])
                        S.add("act", lambda e: e.activation(out=ssq[:], in_=ssq[:], func=AF.Ln, scale=1.0 / 128, bias=self.c_eps[:64, :]), reads=[b_ssq, self.b_const], writes=[b_ssq])
                        S.add("act", lambda e: e.activation(out=ssq[:], in_=ssq[:], func=AF.Exp, scale=-0.5), reads=[b_ssq], writes=[b_ssq])
                        S.add("dve", lambda e: e.tensor_tensor(out=sqt[:].rearrange("p (h v) -> p h v", v=128), in0=osum[:], in1=ssq[:].unsqueeze(2).broadcast_to([64, 16, 128]), op=ALU.mult), reads=b_osum + [b_ssq], writes=[b_sqt])
                        S.add("pool", lambda e, gc=gc: e.tensor_tensor(out=obf[:], in0=sqt[:], in1=gc[:], op=ALU.mult), reads=[b_sqt, gcb], writes=[b_obf])
                        for h in range(16):
                            S.add("pe", lambda e, h=h: e.transpose(self.pstr[:, h * 64:(h + 1) * 64], obf[:, h * 128:(h + 1) * 128], ident[:64, :64]), reads=[b_obf, b_hc], writes=[self.b_pstr])
                        S.add("act", lambda e, c=c: e.activation(out=oT[:, :, c * 64:(c + 1) * 64], in_=self.pstr[:, :].rearrange("p (h t) -> p h t", t=64), func=AF.Copy), reads=[self.b_pstr], writes=[b_oT[c]])

                    def x(kc):
                        return oT[:, kc, :T], b_oT
                    self.linear_fm(wo, KC, 16, 4, x, T, self.post_consume(T))
                    self.post(l, 0, tile, False)
                S.barrier()

    def build(self, kinds):
        nc, S = self.nc, self.S
        NT, L, NS, NP, SEQ = self.NT, self.L, self.NS, self.NP, self.SEQ
        NA = sum(1 for k in kinds if k == 0)
        NB_ = sum(1 for k in kinds if k == 1)
        NC_ = sum(1 for k in kinds if k == 2)
        self.inp("xT", [KC, 128, NT])
        self.inp("cT", [128, KC, 2])
        self.inp("ada_w", [L, 24, 128, KC * 512])
        self.inp("ada_bT", [128, L, 96])
        self.inp("norm_gT", [128, L, 4, KC])
        self.inp("mlp_w_in", [L, 16, 128, KC * 512])
        self.inp("mlp_w_out", [L, 16, 128, 64 * 128])
        self.inp("ropeC", [64, NT])
        self.inp("ropeS", [64, NT])
        self.inp("ident", [128, 128])
        if NA:
            self.inp("hg_w_qf", [NA, 12, 128, KC * 512])
            self.inp("hg_w_ig", [NA, 8, 128, KC * 512])
            self.inp("hg_w_out", [NA, 4, 128, KC * 512])
            self.inp("hg_lbT", [128, 2, L, 16])
            self.inp("hg_gbc", [NA, 64, 2048])
            self.inp("hg_s0", [NA, 2, 128, 16, 128])
            self.inp("hg_masks", [64, 2, 64])
            self.outp("hg_stout", [NA, 2, 2, 128, 16, 128])
        if NB_:
            self.inp("mla_w_down", [NB_, 3, 128, KC * 512])
            self.inp("mla_w_uq", [NB_, 3, 128, 4 * 2048])
            self.inp("mla_w_ukvT", [NB_, 128, 4, 4096])
            self.inp("mla_w_out", [NB_, 4, 128, KC * 512])
            self.inp("mla_gT", [NB_, 128, 2, 4])
            self.inp("mla_ckv_ctxT", [128, 4, 256])
            self.inp("mla_kpe_ctxT", [64, 256])
            self.outp("mla_ckvout", [4, 128, NP])
            self.outp("mla_kpeout", [64, NP])
        if NC_:
            self.inp("swa_w_qk", [NC_, 10, 128, KC * 512])
            self.inp("swa_w_v", [NC_, 1, 128, KC * 512])
            self.inp("swa_w_out", [NC_, 8, 64, 32 * 256])
            self.inp("swa_kctxT", [64, 8, 256])
            self.inp("swa_vctx", [256, 512])
            self.inp("swa_masks", [128, 2, 512])
            self.inp("swa_sink_bc", [NC_, 64, 32])
            self.outp("swa_kout", [64, 8, NP])
            self.outp("swa_vout", [NP, 512])
        self.OUTT = self.outp("yT", [KC, 128, NT])
        self.YT = self.din["xT"]
        YS = self.scratch("YS", [KC, 128, NT])
        self.YS = YS
        with ExitStack() as g:
            self.c_eps = self.sb(g, [128, 1], F32, "eps")
            self.ones_bf = self.sb(g, [128, 128], BF16, "ones")
            self.adab = self.sb(g, [128, L, 96], F32, "adab")
            self.ngT = self.sb(g, [128, L, 4, KC], F32, "ngT")
            self.scT = self.sb(g, [128, KC, 2], BF16, "scT")
            cTf = self.sb(g, [128, KC, 2], F32, "cTf")
            self.modraw = self.sb(g, [128, 96, 2], F32, "modraw")
            self.modA = self.sb(g, [128, 2, KC, 2], F32, "modA")
            self.modG = self.sb(g, [128, 2, KC, 2], F32, "modG")
            self.modS = self.sb(g, [128, 2, KC, 2], F32, "modS")
            self.rstd = self.sb(g, [128, TT], F32, "rstd")
            self.ropeC = self.sb(g, [64, TT], F32, "ropeC")
            self.ropeS = self.sb(g, [64, TT], F32, "ropeS")
            self.b_const, self.b_mod, self.b_rstd, self.b_rope = Buf("const"), Buf("mod"), Buf("rstd"), Buf("rope")
            self.wbufs = [(self.sb(g, [128, 8192], BF16, "wb"), Buf(f"wb{i}")) for i in range(2)]
            self.wrot = 0
            self.tmps = [(self.sb(g, [128, TT], F32, "tmp"), Buf(f"tmp{i}")) for i in range(4)]
            self.tmprot = 0
            self.sqs = [(self.sb(g, [128, TT], BF16, "sq"), Buf(f"sq{i}")) for i in range(2)]
            self.sqrot = 0
            psb = [g.enter_context(nc.psum_tensor(f"ps{i}", [128, 512], F32)) for i in range(7)]
            self.pstr = g.enter_context(nc.psum_tensor("pstr", [128, 1024], BF16))
            self.linps = [(psb[i], Buf(f"lin{i}", True)) for i in range(3)]
            self.linrot = 0
            self.statps = (psb[3], Buf("stat", True))
            self.miscps = [(psb[i], Buf(f"misc{i}", True)) for i in range(4, 7)]
            self.b_pstr = Buf("pstr", True)
            self.accps = [(self.miscps[0], self.miscps[1]), (self.miscps[2], self.statps)]
            self.accrot = 0
            S.add("dve", lambda e: e.memset(self.c_eps[:], EPS), writes=[self.b_const])
            S.add("dve", lambda e: e.memset(self.ones_bf[:], 1.0), writes=[self.b_const])
            S.add("sp", lambda e: e.dma_start(out=self.adab[:], in_=self.din["ada_bT"]), writes=[self.b_const], dma_key="c0")
            S.add("sp", lambda e: e.dma_start(out=self.ngT[:], in_=self.din["norm_gT"]), writes=[self.b_const], dma_key="c1")
            S.add("sp", lambda e: e.dma_start(out=cTf[:], in_=self.din["cT"]), writes=[self.b_const], dma_key="c2")
            S.add("act", lambda e: e.activation(out=self.scT[:], in_=cTf[:], func=AF.Silu), reads=[self.b_const], writes=[self.b_const])
            if NA:
                self.hgrn_setup(g)
            cnt = [0, 0, 0]
            for l in range(L):
                self.adaln(l)
                S.barrier()
                kind = kinds[l]
                if kind == 0:
                    self.hgrn_layer(l, cnt[0])
                elif kind == 1:
                    self.mla_layer(l, cnt[1])
                elif kind == 2:
                    self.swa_layer(l, cnt[2])
                if kind >= 0:
                    cnt[kind] += 1
                    self.YT = YS
                with ExitStack() as ph:
                    self.big = self.sb(ph, [128, KC, TT], F32, "big")
                    self.b_big = [Buf(f"big{c}") for c in range(KC)]
                    self.hT = self.sb(ph, [128, KC, TT], BF16, "hT")
                    self.b_hT = [Buf(f"hT{c}") for c in range(KC)]
                    self.aT = self.sb(ph, [128, 64, TT], BF16, "aT")
                    self.b_aT = [Buf(f"aT{c}") for c in range(64)]
                    for tile in self.tiles:
                        self.mlp(l, tile, l == L - 1)
                    S.barrier()
                self.YT = YS
            S.emit()
        return nc


ROPE_BASE = 10000.0


def _partner():
    d = np.arange(64)
    return np.where(d % 32 < 16, d + 16, d - 16)


def _rope_tables(NS, NP, GW):
    d = np.arange(64)
    inv = ROPE_BASE ** (-(d % 16).astype(np.float32) / 16.0)
    t = np.arange(NS)
    pos = np.where(d[:, None] < 32, (t // GW)[None, :], (t % GW)[None, :]).astype(np.float32)
    ang = pos * inv[:, None].astype(np.float32)
    c = np.cos(ang).astype(np.float32)
    sn = np.sin(ang).astype(np.float32)
    sn = np.where((d % 32 < 16)[:, None], -sn, sn)
    c = np.concatenate([c, np.ones((64, NP), np.float32)], axis=1)
    sn = np.concatenate([sn, np.zeros((64, NP), np.float32)], axis=1)
    return np.ascontiguousarray(c, np.float32), np.ascontiguousarray(sn, np.float32)


def _shared_inputs(inp, L, kinds, NS, NP, GW):
    m = {}
    m["ada_w"] = np.stack([tile_w(inp["ada_w"][l], plain_chunks(96), 4) for l in range(L)])
    m["ada_bT"] = np.ascontiguousarray(fm(inp["ada_b"][:L]))
    m["norm_gT"] = np.ascontiguousarray(fm(inp["norm_g"][:L]))
    m["mlp_w_in"] = np.stack([tile_w(inp["mlp_w_in"][l], plain_chunks(64), 4) for l in range(L)])
    m["mlp_w_out"] = np.stack([tile_w(inp["mlp_w_out"][l], plain_chunks(16), 1) for l in range(L)])
    m["ropeC"], m["ropeS"] = _rope_tables(NS, NP, GW)
    m["ident"] = np.eye(128, dtype=np.float32)
    par = _partner()
    NA = sum(1 for k in kinds if k == 0)
    NB_ = sum(1 for k in kinds if k == 1)
    NC_ = sum(1 for k in kinds if k == 2)
    if NA:
        qf, ig, wo, gbc = [], [], [], []
        for j in range(NA):
            W = inp["hgrn_w_in"][j]
            ch = []
            for h in range(16):
                ch += [np.arange(h * 128, (h + 1) * 128), np.arange(2048 + h * 128, 2048 + (h + 1) * 128), np.arange(4096 + h * 128, 4096 + (h + 1) * 128)]
            qf.append(tile_w(W, ch, 4))
            ig.append(tile_w(W, plain_chunks(32, start=6144), 4))
            wo.append(tile_w(inp["hgrn_w_out"][j], plain_chunks(16), 4))
            gbc.append(np.broadcast_to(np.tile(inp["hgrn_norm_g"][j], 16)[None, :], (64, 2048)))
        m["hg_w_qf"], m["hg_w_ig"], m["hg_w_out"] = np.stack(qf), np.stack(ig), np.stack(wo)
        m["hg_gbc"] = np.ascontiguousarray(np.stack(gbc), np.float32)
        lg = inp["hgrn_lb_logits"][:, :L]
        m["hg_lbT"] = np.ascontiguousarray(lg.reshape(2, L, 16, 128).transpose(3, 0, 1, 2), np.float32)
        s_, t_ = np.arange(64)[:, None], np.arange(64)[None, :]
        m["hg_masks"] = np.ascontiguousarray(np.stack([(s_ <= t_), (s_ >= t_)], axis=1).astype(np.float32))
    if NB_:
        wd, wq, wkv, wo, gT = [], [], [], [], []
        for j in range(NB_):
            W = inp["mla_w_down"][j]
            ch = plain_chunks(8) + [np.arange(1024, 1088), 1024 + par]
            wd.append(tile_w(W, ch, 4))
            U = inp["mla_w_uq"][j]
            ch = [np.arange(h * 192, h * 192 + 128) for h in range(16)]
            for h in range(16):
                ch += [h * 192 + 128 + np.arange(64), h * 192 + 128 + par]
            wq.append(tile_w(U, ch, 16))
            wkv.append(inp["mla_w_ukv"][j].reshape(4, 128, 4096).transpose(1, 0, 2))
            wo.append(tile_w(inp["mla_w_out"][j], plain_chunks(16), 4))
            gT.append(np.stack([fm(inp["mla_q_norm_g"][j]), fm(inp["mla_kv_norm_g"][j])], axis=1))
        m["mla_w_down"], m["mla_w_uq"], m["mla_w_out"] = np.stack(wd), np.stack(wq), np.stack(wo)
        m["mla_w_ukvT"] = np.ascontiguousarray(np.stack(wkv), np.float32)
        m["mla_gT"] = np.ascontiguousarray(np.stack(gT), np.float32)
    if NC_:
        wqk, wv, wo, sk = [], [], [], []
        for j in range(NC_):
            W = inp["swa_w_qkv"][j]
            ch = []
            for h in range(32):
                ch += [h * 64 + np.arange(64), h * 64 + par]
            for g_ in range(8):
                ch += [2048 + g_ * 64 + np.arange(64), 2048 + g_ * 64 + par]
            wqk.append(tile_w(W, ch, 8, cw=64))
            wv.append(tile_w(W, plain_chunks(4, start=2560), 4))
            wo.append(tile_w(inp["swa_w_out"][j], plain_chunks(16), 2, kp=64))
            sk.append(np.broadcast_to(inp["swa_sink"][j][None, :], (64, 32)))
        m["swa_w_qk"], m["swa_w_v"], m["swa_w_out"] = np.stack(wqk), np.stack(wv), np.stack(wo)
        m["swa_sink_bc"] = np.ascontiguousarray(np.stack(sk), np.float32)
        c_, a_ = np.arange(128)[:, None], np.arange(128)[None, :]
        m0 = np.tile((c_ >= a_).astype(np.float32), (1, 4))
        m1 = np.tile((c_ <= a_).astype(np.float32), (1, 4))
        m["swa_masks"] = np.ascontiguousarray(np.stack([m0, m1], axis=1))
    return m


def _host_inputs(inp, core, NS, NP, SEQ, kinds):
    b = core % inp["x_sample"].shape[0]
    nps = NP // SEQ
    xs = inp["x_sample"][b, :NS]
    xp = inp["x_prompt"][core * nps:(core + 1) * nps].reshape(NP, D)
    x = np.concatenate([xs, xp], axis=0)
    m = {}
    m["xT"] = np.ascontiguousarray(x.T.reshape(KC, 128, -1))
    cc = np.stack([inp["c"][b], inp["c_ctx"]], axis=-1)
    m["cT"] = np.ascontiguousarray(cc.reshape(KC, 128, 2).transpose(1, 0, 2))
    if 0 in kinds:
        m["hg_s0"] = np.ascontiguousarray(inp["state_hgrn"][b].transpose(0, 1, 3, 2, 4))
    if 1 in kinds:
        m["mla_ckv_ctxT"] = np.ascontiguousarray(inp["cache_mla_ckv"][b, 0].T.reshape(4, 128, -1).transpose(1, 0, 2))
        m["mla_kpe_ctxT"] = np.ascontiguousarray(inp["cache_mla_kpe"][b, 0].T)
    if 2 in kinds:
        m["swa_kctxT"] = np.ascontiguousarray(inp["cache_swa_k"][b, 0].transpose(2, 1, 0))
        m["swa_vctx"] = np.ascontiguousarray(inp["cache_swa_v"][b, 0].reshape(-1, 512))
    return m


def run(inp, NS, NP, SEQ, L, GRID_W, ncores, kinds):
    inp = {k: np.asarray(v) for k, v in inp.items()}
    p = Prog(NS, NP, SEQ, L, GRID_W)
    nc = p.build(kinds)
    shared = _shared_inputs(inp, L, kinds, NS, NP, GRID_W)
    maps = []
    for c in range(ncores):
        m = dict(shared)
        m.update(_host_inputs(inp, c, NS, NP, SEQ, kinds))
        maps.append(m)
    res = run_bass_kernel_spmd(nc, maps, core_ids=list(range(ncores)))
    return res.results


def assemble(r, NS, NP, SEQ, ncores, nsamp, kinds):
    nps = NP // SEQ
    out = {}
    out["ys"] = np.stack([r[c]["yT"].reshape(D, -1)[:, :NS].T for c in range(nsamp)])
    out["yp"] = np.concatenate([r[c]["yT"].reshape(D, -1)[:, NS:].T.reshape(nps, SEQ, D) for c in range(ncores)])
    if 0 in kinds:
        out["st"] = np.concatenate([r[c]["hg_stout"].transpose(1, 0, 2, 4, 3, 5) for c in range(ncores)])
    if 1 in kinds:
        out["ckv"] = np.concatenate([r[c]["mla_ckvout"].reshape(512, NP).T.reshape(nps, 1, SEQ, 512) for c in range(ncores)])
        out["kpe"] = np.concatenate([r[c]["mla_kpeout"].T.reshape(nps, 1, SEQ, 64) for c in range(ncores)])
    if 2 in kinds:
        out["k"] = np.concatenate([r[c]["swa_kout"].transpose(2, 1, 0).reshape(nps, 1, SEQ, 8, 64) for c in range(ncores)])
        out["v"] = np.concatenate([r[c]["swa_vout"].reshape(nps, 1, SEQ, 8, 64) for c in range(ncores)])
    return out


def kernel(**inputs):
    NS, NP, SEQ, L, GW = 4096, 512, 256, 4, 64
    kinds = [0, 1, 2, 0]
    r = run(inputs, NS, NP, SEQ, L, GW, 8, kinds)
    o = assemble(r, NS, NP, SEQ, 8, 4, kinds)
    f = lambda a: np.ascontiguousarray(a, dtype=np.float32)
    return (f(o["yp"]), f(o["ys"]), f(o["st"]), f(o["ckv"]), f(o["kpe"]), f(o["k"]), f(o["v"]))
```

```python
import numpy as np
from contextlib import ExitStack
import concourse.bass as bass
import concourse.mybir as mybir
from concourse.bass_utils import run_bass_kernel_spmd

F32 = mybir.dt.float32
BF16 = mybir.dt.bfloat16
AF = mybir.ActivationFunctionType
ALU = mybir.AluOpType
AX = mybir.AxisListType

SEM_CAP = 20000
D = 2048
KC = 16
TT = 512
EPS = 1e-6


class Buf:
    __slots__ = ("name", "last_w", "readers", "excl")

    def __init__(self, name="", excl=False):
        self.name = name
        self.last_w = None
        self.readers = []
        self.excl = excl


class Op:
    __slots__ = ("eng", "fn", "deps", "signal", "val", "semi", "dma", "dsem", "dval", "idx")

    def __init__(self, eng, fn, dma):
        self.eng = eng
        self.fn = fn
        self.deps = []
        self.signal = False
        self.val = None
        self.semi = None
        self.dma = dma
        self.dsem = None
        self.dval = None


class Sched:
    ENGS = ("pe", "act", "dve", "pool", "sp")

    def __init__(self, nc):
        self.nc = nc
        self.ops = {e: [] for e in self.ENGS}
        self.dma_cnt = {}
        self.dma_last = {}
        self.pending = {e: [] for e in self.ENGS}
        self.n = 0

    def add(self, eng, fn, reads=(), writes=(), dma_key=None):
        op = Op(eng, fn, dma_key is not None)
        op.idx = self.n
        self.n += 1
        deps = set(self.pending[eng])
        self.pending[eng] = []
        if dma_key is not None and dma_key in self.dma_last:
            deps.add(self.dma_last[dma_key])
        for b in reads:
            if b.last_w is not None:
                deps.add(b.last_w)
            if b.excl:
                for r in b.readers:
                    if r.eng != eng:
                        deps.add(r)
        for b in writes:
            if b.last_w is not None:
                deps.add(b.last_w)
            deps.update(b.readers)
        last = {}
        for d in deps:
            if d.dma:
                op.deps.append(d)
            elif d.eng not in last or last[d.eng].idx < d.idx:
                last[d.eng] = d
        for d in last.values():
            if d.eng == "pe" and eng == "pe" and not op.dma:
                continue
            d.signal = True
            op.deps.append(d)
        for b in reads:
            b.readers.append(op)
        for b in writes:
            b.last_w = op
            b.readers = []
        if dma_key is not None:
            c = self.dma_cnt.get(dma_key, 0) + 16
            self.dma_cnt[dma_key] = c
            op.dsem = dma_key
            op.dval = c
            self.dma_last[dma_key] = op
        self.ops[eng].append(op)
        return op

    def barrier(self):
        snap = []
        for e in self.ENGS:
            for op in reversed(self.ops[e]):
                if not op.dma:
                    op.signal = True
                    snap.append(op)
                    break
        snap.extend(self.dma_last.values())
        for e in self.ENGS:
            self.pending[e] = list(snap)

    def emit(self):
        nc = self.nc
        nsem = {}
        for e in self.ENGS:
            c = 0
            for op in self.ops[e]:
                if op.dma or not op.signal:
                    continue
                op.semi = c // SEM_CAP
                op.val = c % SEM_CAP + 1
                c += 1
            nsem[e] = max(1, (c + SEM_CAP - 1) // SEM_CAP)
        with ExitStack() as st:
            esem = {e: [st.enter_context(nc.semaphore(f"s_{e}_{i}")) for i in range(nsem[e])] for e in self.ENGS}
            dsem = {k: st.enter_context(nc.semaphore(f"d_{i}")) for i, k in enumerate(self.dma_cnt)}
            block = st.enter_context(nc.Block())
            handles = {"pe": block.tensor, "act": block.scalar, "dve": block.vector, "pool": block.gpsimd,
                       "sp": block.sync}

            def make(e):
                def body(eng):
                    waited = {}
                    for op in self.ops[e]:
                        need = {}
                        for d in op.deps:
                            if d.dma:
                                k, v = ("d", d.dsem), d.dval
                            else:
                                k, v = ("e", d.eng, d.semi), d.val
                            if need.get(k, 0) < v:
                                need[k] = v
                        for k, v in need.items():
                            if waited.get(k, 0) >= v:
                                continue
                            waited[k] = v
                            eng.wait_ge(dsem[k[1]] if k[0] == "d" else esem[k[1]][k[2]], v)
                        ins = op.fn(eng)
                        if op.dma:
                            ins.then_inc(dsem[op.dsem], 16)
                        elif op.signal:
                            ins.then_inc(esem[e][op.semi], 1)
                    if e == "sp":
                        for k, tot in self.dma_cnt.items():
                            if waited.get(("d", k), 0) < tot:
                                eng.wait_ge(dsem[k], tot)
                return body

            for e in self.ENGS:
                if self.ops[e] or e == "sp":
                    handles[e](make(e))


def tile_w(W, chunks, spc, kp=128, cw=128):
    K = W.shape[0]
    kc = K // kp
    ns = (len(chunks) + spc - 1) // spc
    out = np.zeros((ns, kp, kc, spc * cw), np.float32)
    Wr = W.reshape(kc, kp, W.shape[1])
    for i, cols in enumerate(chunks):
        s, j = divmod(i, spc)
        out[s, :, :, j * cw:j * cw + len(cols)] = Wr[:, :, cols].transpose(1, 0, 2)
    return out.reshape(ns, kp, kc * spc * cw)


def plain_chunks(n, w=128, start=0):
    return [np.arange(start + i * w, start + (i + 1) * w) for i in range(n)]


def fm(vec):
    v = np.asarray(vec, np.float32)
    v = v.reshape(v.shape[:-1] + (v.shape[-1] // 128, 128))
    return np.ascontiguousarray(np.moveaxis(v, -1, 0))


class Prog:
    def __init__(self, NS, NP, SEQ, DEPTH, GRID_W):
        self.NS, self.NP, self.SEQ, self.L, self.GW = NS, NP, SEQ, DEPTH, GRID_W
        self.NT = NS + NP
        self.tiles = [(t0, TT, 0) for t0 in range(0, NS, TT)] + [(NS + t0, TT, 1) for t0 in range(0, NP, TT)]
        self.nc = bass.Bass("TRN2", target_bir_lowering=False)
        self.S = Sched(self.nc)
        self.din = {}
        self.dout = {}
        self.uid = 0
        self.bYd = {}
        self.rope_t0 = None

    def inp(self, name, shape):
        self.din[name] = self.nc.dram_tensor(name, list(shape), F32, kind="ExternalInput").ap()
        return self.din[name]

    def outp(self, name, shape):
        self.dout[name] = self.nc.dram_tensor(name, list(shape), F32, kind="ExternalOutput").ap()
        return self.dout[name]

    def scratch(self, name, shape, dt=F32):
        return self.nc.dram_tensor(name, list(shape), dt).ap()

    def sb(self, st, shape, dt, name=None):
        self.uid += 1
        return st.enter_context(self.nc.sbuf_tensor(f"{name or 't'}_{self.uid}", list(shape), dt))

    def bY(self, c, t0):
        k = (c, t0)
        if k not in self.bYd:
            self.bYd[k] = Buf(f"Y{c}_{t0}")
        return self.bYd[k]

    def key(self, p="k"):
        self.uid += 1
        return f"{p}{self.uid}"

    def linear_fm(self, wd, KCn, nchunks, spc, x, T, consume, widths=None, kp=128, cw=128):
        S = self.S
        ns = (nchunks + spc - 1) // spc
        sc = spc * cw
        wb = self.wbufs

        def load(s):
            t, b = wb[self.wrot % len(wb)]
            self.wrot += 1
            S.add("pool", lambda e, t=t, s=s: e.dma_start(out=t[:kp, :KCn * sc], in_=wd[s]), writes=[b], dma_key=b.name)
            return t, b

        nxt = load(0)
        for s in range(ns):
            t, b = nxt
            if s + 1 < ns:
                nxt = load(s + 1)
            for j in range(min(spc, nchunks - s * spc)):
                ci = s * spc + j
                w = cw if widths is None else widths[ci]
                ps, pb = self.linps[self.linrot % len(self.linps)]
                self.linrot += 1
                for kc in range(KCn):
                    xa, xb = x(kc)
                    S.add("pe", lambda e, ps=ps, t=t, kc=kc, j=j, w=w, xa=xa: e.matmul(
                        ps[:w, :T], lhsT=t[:kp, kc * sc + j * cw: kc * sc + j * cw + w], rhs=xa,
                        start=(kc == 0), stop=(kc == KCn - 1)), reads=[b] + xb, writes=[pb])
                consume(ci, ps[:w, :T], pb)

    def linear_tm(self, wd, KCn, nslabs, x, T, blk, consume):
        S = self.S
        wb = self.wbufs

        def load(s):
            t, b = wb[self.wrot % len(wb)]
            self.wrot += 1
            S.add("pool", lambda e, t=t, s=s: e.dma_start(out=t[:, :KCn * 512], in_=wd[s]), writes=[b], dma_key=b.name)
            return t, b

        nxt = load(0)
        for s in range(nslabs):
            t, b = nxt
            if s + 1 < nslabs:
                nxt = load(s + 1)
            for tb in range(T // blk):
                ps, pb = self.linps[self.linrot % len(self.linps)]
                self.linrot += 1
                for kc in range(KCn):
                    xa, xb = x(kc, tb)
                    S.add("pe", lambda e, ps=ps, t=t, kc=kc, xa=xa: e.matmul(
                        ps[:blk, :512], lhsT=xa, rhs=t[:, kc * 512:(kc + 1) * 512],
                        start=(kc == 0), stop=(kc == KCn - 1)), reads=[b] + xb, writes=[pb])
                consume(s, tb, ps[:blk, :512], pb)

    def rstd_from(self, ps_ap, pb, n, out_t, out_b, parts=128):
        S = self.S
        T = ps_ap.shape[-1]
        S.add("act", lambda e: e.activation(out=out_t, in_=ps_ap, func=AF.Ln, scale=1.0 / n, bias=self.c_eps[:parts, :]),
              reads=[pb, self.b_const], writes=[out_b])
        S.add("act", lambda e: e.activation(out=out_t, in_=out_t, func=AF.Exp, scale=-0.5), reads=[out_b], writes=[out_b])

    def sumsq_acc(self, src_ap, src_bufs, first, last, T, parts=128):
        S = self.S
        sq, sqb = self.sqs[self.sqrot % len(self.sqs)]
        self.sqrot += 1
        S.add("act", lambda e: e.activation(out=sq[:parts, :T], in_=src_ap, func=AF.Square), reads=src_bufs, writes=[sqb])
        st, stb = self.statps
        S.add("pe", lambda e: e.matmul(st[:, :T], lhsT=self.ones_bf[:parts, :], rhs=sq[:parts, :T], start=first, stop=last),
              reads=[sqb, self.b_const], writes=[stb])

    def pre(self, l, k, tile):
        S = self.S
        t0, T, cond = tile
        big, hT, YT = self.big, self.hT, self.YT
        for c in range(KC):
            S.add("sp", lambda e, c=c: e.dma_start(out=big[:, c, :T], in_=YT[c, :, t0:t0 + T]),
                  reads=[self.bY(c, t0)], writes=[self.b_big[c]], dma_key=f"big{c % 4}")
            self.sumsq_acc(big[:, c, :T], [self.b_big[c]], c == 0, c == KC - 1, T)
        self.rstd_from(self.statps[0][:, :T], self.statps[1], D, self.rstd[:, :T], self.b_rstd)
        for c in range(KC):
            tmp, tb = self.tmps[self.tmprot % len(self.tmps)]
            self.tmprot += 1
            S.add("dve", lambda e, c=c, tmp=tmp: e.tensor_tensor(out=tmp[:, :T], in0=big[:, c, :T], in1=self.rstd[:, :T], op=ALU.mult),
                  reads=[self.b_big[c], self.b_rstd], writes=[tb])
            S.add("act", lambda e, c=c, tmp=tmp: e.activation(out=hT[:, c, :T], in_=tmp[:, :T], func=AF.Identity,
                                                               scale=self.modA[:, k, c, cond:cond + 1], bias=self.modS[:, k, c, cond:cond + 1]),
                  reads=[tb, self.b_mod], writes=[self.b_hT[c]])

    def post_consume(self, T):
        S = self.S
        big = self.big

        def consume(ci, ps, pb):
            S.add("act", lambda e: e.activation(out=big[:, ci, :T], in_=ps, func=AF.Copy), reads=[pb], writes=[self.b_big[ci]])
            self.sumsq_acc(big[:, ci, :T], [self.b_big[ci]], ci == 0, ci == KC - 1, T)
        return consume

    def post(self, l, k, tile, final):
        S = self.S
        t0, T, cond = tile
        big, YT = self.big, self.YT
        self.rstd_from(self.statps[0][:, :T], self.statps[1], D, self.rstd[:, :T], self.b_rstd)
        for c in range(KC):
            tmp, tb = self.tmps[self.tmprot % len(self.tmps)]
            self.tmprot += 1
            S.add("sp", lambda e, c=c, tmp=tmp: e.dma_start(out=tmp[:, :T], in_=YT[c, :, t0:t0 + T]), reads=[self.bY(c, t0)], writes=[tb], dma_key=tb.name)
            S.add("dve", lambda e, c=c: e.scalar_tensor_tensor(out=big[:, c, :T], in0=big[:, c, :T], scalar=self.modG[:, k, c, cond:cond + 1],
                                                                in1=self.rstd[:, :T], op0=ALU.mult, op1=ALU.mult),
                  reads=[self.b_big[c], self.b_rstd, self.b_mod], writes=[self.b_big[c]])
            S.add("pool", lambda e, c=c, tmp=tmp: e.tensor_tensor(out=tmp[:, :T], in0=tmp[:, :T], in1=big[:, c, :T], op=ALU.add),
                  reads=[tb, self.b_big[c]], writes=[tb])
            dst = self.OUTT if final else self.YS
            S.add("sp", lambda e, c=c, tmp=tmp, dst=dst: e.dma_start(out=dst[c, :, t0:t0 + T], in_=tmp[:, :T]), reads=[tb], writes=[self.bY(c, t0)],
                  dma_key=tb.name + "s")

    def adaln(self, l):
        S = self.S
        mr = self.modraw

        def x(kc):
            return self.scT[:, kc, :], [self.b_const]

        def consume(ci, ps, pb):
            S.add("dve", lambda e: e.tensor_scalar(out=mr[:, ci, :], in0=ps, scalar1=self.adab[:, l, ci:ci + 1], scalar2=None, op0=ALU.add),
                  reads=[pb, self.b_const], writes=[self.b_mod])
        self.linear_fm(self.din["ada_w"][l], KC, 96, 4, x, 2, consume)
        for k in range(2):
            sh = mr[:, (3 * k) * 16:(3 * k + 1) * 16, :]
            scl = mr[:, (3 * k + 1) * 16:(3 * k + 2) * 16, :]
            gt = mr[:, (3 * k + 2) * 16:(3 * k + 3) * 16, :]
            gpre = self.ngT[:, l, 2 * k, :].unsqueeze(2).broadcast_to([128, 16, 2])
            gpost = self.ngT[:, l, 2 * k + 1, :].unsqueeze(2).broadcast_to([128, 16, 2])
            S.add("dve", lambda e, k=k, scl=scl, gpre=gpre: e.scalar_tensor_tensor(out=self.modA[:, k, :, :], in0=scl, scalar=1.0, in1=gpre, op0=ALU.add, op1=ALU.mult),
                  reads=[self.b_mod, self.b_const], writes=[self.b_mod])
            S.add("dve", lambda e, k=k, gt=gt, gpost=gpost: e.tensor_tensor(out=self.modG[:, k, :, :], in0=gt, in1=gpost, op=ALU.mult),
                  reads=[self.b_mod, self.b_const], writes=[self.b_mod])
            S.add("dve", lambda e, k=k, sh=sh: e.tensor_copy(out=self.modS[:, k, :, :], in_=sh), reads=[self.b_mod], writes=[self.b_mod])

    def mlp(self, l, tile, final):
        S = self.S
        t0, T, cond = tile
        self.pre(l, 1, tile)
        hT, aT = self.hT, self.aT

        def x1(kc):
            return hT[:, kc, :T], [self.b_hT[kc]]

        def c1(ci, ps, pb):
            tmp, tb = self.tmps[self.tmprot % len(self.tmps)]
            self.tmprot += 1
            S.add("act", lambda e: e.activation(out=tmp[:, :T], in_=ps, func=AF.Relu), reads=[pb], writes=[tb])
            eng = "dve" if ci % 2 == 0 else "pool"
            S.add(eng, lambda e: e.tensor_tensor(out=aT[:, ci, :T], in0=tmp[:, :T], in1=tmp[:, :T], op=ALU.mult), reads=[tb], writes=[self.b_aT[ci]])
        self.linear_fm(self.din["mlp_w_in"][l], KC, 64, 4, x1, T, c1)

        def x2(kc):
            return aT[:, kc, :T], [self.b_aT[kc]]
        self.linear_fm(self.din["mlp_w_out"][l], 64, 16, 1, x2, T, self.post_consume(T))
        self.post(l, 1, tile, final)


    def attend(self, pieces_q, nkb, kpieces, vblk, Tq, dv, scale, maskf, finish):
        S = self.S
        (o_ps, o_pb), (d_ps, d_pb) = self.accps[self.accrot % 2]
        self.accrot += 1
        for kb in range(nkb):
            ps, pb = self.linps[self.linrot % len(self.linps)]
            self.linrot += 1
            kp = kpieces(kb)
            for i, ((qa, qb), (ka, kbufs)) in enumerate(zip(pieces_q, kp)):
                S.add("pe", lambda e, ps=ps, ka=ka, qa=qa, i=i: e.matmul(ps[:, :Tq], lhsT=ka, rhs=qa, start=(i == 0), stop=(i == len(kp) - 1)),
                      reads=qb + kbufs, writes=[pb])
            pt, ptb = self.pts[self.ptrot % len(self.pts)]
            self.ptrot += 1
            S.add("act", lambda e, ps=ps, pt=pt: e.activation(out=pt[:, :Tq], in_=ps[:, :Tq], func=AF.Exp, scale=scale), reads=[pb], writes=[ptb])
            m = maskf(kb) if maskf is not None else None
            if m is not None:
                ma, mb = m
                S.add("pool", lambda e, pt=pt, ma=ma: e.tensor_tensor(out=pt[:, :Tq], in0=pt[:, :Tq], in1=ma, op=ALU.mult), reads=[ptb] + mb, writes=[ptb])
            va, vb = vblk(kb)
            S.add("pe", lambda e, pt=pt, va=va, kb=kb: e.matmul(o_ps[:dv, :Tq], lhsT=va, rhs=pt[:, :Tq], start=(kb == 0), stop=(kb == nkb - 1)),
                  reads=[ptb] + vb, writes=[o_pb])
            S.add("pe", lambda e, pt=pt, kb=kb: e.matmul(d_ps[:dv, :Tq], lhsT=self.ones_bf[:, :dv], rhs=pt[:, :Tq], start=(kb == 0), stop=(kb == nkb - 1)),
                  reads=[ptb, self.b_const], writes=[d_pb])
        finish(o_ps[:dv, :Tq], o_pb, d_ps[:dv, :Tq], d_pb)

    def attn_finish(self, out_ap, out_bufs, sink_ap=None, then=None):
        S = self

        def fin(o_ps, o_pb, d_ps, d_pb):
            rd, rdb = self.tmps[self.tmprot % len(self.tmps)]
            self.tmprot += 1
            p, T = o_ps.shape[0], o_ps.shape[-1]
            rda = rd[:p, :T]
            if sink_ap is not None:
                self.S.add("dve", lambda e: e.tensor_tensor(out=rda.rearrange("p (g t) -> p g t", g=4), in0=d_ps.rearrange("p (g t) -> p g t", g=4), in1=sink_ap, op=ALU.add),
                           reads=[d_pb, self.b_const], writes=[rdb])
                self.S.add("dve", lambda e: e.reciprocal(out=rda, in_=rda), reads=[rdb], writes=[rdb])
            else:
                self.S.add("dve", lambda e: e.reciprocal(out=rda, in_=d_ps), reads=[d_pb], writes=[rdb])
            self.S.add("dve", lambda e: e.tensor_tensor(out=out_ap, in0=o_ps, in1=rda, op=ALU.mult), reads=[o_pb, rdb], writes=out_bufs)
            if then is not None:
                then()
        return fin

    def rope_pair(self, t0, T, dst_fn):
        S = self.S
        st = {}
        if self.rope_t0 != t0:
            self.rope_t0 = t0
            S.add("sp", lambda e: e.dma_start(out=self.ropeC[:, :T], in_=self.din["ropeC"][:, t0:t0 + T]), writes=[self.b_rope], dma_key="c3")
            S.add("sp", lambda e: e.dma_start(out=self.ropeS[:, :T], in_=self.din["ropeS"][:, t0:t0 + T]), writes=[self.b_rope], dma_key="c4")

        def consume(ci, ps, pb):
            if ci % 2 == 0:
                tmp, tb = self.tmps[self.tmprot % len(self.tmps)]
                self.tmprot += 1
                st["a"] = (tmp, tb)
                S.add("dve", lambda e: e.tensor_tensor(out=tmp[:64, :T], in0=ps, in1=self.ropeC[:, :T], op=ALU.mult), reads=[pb, self.b_rope], writes=[tb])
            else:
                tmp, tb = st["a"]
                t2, tb2 = self.tmps[self.tmprot % len(self.tmps)]
                self.tmprot += 1
                S.add("dve", lambda e: e.tensor_tensor(out=t2[:64, :T], in0=ps, in1=self.ropeS[:, :T], op=ALU.mult), reads=[pb, self.b_rope], writes=[tb2])
                da, db = dst_fn(ci // 2)
                S.add("pool", lambda e: e.tensor_tensor(out=da, in0=tmp[:64, :T], in1=t2[:64, :T], op=ALU.add), reads=[tb, tb2], writes=db)
        return consume

    def swa_layer(self, l, j):
        nc, S = self.nc, self.S
        NT, NS, NP, SEQ = self.NT, self.NS, self.NP, self.SEQ
        QT = self.scratch(f"swaQ{l}", [64, 32, NT], BF16)
        KT = self.scratch(f"swaK{l}", [64, 8, NT], BF16)
        VV = self.scratch(f"swaV{l}", [NT, 512], BF16)
        OT = self.scratch(f"swaO{l}", [64, 32, NT], BF16)
        bQ, bK, bV, bO = {}, {}, {}, {}

        def gb(d, k):
            if k not in d:
                d[k] = Buf()
            return d[k]
        wq = self.din["swa_w_qk"][j]
        wv = self.din["swa_w_v"][j]
        wo = self.din["swa_w_out"][j]
        with ExitStack() as ph:
            self.big = self.sb(ph, [128, KC, TT], F32, "big")
            self.b_big = [Buf() for c in range(KC)]
            self.hT = self.sb(ph, [128, KC, TT], BF16, "hT")
            self.b_hT = [Buf() for c in range(KC)]
            hT = self.hT
            qst = self.sb(ph, [64, 40, TT], BF16, "qst")
            b_qst = [Buf() for _ in range(40)]
            vst = self.sb(ph, [128, 4, 512], BF16, "vst")
            b_vst = [Buf() for _ in range(4)]
            kf = self.sb(ph, [64, 8, TT], F32, "kf")
            vf = self.sb(ph, [128, 4, 512], F32, "vf")
            b_kf, b_vf = Buf(), Buf()
            for tile in self.tiles:
                t0, T, cond = tile
                self.pre(l, 0, tile)

                def x(kc):
                    return hT[:, kc, :T], [self.b_hT[kc]]
                rp = self.rope_pair(t0, T, lambda h: (qst[:, h, :T], [b_qst[h]]))

                def cons(ci, ps, pb, rp=rp, cond=cond, T=T):
                    rp(ci, ps, pb)
                    if cond == 1 and ci >= 64 and ci % 2 == 0:
                        S.add("act", lambda e: e.activation(out=kf[:, (ci - 64) // 2, :T], in_=ps, func=AF.Copy), reads=[pb], writes=[b_kf])
                self.linear_fm(wq, KC, 80, 8, x, T, cons, kp=128, cw=64)
                S.add("sp", lambda e, t0=t0, T=T: e.dma_start(out=QT[:, :, t0:t0 + T], in_=qst[:, 0:32, :T]), reads=b_qst[:32], writes=[gb(bQ, t0)], dma_key="swq")
                S.add("sp", lambda e, t0=t0, T=T: e.dma_start(out=KT[:, :, t0:t0 + T], in_=qst[:, 32:40, :T]), reads=b_qst[32:], writes=[gb(bK, t0)], dma_key="swk")

                def xv(kc, tb):
                    return hT[:, kc, tb * 128:(tb + 1) * 128], [self.b_hT[kc]]

                def consv(s_, tb, ps, pb, cond=cond):
                    S.add("act", lambda e: e.activation(out=vst[:, tb, :], in_=ps, func=AF.Copy), reads=[pb], writes=[b_vst[tb]])
                    if cond == 1:
                        S.add("dve", lambda e: e.tensor_copy(out=vf[:, tb, :], in_=ps), reads=[pb], writes=[b_vf])
                self.linear_tm(wv, KC, 1, xv, T, 128, consv)
                S.add("sp", lambda e, t0=t0, T=T: e.dma_start(out=VV[t0:t0 + T, :].rearrange("(b p) n -> p b n", p=128), in_=vst[:, :, :]), reads=b_vst, writes=[gb(bV, t0)], dma_key="swv")
                if cond == 1:
                    p0 = t0 - NS
                    S.add("sp", lambda e, p0=p0, T=T: e.dma_start(out=self.dout["swa_kout"][:, :, p0:p0 + T], in_=kf[:, :, :T]), reads=[b_kf], writes=[Buf()], dma_key="swko")
                    S.add("sp", lambda e, p0=p0, T=T: e.dma_start(out=self.dout["swa_vout"][p0:p0 + T, :].rearrange("(b p) n -> p b n", p=128), in_=vf[:, :, :]), reads=[b_vf], writes=[Buf()], dma_key="swvo")
            S.barrier()
        with ExitStack() as ph:
            self.pts = [(self.sb(ph, [128, TT], BF16, "pt"), Buf()) for _ in range(3)]
            self.ptrot = 0
            kctx = self.sb(ph, [64, 8, 256], BF16, "kctx")
            vctx = self.sb(ph, [128, 2, 512], BF16, "vctx")
            sinkr = self.sb(ph, [64, 32], F32, "sinkr")
            sinke = self.sb(ph, [64, 32], F32, "sinke")
            msk = self.sb(ph, [128, 2, 512], BF16, "msk")
            b_ctx = Buf()
            S.add("pool", lambda e: e.dma_start(out=kctx[:], in_=self.din["swa_kctxT"]), writes=[b_ctx], dma_key="sk1")
            S.add("pool", lambda e: e.dma_start(out=vctx[:], in_=self.din["swa_vctx"].rearrange("(b p) n -> p b n", p=128)), writes=[b_ctx], dma_key="sk2")
            S.add("pool", lambda e: e.dma_start(out=msk[:], in_=self.din["swa_masks"]), writes=[b_ctx], dma_key="sk3")
            S.add("sp", lambda e: e.dma_start(out=sinkr[:], in_=self.din["swa_sink_bc"][j]), writes=[b_ctx], dma_key="sk4")
            S.add("act", lambda e: e.activation(out=sinke[:], in_=sinkr[:], func=AF.Exp), reads=[b_ctx], writes=[b_ctx])
            NB = 2
            qb_ = [(self.sb(ph, [64, 32, 128], BF16, "qb"), Buf()) for _ in range(NB)]
            kl_ = [(self.sb(ph, [64, 8, 384], BF16, "kl"), Buf()) for _ in range(NB)]
            vl_ = [(self.sb(ph, [128, 3, 512], BF16, "vl"), Buf()) for _ in range(NB)]
            ob_ = [(self.sb(ph, [64, 32, 128], BF16, "ob"), Buf()) for _ in range(NB)]
            blocks = [(jb * 128, 0, NS, True) for jb in range(NS // 128)]
            for s_ in range(NP // SEQ):
                blocks += [(NS + s_ * SEQ + jb * 128, NS + s_ * SEQ, NS + (s_ + 1) * SEQ, False) for jb in range(SEQ // 128)]
            scale = 64 ** -0.5
            for bi, (q0, lo, hi, is_s) in enumerate(blocks):
                (qb, qbb), (kl, klb), (vl, vlb), (ob, obb) = qb_[bi % NB], kl_[bi % NB], vl_[bi % NB], ob_[bi % NB]
                tq = (q0 // TT) * TT
                S.add("sp", lambda e, qb=qb, q0=q0: e.dma_start(out=qb[:], in_=QT[:, :, q0:q0 + 128]), reads=[gb(bQ, tq)], writes=[qbb], dma_key="lq%d" % (bi % NB))
                if is_s:
                    kbs = [x_ for x_ in (q0 - 128, q0, q0 + 128) if lo <= x_ < hi]
                else:
                    kbs = list(range(lo, hi, 128))
                for i, k0 in enumerate(kbs):
                    tk = (k0 // TT) * TT
                    S.add("sp", lambda e, kl=kl, i=i, k0=k0: e.dma_start(out=kl[:, :, i * 128:(i + 1) * 128], in_=KT[:, :, k0:k0 + 128]), reads=[gb(bK, tk)], writes=[klb], dma_key="lk%d" % (bi % NB))
                    S.add("sp", lambda e, vl=vl, i=i, k0=k0: e.dma_start(out=vl[:, i, :], in_=VV[k0:k0 + 128, :]), reads=[gb(bV, tk)], writes=[vlb], dma_key="lv%d" % (bi % NB))
                nctx = 2 if is_s else 0
                for g in range(8):
                    def kpieces(kb, g=g, kl=kl, klb=klb):
                        if kb < nctx:
                            return [(kctx[:, g, kb * 128:(kb + 1) * 128], [b_ctx])]
                        i = kb - nctx
                        return [(kl[:, g, i * 128:(i + 1) * 128], [klb])]

                    def vblk(kb, g=g, vl=vl, vlb=vlb):
                        if kb < nctx:
                            return vctx[:, kb, g * 64:(g + 1) * 64], [b_ctx]
                        return vl[:, kb - nctx, g * 64:(g + 1) * 64], [vlb]

                    def maskf(kb, kbs=kbs, q0=q0):
                        if kb < nctx or not is_s:
                            return None
                        k0 = kbs[kb - nctx]
                        if k0 < q0:
                            return msk[:, 0, :], [b_ctx]
                        if k0 > q0:
                            return msk[:, 1, :], [b_ctx]
                        return None
                    qa = qb[:, 4 * g:4 * g + 4, :]
                    oa = ob[:, 4 * g:4 * g + 4, :].rearrange("p g t -> p (g t)")
                    sk = sinke[:, 4 * g:4 * g + 4].unsqueeze(2).broadcast_to([64, 4, 128])
                    self.attend([(qa, [qbb])], nctx + len(kbs), kpieces, vblk, 512, 64, scale, maskf, self.attn_finish(oa, [obb], sink_ap=sk))
                S.add("sp", lambda e, ob=ob, q0=q0: e.dma_start(out=OT[:, :, q0:q0 + 128], in_=ob[:]), reads=[obb], writes=[gb(bO, tq)], dma_key="so%d" % (bi % NB))
            S.barrier()
        with ExitStack() as ph:
            self.big = self.sb(ph, [128, KC, TT], F32, "big")
            self.b_big = [Buf() for c in range(KC)]
            oT = self.sb(ph, [64, 32, TT], BF16, "oTt")
            b_oT = Buf()
            for tile in self.tiles:
                t0, T, cond = tile
                S.add("sp", lambda e, t0=t0, T=T: e.dma_start(out=oT[:, :, :T], in_=OT[:, :, t0:t0 + T]), reads=[gb(bO, t0)], writes=[b_oT], dma_key="swo")

                def x(kc):
                    return oT[:, kc, :T], [b_oT]
                self.linear_fm(wo, 32, 16, 2, x, T, self.post_consume(T), kp=64)
                self.post(l, 0, tile, False)
            S.barrier()


    def mla_layer(self, l, j):
        nc, S = self.nc, self.S
        NT, NS, NP, SEQ = self.NT, self.NS, self.NP, self.SEQ
        NKEY = 256 + NT
        QN = self.scratch(f"mlaQN{l}", [16, 128, NT], BF16)
        QR = self.scratch(f"mlaQR{l}", [16, 64, NT], BF16)
        OT = self.scratch(f"mlaO{l}", [128, 16, NT], BF16)
        bQ, bO = {}, {}

        def gb(d, k):
            if k not in d:
                d[k] = Buf()
            return d[k]
        wdn = self.din["mla_w_down"][j]
        wuq = self.din["mla_w_uq"][j]
        wo = self.din["mla_w_out"][j]
        with ExitStack() as allph:
            ckv_all = self.sb(allph, [128, 4, NKEY], BF16, "ckvall")
            kpe_all = self.sb(allph, [64, NKEY], BF16, "kpeall")
            b_ckv = [Buf() for _ in range((NKEY + TT - 1) // TT + 1)]
            gq = self.sb(allph, [128, 2, 4], F32, "gq")
            b_g = Buf()
            S.add("sp", lambda e: e.dma_start(out=gq[:], in_=self.din["mla_gT"][j]), writes=[b_g], dma_key="mg")
            S.add("pool", lambda e: e.dma_start(out=ckv_all[:, :, 0:256], in_=self.din["mla_ckv_ctxT"]), writes=[b_ckv[0]], dma_key="mc1")
            S.add("pool", lambda e: e.dma_start(out=kpe_all[:, 0:256], in_=self.din["mla_kpe_ctxT"]), writes=[b_ckv[0]], dma_key="mc2")
            with ExitStack() as ph:
                self.big = self.sb(ph, [128, KC, TT], F32, "big")
                self.b_big = [Buf() for c in range(KC)]
                self.hT = self.sb(ph, [128, KC, TT], BF16, "hT")
                self.b_hT = [Buf() for c in range(KC)]
                hT = self.hT
                cf = self.sb(ph, [128, 8, TT], F32, "cf")
                b_cf = [Buf() for _ in range(8)]
                cqn = self.sb(ph, [128, 4, TT], BF16, "cqn")
                b_cqn = [Buf() for _ in range(4)]
                kpf = self.sb(ph, [64, TT], F32, "kpf")
                b_kpf = Buf()
                qn = self.sb(ph, [128, 16, TT], BF16, "qn")
                qr = self.sb(ph, [64, 16, TT], BF16, "qr")
                b_qn = [Buf() for _ in range(16)]
                b_qr = [Buf() for _ in range(16)]
                r2 = self.sb(ph, [128, TT], F32, "r2")
                b_r2 = Buf()
                for ti, tile in enumerate(self.tiles):
                    t0, T, cond = tile
                    k0 = 256 + t0
                    bk = b_ckv[1 + ti]
                    self.pre(l, 0, tile)

                    def x(kc):
                        return hT[:, kc, :T], [self.b_hT[kc]]
                    rp = self.rope_pair(t0, T, lambda h: (kpe_all[:, k0:k0 + T], [bk]))
                    st2 = self.miscps[0]

                    def cons(ci, ps, pb, T=T, rp=rp, cond=cond):
                        if ci < 8:
                            S.add("act", lambda e: e.activation(out=cf[:, ci, :T], in_=ps, func=AF.Copy), reads=[pb], writes=[b_cf[ci]])
                            sq, sqb = self.sqs[self.sqrot % len(self.sqs)]
                            self.sqrot += 1
                            S.add("act", lambda e: e.activation(out=sq[:, :T], in_=cf[:, ci, :T], func=AF.Square), reads=[b_cf[ci]], writes=[sqb])
                            st, stb = self.statps if ci < 4 else st2
                            S.add("pe", lambda e: e.matmul(st[:, :T], lhsT=self.ones_bf[:, :], rhs=sq[:, :T], start=(ci % 4 == 0), stop=(ci % 4 == 3)),
                                  reads=[sqb, self.b_const], writes=[stb])
                        else:
                            rp(ci - 8, ps, pb)
                            if ci == 8 and cond == 1:
                                S.add("act", lambda e: e.activation(out=kpf[:, :T], in_=ps, func=AF.Copy), reads=[pb], writes=[b_kpf])
                    self.linear_fm(wdn, KC, 10, 4, x, T, cons, widths=[128] * 8 + [64, 64])
                    self.rstd_from(self.statps[0][:, :T], self.statps[1], 512, self.rstd[:, :T], self.b_rstd)
                    self.rstd_from(st2[0][:, :T], st2[1], 512, r2[:, :T], b_r2)
                    for c in range(4):
                        S.add("dve", lambda e, c=c, T=T: e.scalar_tensor_tensor(out=cqn[:, c, :T], in0=cf[:, c, :T], scalar=gq[:, 0, c:c + 1], in1=self.rstd[:, :T], op0=ALU.mult, op1=ALU.mult),
                              reads=[b_cf[c], b_g, self.b_rstd], writes=[b_cqn[c]])
                        S.add("dve", lambda e, c=c, T=T: e.scalar_tensor_tensor(out=cf[:, 4 + c, :T], in0=cf[:, 4 + c, :T], scalar=gq[:, 1, c:c + 1], in1=r2[:, :T], op0=ALU.mult, op1=ALU.mult),
                              reads=[b_cf[4 + c], b_g, b_r2], writes=[b_cf[4 + c]])
                        S.add("pool", lambda e, c=c, k0=k0, T=T: e.tensor_copy(out=ckv_all[:, c, k0:k0 + T], in_=cf[:, 4 + c, :T]), reads=[b_cf[4 + c]], writes=[bk])
                    if cond == 1:
                        p0 = t0 - NS
                        S.add("sp", lambda e, p0=p0, T=T: e.dma_start(out=self.dout["mla_ckvout"][:, :, p0:p0 + T].rearrange("c p t -> p c t"), in_=cf[:, 4:8, :T]),
                              reads=b_cf[4:8], writes=[Buf()], dma_key="mco")
                        S.add("sp", lambda e, p0=p0, T=T: e.dma_start(out=self.dout["mla_kpeout"][:, p0:p0 + T], in_=kpf[:, :T]), reads=[b_kpf], writes=[Buf()], dma_key="mko")

                    def xq(kc):
                        return cqn[:, kc, :T], [b_cqn[kc]]
                    rq = self.rope_pair(t0, T, lambda h: (qr[:, h, :T], [b_qr[h]]))

                    def consq(ci, ps, pb, T=T, rq=rq):
                        if ci < 16:
                            S.add("act", lambda e: e.activation(out=qn[:, ci, :T], in_=ps, func=AF.Copy), reads=[pb], writes=[b_qn[ci]])
                        else:
                            rq(ci - 16, ps, pb)
                    self.linear_fm(wuq, 4, 48, 16, xq, T, consq, widths=[128] * 16 + [64] * 32)
                    S.add("sp", lambda e, t0=t0, T=T: e.dma_start(out=QN[:, :, t0:t0 + T].rearrange("h p t -> p h t"), in_=qn[:, :, :T]), reads=b_qn, writes=[gb(bQ, t0)], dma_key="mqn")
                    S.add("sp", lambda e, t0=t0, T=T: e.dma_start(out=QR[:, :, t0:t0 + T].rearrange("h p t -> p h t"), in_=qr[:, :, :T]), reads=b_qr, writes=[gb(bQ, t0)], dma_key="mqr")
                S.barrier()
            with ExitStack() as ph:
                self.pts = [(self.sb(ph, [128, TT], BF16, "pt"), Buf()) for _ in range(3)]
                self.ptrot = 0
                wkv = self.sb(ph, [128, 4, 4096], BF16, "wkv")
                b_wkv = Buf()
                S.add("pool", lambda e: e.dma_start(out=wkv[:], in_=self.din["mla_w_ukvT"][j]), writes=[b_wkv], dma_key="mwkv")
                NB = 2
                kth_ = [(self.sb(ph, [128, NKEY], BF16, "kth"), Buf()) for _ in range(NB)]
                vh_ = [(self.sb(ph, [128, NKEY // 128, 128], BF16, "vh"), Buf()) for _ in range(NB)]
                qnh_ = [(self.sb(ph, [128, NT], BF16, "qnh"), Buf())] * NB
                qrh_ = [(self.sb(ph, [64, NT], BF16, "qrh"), Buf())] * NB
                oh_ = [(self.sb(ph, [128, TT], BF16, "oh"), Buf()) for _ in range(3)]
                orot = 0
                allck = b_ckv
                scale = 192 ** -0.5
                for h in range(16):
                    (kth, kthb), (vh, vhb), (qnh, qnhb), (qrh, qrhb) = kth_[h % NB], vh_[h % NB], qnh_[h % NB], qrh_[h % NB]
                    S.add("sp", lambda e, qnh=qnh, h=h: e.dma_start(out=qnh[:], in_=QN[h]), reads=list(bQ.values()), writes=[qnhb], dma_key="mlq")
                    S.add("sp", lambda e, qrh=qrh, h=h: e.dma_start(out=qrh[:], in_=QR[h]), reads=list(bQ.values()), writes=[qrhb], dma_key="mlr")
                    for kt in range(0, NKEY, TT):
                        w = min(TT, NKEY - kt)
                        ps, pb = self.linps[self.linrot % len(self.linps)]
                        self.linrot += 1
                        for kc in range(4):
                            S.add("pe", lambda e, ps=ps, kc=kc, kt=kt, w=w, h=h: e.matmul(ps[:, :w], lhsT=wkv[:, kc, h * 256:h * 256 + 128], rhs=ckv_all[:, kc, kt:kt + w], start=(kc == 0), stop=(kc == 3)),
                                  reads=[b_wkv] + allck, writes=[pb])
                        S.add("act", lambda e, ps=ps, kt=kt, w=w, kth=kth: e.activation(out=kth[:, kt:kt + w], in_=ps[:, :w], func=AF.Copy), reads=[pb], writes=[kthb])
                    for kb4 in range(0, NKEY // 128, 4):
                        nb4 = min(4, NKEY // 128 - kb4)
                        ps, pb = self.linps[self.linrot % len(self.linps)]
                        self.linrot += 1
                        for i in range(nb4):
                            kb = kb4 + i
                            for kc in range(4):
                                S.add("pe", lambda e, ps=ps, kc=kc, kb=kb, i=i, h=h: e.matmul(ps[:, i * 128:(i + 1) * 128], lhsT=ckv_all[:, kc, kb * 128:(kb + 1) * 128], rhs=wkv[:, kc, h * 256 + 128:h * 256 + 256], start=(kc == 0), stop=(kc == 3)),
                                      reads=[b_wkv] + allck, writes=[pb])
                        S.add("dve", lambda e, ps=ps, kb4=kb4, nb4=nb4, vh=vh: e.tensor_copy(out=vh[:, kb4:kb4 + nb4, :].rearrange("p b d -> p (b d)"), in_=ps[:, :nb4 * 128]), reads=[pb], writes=[vhb])
                    qts = [(t0, TT, 0, (256 + NS) // 128) for t0 in range(0, NS, TT)]
                    for s_ in range(NP // SEQ):
                        qts.append((NS + s_ * SEQ, SEQ, (256 + NS + s_ * SEQ) // 128, SEQ // 128))
                    for (q0, Tq, kb0, nkb) in qts:
                        oh, ohb = oh_[orot % 3]
                        orot += 1

                        def kpieces(kb, kb0=kb0, kth=kth, kthb=kthb):
                            a = (kb0 + kb) * 128
                            return [(kth[:, a:a + 128], [kthb]), (kpe_all[:, a:a + 128], allck)]

                        def vblk(kb, kb0=kb0, vh=vh, vhb=vhb):
                            return vh[:, kb0 + kb, :], [vhb]
                        tq = (q0 // TT) * TT
                        self.attend([(qnh[:, q0:q0 + Tq], [qnhb]), (qrh[:, q0:q0 + Tq], [qrhb])], nkb, kpieces, vblk, Tq, 128, scale, None,
                                    self.attn_finish(oh[:, :Tq], [ohb]))
                        S.add("sp", lambda e, oh=oh, h=h, q0=q0, Tq=Tq: e.dma_start(out=OT[:, h, q0:q0 + Tq], in_=oh[:, :Tq]), reads=[ohb], writes=[gb(bO, (tq, h, q0))], dma_key="mo%d" % (orot % 3))
                S.barrier()
        with ExitStack() as ph:
            self.big = self.sb(ph, [128, KC, TT], F32, "big")
            self.b_big = [Buf() for c in range(KC)]
            oT = self.sb(ph, [128, 16, TT], BF16, "oTt")
            b_oT = Buf()
            for tile in self.tiles:
                t0, T, cond = tile
                S.add("sp", lambda e, t0=t0, T=T: e.dma_start(out=oT[:, :, :T], in_=OT[:, :, t0:t0 + T]), reads=[b for k_, b in bO.items() if k_[0] == t0], writes=[b_oT], dma_key="mlo")

                def x(kc):
                    return oT[:, kc, :T], [b_oT]
                self.linear_fm(wo, KC, 16, 4, x, T, self.post_consume(T))
                self.post(l, 0, tile, False)
            S.barrier()


    def hgrn_setup(self, g):
        S, L = self.S, self.L
        lg = self.sb(g, [128, 2, L, 16], F32, "lbl")
        self.lb = self.sb(g, [128, 2, L, 16], F32, "lb")
        self.oml = self.sb(g, [128, 2, L, 16], F32, "oml")
        sm = self.sb(g, [128, 2, 16], F32, "lbs")
        self.b_lb = Buf()
        b = self.b_lb
        S.add("sp", lambda e: e.dma_start(out=lg[:], in_=self.din["hg_lbT"]), writes=[b], dma_key="hlb")
        S.add("act", lambda e: e.activation(out=lg[:], in_=lg[:], func=AF.Exp), reads=[b], writes=[b])
        S.add("dve", lambda e: e.tensor_copy(out=sm[:], in_=lg[:, :, 0, :]), reads=[b], writes=[b])
        for i in range(1, L):
            S.add("dve", lambda e, i=i: e.tensor_tensor(out=sm[:], in0=sm[:], in1=lg[:, :, i, :], op=ALU.add), reads=[b], writes=[b])
        S.add("dve", lambda e: e.reciprocal(out=sm[:], in_=sm[:]), reads=[b], writes=[b])
        S.add("dve", lambda e: e.memset(self.lb[:, :, 0, :], 0.0), writes=[b])
        for i in range(1, L):
            S.add("dve", lambda e, i=i: e.tensor_tensor(out=lg[:, :, i, :], in0=lg[:, :, i, :], in1=sm[:], op=ALU.mult), reads=[b], writes=[b])
            S.add("dve", lambda e, i=i: e.tensor_tensor(out=self.lb[:, :, i, :], in0=self.lb[:, :, i - 1, :], in1=lg[:, :, i, :], op=ALU.add), reads=[b], writes=[b])
        S.add("dve", lambda e: e.tensor_scalar(out=self.oml[:], in0=self.lb[:], scalar1=-1.0, scalar2=1.0, op0=ALU.mult, op1=ALU.add), reads=[b], writes=[b])

    def hgrn_layer(self, l, j):
        nc, S = self.nc, self.S
        NT, NS, NP, SEQ = self.NT, self.NS, self.NP, self.SEQ
        NCH = NT // 64
        Q2 = self.scratch(f"hgQ2{l}", [16, 128, NT], BF16)
        K2 = self.scratch(f"hgK2{l}", [16, 128, NT], BF16)
        D2 = self.scratch(f"hgD2{l}", [16, 128, NCH, 3], F32)
        V64 = self.scratch(f"hgV{l}", [NT, 2048], BF16)
        GS = self.scratch(f"hgG{l}", [NT, 2048], BF16)
        O1 = self.scratch(f"hgO1{l}", [NT, 2048], F32)
        bsc = {}

        def gb(k):
            if k not in bsc:
                bsc[k] = Buf()
            return bsc[k]
        wqf = self.din["hg_w_qf"][j]
        wig = self.din["hg_w_ig"][j]
        wo = self.din["hg_w_out"][j]
        lb, oml = self.lb, self.oml
        with ExitStack() as allph:
            Sst = [self.sb(allph, [128, 16, 128], F32, "Sst") for _ in range(2)]
            b_S = [[Buf() for _ in range(16)] for _ in range(2)]
            hmask = self.sb(allph, [64, 2, 64], BF16, "hmask")
            ident = self.sb(allph, [128, 128], BF16, "ident")
            onesf = self.sb(allph, [128, TT], F32, "onesf")
            hgbc = self.sb(allph, [64, 2048], F32, "hgbc")
            b_hc = Buf()
            S.add("pool", lambda e: e.dma_start(out=hmask[:], in_=self.din["hg_masks"]), writes=[b_hc], dma_key="hm1")
            S.add("pool", lambda e: e.dma_start(out=ident[:], in_=self.din["ident"]), writes=[b_hc], dma_key="hm2")
            S.add("sp", lambda e: e.dma_start(out=hgbc[:], in_=self.din["hg_gbc"][j]), writes=[b_hc], dma_key="hm3")
            S.add("dve", lambda e: e.memset(onesf[:], 1.0), writes=[b_hc])
            sbf_ = [(self.sb(allph, [128, 128], BF16, "sbf"), Buf()) for _ in range(4)]
            t1_ = [(self.sb(allph, [128, 128], F32, "t1"), Buf()) for _ in range(3)]
            amf8 = self.sb(allph, [64, 512], F32, "amf8")
            am8 = self.sb(allph, [64, 512], BF16, "am8")
            ktok8 = self.sb(allph, [64, 8, 128], BF16, "ktok8")
            b_amf8, b_am8, b_ktok8 = Buf(), Buf(), Buf()
            hmaskf = self.sb(allph, [64, 2, 64], F32, "hmaskf")
            S.add("sp", lambda e: e.dma_start(out=hmaskf[:], in_=self.din["hg_masks"]), writes=[b_hc], dma_key="hm4")
            rot = {"sbf": 0, "t1": 0}
            par = [0] * 16
            pa8, b_pa8 = self.miscps[0]
            pk8 = [self.miscps[1], self.miscps[2]]

            def nxt(lst, k):
                r = lst[rot[k] % len(lst)]
                rot[k] += 1
                return r

            def mk_memset(h):
                def f():
                    st_, b_ = Sst[par[h]], b_S[par[h]][h]
                    S.add("pool", lambda e: e.memset(st_[:, h, :], 0.0), writes=[b_])
                return f

            def mk_stout(h, sq_, d):
                def f():
                    st_, b_ = Sst[par[h]], b_S[par[h]][h]
                    S.add("sp", lambda e: e.dma_start(out=self.dout["hg_stout"][j, sq_, d, :, h, :], in_=st_[:, h, :]), reads=[b_], writes=[Buf()], dma_key="hso%d" % (h % 4))
                return f

            def scan_group(d, items, po2, sink4):
                def amm(i, it):
                    S.add("pe", lambda e: e.matmul(pa8[:64, i * 64:(i + 1) * 64], lhsT=it["kt"], rhs=it["qt"], start=True, stop=True),
                          reads=it["qtb"] + it["ktb"], writes=[b_pa8])
                for i, it in enumerate(items):
                    amm(i, it)
                S.add("dve", lambda e: e.tensor_scalar(out=amf8[:], in0=pa8[:64, :512], scalar1=1e30, scalar2=-1e30, op0=ALU.min, op1=ALU.max), reads=[b_pa8], writes=[b_amf8])
                S.add("pool", lambda e: e.tensor_tensor(out=am8[:].rearrange("p (c t) -> p c t", t=64), in0=amf8[:].rearrange("p (c t) -> p c t", t=64),
                                                         in1=hmaskf[:, d, :].unsqueeze(1).broadcast_to([64, 8, 64]), op=ALU.mult), reads=[b_amf8, b_hc], writes=[b_am8])

                def tr(i, it):
                    S.add("pe", lambda e: e.transpose(self.pstr[:64, i * 128:(i + 1) * 128], it["kt"], ident[:, :]), reads=it["ktb"] + [b_hc], writes=[self.b_pstr])
                for i, it in enumerate(items):
                    tr(i, it)
                S.add("act", lambda e: e.activation(out=ktok8[:].rearrange("p c k -> p (c k)"), in_=self.pstr[:64, :1024], func=AF.Copy), reads=[self.b_pstr], writes=[b_ktok8])

                def kvmm(i, it):
                    pk, pkb = pk8[i // 4]
                    col = (i % 4) * 128
                    S.add("pe", lambda e: e.matmul(pk[:, col:col + 128], lhsT=ktok8[:, i, :], rhs=it["v"], start=True, stop=True), reads=[b_ktok8] + it["vb"], writes=[pkb])
                for i, it in enumerate(items):
                    kvmm(i, it)

                def step(i, it):
                    h, dv, dvb = it["h"], it["dv"], it["dvb"]
                    for f in it["pre"]:
                        f()
                    cur = par[h]
                    new = 1 - cur
                    Sc, Sn = Sst[cur], Sst[new]
                    bc, bn = b_S[cur][h], b_S[new][h]
                    pk, pkb = pk8[i // 4]
                    po, pob = po2[i // 4]
                    col = (i % 4) * 128
                    sbf, sbfb = nxt(sbf_, "sbf")
                    S.add("pool", lambda e: e.tensor_scalar(out=sbf[:], in0=Sc[:, h, :], scalar1=dv[:, 0:1], scalar2=None, op0=ALU.mult), reads=[bc] + dvb, writes=[sbfb])
                    S.add("pe", lambda e: e.matmul(po[:64, col:col + 128], lhsT=it["qt"], rhs=sbf[:], start=True, stop=False), reads=it["qtb"] + [sbfb], writes=[pob])
                    S.add("pe", lambda e: e.matmul(po[:64, col:col + 128], lhsT=am8[:, i * 64:(i + 1) * 64], rhs=it["v"], start=False, stop=True), reads=[b_am8] + it["vb"], writes=[pob])
                    t1, t1b = nxt(t1_, "t1")
                    S.add("dve", lambda e: e.tensor_scalar(out=t1[:], in0=Sc[:, h, :], scalar1=dv[:, 2:3], scalar2=None, op0=ALU.mult), reads=[bc] + dvb, writes=[t1b])
                    S.add("dve", lambda e: e.scalar_tensor_tensor(out=Sn[:, h, :], in0=pk[:, col:col + 128], scalar=dv[:, 1:2], in1=t1[:], op0=ALU.mult, op1=ALU.add),
                          reads=[pkb, t1b] + dvb, writes=[bn])
                    par[h] = new
                    for f in it["post"]:
                        f()
                    if i % 4 == 3:
                        sink4(i // 4, po[:64, :512], pob, items[i - 3:i + 1])
                for i, it in enumerate(items):
                    step(i, it)

            with ExitStack() as ph:
                self.big = self.sb(ph, [128, KC, TT], F32, "big")
                self.b_big = [Buf() for c in range(KC)]
                self.hT = self.sb(ph, [128, KC, TT], BF16, "hT")
                self.b_hT = [Buf() for c in range(KC)]
                hT = self.hT
                v64 = self.sb(ph, [64, 8, 2048], BF16, "v64")
                b_v64 = [Buf() for _ in range(8)]
                gst_ = [(self.sb(ph, [64, 512], BF16, "gst"), Buf()) for _ in range(2)]
                gtmp = self.sb(ph, [64, 512], F32, "gtmp")
                b_gtmp = Buf()
                qs_ = [(self.sb(ph, [128, TT], F32, "qs"), Buf()) for _ in range(2)]
                ft = {n: (self.sb(ph, [128, TT], F32, n), Buf()) for n in ["f", "g", "B", "X", "E", "eq", "ek"]}
                qk_ = [[(self.sb(ph, [128, TT], BF16, "qkt"), Buf()) for _ in range(2)] for _ in range(4)]
                dvt_ = [(self.sb(ph, [128, 8, 3], F32, "dvt"), Buf()) for _ in range(4)]
                o1h_ = [(self.sb(ph, [64, 8, 128], F32, "o1h"), Buf()) for _ in range(2)]
                S.add("sp", lambda e: e.dma_start(out=Sst[0][:], in_=self.din["hg_s0"][j, 0]), writes=b_S[0], dma_key="hs0")
                hcount = 0
                lin_saved = self.linps
                po2_p1 = [self.statps, lin_saved[2]]
                self.linps = lin_saved[:2]
                for ti, tile in enumerate(self.tiles):
                    t0, T, cond = tile
                    self.pre(l, 0, tile)

                    def xv(kc, tb):
                        return hT[:, kc, tb * 64:(tb + 1) * 64], [self.b_hT[kc]]

                    def consv(s_, tb, ps, pb, t0=t0):
                        if s_ < 4:
                            S.add("act", lambda e: e.activation(out=v64[:, tb, s_ * 512:(s_ + 1) * 512], in_=ps, func=AF.Copy), reads=[pb], writes=[b_v64[tb]])
                        else:
                            gst, gstb = gst_[(s_ * 8 + tb) % 2]
                            S.add("act", lambda e: e.activation(out=gtmp[:], in_=ps, func=AF.Silu), reads=[pb], writes=[b_gtmp])
                            S.add("dve", lambda e: e.tensor_tensor(out=gst[:], in0=gtmp[:], in1=hgbc[:, (s_ - 4) * 512:(s_ - 3) * 512], op=ALU.mult), reads=[b_gtmp, b_hc], writes=[gstb])
                            r0 = t0 + tb * 64
                            S.add("sp", lambda e: e.dma_start(out=GS[r0:r0 + 64, (s_ - 4) * 512:(s_ - 3) * 512], in_=gst[:]), reads=[gstb], writes=[gb(("G", t0))], dma_key="hg%d" % ((s_ * 8 + tb) % 2))
                    self.linear_tm(wig, KC, 8, xv, T, 64, consv)
                    S.add("sp", lambda e, t0=t0: e.dma_start(out=V64[t0:t0 + TT, :].rearrange("(c p) n -> p c n", p=64), in_=v64[:]), reads=b_v64, writes=[gb(("V", t0))], dma_key="hv")

                    def x(kc):
                        return hT[:, kc, :T], [self.b_hT[kc]]
                    stq = {}

                    def cons(ci, ps, pb, t0=t0, cond=cond, ti=ti):
                        h, kind = divmod(ci, 3)
                        if kind == 0:
                            qs, qsb = qs_[h % 2]
                            stq["qs"] = (qs, qsb)
                            S.add("act", lambda e: e.activation(out=qs[:], in_=ps, func=AF.Silu), reads=[pb], writes=[qsb])
                            return
                        d = kind - 1
                        qs, qsb = stq["qs"]
                        (f, fb), (g_, gb_), (B, Bb), (X, Xb), (E, Eb), (eq, eqb), (ek, ekb) = [ft[n] for n in ["f", "g", "B", "X", "E", "eq", "ek"]]
                        S.add("act", lambda e: e.activation(out=f[:], in_=ps, func=AF.Sigmoid), reads=[pb], writes=[fb])
                        S.add("dve", lambda e: e.tensor_scalar(out=f[:], in0=f[:], scalar1=oml[:, d, l, h:h + 1], scalar2=lb[:, d, l, h:h + 1], op0=ALU.mult, op1=ALU.add), reads=[fb, self.b_lb], writes=[fb])
                        S.add("act", lambda e: e.activation(out=g_[:], in_=f[:], func=AF.Ln), reads=[fb], writes=[gb_])
                        S.add("dve", lambda e: e.tensor_tensor_scan(out=B[:], data0=onesf[:], data1=g_[:], initial=0.0, op0=ALU.mult, op1=ALU.add), reads=[gb_, b_hc], writes=[Bb])
                        S.add("pool", lambda e: e.tensor_tensor(out=X[:], in0=B[:], in1=g_[:], op=ALU.subtract), reads=[Bb, gb_], writes=[Xb])
                        S.add("pool", lambda e: e.tensor_scalar(out=f[:], in0=f[:], scalar1=-1.0, scalar2=1.0, op0=ALU.mult, op1=ALU.add), reads=[fb], writes=[fb])
                        B3 = B[:].rearrange("p (c t) -> p c t", t=64)
                        X3 = X[:].rearrange("p (c t) -> p c t", t=64)
                        E3 = E[:].rearrange("p (c t) -> p c t", t=64)
                        if d == 0:
                            S.add("dve", lambda e: e.tensor_tensor(out=E3, in0=B3, in1=B3[:, :, 32:33].broadcast_to([128, 8, 64]), op=ALU.subtract), reads=[Bb], writes=[Eb])
                        else:
                            S.add("dve", lambda e: e.tensor_tensor(out=E3, in0=X3[:, :, 32:33].broadcast_to([128, 8, 64]), in1=X3, op=ALU.subtract), reads=[Xb], writes=[Eb])
                        S.add("act", lambda e: e.activation(out=eq[:], in_=E[:], func=AF.Exp), reads=[Eb], writes=[eqb])
                        S.add("act", lambda e: e.activation(out=ek[:], in_=E[:], func=AF.Exp, scale=-1.0), reads=[Eb], writes=[ekb])
                        (qt, qtb) = qk_[2 * d][hcount_ref[0] % 2]
                        (kt, ktb) = qk_[2 * d + 1][hcount_ref[0] % 2]
                        (dvt, dvb) = dvt_[2 * d + hcount_ref[0] % 2]
                        S.add("dve", lambda e: e.scalar_tensor_tensor(out=qt[:], in0=qs[:], scalar=128 ** -0.5, in1=eq[:], op0=ALU.mult, op1=ALU.mult), reads=[qsb, eqb], writes=[qtb])
                        S.add("pool", lambda e: e.tensor_tensor(out=kt[:], in0=f[:], in1=ek[:], op=ALU.mult), reads=[fb, ekb], writes=[ktb])
                        mid = (B3 if d == 0 else X3)[:, :, 32:33]
                        if d == 0:
                            S.add("pool", lambda e: e.tensor_tensor(out=dvt[:, :, 0:1], in0=mid, in1=X3[:, :, 0:1], op=ALU.subtract), reads=[Bb, Xb], writes=[dvb])
                            S.add("pool", lambda e: e.tensor_tensor(out=dvt[:, :, 1:2], in0=B3[:, :, 63:64], in1=mid, op=ALU.subtract), reads=[Bb, Xb], writes=[dvb])
                        else:
                            S.add("pool", lambda e: e.tensor_tensor(out=dvt[:, :, 0:1], in0=B3[:, :, 63:64], in1=mid, op=ALU.subtract), reads=[Bb, Xb], writes=[dvb])
                            S.add("pool", lambda e: e.tensor_tensor(out=dvt[:, :, 1:2], in0=mid, in1=X3[:, :, 0:1], op=ALU.subtract), reads=[Bb, Xb], writes=[dvb])
                        S.add("pool", lambda e: e.tensor_tensor(out=dvt[:, :, 2:3], in0=B3[:, :, 63:64], in1=X3[:, :, 0:1], op=ALU.subtract), reads=[Bb, Xb], writes=[dvb])
                        S.add("act", lambda e: e.activation(out=dvt[:], in_=dvt[:], func=AF.Exp), reads=[dvb], writes=[dvb])
                        if d == 0:
                            o1h, o1hb = o1h_[h % 2]
                            items = []
                            for c in range(8):
                                pre_, post_ = [], []
                                if cond == 1 and c % (SEQ // 64) == 0:
                                    pre_.append(mk_memset(h))
                                if cond == 1 and (c + 1) % (SEQ // 64) == 0:
                                    post_.append(mk_stout(h, c // (SEQ // 64), 0))
                                items.append(dict(h=h, qt=qt[:, c * 64:(c + 1) * 64], qtb=[qtb], kt=kt[:, c * 64:(c + 1) * 64], ktb=[ktb], dv=dvt[:, c, :], dvb=[dvb],
                                                  v=v64[:, c, h * 128:(h + 1) * 128], vb=[b_v64[c]], pre=pre_, post=post_))

                            def do_scan(items=items, o1h=o1h, o1hb=o1hb, h=h):
                                def sink4(half, po_ap, pob, its):
                                    S.add("act", lambda e: e.activation(out=o1h[:, half * 4:(half + 1) * 4, :], in_=po_ap.rearrange("p (c v) -> p c v", v=128), func=AF.Copy), reads=[pob], writes=[o1hb])
                                scan_group(0, items, po2_p1, sink4)
                                S.add("sp", lambda e: e.dma_start(out=O1[t0:t0 + TT, h * 128:(h + 1) * 128].rearrange("(c p) v -> p c v", p=64), in_=o1h[:]), reads=[o1hb], writes=[gb(("O", t0, h))], dma_key="ho%d" % (h % 2))
                            if pend_scan:
                                pend_scan.pop()()
                            pend_scan.append(do_scan)
                        else:
                            S.add("sp", lambda e: e.dma_start(out=Q2[h, :, t0:t0 + TT], in_=qt[:]), reads=[qtb], writes=[gb(("Q", t0, h))], dma_key="hq%d" % (hcount_ref[0] % 2))
                            S.add("sp", lambda e: e.dma_start(out=K2[h, :, t0:t0 + TT], in_=kt[:]), reads=[ktb], writes=[gb(("K", t0, h))], dma_key="hk%d" % (hcount_ref[0] % 2))
                            S.add("sp", lambda e: e.dma_start(out=D2[h, :, ti * 8:(ti + 1) * 8, :], in_=dvt[:]), reads=[dvb], writes=[gb(("D", t0, h))], dma_key="hd%d" % (hcount_ref[0] % 2))
                            hcount_ref[0] += 1
                    hcount_ref = [hcount]
                    pend_scan = []
                    self.linear_fm(wqf, KC, 48, 4, x, T, cons)
                    while pend_scan:
                        pend_scan.pop()()
                    hcount = hcount_ref[0]
                self.linps = lin_saved
                S.barrier()
            with ExitStack() as ph:
                self.big = self.sb(ph, [128, KC, TT], F32, "big")
                self.b_big = [Buf() for c in range(KC)]
                oT = self.sb(ph, [128, 16, TT], BF16, "oT")
                b_oT = [Buf() for _ in range(8)]
                NB = 2
                q2c_ = [(self.sb(ph, [128, 16, 64], BF16, "q2c"), Buf()) for _ in range(NB)]
                k2c_ = [(self.sb(ph, [128, 16, 64], BF16, "k2c"), Buf()) for _ in range(NB)]
                vc_ = [(self.sb(ph, [64, 2048], BF16, "vc"), Buf()) for _ in range(NB)]
                gc_ = [(self.sb(ph, [64, 2048], BF16, "gc"), Buf()) for _ in range(NB)]
                o1c_ = [(self.sb(ph, [64, 2048], F32, "o1c"), Buf()) for _ in range(NB)]
                d2t = self.sb(ph, [128, 16, 8, 3], F32, "d2t")
                b_d2t = Buf()
                osum = self.sb(ph, [64, 16, 128], F32, "osum")
                b_osum = [Buf() for _ in range(16)]
                sqt = self.sb(ph, [64, 2048], F32, "sqt")
                b_sqt = Buf()
                ssq = self.sb(ph, [64, 16], F32, "ssq")
                b_ssq = Buf()
                obf = self.sb(ph, [64, 2048], BF16, "obf")
                b_obf = Buf()
                order = [t for t in self.tiles if t[2] == 1] + [t for t in reversed(self.tiles) if t[2] == 0]
                po2_p2 = [self.linps[0], self.linps[1]]
                first_sample = True
                ci_ = 0
                for tile in order:
                    t0, T, cond = tile
                    ti = t0 // TT
                    if cond == 0 and first_sample:
                        first_sample = False
                        p0 = par[0]
                        assert all(p == p0 for p in par)
                        S.add("sp", lambda e, p0=p0: e.dma_start(out=Sst[p0][:], in_=self.din["hg_s0"][j, 1]), writes=b_S[p0], dma_key="hs1")
                    S.add("sp", lambda e, ti=ti: e.dma_start(out=d2t[:], in_=D2[:, :, ti * 8:(ti + 1) * 8, :].rearrange("h p c k -> p h c k")), reads=[gb(("D", t0, h)) for h in range(16)], writes=[b_d2t], dma_key="hd2")
                    for c in reversed(range(8)):
                        r0 = t0 + c * 64
                        (q2c, q2b), (k2c, k2b), (vc, vcb), (gc, gcb), (o1c, o1b) = q2c_[ci_ % NB], k2c_[ci_ % NB], vc_[ci_ % NB], gc_[ci_ % NB], o1c_[ci_ % NB]
                        kk = ci_ % NB
                        ci_ += 1
                        S.add("sp", lambda e, q2c=q2c, r0=r0: e.dma_start(out=q2c[:], in_=Q2[:, :, r0:r0 + 64].rearrange("h p t -> p h t")), reads=[gb(("Q", t0, h)) for h in range(16)], writes=[q2b], dma_key="p2q%d" % kk)
                        S.add("sp", lambda e, k2c=k2c, r0=r0: e.dma_start(out=k2c[:], in_=K2[:, :, r0:r0 + 64].rearrange("h p t -> p h t")), reads=[gb(("K", t0, h)) for h in range(16)], writes=[k2b], dma_key="p2k%d" % kk)
                        S.add("sp", lambda e, vc=vc, r0=r0: e.dma_start(out=vc[:], in_=V64[r0:r0 + 64, :]), reads=[gb(("V", t0))], writes=[vcb], dma_key="p2v%d" % kk)
                        S.add("sp", lambda e, gc=gc, r0=r0: e.dma_start(out=gc[:], in_=GS[r0:r0 + 64, :]), reads=[gb(("G", t0))], writes=[gcb], dma_key="p2g%d" % kk)
                        S.add("sp", lambda e, o1c=o1c, r0=r0: e.dma_start(out=o1c[:], in_=O1[r0:r0 + 64, :]), reads=[gb(("O", t0, h)) for h in range(16)], writes=[o1b], dma_key="p2o%d" % kk)
                        for g0 in (0, 8):
                            items = []
                            for h in range(g0, g0 + 8):
                                pre_, post_ = [], []
                                if cond == 1 and (c + 1) % (SEQ // 64) == 0:
                                    pre_.append(mk_memset(h))
                                if cond == 1 and c % (SEQ // 64) == 0:
                                    post_.append(mk_stout(h, c // (SEQ // 64), 1))
                                items.append(dict(h=h, qt=q2c[:, h, :], qtb=[q2b], kt=k2c[:, h, :], ktb=[k2b], dv=d2t[:, h, c, :], dvb=[b_d2t],
                                                  v=vc[:, h * 128:(h + 1) * 128], vb=[vcb], pre=pre_, post=post_))

                            def sink4(half, po_ap, pob, its, o1c=o1c, o1b=o1b):
                                h0 = its[0]["h"]
                                S.add("dve", lambda e: e.tensor_tensor(out=osum[:, h0:h0 + 4, :], in0=po_ap.rearrange("p (h v) -> p h v", v=128),
                                                                        in1=o1c[:, h0 * 128:(h0 + 4) * 128].rearrange("p (h v) -> p h v", v=128), op=ALU.add),
                                      reads=[pob, o1b], writes=[b_osum[hh] for hh in range(h0, h0 + 4)])
                            scan_group(1, items, po2_p2, sink4)
                        of = osum[:].rearrange("p h v -> p (h v)")
                        S.add("act", lambda e: e.activation(out=sqt[:], in_=of, func=AF.Square), reads=b_osum, writes=[b_sqt])
                        S.add("dve", lambda e: e.tensor_reduce(out=ssq[:], in_=sqt[:].rearrange("p (h v) -> p h v", v=128), axis=AX.X, op=ALU.add), reads=[b_sqt], writes=[b_ssq])
                        S.add("act", lambda e: e.activation(out=ssq[:], in_=ssq[:], func=AF.Ln, scale=1.0 / 128, bias=self.c_eps[:64, :]), reads=[b_ssq, self.b_const], writes=[b_ssq])
                        S.add("act", lambda e: e.activation(out=ssq[:], in_=ssq[:], func=AF.Exp, scale=-0.5), reads=[b_ssq], writes=[b_ssq])
                        S.add("dve", lambda e: e.tensor_tensor(out=sqt[:].rearrange("p (h v) -> p h v", v=128), in0=osum[:], in1=ssq[:].unsqueeze(2).broadcast_to([64, 16, 128]), op=ALU.mult), reads=b_osum + [b_ssq], writes=[b_sqt])
                        S.add("pool", lambda e, gc=gc: e.tensor_tensor(out=obf[:], in0=sqt[:], in1=gc[:], op=ALU.mult), reads=[b_sqt, gcb], writes=[b_obf])
                        for h in range(16):
                            S.add("pe", lambda e, h=h: e.transpose(self.pstr[:, h * 64:(h + 1) * 64], obf[:, h * 128:(h + 1) * 128], ident[:64, :64]), reads=[b_obf, b_hc], writes=[self.b_pstr])
                        S.add("act", lambda e, c=c: e.activation(out=oT[:, :, c * 64:(c + 1) * 64], in_=self.pstr[:, :].rearrange("p (h t) -> p h t", t=64), func=AF.Copy), reads=[self.b_pstr], writes=[b_oT[c]])

                    def x(kc):
                        return oT[:, kc, :T], b_oT
                    self.linear_fm(wo, KC, 16, 4, x, T, self.post_consume(T))
                    self.post(l, 0, tile, False)
                S.barrier()

    def build(self, kinds):
        nc, S = self.nc, self.S
        NT, L, NS, NP, SEQ = self.NT, self.L, self.NS, self.NP, self.SEQ
        NA = sum(1 for k in kinds if k == 0)
        NB_ = sum(1 for k in kinds if k == 1)
        NC_ = sum(1 for k in kinds if k == 2)
        self.inp("xT", [KC, 128, NT])
        self.inp("cT", [128, KC, 2])
        self.inp("ada_w", [L, 24, 128, KC * 512])
        self.inp("ada_bT", [128, L, 96])
        self.inp("norm_gT", [128, L, 4, KC])
        self.inp("mlp_w_in", [L, 16, 128, KC * 512])
        self.inp("mlp_w_out", [L, 16, 128, 64 * 128])
        self.inp("ropeC", [64, NT])
        self.inp("ropeS", [64, NT])
        self.inp("ident", [128, 128])
        if NA:
            self.inp("hg_w_qf", [NA, 12, 128, KC * 512])
            self.inp("hg_w_ig", [NA, 8, 128, KC * 512])
            self.inp("hg_w_out", [NA, 4, 128, KC * 512])
            self.inp("hg_lbT", [128, 2, L, 16])
            self.inp("hg_gbc", [NA, 64, 2048])
            self.inp("hg_s0", [NA, 2, 128, 16, 128])
            self.inp("hg_masks", [64, 2, 64])
            self.outp("hg_stout", [NA, 2, 2, 128, 16, 128])
        if NB_:
            self.inp("mla_w_down", [NB_, 3, 128, KC * 512])
            self.inp("mla_w_uq", [NB_, 3, 128, 4 * 2048])
            self.inp("mla_w_ukvT", [NB_, 128, 4, 4096])
            self.inp("mla_w_out", [NB_, 4, 128, KC * 512])
            self.inp("mla_gT", [NB_, 128, 2, 4])
            self.inp("mla_ckv_ctxT", [128, 4, 256])
            self.inp("mla_kpe_ctxT", [64, 256])
            self.outp("mla_ckvout", [4, 128, NP])
            self.outp("mla_kpeout", [64, NP])
        if NC_:
            self.inp("swa_w_qk", [NC_, 10, 128, KC * 512])
            self.inp("swa_w_v", [NC_, 1, 128, KC * 512])
            self.inp("swa_w_out", [NC_, 8, 64, 32 * 256])
            self.inp("swa_kctxT", [64, 8, 256])
            self.inp("swa_vctx", [256, 512])
            self.inp("swa_masks", [128, 2, 512])
            self.inp("swa_sink_bc", [NC_, 64, 32])
            self.outp("swa_kout", [64, 8, NP])
            self.outp("swa_vout", [NP, 512])
        self.OUTT = self.outp("yT", [KC, 128, NT])
        self.YT = self.din["xT"]
        YS = self.scratch("YS", [KC, 128, NT])
        self.YS = YS
        with ExitStack() as g:
            self.c_eps = self.sb(g, [128, 1], F32, "eps")
            self.ones_bf = self.sb(g, [128, 128], BF16, "ones")
            self.adab = self.sb(g, [128, L, 96], F32, "adab")
            self.ngT = self.sb(g, [128, L, 4, KC], F32, "ngT")
            self.scT = self.sb(g, [128, KC, 2], BF16, "scT")
            cTf = self.sb(g, [128, KC, 2], F32, "cTf")
            self.modraw = self.sb(g, [128, 96, 2], F32, "modraw")
            self.modA = self.sb(g, [128, 2, KC, 2], F32, "modA")
            self.modG = self.sb(g, [128, 2, KC, 2], F32, "modG")
            self.modS = self.sb(g, [128, 2, KC, 2], F32, "modS")
            self.rstd = self.sb(g, [128, TT], F32, "rstd")
            self.ropeC = self.sb(g, [64, TT], F32, "ropeC")
            self.ropeS = self.sb(g, [64, TT], F32, "ropeS")
            self.b_const, self.b_mod, self.b_rstd, self.b_rope = Buf("const"), Buf("mod"), Buf("rstd"), Buf("rope")
            self.wbufs = [(self.sb(g, [128, 8192], BF16, "wb"), Buf(f"wb{i}")) for i in range(2)]
            self.wrot = 0
            self.tmps = [(self.sb(g, [128, TT], F32, "tmp"), Buf(f"tmp{i}")) for i in range(4)]
            self.tmprot = 0
            self.sqs = [(self.sb(g, [128, TT], BF16, "sq"), Buf(f"sq{i}")) for i in range(2)]
            self.sqrot = 0
            psb = [g.enter_context(nc.psum_tensor(f"ps{i}", [128, 512], F32)) for i in range(7)]
            self.pstr = g.enter_context(nc.psum_tensor("pstr", [128, 1024], BF16))
            self.linps = [(psb[i], Buf(f"lin{i}", True)) for i in range(3)]
            self.linrot = 0
            self.statps = (psb[3], Buf("stat", True))
            self.miscps = [(psb[i], Buf(f"misc{i}", True)) for i in range(4, 7)]
            self.b_pstr = Buf("pstr", True)
            self.accps = [(self.miscps[0], self.miscps[1]), (self.miscps[2], self.statps)]
            self.accrot = 0
            S.add("dve", lambda e: e.memset(self.c_eps[:], EPS), writes=[self.b_const])
            S.add("dve", lambda e: e.memset(self.ones_bf[:], 1.0), writes=[self.b_const])
            S.add("sp", lambda e: e.dma_start(out=self.adab[:], in_=self.din["ada_bT"]), writes=[self.b_const], dma_key="c0")
            S.add("sp", lambda e: e.dma_start(out=self.ngT[:], in_=self.din["norm_gT"]), writes=[self.b_const], dma_key="c1")
            S.add("sp", lambda e: e.dma_start(out=cTf[:], in_=self.din["cT"]), writes=[self.b_const], dma_key="c2")
            S.add("act", lambda e: e.activation(out=self.scT[:], in_=cTf[:], func=AF.Silu), reads=[self.b_const], writes=[self.b_const])
            if NA:
                self.hgrn_setup(g)
            cnt = [0, 0, 0]
            for l in range(L):
                self.adaln(l)
                S.barrier()
                kind = kinds[l]
                if kind == 0:
                    self.hgrn_layer(l, cnt[0])
                elif kind == 1:
                    self.mla_layer(l, cnt[1])
                elif kind == 2:
                    self.swa_layer(l, cnt[2])
                if kind >= 0:
                    cnt[kind] += 1
                    self.YT = YS
                with ExitStack() as ph:
                    self.big = self.sb(ph, [128, KC, TT], F32, "big")
                    self.b_big = [Buf(f"big{c}") for c in range(KC)]
                    self.hT = self.sb(ph, [128, KC, TT], BF16, "hT")
                    self.b_hT = [Buf(f"hT{c}") for c in range(KC)]
                    self.aT = self.sb(ph, [128, 64, TT], BF16, "aT")
                    self.b_aT = [Buf(f"aT{c}") for c in range(64)]
                    for tile in self.tiles:
                        self.mlp(l, tile, l == L - 1)
                    S.barrier()
                self.YT = YS
            S.emit()
        return nc


ROPE_BASE = 10000.0


def _partner():
    d = np.arange(64)
    return np.where(d % 32 < 16, d + 16, d - 16)


def _rope_tables(NS, NP, GW):
    d = np.arange(64)
    inv = ROPE_BASE ** (-(d % 16).astype(np.float32) / 16.0)
    t = np.arange(NS)
    pos = np.where(d[:, None] < 32, (t // GW)[None, :], (t % GW)[None, :]).astype(np.float32)
    ang = pos * inv[:, None].astype(np.float32)
    c = np.cos(ang).astype(np.float32)
    sn = np.sin(ang).astype(np.float32)
    sn = np.where((d % 32 < 16)[:, None], -sn, sn)
    c = np.concatenate([c, np.ones((64, NP), np.float32)], axis=1)
    sn = np.concatenate([sn, np.zeros((64, NP), np.float32)], axis=1)
    return np.ascontiguousarray(c, np.float32), np.ascontiguousarray(sn, np.float32)


def _shared_inputs(inp, L, kinds, NS, NP, GW):
    m = {}
    m["ada_w"] = np.stack([tile_w(inp["ada_w"][l], plain_chunks(96), 4) for l in range(L)])
    m["ada_bT"] = np.ascontiguousarray(fm(inp["ada_b"][:L]))
    m["norm_gT"] = np.ascontiguousarray(fm(inp["norm_g"][:L]))
    m["mlp_w_in"] = np.stack([tile_w(inp["mlp_w_in"][l], plain_chunks(64), 4) for l in range(L)])
    m["mlp_w_out"] = np.stack([tile_w(inp["mlp_w_out"][l], plain_chunks(16), 1) for l in range(L)])
    m["ropeC"], m["ropeS"] = _rope_tables(NS, NP, GW)
    m["ident"] = np.eye(128, dtype=np.float32)
    par = _partner()
    NA = sum(1 for k in kinds if k == 0)
    NB_ = sum(1 for k in kinds if k == 1)
    NC_ = sum(1 for k in kinds if k == 2)
    if NA:
        qf, ig, wo, gbc = [], [], [], []
        for j in range(NA):
            W = inp["hgrn_w_in"][j]
            ch = []
            for h in range(16):
                ch += [np.arange(h * 128, (h + 1) * 128), np.arange(2048 + h * 128, 2048 + (h + 1) * 128), np.arange(4096 + h * 128, 4096 + (h + 1) * 128)]
            qf.append(tile_w(W, ch, 4))
            ig.append(tile_w(W, plain_chunks(32, start=6144), 4))
            wo.append(tile_w(inp["hgrn_w_out"][j], plain_chunks(16), 4))
            gbc.append(np.broadcast_to(np.tile(inp["hgrn_norm_g"][j], 16)[None, :], (64, 2048)))
        m["hg_w_qf"], m["hg_w_ig"], m["hg_w_out"] = np.stack(qf), np.stack(ig), np.stack(wo)
        m["hg_gbc"] = np.ascontiguousarray(np.stack(gbc), np.float32)
        lg = inp["hgrn_lb_logits"][:, :L]
        m["hg_lbT"] = np.ascontiguousarray(lg.reshape(2, L, 16, 128).transpose(3, 0, 1, 2), np.float32)
        s_, t_ = np.arange(64)[:, None], np.arange(64)[None, :]
        m["hg_masks"] = np.ascontiguousarray(np.stack([(s_ <= t_), (s_ >= t_)], axis=1).astype(np.float32))
    if NB_:
        wd, wq, wkv, wo, gT = [], [], [], [], []
        for j in range(NB_):
            W = inp["mla_w_down"][j]
            ch = plain_chunks(8) + [np.arange(1024, 1088), 1024 + par]
            wd.append(tile_w(W, ch, 4))
            U = inp["mla_w_uq"][j]
            ch = [np.arange(h * 192, h * 192 + 128) for h in range(16)]
            for h in range(16):
                ch += [h * 192 + 128 + np.arange(64), h * 192 + 128 + par]
            wq.append(tile_w(U, ch, 16))
            wkv.append(inp["mla_w_ukv"][j].reshape(4, 128, 4096).transpose(1, 0, 2))
            wo.append(tile_w(inp["mla_w_out"][j], plain_chunks(16), 4))
            gT.append(np.stack([fm(inp["mla_q_norm_g"][j]), fm(inp["mla_kv_norm_g"][j])], axis=1))
        m["mla_w_down"], m["mla_w_uq"], m["mla_w_out"] = np.stack(wd), np.stack(wq), np.stack(wo)
        m["mla_w_ukvT"] = np.ascontiguousarray(np.stack(wkv), np.float32)
        m["mla_gT"] = np.ascontiguousarray(np.stack(gT), np.float32)
    if NC_:
        wqk, wv, wo, sk = [], [], [], []
        for j in range(NC_):
            W = inp["swa_w_qkv"][j]
            ch = []
            for h in range(32):
                ch += [h * 64 + np.arange(64), h * 64 + par]
            for g_ in range(8):
                ch += [2048 + g_ * 64 + np.arange(64), 2048 + g_ * 64 + par]
            wqk.append(tile_w(W, ch, 8, cw=64))
            wv.append(tile_w(W, plain_chunks(4, start=2560), 4))
            wo.append(tile_w(inp["swa_w_out"][j], plain_chunks(16), 2, kp=64))
            sk.append(np.broadcast_to(inp["swa_sink"][j][None, :], (64, 32)))
        m["swa_w_qk"], m["swa_w_v"], m["swa_w_out"] = np.stack(wqk), np.stack(wv), np.stack(wo)
        m["swa_sink_bc"] = np.ascontiguousarray(np.stack(sk), np.float32)
        c_, a_ = np.arange(128)[:, None], np.arange(128)[None, :]
        m0 = np.tile((c_ >= a_).astype(np.float32), (1, 4))
        m1 = np.tile((c_ <= a_).astype(np.float32), (1, 4))
        m["swa_masks"] = np.ascontiguousarray(np.stack([m0, m1], axis=1))
    return m


def _host_inputs(inp, core, NS, NP, SEQ, kinds):
    b = core % inp["x_sample"].shape[0]
    nps = NP // SEQ
    xs = inp["x_sample"][b, :NS]
    xp = inp["x_prompt"][core * nps:(core + 1) * nps].reshape(NP, D)
    x = np.concatenate([xs, xp], axis=0)
    m = {}
    m["xT"] = np.ascontiguousarray(x.T.reshape(KC, 128, -1))
    cc = np.stack([inp["c"][b], inp["c_ctx"]], axis=-1)
    m["cT"] = np.ascontiguousarray(cc.reshape(KC, 128, 2).transpose(1, 0, 2))
    if 0 in kinds:
        m["hg_s0"] = np.ascontiguousarray(inp["state_hgrn"][b].transpose(0, 1, 3, 2, 4))
    if 1 in kinds:
        m["mla_ckv_ctxT"] = np.ascontiguousarray(inp["cache_mla_ckv"][b, 0].T.reshape(4, 128, -1).transpose(1, 0, 2))
        m["mla_kpe_ctxT"] = np.ascontiguousarray(inp["cache_mla_kpe"][b, 0].T)
    if 2 in kinds:
        m["swa_kctxT"] = np.ascontiguousarray(inp["cache_swa_k"][b, 0].transpose(2, 1, 0))
        m["swa_vctx"] = np.ascontiguousarray(inp["cache_swa_v"][b, 0].reshape(-1, 512))
    return m


def run(inp, NS, NP, SEQ, L, GRID_W, ncores, kinds):
    inp = {k: np.asarray(v) for k, v in inp.items()}
    p = Prog(NS, NP, SEQ, L, GRID_W)
    nc = p.build(kinds)
    shared = _shared_inputs(inp, L, kinds, NS, NP, GRID_W)
    maps = []
    for c in range(ncores):
        m = dict(shared)
        m.update(_host_inputs(inp, c, NS, NP, SEQ, kinds))
        maps.append(m)
    res = run_bass_kernel_spmd(nc, maps, core_ids=list(range(ncores)))
    return res.results


def assemble(r, NS, NP, SEQ, ncores, nsamp, kinds):
    nps = NP // SEQ
    out = {}
    out["ys"] = np.stack([r[c]["yT"].reshape(D, -1)[:, :NS].T for c in range(nsamp)])
    out["yp"] = np.concatenate([r[c]["yT"].reshape(D, -1)[:, NS:].T.reshape(nps, SEQ, D) for c in range(ncores)])
    if 0 in kinds:
        out["st"] = np.concatenate([r[c]["hg_stout"].transpose(1, 0, 2, 4, 3, 5) for c in range(ncores)])
    if 1 in kinds:
        out["ckv"] = np.concatenate([r[c]["mla_ckvout"].reshape(512, NP).T.reshape(nps, 1, SEQ, 512) for c in range(ncores)])
        out["kpe"] = np.concatenate([r[c]["mla_kpeout"].T.reshape(nps, 1, SEQ, 64) for c in range(ncores)])
    if 2 in kinds:
        out["k"] = np.concatenate([r[c]["swa_kout"].transpose(2, 1, 0).reshape(nps, 1, SEQ, 8, 64) for c in range(ncores)])
        out["v"] = np.concatenate([r[c]["swa_vout"].reshape(nps, 1, SEQ, 8, 64) for c in range(ncores)])
    return out


def kernel(**inputs):
    NS, NP, SEQ, L, GW = 4096, 512, 256, 4, 64
    kinds = [0, 1, 2, 0]
    r = run(inputs, NS, NP, SEQ, L, GW, 8, kinds)
    o = assemble(r, NS, NP, SEQ, 8, 4, kinds)
    f = lambda a: np.ascontiguousarray(a, dtype=np.float32)
    return (f(o["yp"]), f(o["ys"]), f(o["st"]), f(o["ckv"]), f(o["kpe"]), f(o["k"]), f(o["v"]))
```

```python
import numpy as np
from contextlib import ExitStack
import concourse.bass as bass
import concourse.mybir as mybir
from concourse.bass_utils import run_bass_kernel_spmd

F32 = mybir.dt.float32
BF16 = mybir.dt.bfloat16
AF = mybir.ActivationFunctionType
ALU = mybir.AluOpType
AX = mybir.AxisListType

SEM_CAP = 20000
D = 2048
KC = 16
TT = 512
EPS = 1e-6


class Buf:
    __slots__ = ("name", "last_w", "readers", "excl")

    def __init__(self, name="", excl=False):
        self.name = name
        self.last_w = None
        self.readers = []
        self.excl = excl


class Op:
    __slots__ = ("eng", "fn", "deps", "signal", "val", "semi", "dma", "dsem", "dval", "idx")

    def __init__(self, eng, fn, dma):
        self.eng = eng
        self.fn = fn
        self.deps = []
        self.signal = False
        self.val = None
        self.semi = None
        self.dma = dma
        self.dsem = None
        self.dval = None


class Sched:
    ENGS = ("pe", "act", "dve", "pool", "sp")

    def __init__(self, nc):
        self.nc = nc
        self.ops = {e: [] for e in self.ENGS}
        self.dma_cnt = {}
        self.dma_last = {}
        self.pending = {e: [] for e in self.ENGS}
        self.n = 0

    def add(self, eng, fn, reads=(), writes=(), dma_key=None):
        op = Op(eng, fn, dma_key is not None)
        op.idx = self.n
        self.n += 1
        deps = set(self.pending[eng])
        self.pending[eng] = []
        if dma_key is not None and dma_key in self.dma_last:
            deps.add(self.dma_last[dma_key])
        for b in reads:
            if b.last_w is not None:
                deps.add(b.last_w)
            if b.excl:
                for r in b.readers:
                    if r.eng != eng:
                        deps.add(r)
        for b in writes:
            if b.last_w is not None:
                deps.add(b.last_w)
            deps.update(b.readers)
        last = {}
        for d in deps:
            if d.dma:
                op.deps.append(d)
            elif d.eng not in last or last[d.eng].idx < d.idx:
                last[d.eng] = d
        for d in last.values():
            if d.eng == "pe" and eng == "pe" and not op.dma:
                continue
            d.signal = True
            op.deps.append(d)
        for b in reads:
            b.readers.append(op)
        for b in writes:
            b.last_w = op
            b.readers = []
        if dma_key is not None:
            c = self.dma_cnt.get(dma_key, 0) + 16
            self.dma_cnt[dma_key] = c
            op.dsem = dma_key
            op.dval = c
            self.dma_last[dma_key] = op
        self.ops[eng].append(op)
        return op

    def barrier(self):
        snap = []
        for e in self.ENGS:
            for op in reversed(self.ops[e]):
                if not op.dma:
                    op.signal = True
                    snap.append(op)
                    break
        snap.extend(self.dma_last.values())
        for e in self.ENGS:
            self.pending[e] = list(snap)

    def emit(self):
        nc = self.nc
        nsem = {}
        for e in self.ENGS:
            c = 0
            for op in self.ops[e]:
                if op.dma or not op.signal:
                    continue
                op.semi = c // SEM_CAP
                op.val = c % SEM_CAP + 1
                c += 1
            nsem[e] = max(1, (c + SEM_CAP - 1) // SEM_CAP)
        with ExitStack() as st:
            esem = {e: [st.enter_context(nc.semaphore(f"s_{e}_{i}")) for i in range(nsem[e])] for e in self.ENGS}
            dsem = {k: st.enter_context(nc.semaphore(f"d_{i}")) for i, k in enumerate(self.dma_cnt)}
            block = st.enter_context(nc.Block())
            handles = {"pe": block.tensor, "act": block.scalar, "dve": block.vector, "pool": block.gpsimd,
                       "sp": block.sync}

            def make(e):
                def body(eng):
                    waited = {}
                    for op in self.ops[e]:
                        need = {}
                        for d in op.deps:
                            if d.dma:
                                k, v = ("d", d.dsem), d.dval
                            else:
                                k, v = ("e", d.eng, d.semi), d.val
                            if need.get(k, 0) < v:
                                need[k] = v
                        for k, v in need.items():
                            if waited.get(k, 0) >= v:
                                continue
                            waited[k] = v
                            eng.wait_ge(dsem[k[1]] if k[0] == "d" else esem[k[1]][k[2]], v)
                        ins = op.fn(eng)
                        if op.dma:
                            ins.then_inc(dsem[op.dsem], 16)
                        elif op.signal:
                            ins.then_inc(esem[e][op.semi], 1)
                    if e == "sp":
                        for k, tot in self.dma_cnt.items():
                            if waited.get(("d", k), 0) < tot:
                                eng.wait_ge(dsem[k], tot)
                return body

            for e in self.ENGS:
                if self.ops[e] or e == "sp":
                    handles[e](make(e))


def tile_w(W, chunks, spc, kp=128, cw=128):
    K = W.shape[0]
    kc = K // kp
    ns = (len(chunks) + spc - 1) // spc
    out = np.zeros((ns, kp, kc, spc * cw), np.float32)
    Wr = W.reshape(kc, kp, W.shape[1])
    for i, cols in enumerate(chunks):
        s, j = divmod(i, spc)
        out[s, :, :, j * cw:j * cw + len(cols)] = Wr[:, :, cols].transpose(1, 0, 2)
    return out.reshape(ns, kp, kc * spc * cw)


def plain_chunks(n, w=128, start=0):
    return [np.arange(start + i * w, start + (i + 1) * w) for i in range(n)]


def fm(vec):
    v = np.asarray(vec, np.float32)
    v = v.reshape(v.shape[:-1] + (v.shape[-1] // 128, 128))
    return np.ascontiguousarray(np.moveaxis(v, -1, 0))


class Prog:
    def __init__(self, NS, NP, SEQ, DEPTH, GRID_W):
        self.NS, self.NP, self.SEQ, self.L, self.GW = NS, NP, SEQ, DEPTH, GRID_W
        self.NT = NS + NP
        self.tiles = [(t0, TT, 0) for t0 in range(0, NS, TT)] + [(NS + t0, TT, 1) for t0 in range(0, NP, TT)]
        self.nc = bass.Bass("TRN2", target_bir_lowering=False)
        self.S = Sched(self.nc)
        self.din = {}
        self.dout = {}
        self.uid = 0
        self.bYd = {}
        self.rope_t0 = None

    def inp(self, name, shape):
        self.din[name] = self.nc.dram_tensor(name, list(shape), F32, kind="ExternalInput").ap()
        return self.din[name]

    def outp(self, name, shape):
        self.dout[name] = self.nc.dram_tensor(name, list(shape), F32, kind="ExternalOutput").ap()
        return self.dout[name]

    def scratch(self, name, shape, dt=F32):
        return self.nc.dram_tensor(name, list(shape), dt).ap()

    def sb(self, st, shape, dt, name=None):
        self.uid += 1
        return st.enter_context(self.nc.sbuf_tensor(f"{name or 't'}_{self.uid}", list(shape), dt))

    def bY(self, c, t0):
        k = (c, t0)
        if k not in self.bYd:
            self.bYd[k] = Buf(f"Y{c}_{t0}")
        return self.bYd[k]

    def key(self, p="k"):
        self.uid += 1
        return f"{p}{self.uid}"

    def linear_fm(self, wd, KCn, nchunks, spc, x, T, consume, widths=None, kp=128, cw=128):
        S = self.S
        ns = (nchunks + spc - 1) // spc
        sc = spc * cw
        wb = self.wbufs

        def load(s):
            t, b = wb[self.wrot % len(wb)]
            self.wrot += 1
            S.add("pool", lambda e, t=t, s=s: e.dma_start(out=t[:kp, :KCn * sc], in_=wd[s]), writes=[b], dma_key=b.name)
            return t, b

        nxt = load(0)
        for s in range(ns):
            t, b = nxt
            if s + 1 < ns:
                nxt = load(s + 1)
            for j in range(min(spc, nchunks - s * spc)):
                ci = s * spc + j
                w = cw if widths is None else widths[ci]
                ps, pb = self.linps[self.linrot % len(self.linps)]
                self.linrot += 1
                for kc in range(KCn):
                    xa, xb = x(kc)
                    S.add("pe", lambda e, ps=ps, t=t, kc=kc, j=j, w=w, xa=xa: e.matmul(
                        ps[:w, :T], lhsT=t[:kp, kc * sc + j * cw: kc * sc + j * cw + w], rhs=xa,
                        start=(kc == 0), stop=(kc == KCn - 1)), reads=[b] + xb, writes=[pb])
                consume(ci, ps[:w, :T], pb)

    def linear_tm(self, wd, KCn, nslabs, x, T, blk, consume):
        S = self.S
        wb = self.wbufs

        def load(s):
            t, b = wb[self.wrot % len(wb)]
            self.wrot += 1
            S.add("pool", lambda e, t=t, s=s: e.dma_start(out=t[:, :KCn * 512], in_=wd[s]), writes=[b], dma_key=b.name)
            return t, b

        nxt = load(0)
        for s in range(nslabs):
            t, b = nxt
            if s + 1 < nslabs:
                nxt = load(s + 1)
            for tb in range(T // blk):
                ps, pb = self.linps[self.linrot % len(self.linps)]
                self.linrot += 1
                for kc in range(KCn):
                    xa, xb = x(kc, tb)
                    S.add("pe", lambda e, ps=ps, t=t, kc=kc, xa=xa: e.matmul(
                        ps[:blk, :512], lhsT=xa, rhs=t[:, kc * 512:(kc + 1) * 512],
                        start=(kc == 0), stop=(kc == KCn - 1)), reads=[b] + xb, writes=[pb])
                consume(s, tb, ps[:blk, :512], pb)

    def rstd_from(self, ps_ap, pb, n, out_t, out_b, parts=128):
        S = self.S
        T = ps_ap.shape[-1]
        S.add("act", lambda e: e.activation(out=out_t, in_=ps_ap, func=AF.Ln, scale=1.0 / n, bias=self.c_eps[:parts, :]),
              reads=[pb, self.b_const], writes=[out_b])
        S.add("act", lambda e: e.activation(out=out_t, in_=out_t, func=AF.Exp, scale=-0.5), reads=[out_b], writes=[out_b])

    def sumsq_acc(self, src_ap, src_bufs, first, last, T, parts=128):
        S = self.S
        sq, sqb = self.sqs[self.sqrot % len(self.sqs)]
        self.sqrot += 1
        S.add("act", lambda e: e.activation(out=sq[:parts, :T], in_=src_ap, func=AF.Square), reads=src_bufs, writes=[sqb])
        st, stb = self.statps
        S.add("pe", lambda e: e.matmul(st[:, :T], lhsT=self.ones_bf[:parts, :], rhs=sq[:parts, :T], start=first, stop=last),
              reads=[sqb, self.b_const], writes=[stb])

    def pre(self, l, k, tile):
        S = self.S
        t0, T, cond = tile
        big, hT, YT = self.big, self.hT, self.YT
        for c in range(KC):
            S.add("sp", lambda e, c=c: e.dma_start(out=big[:, c, :T], in_=YT[c, :, t0:t0 + T]),
                  reads=[self.bY(c, t0)], writes=[self.b_big[c]], dma_key=f"big{c % 4}")
            self.sumsq_acc(big[:, c, :T], [self.b_big[c]], c == 0, c == KC - 1, T)
        self.rstd_from(self.statps[0][:, :T], self.statps[1], D, self.rstd[:, :T], self.b_rstd)
        for c in range(KC):
            tmp, tb = self.tmps[self.tmprot % len(self.tmps)]
            self.tmprot += 1
            S.add("dve", lambda e, c=c, tmp=tmp: e.tensor_tensor(out=tmp[:, :T], in0=big[:, c, :T], in1=self.rstd[:, :T], op=ALU.mult),
                  reads=[self.b_big[c], self.b_rstd], writes=[tb])
            S.add("act", lambda e, c=c, tmp=tmp: e.activation(out=hT[:, c, :T], in_=tmp[:, :T], func=AF.Identity,
                                                               scale=self.modA[:, k, c, cond:cond + 1], bias=self.modS[:, k, c, cond:cond + 1]),
                  reads=[tb, self.b_mod], writes=[self.b_hT[c]])

    def post_consume(self, T):
        S = self.S
        big = self.big

        def consume(ci, ps, pb):
            S.add("act", lambda e: e.activation(out=big[:, ci, :T], in_=ps, func=AF.Copy), reads=[pb], writes=[self.b_big[ci]])
            self.sumsq_acc(big[:, ci, :T], [self.b_big[ci]], ci == 0, ci == KC - 1, T)
        return consume

    def post(self, l, k, tile, final):
        S = self.S
        t0, T, cond = tile
        big, YT = self.big, self.YT
        self.rstd_from(self.statps[0][:, :T], self.statps[1], D, self.rstd[:, :T], self.b_rstd)
        for c in range(KC):
            tmp, tb = self.tmps[self.tmprot % len(self.tmps)]
            self.tmprot += 1
            S.add("sp", lambda e, c=c, tmp=tmp: e.dma_start(out=tmp[:, :T], in_=YT[c, :, t0:t0 + T]), reads=[self.bY(c, t0)], writes=[tb], dma_key=tb.name)
            S.add("dve", lambda e, c=c: e.scalar_tensor_tensor(out=big[:, c, :T], in0=big[:, c, :T], scalar=self.modG[:, k, c, cond:cond + 1],
                                                                in1=self.rstd[:, :T], op0=ALU.mult, op1=ALU.mult),
                  reads=[self.b_big[c], self.b_rstd, self.b_mod], writes=[self.b_big[c]])
            S.add("pool", lambda e, c=c, tmp=tmp: e.tensor_tensor(out=tmp[:, :T], in0=tmp[:, :T], in1=big[:, c, :T], op=ALU.add),
                  reads=[tb, self.b_big[c]], writes=[tb])
            dst = self.OUTT if final else self.YS
            S.add("sp", lambda e, c=c, tmp=tmp, dst=dst: e.dma_start(out=dst[c, :, t0:t0 + T], in_=tmp[:, :T]), reads=[tb], writes=[self.bY(c, t0)],
                  dma_key=tb.name + "s")

    def adaln(self, l):
        S = self.S
        mr = self.modraw

        def x(kc):
            return self.scT[:, kc, :], [self.b_const]

        def consume(ci, ps, pb):
            S.add("dve", lambda e: e.tensor_scalar(out=mr[:, ci, :], in0=ps, scalar1=self.adab[:, l, ci:ci + 1], scalar2=None, op0=ALU.add),
                  reads=[pb, self.b_const], writes=[self.b_mod])
        self.linear_fm(self.din["ada_w"][l], KC, 96, 4, x, 2, consume)
        for k in range(2):
            sh = mr[:, (3 * k) * 16:(3 * k + 1) * 16, :]
            scl = mr[:, (3 * k + 1) * 16:(3 * k + 2) * 16, :]
            gt = mr[:, (3 * k + 2) * 16:(3 * k + 3) * 16, :]
            gpre = self.ngT[:, l, 2 * k, :].unsqueeze(2).broadcast_to([128, 16, 2])
            gpost = self.ngT[:, l, 2 * k + 1, :].unsqueeze(2).broadcast_to([128, 16, 2])
            S.add("dve", lambda e, k=k, scl=scl, gpre=gpre: e.scalar_tensor_tensor(out=self.modA[:, k, :, :], in0=scl, scalar=1.0, in1=gpre, op0=ALU.add, op1=ALU.mult),
                  reads=[self.b_mod, self.b_const], writes=[self.b_mod])
            S.add("dve", lambda e, k=k, gt=gt, gpost=gpost: e.tensor_tensor(out=self.modG[:, k, :, :], in0=gt, in1=gpost, op=ALU.mult),
                  reads=[self.b_mod, self.b_const], writes=[self.b_mod])
            S.add("dve", lambda e, k=k, sh=sh: e.tensor_copy(out=self.modS[:, k, :, :], in_=sh), reads=[self.b_mod], writes=[self.b_mod])

    def mlp(self, l, tile, final):
        S = self.S
        t0, T, cond = tile
        self.pre(l, 1, tile)
        hT, aT = self.hT, self.aT

        def x1(kc):
            return hT[:, kc, :T], [self.b_hT[kc]]

        def c1(ci, ps, pb):
            tmp, tb = self.tmps[self.tmprot % len(self.tmps)]
            self.tmprot += 1
            S.add("act", lambda e: e.activation(out=tmp[:, :T], in_=ps, func=AF.Relu), reads=[pb], writes=[tb])
            eng = "dve" if ci % 2 == 0 else "pool"
            S.add(eng, lambda e: e.tensor_tensor(out=aT[:, ci, :T], in0=tmp[:, :T], in1=tmp[:, :T], op=ALU.mult), reads=[tb], writes=[self.b_aT[ci]])
        self.linear_fm(self.din["mlp_w_in"][l], KC, 64, 4, x1, T, c1)

        def x2(kc):
            return aT[:, kc, :T], [self.b_aT[kc]]
        self.linear_fm(self.din["mlp_w_out"][l], 64, 16, 1, x2, T, self.post_consume(T))
        self.post(l, 1, tile, final)


    def attend(self, pieces_q, nkb, kpieces, vblk, Tq, dv, scale, maskf, finish):
        S = self.S
        (o_ps, o_pb), (d_ps, d_pb) = self.accps[self.accrot % 2]
        self.accrot += 1
        for kb in range(nkb):
            ps, pb = self.linps[self.linrot % len(self.linps)]
            self.linrot += 1
            kp = kpieces(kb)
            for i, ((qa, qb), (ka, kbufs)) in enumerate(zip(pieces_q, kp)):
                S.add("pe", lambda e, ps=ps, ka=ka, qa=qa, i=i: e.matmul(ps[:, :Tq], lhsT=ka, rhs=qa, start=(i == 0), stop=(i == len(kp) - 1)),
                      reads=qb + kbufs, writes=[pb])
            pt, ptb = self.pts[self.ptrot % len(self.pts)]
            self.ptrot += 1
            S.add("act", lambda e, ps=ps, pt=pt: e.activation(out=pt[:, :Tq], in_=ps[:, :Tq], func=AF.Exp, scale=scale), reads=[pb], writes=[ptb])
            m = maskf(kb) if maskf is not None else None
            if m is not None:
                ma, mb = m
                S.add("pool", lambda e, pt=pt, ma=ma: e.tensor_tensor(out=pt[:, :Tq], in0=pt[:, :Tq], in1=ma, op=ALU.mult), reads=[ptb] + mb, writes=[ptb])
            va, vb = vblk(kb)
            S.add("pe", lambda e, pt=pt, va=va, kb=kb: e.matmul(o_ps[:dv, :Tq], lhsT=va, rhs=pt[:, :Tq], start=(kb == 0), stop=(kb == nkb - 1)),
                  reads=[ptb] + vb, writes=[o_pb])
            S.add("pe", lambda e, pt=pt, kb=kb: e.matmul(d_ps[:dv, :Tq], lhsT=self.ones_bf[:, :dv], rhs=pt[:, :Tq], start=(kb == 0), stop=(kb == nkb - 1)),
                  reads=[ptb, self.b_const], writes=[d_pb])
        finish(o_ps[:dv, :Tq], o_pb, d_ps[:dv, :Tq], d_pb)

    def attn_finish(self, out_ap, out_bufs, sink_ap=None, then=None):
        S = self

        def fin(o_ps, o_pb, d_ps, d_pb):
            rd, rdb = self.tmps[self.tmprot % len(self.tmps)]
            self.tmprot += 1
            p, T = o_ps.shape[0], o_ps.shape[-1]
            rda = rd[:p, :T]
            if sink_ap is not None:
                self.S.add("dve", lambda e: e.tensor_tensor(out=rda.rearrange("p (g t) -> p g t", g=4), in0=d_ps.rearrange("p (g t) -> p g t", g=4), in1=sink_ap, op=ALU.add),
                           reads=[d_pb, self.b_const], writes=[rdb])
                self.S.add("dve", lambda e: e.reciprocal(out=rda, in_=rda), reads=[rdb], writes=[rdb])
            else:
                self.S.add("dve", lambda e: e.reciprocal(out=rda, in_=d_ps), reads=[d_pb], writes=[rdb])
            self.S.add("dve", lambda e: e.tensor_tensor(out=out_ap, in0=o_ps, in1=rda, op=ALU.mult), reads=[o_pb, rdb], writes=out_bufs)
            if then is not None:
                then()
        return fin

    def rope_pair(self, t0, T, dst_fn):
        S = self.S
        st = {}
        if self.rope_t0 != t0:
            self.rope_t0 = t0
            S.add("sp", lambda e: e.dma_start(out=self.ropeC[:, :T], in_=self.din["ropeC"][:, t0:t0 + T]), writes=[self.b_rope], dma_key="c3")
            S.add("sp", lambda e: e.dma_start(out=self.ropeS[:, :T], in_=self.din["ropeS"][:, t0:t0 + T]), writes=[self.b_rope], dma_key="c4")

        def consume(ci, ps, pb):
            if ci % 2 == 0:
                tmp, tb = self.tmps[self.tmprot % len(self.tmps)]
                self.tmprot += 1
                st["a"] = (tmp, tb)
                S.add("dve", lambda e: e.tensor_tensor(out=tmp[:64, :T], in0=ps, in1=self.ropeC[:, :T], op=ALU.mult), reads=[pb, self.b_rope], writes=[tb])
            else:
                tmp, tb = st["a"]
                t2, tb2 = self.tmps[self.tmprot % len(self.tmps)]
                self.tmprot += 1
                S.add("dve", lambda e: e.tensor_tensor(out=t2[:64, :T], in0=ps, in1=self.ropeS[:, :T], op=ALU.mult), reads=[pb, self.b_rope], writes=[tb2])
                da, db = dst_fn(ci // 2)
                S.add("pool", lambda e: e.tensor_tensor(out=da, in0=tmp[:64, :T], in1=t2[:64, :T], op=ALU.add), reads=[tb, tb2], writes=db)
        return consume

    def swa_layer(self, l, j):
        nc, S = self.nc, self.S
        NT, NS, NP, SEQ = self.NT, self.NS, self.NP, self.SEQ
        QT = self.scratch(f"swaQ{l}", [64, 32, NT], BF16)
        KT = self.scratch(f"swaK{l}", [64, 8, NT], BF16)
        VV = self.scratch(f"swaV{l}", [NT, 512], BF16)
        OT = self.scratch(f"swaO{l}", [64, 32, NT], BF16)
        bQ, bK, bV, bO = {}, {}, {}, {}

        def gb(d, k):
            if k not in d:
                d[k] = Buf()
            return d[k]
        wq = self.din["swa_w_qk"][j]
        wv = self.din["swa_w_v"][j]
        wo = self.din["swa_w_out"][j]
        with ExitStack() as ph:
            self.big = self.sb(ph, [128, KC, TT], F32, "big")
            self.b_big = [Buf() for c in range(KC)]
            self.hT = self.sb(ph, [128, KC, TT], BF16, "hT")
            self.b_hT = [Buf() for c in range(KC)]
            hT = self.hT
            qst = self.sb(ph, [64, 40, TT], BF16, "qst")
            b_qst = [Buf() for _ in range(40)]
            vst = self.sb(ph, [128, 4, 512], BF16, "vst")
            b_vst = [Buf() for _ in range(4)]
            kf = self.sb(ph, [64, 8, TT], F32, "kf")
            vf = self.sb(ph, [128, 4, 512], F32, "vf")
            b_kf, b_vf = Buf(), Buf()
            for tile in self.tiles:
                t0, T, cond = tile
                self.pre(l, 0, tile)

                def x(kc):
                    return hT[:, kc, :T], [self.b_hT[kc]]
                rp = self.rope_pair(t0, T, lambda h: (qst[:, h, :T], [b_qst[h]]))

                def cons(ci, ps, pb, rp=rp, cond=cond, T=T):
                    rp(ci, ps, pb)
                    if cond == 1 and ci >= 64 and ci % 2 == 0:
                        S.add("act", lambda e: e.activation(out=kf[:, (ci - 64) // 2, :T], in_=ps, func=AF.Copy), reads=[pb], writes=[b_kf])
                self.linear_fm(wq, KC, 80, 8, x, T, cons, kp=128, cw=64)
                S.add("sp", lambda e, t0=t0, T=T: e.dma_start(out=QT[:, :, t0:t0 + T], in_=qst[:, 0:32, :T]), reads=b_qst[:32], writes=[gb(bQ, t0)], dma_key="swq")
                S.add("sp", lambda e, t0=t0, T=T: e.dma_start(out=KT[:, :, t0:t0 + T], in_=qst[:, 32:40, :T]), reads=b_qst[32:], writes=[gb(bK, t0)], dma_key="swk")

                def xv(kc, tb):
                    return hT[:, kc, tb * 128:(tb + 1) * 128], [self.b_hT[kc]]

                def consv(s_, tb, ps, pb, cond=cond):
                    S.add("act", lambda e: e.activation(out=vst[:, tb, :], in_=ps, func=AF.Copy), reads=[pb], writes=[b_vst[tb]])
                    if cond == 1:
                        S.add("dve", lambda e: e.tensor_copy(out=vf[:, tb, :], in_=ps), reads=[pb], writes=[b_vf])
                self.linear_tm(wv, KC, 1, xv, T, 128, consv)
                S.add("sp", lambda e, t0=t0, T=T: e.dma_start(out=VV[t0:t0 + T, :].rearrange("(b p) n -> p b n", p=128), in_=vst[:, :, :]), reads=b_vst, writes=[gb(bV, t0)], dma_key="swv")
                if cond == 1:
                    p0 = t0 - NS
                    S.add("sp", lambda e, p0=p0, T=T: e.dma_start(out=self.dout["swa_kout"][:, :, p0:p0 + T], in_=kf[:, :, :T]), reads=[b_kf], writes=[Buf()], dma_key="swko")
                    S.add("sp", lambda e, p0=p0, T=T: e.dma_start(out=self.dout["swa_vout"][p0:p0 + T, :].rearrange("(b p) n -> p b n", p=128), in_=vf[:, :, :]), reads=[b_vf], writes=[Buf()], dma_key="swvo")
            S.barrier()
        with ExitStack() as ph:
            self.pts = [(self.sb(ph, [128, TT], BF16, "pt"), Buf()) for _ in range(3)]
            self.ptrot = 0
            kctx = self.sb(ph, [64, 8, 256], BF16, "kctx")
            vctx = self.sb(ph, [128, 2, 512], BF16, "vctx")
            sinkr = self.sb(ph, [64, 32], F32, "sinkr")
            sinke = self.sb(ph, [64, 32], F32, "sinke")
            msk = self.sb(ph, [128, 2, 512], BF16, "msk")
            b_ctx = Buf()
            S.add("pool", lambda e: e.dma_start(out=kctx[:], in_=self.din["swa_kctxT"]), writes=[b_ctx], dma_key="sk1")
            S.add("pool", lambda e: e.dma_start(out=vctx[:], in_=self.din["swa_vctx"].rearrange("(b p) n -> p b n", p=128)), writes=[b_ctx], dma_key="sk2")
            S.add("pool", lambda e: e.dma_start(out=msk[:], in_=self.din["swa_masks"]), writes=[b_ctx], dma_key="sk3")
            S.add("sp", lambda e: e.dma_start(out=sinkr[:], in_=self.din["swa_sink_bc"][j]), writes=[b_ctx], dma_key="sk4")
            S.add("act", lambda e: e.activation(out=sinke[:], in_=sinkr[:], func=AF.Exp), reads=[b_ctx], writes=[b_ctx])
            NB = 2
            qb_ = [(self.sb(ph, [64, 32, 128], BF16, "qb"), Buf()) for _ in range(NB)]
            kl_ = [(self.sb(ph, [64, 8, 384], BF16, "kl"), Buf()) for _ in range(NB)]
            vl_ = [(self.sb(ph, [128, 3, 512], BF16, "vl"), Buf()) for _ in range(NB)]
            ob_ = [(self.sb(ph, [64, 32, 128], BF16, "ob"), Buf()) for _ in range(NB)]
            blocks = [(jb * 128, 0, NS, True) for jb in range(NS // 128)]
            for s_ in range(NP // SEQ):
                blocks += [(NS + s_ * SEQ + jb * 128, NS + s_ * SEQ, NS + (s_ + 1) * SEQ, False) for jb in range(SEQ // 128)]
            scale = 64 ** -0.5
            for bi, (q0, lo, hi, is_s) in enumerate(blocks):
                (qb, qbb), (kl, klb), (vl, vlb), (ob, obb) = qb_[bi % NB], kl_[bi % NB], vl_[bi % NB], ob_[bi % NB]
                tq = (q0 // TT) * TT
                S.add("sp", lambda e, qb=qb, q0=q0: e.dma_start(out=qb[:], in_=QT[:, :, q0:q0 + 128]), reads=[gb(bQ, tq)], writes=[qbb], dma_key="lq%d" % (bi % NB))
                if is_s:
                    kbs = [x_ for x_ in (q0 - 128, q0, q0 + 128) if lo <= x_ < hi]
                else:
                    kbs = list(range(lo, hi, 128))
                for i, k0 in enumerate(kbs):
                    tk = (k0 // TT) * TT
                    S.add("sp", lambda e, kl=kl, i=i, k0=k0: e.dma_start(out=kl[:, :, i * 128:(i + 1) * 128], in_=KT[:, :, k0:k0 + 128]), reads=[gb(bK, tk)], writes=[klb], dma_key="lk%d" % (bi % NB))
                    S.add("sp", lambda e, vl=vl, i=i, k0=k0: e.dma_start(out=vl[:, i, :], in_=VV[k0:k0 + 128, :]), reads=[gb(bV, tk)], writes=[vlb], dma_key="lv%d" % (bi % NB))
                nctx = 2 if is_s else 0
                for g in range(8):
                    def kpieces(kb, g=g, kl=kl, klb=klb):
                        if kb < nctx:
                            return [(kctx[:, g, kb * 128:(kb + 1) * 128], [b_ctx])]
                        i = kb - nctx
                        return [(kl[:, g, i * 128:(i + 1) * 128], [klb])]

                    def vblk(kb, g=g, vl=vl, vlb=vlb):
                        if kb < nctx:
                            return vctx[:, kb, g * 64:(g + 1) * 64], [b_ctx]
                        return vl[:, kb - nctx, g * 64:(g + 1) * 64], [vlb]

                    def maskf(kb, kbs=kbs, q0=q0):
                        if kb < nctx or not is_s:
                            return None
                        k0 = kbs[kb - nctx]
                        if k0 < q0:
                            return msk[:, 0, :], [b_ctx]
                        if k0 > q0:
                            return msk[:, 1, :], [b_ctx]
                        return None
                    qa = qb[:, 4 * g:4 * g + 4, :]
                    oa = ob[:, 4 * g:4 * g + 4, :].rearrange("p g t -> p (g t)")
                    sk = sinke[:, 4 * g:4 * g + 4].unsqueeze(2).broadcast_to([64, 4, 128])
                    self.attend([(qa, [qbb])], nctx + len(kbs), kpieces, vblk, 512, 64, scale, maskf, self.attn_finish(oa, [obb], sink_ap=sk))
                S.add("sp", lambda e, ob=ob, q0=q0: e.dma_start(out=OT[:, :, q0:q0 + 128], in_=ob[:]), reads=[obb], writes=[gb(bO, tq)], dma_key="so%d" % (bi % NB))
            S.barrier()
        with ExitStack() as ph:
            self.big = self.sb(ph, [128, KC, TT], F32, "big")
            self.b_big = [Buf() for c in range(KC)]
            oT = self.sb(ph, [64, 32, TT], BF16, "oTt")
            b_oT = Buf()
            for tile in self.tiles:
                t0, T, cond = tile
                S.add("sp", lambda e, t0=t0, T=T: e.dma_start(out=oT[:, :, :T], in_=OT[:, :, t0:t0 + T]), reads=[gb(bO, t0)], writes=[b_oT], dma_key="swo")

                def x(kc):
                    return oT[:, kc, :T], [b_oT]
                self.linear_fm(wo, 32, 16, 2, x, T, self.post_consume(T), kp=64)
                self.post(l, 0, tile, False)
            S.barrier()


    def mla_layer(self, l, j):
        nc, S = self.nc, self.S
        NT, NS, NP, SEQ = self.NT, self.NS, self.NP, self.SEQ
        NKEY = 256 + NT
        QN = self.scratch(f"mlaQN{l}", [16, 128, NT], BF16)
        QR = self.scratch(f"mlaQR{l}", [16, 64, NT], BF16)
        OT = self.scratch(f"mlaO{l}", [128, 16, NT], BF16)
        bQ, bO = {}, {}

        def gb(d, k):
            if k not in d:
                d[k] = Buf()
            return d[k]
        wdn = self.din["mla_w_down"][j]
        wuq = self.din["mla_w_uq"][j]
        wo = self.din["mla_w_out"][j]
        with ExitStack() as allph:
            ckv_all = self.sb(allph, [128, 4, NKEY], BF16, "ckvall")
            kpe_all = self.sb(allph, [64, NKEY], BF16, "kpeall")
            b_ckv = [Buf() for _ in range((NKEY + TT - 1) // TT + 1)]
            gq = self.sb(allph, [128, 2, 4], F32, "gq")
            b_g = Buf()
            S.add("sp", lambda e: e.dma_start(out=gq[:], in_=self.din["mla_gT"][j]), writes=[b_g], dma_key="mg")
            S.add("pool", lambda e: e.dma_start(out=ckv_all[:, :, 0:256], in_=self.din["mla_ckv_ctxT"]), writes=[b_ckv[0]], dma_key="mc1")
            S.add("pool", lambda e: e.dma_start(out=kpe_all[:, 0:256], in_=self.din["mla_kpe_ctxT"]), writes=[b_ckv[0]], dma_key="mc2")
            with ExitStack() as ph:
                self.big = self.sb(ph, [128, KC, TT], F32, "big")
                self.b_big = [Buf() for c in range(KC)]
                self.hT = self.sb(ph, [128, KC, TT], BF16, "hT")
                self.b_hT = [Buf() for c in range(KC)]
                hT = self.hT
                cf = self.sb(ph, [128, 8, TT], F32, "cf")
                b_cf = [Buf() for _ in range(8)]
                cqn = self.sb(ph, [128, 4, TT], BF16, "cqn")
                b_cqn = [Buf() for _ in range(4)]
                kpf = self.sb(ph, [64, TT], F32, "kpf")
                b_kpf = Buf()
                qn = self.sb(ph, [128, 16, TT], BF16, "qn")
                qr = self.sb(ph, [64, 16, TT], BF16, "qr")
                b_qn = [Buf() for _ in range(16)]
                b_qr = [Buf() for _ in range(16)]
                r2 = self.sb(ph, [128, TT], F32, "r2")
                b_r2 = Buf()
                for ti, tile in enumerate(self.tiles):
                    t0, T, cond = tile
                    k0 = 256 + t0
                    bk = b_ckv[1 + ti]
                    self.pre(l, 0, tile)

                    def x(kc):
                        return hT[:, kc, :T], [self.b_hT[kc]]
                    rp = self.rope_pair(t0, T, lambda h: (kpe_all[:, k0:k0 + T], [bk]))
                    st2 = self.miscps[0]

                    def cons(ci, ps, pb, T=T, rp=rp, cond=cond):
                        if ci < 8:
                            S.add("act", lambda e: e.activation(out=cf[:, ci, :T], in_=ps, func=AF.Copy), reads=[pb], writes=[b_cf[ci]])
                            sq, sqb = self.sqs[self.sqrot % len(self.sqs)]
                            self.sqrot += 1
                            S.add("act", lambda e: e.activation(out=sq[:, :T], in_=cf[:, ci, :T], func=AF.Square), reads=[b_cf[ci]], writes=[sqb])
                            st, stb = self.statps if ci < 4 else st2
                            S.add("pe", lambda e: e.matmul(st[:, :T], lhsT=self.ones_bf[:, :], rhs=sq[:, :T], start=(ci % 4 == 0), stop=(ci % 4 == 3)),
                                  reads=[sqb, self.b_const], writes=[stb])
                        else:
                            rp(ci - 8, ps, pb)
                            if ci == 8 and cond == 1:
                                S.add("act", lambda e: e.activation(out=kpf[:, :T], in_=ps, func=AF.Copy), reads=[pb], writes=[b_kpf])
                    self.linear_fm(wdn, KC, 10, 4, x, T, cons, widths=[128] * 8 + [64, 64])
                    self.rstd_from(self.statps[0][:, :T], self.statps[1], 512, self.rstd[:, :T], self.b_rstd)
                    self.rstd_from(st2[0][:, :T], st2[1], 512, r2[:, :T], b_r2)
                    for c in range(4):
                        S.add("dve", lambda e, c=c, T=T: e.scalar_tensor_tensor(out=cqn[:, c, :T], in0=cf[:, c, :T], scalar=gq[:, 0, c:c + 1], in1=self.rstd[:, :T], op0=ALU.mult, op1=ALU.mult),
                              reads=[b_cf[c], b_g, self.b_rstd], writes=[b_cqn[c]])
                        S.add("dve", lambda e, c=c, T=T: e.scalar_tensor_tensor(out=cf[:, 4 + c, :T], in0=cf[:, 4 + c, :T], scalar=gq[:, 1, c:c + 1], in1=r2[:, :T], op0=ALU.mult, op1=ALU.mult),
                              reads=[b_cf[4 + c], b_g, b_r2], writes=[b_cf[4 + c]])
                        S.add("pool", lambda e, c=c, k0=k0, T=T: e.tensor_copy(out=ckv_all[:, c, k0:k0 + T], in_=cf[:, 4 + c, :T]), reads=[b_cf[4 + c]], writes=[bk])
                    if cond == 1:
                        p0 = t0 - NS
                        S.add("sp", lambda e, p0=p0, T=T: e.dma_start(out=self.dout["mla_ckvout"][:, :, p0:p0 + T].rearrange("c p t -> p c t"), in_=cf[:, 4:8, :T]),
                              reads=b_cf[4:8], writes=[Buf()], dma_key="mco")
                        S.add("sp", lambda e, p0=p0, T=T: e.dma_start(out=self.dout["mla_kpeout"][:, p0:p0 + T], in_=kpf[:, :T]), reads=[b_kpf], writes=[Buf()], dma_key="mko")

                    def xq(kc):
                        return cqn[:, kc, :T], [b_cqn[kc]]
                    rq = self.rope_pair(t0, T, lambda h: (qr[:, h, :T], [b_qr[h]]))

                    def consq(ci, ps, pb, T=T, rq=rq):
                        if ci < 16:
                            S.add("act", lambda e: e.activation(out=qn[:, ci, :T], in_=ps, func=AF.Copy), reads=[pb], writes=[b_qn[ci]])
                        else:
                            rq(ci - 16, ps, pb)
                    self.linear_fm(wuq, 4, 48, 16, xq, T, consq, widths=[128] * 16 + [64] * 32)
                    S.add("sp", lambda e, t0=t0, T=T: e.dma_start(out=QN[:, :, t0:t0 + T].rearrange("h p t -> p h t"), in_=qn[:, :, :T]), reads=b_qn, writes=[gb(bQ, t0)], dma_key="mqn")
                    S.add("sp", lambda e, t0=t0, T=T: e.dma_start(out=QR[:, :, t0:t0 + T].rearrange("h p t -> p h t"), in_=qr[:, :, :T]), reads=b_qr, writes=[gb(bQ, t0)], dma_key="mqr")
                S.barrier()
            with ExitStack() as ph:
                self.pts = [(self.sb(ph, [128, TT], BF16, "pt"), Buf()) for _ in range(3)]
                self.ptrot = 0
                wkv = self.sb(ph, [128, 4, 4096], BF16, "wkv")
                b_wkv = Buf()
                S.add("pool", lambda e: e.dma_start(out=wkv[:], in_=self.din["mla_w_ukvT"][j]), writes=[b_wkv], dma_key="mwkv")
                NB = 2
                kth_ = [(self.sb(ph, [128, NKEY], BF16, "kth"), Buf()) for _ in range(NB)]
                vh_ = [(self.sb(ph, [128, NKEY // 128, 128], BF16, "vh"), Buf()) for _ in range(NB)]
                qnh_ = [(self.sb(ph, [128, NT], BF16, "qnh"), Buf())] * NB
                qrh_ = [(self.sb(ph, [64, NT], BF16, "qrh"), Buf())] * NB
                oh_ = [(self.sb(ph, [128, TT], BF16, "oh"), Buf()) for _ in range(3)]
                orot = 0
                allck = b_ckv
                scale = 192 ** -0.5
                for h in range(16):
                    (kth, kthb), (vh, vhb), (qnh, qnhb), (qrh, qrhb) = kth_[h % NB], vh_[h % NB], qnh_[h % NB], qrh_[h % NB]
                    S.add("sp", lambda e, qnh=qnh, h=h: e.dma_start(out=qnh[:], in_=QN[h]), reads=list(bQ.values()), writes=[qnhb], dma_key="mlq")
                    S.add("sp", lambda e, qrh=qrh, h=h: e.dma_start(out=qrh[:], in_=QR[h]), reads=list(bQ.values()), writes=[qrhb], dma_key="mlr")
                    for kt in range(0, NKEY, TT):
                        w = min(TT, NKEY - kt)
                        ps, pb = self.linps[self.linrot % len(self.linps)]
                        self.linrot += 1
                        for kc in range(4):
                            S.add("pe", lambda e, ps=ps, kc=kc, kt=kt, w=w, h=h: e.matmul(ps[:, :w], lhsT=wkv[:, kc, h * 256:h * 256 + 128], rhs=ckv_all[:, kc, kt:kt + w], start=(kc == 0), stop=(kc == 3)),
                                  reads=[b_wkv] + allck, writes=[pb])
                        S.add("act", lambda e, ps=ps, kt=kt, w=w, kth=kth: e.activation(out=kth[:, kt:kt + w], in_=ps[:, :w], func=AF.Copy), reads=[pb], writes=[kthb])
                    for kb4 in range(0, NKEY // 128, 4):
                        nb4 = min(4, NKEY // 128 - kb4)
                        ps, pb = self.linps[self.linrot % len(self.linps)]
                        self.linrot += 1
                        for i in range(nb4):
                            kb = kb4 + i
                            for kc in range(4):
                                S.add("pe", lambda e, ps=ps, kc=kc, kb=kb, i=i, h=h: e.matmul(ps[:, i * 128:(i + 1) * 128], lhsT=ckv_all[:, kc, kb * 128:(kb + 1) * 128], rhs=wkv[:, kc, h * 256 + 128:h * 256 + 256], start=(kc == 0), stop=(kc == 3)),
                                      reads=[b_wkv] + allck, writes=[pb])
                        S.add("dve", lambda e, ps=ps, kb4=kb4, nb4=nb4, vh=vh: e.tensor_copy(out=vh[:, kb4:kb4 + nb4, :].rearrange("p b d -> p (b d)"), in_=ps[:, :nb4 * 128]), reads=[pb], writes=[vhb])
                    qts = [(t0, TT, 0, (256 + NS) // 128) for t0 in range(0, NS, TT)]
                    for s_ in range(NP // SEQ):
                        qts.append((NS + s_ * SEQ, SEQ, (256 + NS + s_ * SEQ) // 128, SEQ // 128))
                    for (q0, Tq, kb0, nkb) in qts:
                        oh, ohb = oh_[orot % 3]
                        orot += 1

                        def kpieces(kb, kb0=kb0, kth=kth, kthb=kthb):
                            a = (kb0 + kb) * 128
                            return [(kth[:, a:a + 128], [kthb]), (kpe_all[:, a:a + 128], allck)]

                        def vblk(kb, kb0=kb0, vh=vh, vhb=vhb):
                            return vh[:, kb0 + kb, :], [vhb]
                        tq = (q0 // TT) * TT
                        self.attend([(qnh[:, q0:q0 + Tq], [qnhb]), (qrh[:, q0:q0 + Tq], [qrhb])], nkb, kpieces, vblk, Tq, 128, scale, None,
                                    self.attn_finish(oh[:, :Tq], [ohb]))
                        S.add("sp", lambda e, oh=oh, h=h, q0=q0, Tq=Tq: e.dma_start(out=OT[:, h, q0:q0 + Tq], in_=oh[:, :Tq]), reads=[ohb], writes=[gb(bO, (tq, h, q0))], dma_key="mo%d" % (orot % 3))
                S.barrier()
        with ExitStack() as ph:
            self.big = self.sb(ph, [128, KC, TT], F32, "big")
            self.b_big = [Buf() for c in range(KC)]
            oT = self.sb(ph, [128, 16, TT], BF16, "oTt")
            b_oT = Buf()
            for tile in self.tiles:
                t0, T, cond = tile
                S.add("sp", lambda e, t0=t0, T=T: e.dma_start(out=oT[:, :, :T], in_=OT[:, :, t0:t0 + T]), reads=[b for k_, b in bO.items() if k_[0] == t0], writes=[b_oT], dma_key="mlo")

                def x(kc):
                    return oT[:, kc, :T], [b_oT]
                self.linear_fm(wo, KC, 16, 4, x, T, self.post_consume(T))
                self.post(l, 0, tile, False)
            S.barrier()


    def hgrn_setup(self, g):
        S, L = self.S, self.L
        lg = self.sb(g, [128, 2, L, 16], F32, "lbl")
        self.lb = self.sb(g, [128, 2, L, 16], F32, "lb")
        self.oml = self.sb(g, [128, 2, L, 16], F32, "oml")
        sm = self.sb(g, [128, 2, 16], F32, "lbs")
        self.b_lb = Buf()
        b = self.b_lb
        S.add("sp", lambda e: e.dma_start(out=lg[:], in_=self.din["hg_lbT"]), writes=[b], dma_key="hlb")
        S.add("act", lambda e: e.activation(out=lg[:], in_=lg[:], func=AF.Exp), reads=[b], writes=[b])
        S.add("dve", lambda e: e.tensor_copy(out=sm[:], in_=lg[:, :, 0, :]), reads=[b], writes=[b])
        for i in range(1, L):
            S.add("dve", lambda e, i=i: e.tensor_tensor(out=sm[:], in0=sm[:], in1=lg[:, :, i, :], op=ALU.add), reads=[b], writes=[b])
        S.add("dve", lambda e: e.reciprocal(out=sm[:], in_=sm[:]), reads=[b], writes=[b])
        S.add("dve", lambda e: e.memset(self.lb[:, :, 0, :], 0.0), writes=[b])
        for i in range(1, L):
            S.add("dve", lambda e, i=i: e.tensor_tensor(out=lg[:, :, i, :], in0=lg[:, :, i, :], in1=sm[:], op=ALU.mult), reads=[b], writes=[b])
            S.add("dve", lambda e, i=i: e.tensor_tensor(out=self.lb[:, :, i, :], in0=self.lb[:, :, i - 1, :], in1=lg[:, :, i, :], op=ALU.add), reads=[b], writes=[b])
        S.add("dve", lambda e: e.tensor_scalar(out=self.oml[:], in0=self.lb[:], scalar1=-1.0, scalar2=1.0, op0=ALU.mult, op1=ALU.add), reads=[b], writes=[b])

    def hgrn_layer(self, l, j):
        nc, S = self.nc, self.S
        NT, NS, NP, SEQ = self.NT, self.NS, self.NP, self.SEQ
        NCH = NT // 64
        Q2 = self.scratch(f"hgQ2{l}", [16, 128, NT], BF16)
        K2 = self.scratch(f"hgK2{l}", [16, 128, NT], BF16)
        D2 = self.scratch(f"hgD2{l}", [16, 128, NCH, 3], F32)
        V64 = self.scratch(f"hgV{l}", [NT, 2048], BF16)
        GS = self.scratch(f"hgG{l}", [NT, 2048], BF16)
        O1 = self.scratch(f"hgO1{l}", [NT, 2048], F32)
        bsc = {}

        def gb(k):
            if k not in bsc:
                bsc[k] = Buf()
            return bsc[k]
        wqf = self.din["hg_w_qf"][j]
        wig = self.din["hg_w_ig"][j]
        wo = self.din["hg_w_out"][j]
        lb, oml = self.lb, self.oml
        with ExitStack() as allph:
            Sst = [self.sb(allph, [128, 16, 128], F32, "Sst") for _ in range(2)]
            b_S = [[Buf() for _ in range(16)] for _ in range(2)]
            ident = self.sb(allph, [128, 128], BF16, "ident")
            onesf = self.sb(allph, [128, TT], F32, "onesf")
            hgbc = self.sb(allph, [64, 2048], F32, "hgbc")
            b_hc = Buf()
            S.add("pool", lambda e: e.dma_start(out=ident[:], in_=self.din["ident"]), writes=[b_hc], dma_key="hm2")
            S.add("sp", lambda e: e.dma_start(out=hgbc[:], in_=self.din["hg_gbc"][j]), writes=[b_hc], dma_key="hm3")
            S.add("dve", lambda e: e.memset(onesf[:], 1.0), writes=[b_hc])
            sbf_ = [(self.sb(allph, [128, 128], BF16, "sbf"), Buf()) for _ in range(4)]
            t1_ = [(self.sb(allph, [128, 128], F32, "t1"), Buf()) for _ in range(3)]
            am8 = self.sb(allph, [64, 512], BF16, "am8")
            ktok8 = self.sb(allph, [64, 8, 128], BF16, "ktok8")
            b_am8, b_ktok8 = Buf(), Buf()
            hmaskf = self.sb(allph, [64, 2, 64], F32, "hmaskf")
            S.add("sp", lambda e: e.dma_start(out=hmaskf[:], in_=self.din["hg_masks"]), writes=[b_hc], dma_key="hm4")
            rot = {"sbf": 0, "t1": 0}
            par = [0] * 16
            pa8, b_pa8 = self.miscps[0]
            pk8 = [self.miscps[1], self.miscps[2]]

            def nxt(lst, k):
                r = lst[rot[k] % len(lst)]
                rot[k] += 1
                return r

            def mk_memset(h):
                def f():
                    st_, b_ = Sst[par[h]], b_S[par[h]][h]
                    S.add("pool", lambda e: e.memset(st_[:, h, :], 0.0), writes=[b_])
                return f

            def mk_stout(h, sq_, d):
                def f():
                    st_, b_ = Sst[par[h]], b_S[par[h]][h]
                    S.add("sp", lambda e: e.dma_start(out=self.dout["hg_stout"][j, sq_, d, :, h, :], in_=st_[:, h, :]), reads=[b_], writes=[Buf()], dma_key="hso%d" % (h % 4))
                return f

            def scan_group(d, items, po2, sink4):
                def amm(i, it):
                    S.add("pe", lambda e: e.matmul(pa8[:64, i * 64:(i + 1) * 64], lhsT=it["kt"], rhs=it["qt"], start=True, stop=True),
                          reads=it["qtb"] + it["ktb"], writes=[b_pa8])
                for i, it in enumerate(items):
                    amm(i, it)
                S.add("dve", lambda e: e.tensor_scalar(out=am8[:], in0=pa8[:64, :512], scalar1=1e30, scalar2=-1e30, op0=ALU.min, op1=ALU.max), reads=[b_pa8], writes=[b_am8])
                S.add("pool", lambda e: e.tensor_tensor(out=am8[:].rearrange("p (c t) -> p c t", t=64), in0=am8[:].rearrange("p (c t) -> p c t", t=64),
                                                         in1=hmaskf[:, d, :].unsqueeze(1).broadcast_to([64, 8, 64]), op=ALU.mult), reads=[b_am8, b_hc], writes=[b_am8])

                def tr(i, it):
                    S.add("pe", lambda e: e.transpose(self.pstr[:64, i * 128:(i + 1) * 128], it["kt"], ident[:, :]), reads=it["ktb"] + [b_hc], writes=[self.b_pstr])
                for i, it in enumerate(items):
                    tr(i, it)
                S.add("act", lambda e: e.activation(out=ktok8[:].rearrange("p c k -> p (c k)"), in_=self.pstr[:64, :1024], func=AF.Copy), reads=[self.b_pstr], writes=[b_ktok8])

                def kvmm(i, it):
                    pk, pkb = pk8[i // 4]
                    col = (i % 4) * 128
                    S.add("pe", lambda e: e.matmul(pk[:, col:col + 128], lhsT=ktok8[:, i, :], rhs=it["v"], start=True, stop=True), reads=[b_ktok8] + it["vb"], writes=[pkb])
                for i, it in enumerate(items):
                    kvmm(i, it)

                def step(i, it):
                    h, dv, dvb = it["h"], it["dv"], it["dvb"]
                    for f in it["pre"]:
                        f()
                    cur = par[h]
                    new = 1 - cur
                    Sc, Sn = Sst[cur], Sst[new]
                    bc, bn = b_S[cur][h], b_S[new][h]
                    pk, pkb = pk8[i // 4]
                    po, pob = po2[i // 4]
                    col = (i % 4) * 128
                    sbf, sbfb = nxt(sbf_, "sbf")
                    S.add("pool", lambda e: e.tensor_scalar(out=sbf[:], in0=Sc[:, h, :], scalar1=dv[:, 0:1], scalar2=None, op0=ALU.mult), reads=[bc] + dvb, writes=[sbfb])
                    S.add("pe", lambda e: e.matmul(po[:64, col:col + 128], lhsT=it["qt"], rhs=sbf[:], start=True, stop=False), reads=it["qtb"] + [sbfb], writes=[pob])
                    S.add("pe", lambda e: e.matmul(po[:64, col:col + 128], lhsT=am8[:, i * 64:(i + 1) * 64], rhs=it["v"], start=False, stop=True), reads=[b_am8] + it["vb"], writes=[pob])
                    t1, t1b = nxt(t1_, "t1")
                    S.add("dve", lambda e: e.tensor_scalar(out=t1[:], in0=Sc[:, h, :], scalar1=dv[:, 2:3], scalar2=None, op0=ALU.mult), reads=[bc] + dvb, writes=[t1b])
                    S.add("dve", lambda e: e.scalar_tensor_tensor(out=Sn[:, h, :], in0=pk[:, col:col + 128], scalar=dv[:, 1:2], in1=t1[:], op0=ALU.mult, op1=ALU.add),
                          reads=[pkb, t1b] + dvb, writes=[bn])
                    par[h] = new
                    for f in it["post"]:
                        f()
                    if i % 4 == 3:
                        sink4(i // 4, po[:64, :512], pob, items[i - 3:i + 1])
                for i, it in enumerate(items):
                    step(i, it)

            with ExitStack() as ph:
                self.big = self.sb(ph, [128, KC, TT], F32, "big")
                self.b_big = [Buf() for c in range(KC)]
                self.hT = self.sb(ph, [128, KC, TT], BF16, "hT")
                self.b_hT = [Buf() for c in range(KC)]
                hT = self.hT
                v64 = self.sb(ph, [64, 8, 2048], BF16, "v64")
                b_v64 = [Buf() for _ in range(8)]
                gst_ = [(self.sb(ph, [64, 512], BF16, "gst"), Buf()) for _ in range(2)]
                gtmp = self.sb(ph, [64, 512], F32, "gtmp")
                b_gtmp = Buf()
                qs_ = [(self.sb(ph, [128, TT], F32, "qs"), Buf()) for _ in range(2)]
                ft = {n: (self.sb(ph, [128, TT], F32, n), Buf()) for n in ["f", "g", "B", "X", "E", "eq", "ek"]}
                qk_ = [[(self.sb(ph, [128, TT], BF16, "qkt"), Buf()) for _ in range(2)] for _ in range(4)]
                dvt_ = [(self.sb(ph, [128, 8, 3], F32, "dvt"), Buf()) for _ in range(4)]
                o1h_ = [(self.sb(ph, [64, 8, 128], F32, "o1h"), Buf()) for _ in range(2)]
                S.add("sp", lambda e: e.dma_start(out=Sst[0][:], in_=self.din["hg_s0"][j, 0]), writes=b_S[0], dma_key="hs0")
                hcount = 0
                lin_saved = self.linps
                po2_p1 = [self.statps, lin_saved[2]]
                self.linps = lin_saved[:2]
                for ti, tile in enumerate(self.tiles):
                    t0, T, cond = tile
                    self.pre(l, 0, tile)

                    def xv(kc, tb):
                        return hT[:, kc, tb * 64:(tb + 1) * 64], [self.b_hT[kc]]

                    def consv(s_, tb, ps, pb, t0=t0):
                        if s_ < 4:
                            S.add("act", lambda e: e.activation(out=v64[:, tb, s_ * 512:(s_ + 1) * 512], in_=ps, func=AF.Copy), reads=[pb], writes=[b_v64[tb]])
                        else:
                            gst, gstb = gst_[(s_ * 8 + tb) % 2]
                            S.add("act", lambda e: e.activation(out=gtmp[:], in_=ps, func=AF.Silu), reads=[pb], writes=[b_gtmp])
                            S.add("dve", lambda e: e.tensor_tensor(out=gst[:], in0=gtmp[:], in1=hgbc[:, (s_ - 4) * 512:(s_ - 3) * 512], op=ALU.mult), reads=[b_gtmp, b_hc], writes=[gstb])
                            r0 = t0 + tb * 64
                            S.add("sp", lambda e: e.dma_start(out=GS[r0:r0 + 64, (s_ - 4) * 512:(s_ - 3) * 512], in_=gst[:]), reads=[gstb], writes=[gb(("G", t0))], dma_key="hg%d" % ((s_ * 8 + tb) % 2))
                    self.linear_tm(wig, KC, 8, xv, T, 64, consv)
                    S.add("sp", lambda e, t0=t0: e.dma_start(out=V64[t0:t0 + TT, :].rearrange("(c p) n -> p c n", p=64), in_=v64[:]), reads=b_v64, writes=[gb(("V", t0))], dma_key="hv")

                    def x(kc):
                        return hT[:, kc, :T], [self.b_hT[kc]]
                    stq = {}

                    def cons(ci, ps, pb, t0=t0, cond=cond, ti=ti):
                        h, kind = divmod(ci, 3)
                        if kind == 0:
                            qs, qsb = qs_[h % 2]
                            stq["qs"] = (qs, qsb)
                            S.add("act", lambda e: e.activation(out=qs[:], in_=ps, func=AF.Silu), reads=[pb], writes=[qsb])
                            return
                        d = kind - 1
                        qs, qsb = stq["qs"]
                        (f, fb), (g_, gb_), (B, Bb), (X, Xb), (E, Eb), (eq, eqb), (ek, ekb) = [ft[n] for n in ["f", "g", "B", "X", "E", "eq", "ek"]]
                        S.add("act", lambda e: e.activation(out=f[:], in_=ps, func=AF.Sigmoid), reads=[pb], writes=[fb])
                        S.add("dve", lambda e: e.tensor_scalar(out=f[:], in0=f[:], scalar1=oml[:, d, l, h:h + 1], scalar2=lb[:, d, l, h:h + 1], op0=ALU.mult, op1=ALU.add), reads=[fb, self.b_lb], writes=[fb])
                        S.add("act", lambda e: e.activation(out=g_[:], in_=f[:], func=AF.Ln), reads=[fb], writes=[gb_])
                        S.add("dve", lambda e: e.tensor_tensor_scan(out=B[:], data0=onesf[:], data1=g_[:], initial=0.0, op0=ALU.mult, op1=ALU.add), reads=[gb_, b_hc], writes=[Bb])
                        S.add("pool", lambda e: e.tensor_tensor(out=X[:], in0=B[:], in1=g_[:], op=ALU.subtract), reads=[Bb, gb_], writes=[Xb])
                        S.add("pool", lambda e: e.tensor_scalar(out=f[:], in0=f[:], scalar1=-1.0, scalar2=1.0, op0=ALU.mult, op1=ALU.add), reads=[fb], writes=[fb])
                        B3 = B[:].rearrange("p (c t) -> p c t", t=64)
                        X3 = X[:].rearrange("p (c t) -> p c t", t=64)
                        E3 = E[:].rearrange("p (c t) -> p c t", t=64)
                        if d == 0:
                            S.add("dve", lambda e: e.tensor_tensor(out=E3, in0=B3, in1=B3[:, :, 32:33].broadcast_to([128, 8, 64]), op=ALU.subtract), reads=[Bb], writes=[Eb])
                        else:
                            S.add("dve", lambda e: e.tensor_tensor(out=E3, in0=X3[:, :, 32:33].broadcast_to([128, 8, 64]), in1=X3, op=ALU.subtract), reads=[Xb], writes=[Eb])
                        S.add("act", lambda e: e.activation(out=eq[:], in_=E[:], func=AF.Exp), reads=[Eb], writes=[eqb])
                        S.add("act", lambda e: e.activation(out=ek[:], in_=E[:], func=AF.Exp, scale=-1.0), reads=[Eb], writes=[ekb])
                        (qt, qtb) = qk_[2 * d][hcount_ref[0] % 2]
                        (kt, ktb) = qk_[2 * d + 1][hcount_ref[0] % 2]
                        (dvt, dvb) = dvt_[2 * d + hcount_ref[0] % 2]
                        S.add("dve", lambda e: e.scalar_tensor_tensor(out=qt[:], in0=qs[:], scalar=128 ** -0.5, in1=eq[:], op0=ALU.mult, op1=ALU.mult), reads=[qsb, eqb], writes=[qtb])
                        S.add("pool", lambda e: e.tensor_tensor(out=kt[:], in0=f[:], in1=ek[:], op=ALU.mult), reads=[fb, ekb], writes=[ktb])
                        mid = (B3 if d == 0 else X3)[:, :, 32:33]
                        if d == 0:
                            S.add("pool", lambda e: e.tensor_tensor(out=dvt[:, :, 0:1], in0=mid, in1=X3[:, :, 0:1], op=ALU.subtract), reads=[Bb, Xb], writes=[dvb])
                            S.add("pool", lambda e: e.tensor_tensor(out=dvt[:, :, 1:2], in0=B3[:, :, 63:64], in1=mid, op=ALU.subtract), reads=[Bb, Xb], writes=[dvb])
                        else:
                            S.add("pool", lambda e: e.tensor_tensor(out=dvt[:, :, 0:1], in0=B3[:, :, 63:64], in1=mid, op=ALU.subtract), reads=[Bb, Xb], writes=[dvb])
                            S.add("pool", lambda e: e.tensor_tensor(out=dvt[:, :, 1:2], in0=mid, in1=X3[:, :, 0:1], op=ALU.subtract), reads=[Bb, Xb], writes=[dvb])
                        S.add("pool", lambda e: e.tensor_tensor(out=dvt[:, :, 2:3], in0=B3[:, :, 63:64], in1=X3[:, :, 0:1], op=ALU.subtract), reads=[Bb, Xb], writes=[dvb])
                        S.add("act", lambda e: e.activation(out=dvt[:], in_=dvt[:], func=AF.Exp), reads=[dvb], writes=[dvb])
                        if d == 0:
                            o1h, o1hb = o1h_[h % 2]
                            items = []
                            for c in range(8):
                                pre_, post_ = [], []
                                if cond == 1 and c % (SEQ // 64) == 0:
                                    pre_.append(mk_memset(h))
                                if cond == 1 and (c + 1) % (SEQ // 64) == 0:
                                    post_.append(mk_stout(h, c // (SEQ // 64), 0))
                                items.append(dict(h=h, qt=qt[:, c * 64:(c + 1) * 64], qtb=[qtb], kt=kt[:, c * 64:(c + 1) * 64], ktb=[ktb], dv=dvt[:, c, :], dvb=[dvb],
                                                  v=v64[:, c, h * 128:(h + 1) * 128], vb=[b_v64[c]], pre=pre_, post=post_))

                            def do_scan(items=items, o1h=o1h, o1hb=o1hb, h=h):
                                def sink4(half, po_ap, pob, its):
                                    S.add("act", lambda e: e.activation(out=o1h[:, half * 4:(half + 1) * 4, :], in_=po_ap.rearrange("p (c v) -> p c v", v=128), func=AF.Copy), reads=[pob], writes=[o1hb])
                                scan_group(0, items, po2_p1, sink4)
                                S.add("sp", lambda e: e.dma_start(out=O1[t0:t0 + TT, h * 128:(h + 1) * 128].rearrange("(c p) v -> p c v", p=64), in_=o1h[:]), reads=[o1hb], writes=[gb(("O", t0, h))], dma_key="ho%d" % (h % 2))
                            if pend_scan:
                                pend_scan.pop()()
                            pend_scan.append(do_scan)
                        else:
                            S.add("sp", lambda e: e.dma_start(out=Q2[h, :, t0:t0 + TT], in_=qt[:]), reads=[qtb], writes=[gb(("Q", t0, h))], dma_key="hq%d" % (hcount_ref[0] % 2))
                            S.add("sp", lambda e: e.dma_start(out=K2[h, :, t0:t0 + TT], in_=kt[:]), reads=[ktb], writes=[gb(("K", t0, h))], dma_key="hk%d" % (hcount_ref[0] % 2))
                            S.add("sp", lambda e: e.dma_start(out=D2[h, :, ti * 8:(ti + 1) * 8, :], in_=dvt[:]), reads=[dvb], writes=[gb(("D", t0, h))], dma_key="hd%d" % (hcount_ref[0] % 2))
                            hcount_ref[0] += 1
                    hcount_ref = [hcount]
                    pend_scan = []
                    self.linear_fm(wqf, KC, 48, 4, x, T, cons)
                    while pend_scan:
                        pend_scan.pop()()
                    hcount = hcount_ref[0]
                self.linps = lin_saved
                S.barrier()
            with ExitStack() as ph:
                self.big = self.sb(ph, [128, KC, TT], F32, "big")
                self.b_big = [Buf() for c in range(KC)]
                oT = self.sb(ph, [128, 16, TT], BF16, "oT")
                b_oT = [Buf() for _ in range(8)]
                NB = 2
                q2c_ = [(self.sb(ph, [128, 16, 64], BF16, "q2c"), Buf()) for _ in range(NB)]
                k2c_ = [(self.sb(ph, [128, 16, 64], BF16, "k2c"), Buf()) for _ in range(NB)]
                vc_ = [(self.sb(ph, [64, 2048], BF16, "vc"), Buf()) for _ in range(NB)]
                gc_ = [(self.sb(ph, [64, 2048], BF16, "gc"), Buf()) for _ in range(NB)]
                o1c_ = [(self.sb(ph, [64, 2048], F32, "o1c"), Buf()) for _ in range(NB)]
                d2t = self.sb(ph, [128, 16, 8, 3], F32, "d2t")
                b_d2t = Buf()
                osum = self.sb(ph, [64, 16, 128], F32, "osum")
                b_osum = [Buf() for _ in range(16)]
                sqt = self.sb(ph, [64, 2048], F32, "sqt")
                b_sqt = Buf()
                ssq = self.sb(ph, [64, 16], F32, "ssq")
                b_ssq = Buf()
                obf = self.sb(ph, [64, 2048], BF16, "obf")
                b_obf = Buf()
                order = [t for t in self.tiles if t[2] == 1] + [t for t in reversed(self.tiles) if t[2] == 0]
                po2_p2 = [self.linps[0], self.linps[1]]
                first_sample = True
                ci_ = 0
                for tile in order:
                    t0, T, cond = tile
                    ti = t0 // TT
                    if cond == 0 and first_sample:
                        first_sample = False
                        p0 = par[0]
                        assert all(p == p0 for p in par)
                        S.add("sp", lambda e, p0=p0: e.dma_start(out=Sst[p0][:], in_=self.din["hg_s0"][j, 1]), writes=b_S[p0], dma_key="hs1")
                    S.add("sp", lambda e, ti=ti: e.dma_start(out=d2t[:], in_=D2[:, :, ti * 8:(ti + 1) * 8, :].rearrange("h p c k -> p h c k")), reads=[gb(("D", t0, h)) for h in range(16)], writes=[b_d2t], dma_key="hd2")
                    for c in reversed(range(8)):
                        r0 = t0 + c * 64
                        (q2c, q2b), (k2c, k2b), (vc, vcb), (gc, gcb), (o1c, o1b) = q2c_[ci_ % NB], k2c_[ci_ % NB], vc_[ci_ % NB], gc_[ci_ % NB], o1c_[ci_ % NB]
                        kk = ci_ % NB
                        ci_ += 1
                        S.add("sp", lambda e, q2c=q2c, r0=r0: e.dma_start(out=q2c[:], in_=Q2[:, :, r0:r0 + 64].rearrange("h p t -> p h t")), reads=[gb(("Q", t0, h)) for h in range(16)], writes=[q2b], dma_key="p2q%d" % kk)
                        S.add("sp", lambda e, k2c=k2c, r0=r0: e.dma_start(out=k2c[:], in_=K2[:, :, r0:r0 + 64].rearrange("h p t -> p h t")), reads=[gb(("K", t0, h)) for h in range(16)], writes=[k2b], dma_key="p2k%d" % kk)
                        S.add("sp", lambda e, vc=vc, r0=r0: e.dma_start(out=vc[:], in_=V64[r0:r0 + 64, :]), reads=[gb(("V", t0))], writes=[vcb], dma_key="p2v%d" % kk)
                        S.add("sp", lambda e, gc=gc, r0=r0: e.dma_start(out=gc[:], in_=GS[r0:r0 + 64, :]), reads=[gb(("G", t0))], writes=[gcb], dma_key="p2g%d" % kk)
                        S.add("sp", lambda e, o1c=o1c, r0=r0: e.dma_start(out=o1c[:], in_=O1[r0:r0 + 64, :]), reads=[gb(("O", t0, h)) for h in range(16)], writes=[o1b], dma_key="p2o%d" % kk)
                        for g0 in (0, 8):
                            items = []
                            for h in range(g0, g0 + 8):
                                pre_, post_ = [], []
                                if cond == 1 and (c + 1) % (SEQ // 64) == 0:
                                    pre_.append(mk_memset(h))
                                if cond == 1 and c % (SEQ // 64) == 0:
                                    post_.append(mk_stout(h, c // (SEQ // 64), 1))
                                items.append(dict(h=h, qt=q2c[:, h, :], qtb=[q2b], kt=k2c[:, h, :], ktb=[k2b], dv=d2t[:, h, c, :], dvb=[b_d2t],
                                                  v=vc[:, h * 128:(h + 1) * 128], vb=[vcb], pre=pre_, post=post_))

                            def sink4(half, po_ap, pob, its, o1c=o1c, o1b=o1b):
                                h0 = its[0]["h"]
                                S.add("dve", lambda e: e.tensor_tensor(out=osum[:, h0:h0 + 4, :], in0=po_ap.rearrange("p (h v) -> p h v", v=128),
                                                                        in1=o1c[:, h0 * 128:(h0 + 4) * 128].rearrange("p (h v) -> p h v", v=128), op=ALU.add),
                                      reads=[pob, o1b], writes=[b_osum[hh] for hh in range(h0, h0 + 4)])
                            scan_group(1, items, po2_p2, sink4)
                        of = osum[:].rearrange("p h v -> p (h v)")
                        S.add("act", lambda e: e.activation(out=sqt[:], in_=of, func=AF.Square), reads=b_osum, writes=[b_sqt])
                        S.add("dve", lambda e: e.tensor_reduce(out=ssq[:], in_=sqt[:].rearrange("p (h v) -> p h v", v=128), axis=AX.X, op=ALU.add), reads=[b_sqt], writes=[b_ssq])
                        S.add("act", lambda e: e.activation(out=ssq[:], in_=ssq[:], func=AF.Ln, scale=1.0 / 128, bias=self.c_eps[:64, :]), reads=[b_ssq, self.b_const], writes=[b_ssq])
                        S.add("act", lambda e: e.activation(out=ssq[:], in_=ssq[:], func=AF.Exp, scale=-0.5), reads=[b_ssq], writes=[b_ssq])
                        S.add("dve", lambda e: e.tensor_tensor(out=sqt[:].rearrange("p (h v) -> p h v", v=128), in0=osum[:], in1=ssq[:].unsqueeze(2).broadcast_to([64, 16, 128]), op=ALU.mult), reads=b_osum + [b_ssq], writes=[b_sqt])
                        S.add("pool", lambda e, gc=gc: e.tensor_tensor(out=obf[:], in0=sqt[:], in1=gc[:], op=ALU.mult), reads=[b_sqt, gcb], writes=[b_obf])
                        for h in range(16):
                            S.add("pe", lambda e, h=h: e.transpose(self.pstr[:, h * 64:(h + 1) * 64], obf[:, h * 128:(h + 1) * 128], ident[:64, :64]), reads=[b_obf, b_hc], writes=[self.b_pstr])
                        S.add("act", lambda e, c=c: e.activation(out=oT[:, :, c * 64:(c + 1) * 64], in_=self.pstr[:, :].rearrange("p (h t) -> p h t", t=64), func=AF.Copy), reads=[self.b_pstr], writes=[b_oT[c]])

                    def x(kc):
                        return oT[:, kc, :T], b_oT
                    self.linear_fm(wo, KC, 16, 4, x, T, self.post_consume(T))
                    self.post(l, 0, tile, False)
                S.barrier()

    def build(self, kinds):
        nc, S = self.nc, self.S
        NT, L, NS, NP, SEQ = self.NT, self.L, self.NS, self.NP, self.SEQ
        NA = sum(1 for k in kinds if k == 0)
        NB_ = sum(1 for k in kinds if k == 1)
        NC_ = sum(1 for k in kinds if k == 2)
        self.inp("xT", [KC, 128, NT])
        self.inp("cT", [128, KC, 2])
        self.inp("ada_w", [L, 24, 128, KC * 512])
        self.inp("ada_bT", [128, L, 96])
        self.inp("norm_gT", [128, L, 4, KC])
        self.inp("mlp_w_in", [L, 16, 128, KC * 512])
        self.inp("mlp_w_out", [L, 16, 128, 64 * 128])
        self.inp("ropeC", [64, NT])
        self.inp("ropeS", [64, NT])
        self.inp("ident", [128, 128])
        if NA:
            self.inp("hg_w_qf", [NA, 12, 128, KC * 512])
            self.inp("hg_w_ig", [NA, 8, 128, KC * 512])
            self.inp("hg_w_out", [NA, 4, 128, KC * 512])
            self.inp("hg_lbT", [128, 2, L, 16])
            self.inp("hg_gbc", [NA, 64, 2048])
            self.inp("hg_s0", [NA, 2, 128, 16, 128])
            self.inp("hg_masks", [64, 2, 64])
            self.outp("hg_stout", [NA, 2, 2, 128, 16, 128])
        if NB_:
            self.inp("mla_w_down", [NB_, 3, 128, KC * 512])
            self.inp("mla_w_uq", [NB_, 3, 128, 4 * 2048])
            self.inp("mla_w_ukvT", [NB_, 128, 4, 4096])
            self.inp("mla_w_out", [NB_, 4, 128, KC * 512])
            self.inp("mla_gT", [NB_, 128, 2, 4])
            self.inp("mla_ckv_ctxT", [128, 4, 256])
            self.inp("mla_kpe_ctxT", [64, 256])
            self.outp("mla_ckvout", [4, 128, NP])
            self.outp("mla_kpeout", [64, NP])
        if NC_:
            self.inp("swa_w_qk", [NC_, 10, 128, KC * 512])
            self.inp("swa_w_v", [NC_, 1, 128, KC * 512])
            self.inp("swa_w_out", [NC_, 8, 64, 32 * 256])
            self.inp("swa_kctxT", [64, 8, 256])
            self.inp("swa_vctx", [256, 512])
            self.inp("swa_masks", [128, 2, 512])
            self.inp("swa_sink_bc", [NC_, 64, 32])
            self.outp("swa_kout", [64, 8, NP])
            self.outp("swa_vout", [NP, 512])
        self.OUTT = self.outp("yT", [KC, 128, NT])
        self.YT = self.din["xT"]
        YS = self.scratch("YS", [KC, 128, NT])
        self.YS = YS
        with ExitStack() as g:
            self.c_eps = self.sb(g, [128, 1], F32, "eps")
            self.ones_bf = self.sb(g, [128, 128], BF16, "ones")
            self.adab = self.sb(g, [128, L, 96], F32, "adab")
            self.ngT = self.sb(g, [128, L, 4, KC], F32, "ngT")
            self.scT = self.sb(g, [128, KC, 2], BF16, "scT")
            cTf = self.sb(g, [128, KC, 2], F32, "cTf")
            self.modraw = self.sb(g, [128, 96, 2], F32, "modraw")
            self.modA = self.sb(g, [128, 2, KC, 2], F32, "modA")
            self.modG = self.sb(g, [128, 2, KC, 2], F32, "modG")
            self.modS = self.sb(g, [128, 2, KC, 2], F32, "modS")
            self.rstd = self.sb(g, [128, TT], F32, "rstd")
            self.ropeC = self.sb(g, [64, TT], F32, "ropeC")
            self.ropeS = self.sb(g, [64, TT], F32, "ropeS")
            self.b_const, self.b_mod, self.b_rstd, self.b_rope = Buf("const"), Buf("mod"), Buf("rstd"), Buf("rope")
            self.wbufs = [(self.sb(g, [128, 8192], BF16, "wb"), Buf(f"wb{i}")) for i in range(2)]
            self.wrot = 0
            self.tmps = [(self.sb(g, [128, TT], F32, "tmp"), Buf(f"tmp{i}")) for i in range(4)]
            self.tmprot = 0
            self.sqs = [(self.sb(g, [128, TT], BF16, "sq"), Buf(f"sq{i}")) for i in range(2)]
            self.sqrot = 0
            psb = [g.enter_context(nc.psum_tensor(f"ps{i}", [128, 512], F32)) for i in range(7)]
            self.pstr = g.enter_context(nc.psum_tensor("pstr", [128, 1024], BF16))
            self.linps = [(psb[i], Buf(f"lin{i}", True)) for i in range(3)]
            self.linrot = 0
            self.statps = (psb[3], Buf("stat", True))
            self.miscps = [(psb[i], Buf(f"misc{i}", True)) for i in range(4, 7)]
            self.b_pstr = Buf("pstr", True)
            self.accps = [(self.miscps[0], self.miscps[1]), (self.miscps[2], self.statps)]
            self.accrot = 0
            S.add("dve", lambda e: e.memset(self.c_eps[:], EPS), writes=[self.b_const])
            S.add("dve", lambda e: e.memset(self.ones_bf[:], 1.0), writes=[self.b_const])
            S.add("sp", lambda e: e.dma_start(out=self.adab[:], in_=self.din["ada_bT"]), writes=[self.b_const], dma_key="c0")
            S.add("sp", lambda e: e.dma_start(out=self.ngT[:], in_=self.din["norm_gT"]), writes=[self.b_const], dma_key="c1")
            S.add("sp", lambda e: e.dma_start(out=cTf[:], in_=self.din["cT"]), writes=[self.b_const], dma_key="c2")
            S.add("act", lambda e: e.activation(out=self.scT[:], in_=cTf[:], func=AF.Silu), reads=[self.b_const], writes=[self.b_const])
            if NA:
                self.hgrn_setup(g)
            cnt = [0, 0, 0]
            for l in range(L):
                self.adaln(l)
                S.barrier()
                kind = kinds[l]
                if kind == 0:
                    self.hgrn_layer(l, cnt[0])
                elif kind == 1:
                    self.mla_layer(l, cnt[1])
                elif kind == 2:
                    self.swa_layer(l, cnt[2])
                if kind >= 0:
                    cnt[kind] += 1
                    self.YT = YS
                with ExitStack() as ph:
                    self.big = self.sb(ph, [128, KC, TT], F32, "big")
                    self.b_big = [Buf(f"big{c}") for c in range(KC)]
                    self.hT = self.sb(ph, [128, KC, TT], BF16, "hT")
                    self.b_hT = [Buf(f"hT{c}") for c in range(KC)]
                    self.aT = self.sb(ph, [128, 64, TT], BF16, "aT")
                    self.b_aT = [Buf(f"aT{c}") for c in range(64)]
                    for tile in self.tiles:
                        self.mlp(l, tile, l == L - 1)
                    S.barrier()
                self.YT = YS
            S.emit()
        return nc


ROPE_BASE = 10000.0


def _partner():
    d = np.arange(64)
    return np.where(d % 32 < 16, d + 16, d - 16)


def _rope_tables(NS, NP, GW):
    d = np.arange(64)
    inv = ROPE_BASE ** (-(d % 16).astype(np.float32) / 16.0)
    t = np.arange(NS)
    pos = np.where(d[:, None] < 32, (t // GW)[None, :], (t % GW)[None, :]).astype(np.float32)
    ang = pos * inv[:, None].astype(np.float32)
    c = np.cos(ang).astype(np.float32)
    sn = np.sin(ang).astype(np.float32)
    sn = np.where((d % 32 < 16)[:, None], -sn, sn)
    c = np.concatenate([c, np.ones((64, NP), np.float32)], axis=1)
    sn = np.concatenate([sn, np.zeros((64, NP), np.float32)], axis=1)
    return np.ascontiguousarray(c, np.float32), np.ascontiguousarray(sn, np.float32)


def _shared_inputs(inp, L, kinds, NS, NP, GW):
    m = {}
    m["ada_w"] = np.stack([tile_w(inp["ada_w"][l], plain_chunks(96), 4) for l in range(L)])
    m["ada_bT"] = np.ascontiguousarray(fm(inp["ada_b"][:L]))
    m["norm_gT"] = np.ascontiguousarray(fm(inp["norm_g"][:L]))
    m["mlp_w_in"] = np.stack([tile_w(inp["mlp_w_in"][l], plain_chunks(64), 4) for l in range(L)])
    m["mlp_w_out"] = np.stack([tile_w(inp["mlp_w_out"][l], plain_chunks(16), 1) for l in range(L)])
    m["ropeC"], m["ropeS"] = _rope_tables(NS, NP, GW)
    m["ident"] = np.eye(128, dtype=np.float32)
    par = _partner()
    NA = sum(1 for k in kinds if k == 0)
    NB_ = sum(1 for k in kinds if k == 1)
    NC_ = sum(1 for k in kinds if k == 2)
    if NA:
        qf, ig, wo, gbc = [], [], [], []
        for j in range(NA):
            W = inp["hgrn_w_in"][j]
            ch = []
            for h in range(16):
                ch += [np.arange(h * 128, (h + 1) * 128), np.arange(2048 + h * 128, 2048 + (h + 1) * 128), np.arange(4096 + h * 128, 4096 + (h + 1) * 128)]
            qf.append(tile_w(W, ch, 4))
            ig.append(tile_w(W, plain_chunks(32, start=6144), 4))
            wo.append(tile_w(inp["hgrn_w_out"][j], plain_chunks(16), 4))
            gbc.append(np.broadcast_to(np.tile(inp["hgrn_norm_g"][j], 16)[None, :], (64, 2048)))
        m["hg_w_qf"], m["hg_w_ig"], m["hg_w_out"] = np.stack(qf), np.stack(ig), np.stack(wo)
        m["hg_gbc"] = np.ascontiguousarray(np.stack(gbc), np.float32)
        lg = inp["hgrn_lb_logits"][:, :L]
        m["hg_lbT"] = np.ascontiguousarray(lg.reshape(2, L, 16, 128).transpose(3, 0, 1, 2), np.float32)
        s_, t_ = np.arange(64)[:, None], np.arange(64)[None, :]
        m["hg_masks"] = np.ascontiguousarray(np.stack([(s_ <= t_), (s_ >= t_)], axis=1).astype(np.float32))
    if NB_:
        wd, wq, wkv, wo, gT = [], [], [], [], []
        for j in range(NB_):
            W = inp["mla_w_down"][j]
            ch = plain_chunks(8) + [np.arange(1024, 1088), 1024 + par]
            wd.append(tile_w(W, ch, 4))
            U = inp["mla_w_uq"][j]
            ch = [np.arange(h * 192, h * 192 + 128) for h in range(16)]
            for h in range(16):
                ch += [h * 192 + 128 + np.arange(64), h * 192 + 128 + par]
            wq.append(tile_w(U, ch, 16))
            wkv.append(inp["mla_w_ukv"][j].reshape(4, 128, 4096).transpose(1, 0, 2))
            wo.append(tile_w(inp["mla_w_out"][j], plain_chunks(16), 4))
            gT.append(np.stack([fm(inp["mla_q_norm_g"][j]), fm(inp["mla_kv_norm_g"][j])], axis=1))
        m["mla_w_down"], m["mla_w_uq"], m["mla_w_out"] = np.stack(wd), np.stack(wq), np.stack(wo)
        m["mla_w_ukvT"] = np.ascontiguousarray(np.stack(wkv), np.float32)
        m["mla_gT"] = np.ascontiguousarray(np.stack(gT), np.float32)
    if NC_:
        wqk, wv, wo, sk = [], [], [], []
        for j in range(NC_):
            W = inp["swa_w_qkv"][j]
            ch = []
            for h in range(32):
                ch += [h * 64 + np.arange(64), h * 64 + par]
            for g_ in range(8):
                ch += [2048 + g_ * 64 + np.arange(64), 2048 + g_ * 64 + par]
            wqk.append(tile_w(W, ch, 8, cw=64))
            wv.append(tile_w(W, plain_chunks(4, start=2560), 4))
            wo.append(tile_w(inp["swa_w_out"][j], plain_chunks(16), 2, kp=64))
            sk.append(np.broadcast_to(inp["swa_sink"][j][None, :], (64, 32)))
        m["swa_w_qk"], m["swa_w_v"], m["swa_w_out"] = np.stack(wqk), np.stack(wv), np.stack(wo)
        m["swa_sink_bc"] = np.ascontiguousarray(np.stack(sk), np.float32)
        c_, a_ = np.arange(128)[:, None], np.arange(128)[None, :]
        m0 = np.tile((c_ >= a_).astype(np.float32), (1, 4))
        m1 = np.tile((c_ <= a_).astype(np.float32), (1, 4))
        m["swa_masks"] = np.ascontiguousarray(np.stack([m0, m1], axis=1))
    return m


def _host_inputs(inp, core, NS, NP, SEQ, kinds):
    b = core % inp["x_sample"].shape[0]
    nps = NP // SEQ
    xs = inp["x_sample"][b, :NS]
    xp = inp["x_prompt"][core * nps:(core + 1) * nps].reshape(NP, D)
    x = np.concatenate([xs, xp], axis=0)
    m = {}
    m["xT"] = np.ascontiguousarray(x.T.reshape(KC, 128, -1))
    cc = np.stack([inp["c"][b], inp["c_ctx"]], axis=-1)
    m["cT"] = np.ascontiguousarray(cc.reshape(KC, 128, 2).transpose(1, 0, 2))
    if 0 in kinds:
        m["hg_s0"] = np.ascontiguousarray(inp["state_hgrn"][b].transpose(0, 1, 3, 2, 4))
    if 1 in kinds:
        m["mla_ckv_ctxT"] = np.ascontiguousarray(inp["cache_mla_ckv"][b, 0].T.reshape(4, 128, -1).transpose(1, 0, 2))
        m["mla_kpe_ctxT"] = np.ascontiguousarray(inp["cache_mla_kpe"][b, 0].T)
    if 2 in kinds:
        m["swa_kctxT"] = np.ascontiguousarray(inp["cache_swa_k"][b, 0].transpose(2, 1, 0))
        m["swa_vctx"] = np.ascontiguousarray(inp["cache_swa_v"][b, 0].reshape(-1, 512))
    return m


def run(inp, NS, NP, SEQ, L, GRID_W, ncores, kinds):
    inp = {k: np.asarray(v) for k, v in inp.items()}
    p = Prog(NS, NP, SEQ, L, GRID_W)
    nc = p.build(kinds)
    shared = _shared_inputs(inp, L, kinds, NS, NP, GRID_W)
    maps = []
    for c in range(ncores):
        m = dict(shared)
        m.update(_host_inputs(inp, c, NS, NP, SEQ, kinds))
        maps.append(m)
    res = run_bass_kernel_spmd(nc, maps, core_ids=list(range(ncores)))
    return res.results


def assemble(r, NS, NP, SEQ, ncores, nsamp, kinds):
    nps = NP // SEQ
    out = {}
    out["ys"] = np.stack([r[c]["yT"].reshape(D, -1)[:, :NS].T for c in range(nsamp)])
    out["yp"] = np.concatenate([r[c]["yT"].reshape(D, -1)[:, NS:].T.reshape(nps, SEQ, D) for c in range(ncores)])
    if 0 in kinds:
        out["st"] = np.concatenate([r[c]["hg_stout"].transpose(1, 0, 2, 4, 3, 5) for c in range(ncores)])
    if 1 in kinds:
        out["ckv"] = np.concatenate([r[c]["mla_ckvout"].reshape(512, NP).T.reshape(nps, 1, SEQ, 512) for c in range(ncores)])
        out["kpe"] = np.concatenate([r[c]["mla_kpeout"].T.reshape(nps, 1, SEQ, 64) for c in range(ncores)])
    if 2 in kinds:
        out["k"] = np.concatenate([r[c]["swa_kout"].transpose(2, 1, 0).reshape(nps, 1, SEQ, 8, 64) for c in range(ncores)])
        out["v"] = np.concatenate([r[c]["swa_vout"].reshape(nps, 1, SEQ, 8, 64) for c in range(ncores)])
    return out


def kernel(**inputs):
    NS, NP, SEQ, L, GW = 4096, 512, 256, 4, 64
    kinds = [0, 1, 2, 0]
    r = run(inputs, NS, NP, SEQ, L, GW, 8, kinds)
    o = assemble(r, NS, NP, SEQ, 8, 4, kinds)
    f = lambda a: np.ascontiguousarray(a, dtype=np.float32)
    return (f(o["yp"]), f(o["ys"]), f(o["st"]), f(o["ckv"]), f(o["kpe"]), f(o["k"]), f(o["v"]))
```

```python
import numpy as np
from contextlib import ExitStack
import concourse.bass as bass
import concourse.mybir as mybir
from concourse.bass_utils import run_bass_kernel_spmd

F32 = mybir.dt.float32
BF16 = mybir.dt.bfloat16
AF = mybir.ActivationFunctionType
ALU = mybir.AluOpType
AX = mybir.AxisListType

SEM_CAP = 20000
D = 2048
KC = 16
TT = 512
EPS = 1e-6


class Buf:
    __slots__ = ("name", "last_w", "readers", "excl")

    def __init__(self, name="", excl=False):
        self.name = name
        self.last_w = None
        self.readers = []
        self.excl = excl


class Op:
    __slots__ = ("eng", "fn", "deps", "signal", "val", "semi", "dma", "dsem", "dval", "idx")

    def __init__(self, eng, fn, dma):
        self.eng = eng
        self.fn = fn
        self.deps = []
        self.signal = False
        self.val = None
        self.semi = None
        self.dma = dma
        self.dsem = None
        self.dval = None


class Sched:
    ENGS = ("pe", "act", "dve", "pool", "sp")

    def __init__(self, nc):
        self.nc = nc
        self.ops = {e: [] for e in self.ENGS}
        self.dma_cnt = {}
        self.dma_last = {}
        self.pending = {e: [] for e in self.ENGS}
        self.n = 0

    def add(self, eng, fn, reads=(), writes=(), dma_key=None):
        op = Op(eng, fn, dma_key is not None)
        op.idx = self.n
        self.n += 1
        deps = set(self.pending[eng])
        self.pending[eng] = []
        if dma_key is not None and dma_key in self.dma_last:
            deps.add(self.dma_last[dma_key])
        for b in reads:
            if b.last_w is not None:
                deps.add(b.last_w)
            if b.excl:
                for r in b.readers:
                    if r.eng != eng:
                        deps.add(r)
        for b in writes:
            if b.last_w is not None:
                deps.add(b.last_w)
            deps.update(b.readers)
        last = {}
        for d in deps:
            if d.dma:
                op.deps.append(d)
            elif d.eng not in last or last[d.eng].idx < d.idx:
                last[d.eng] = d
        for d in last.values():
            if d.eng == "pe" and eng == "pe" and not op.dma:
                continue
            d.signal = True
            op.deps.append(d)
        for b in reads:
            b.readers.append(op)
        for b in writes:
            b.last_w = op
            b.readers = []
        if dma_key is not None:
            c = self.dma_cnt.get(dma_key, 0) + 16
            self.dma_cnt[dma_key] = c
            op.dsem = dma_key
            op.dval = c
            self.dma_last[dma_key] = op
        self.ops[eng].append(op)
        return op

    def barrier(self):
        snap = []
        for e in self.ENGS:
            for op in reversed(self.ops[e]):
                if not op.dma:
                    op.signal = True
                    snap.append(op)
                    break
        snap.extend(self.dma_last.values())
        for e in self.ENGS:
            self.pending[e] = list(snap)

    def emit(self):
        nc = self.nc
        nsem = {}
        for e in self.ENGS:
            c = 0
            for op in self.ops[e]:
                if op.dma or not op.signal:
                    continue
                op.semi = c // SEM_CAP
                op.val = c % SEM_CAP + 1
                c += 1
            nsem[e] = max(1, (c + SEM_CAP - 1) // SEM_CAP)
        with ExitStack() as st:
            esem = {e: [st.enter_context(nc.semaphore(f"s_{e}_{i}")) for i in range(nsem[e])] for e in self.ENGS}
            dsem = {k: st.enter_context(nc.semaphore(f"d_{i}")) for i, k in enumerate(self.dma_cnt)}
            block = st.enter_context(nc.Block())
            handles = {"pe": block.tensor, "act": block.scalar, "dve": block.vector, "pool": block.gpsimd,
                       "sp": block.sync}

            def make(e):
                def body(eng):
                    waited = {}
                    for op in self.ops[e]:
                        need = {}
                        for d in op.deps:
                            if d.dma:
                                k, v = ("d", d.dsem), d.dval
                            else:
                                k, v = ("e", d.eng, d.semi), d.val
                            if need.get(k, 0) < v:
                                need[k] = v
                        for k, v in need.items():
                            if waited.get(k, 0) >= v:
                                continue
                            waited[k] = v
                            eng.wait_ge(dsem[k[1]] if k[0] == "d" else esem[k[1]][k[2]], v)
                        ins = op.fn(eng)
                        if op.dma:
                            ins.then_inc(dsem[op.dsem], 16)
                        elif op.signal:
                            ins.then_inc(esem[e][op.semi], 1)
                    if e == "sp":
                        for k, tot in self.dma_cnt.items():
                            if waited.get(("d", k), 0) < tot:
                                eng.wait_ge(dsem[k], tot)
                return body

            for e in self.ENGS:
                if self.ops[e] or e == "sp":
                    handles[e](make(e))


def tile_w(W, chunks, spc, kp=128, cw=128):
    K = W.shape[0]
    kc = K // kp
    ns = (len(chunks) + spc - 1) // spc
    out = np.zeros((ns, kp, kc, spc * cw), np.float32)
    Wr = W.reshape(kc, kp, W.shape[1])
    for i, cols in enumerate(chunks):
        s, j = divmod(i, spc)
        out[s, :, :, j * cw:j * cw + len(cols)] = Wr[:, :, cols].transpose(1, 0, 2)
    return out.reshape(ns, kp, kc * spc * cw)


def plain_chunks(n, w=128, start=0):
    return [np.arange(start + i * w, start + (i + 1) * w) for i in range(n)]


def fm(vec):
    v = np.asarray(vec, np.float32)
    v = v.reshape(v.shape[:-1] + (v.shape[-1] // 128, 128))
    return np.ascontiguousarray(np.moveaxis(v, -1, 0))


class Prog:
    def __init__(self, NS, NP, SEQ, DEPTH, GRID_W):
        self.NS, self.NP, self.SEQ, self.L, self.GW = NS, NP, SEQ, DEPTH, GRID_W
        self.NT = NS + NP
        self.tiles = [(t0, TT, 0) for t0 in range(0, NS, TT)] + [(NS + t0, TT, 1) for t0 in range(0, NP, TT)]
        self.nc = bass.Bass("TRN2", target_bir_lowering=False)
        self.S = Sched(self.nc)
        self.din = {}
        self.dout = {}
        self.uid = 0
        self.bYd = {}
        self.rope_t0 = None

    def inp(self, name, shape):
        self.din[name] = self.nc.dram_tensor(name, list(shape), F32, kind="ExternalInput").ap()
        return self.din[name]

    def outp(self, name, shape):
        self.dout[name] = self.nc.dram_tensor(name, list(shape), F32, kind="ExternalOutput").ap()
        return self.dout[name]

    def scratch(self, name, shape, dt=F32):
        return self.nc.dram_tensor(name, list(shape), dt).ap()

    def sb(self, st, shape, dt, name=None):
        self.uid += 1
        return st.enter_context(self.nc.sbuf_tensor(f"{name or 't'}_{self.uid}", list(shape), dt))

    def bY(self, c, t0):
        k = (c, t0)
        if k not in self.bYd:
            self.bYd[k] = Buf(f"Y{c}_{t0}")
        return self.bYd[k]

    def key(self, p="k"):
        self.uid += 1
        return f"{p}{self.uid}"

    def linear_fm(self, wd, KCn, nchunks, spc, x, T, consume, widths=None, kp=128, cw=128):
        S = self.S
        ns = (nchunks + spc - 1) // spc
        sc = spc * cw
        wb = self.wbufs

        def load(s):
            t, b = wb[self.wrot % len(wb)]
            self.wrot += 1
            S.add("pool", lambda e, t=t, s=s: e.dma_start(out=t[:kp, :KCn * sc], in_=wd[s]), writes=[b], dma_key=b.name)
            return t, b

        nxt = load(0)
        for s in range(ns):
            t, b = nxt
            if s + 1 < ns:
                nxt = load(s + 1)
            for j in range(min(spc, nchunks - s * spc)):
                ci = s * spc + j
                w = cw if widths is None else widths[ci]
                ps, pb = self.linps[self.linrot % len(self.linps)]
                self.linrot += 1
                for kc in range(KCn):
                    xa, xb = x(kc)
                    S.add("pe", lambda e, ps=ps, t=t, kc=kc, j=j, w=w, xa=xa: e.matmul(
                        ps[:w, :T], lhsT=t[:kp, kc * sc + j * cw: kc * sc + j * cw + w], rhs=xa,
                        start=(kc == 0), stop=(kc == KCn - 1)), reads=[b] + xb, writes=[pb])
                consume(ci, ps[:w, :T], pb)

    def linear_tm(self, wd, KCn, nslabs, x, T, blk, consume):
        S = self.S
        wb = self.wbufs

        def load(s):
            t, b = wb[self.wrot % len(wb)]
            self.wrot += 1
            S.add("pool", lambda e, t=t, s=s: e.dma_start(out=t[:, :KCn * 512], in_=wd[s]), writes=[b], dma_key=b.name)
            return t, b

        nxt = load(0)
        for s in range(nslabs):
            t, b = nxt
            if s + 1 < nslabs:
                nxt = load(s + 1)
            for tb in range(T // blk):
                ps, pb = self.linps[self.linrot % len(self.linps)]
                self.linrot += 1
                for kc in range(KCn):
                    xa, xb = x(kc, tb)
                    S.add("pe", lambda e, ps=ps, t=t, kc=kc, xa=xa: e.matmul(
                        ps[:blk, :512], lhsT=xa, rhs=t[:, kc * 512:(kc + 1) * 512],
                        start=(kc == 0), stop=(kc == KCn - 1)), reads=[b] + xb, writes=[pb])
                consume(s, tb, ps[:blk, :512], pb)

    def rstd_from(self, ps_ap, pb, n, out_t, out_b, parts=128):
        S = self.S
        T = ps_ap.shape[-1]
        S.add("act", lambda e: e.activation(out=out_t, in_=ps_ap, func=AF.Ln, scale=1.0 / n, bias=self.c_eps[:parts, :]),
              reads=[pb, self.b_const], writes=[out_b])
        S.add("act", lambda e: e.activation(out=out_t, in_=out_t, func=AF.Exp, scale=-0.5), reads=[out_b], writes=[out_b])

    def sumsq_acc(self, src_ap, src_bufs, first, last, T, parts=128):
        S = self.S
        sq, sqb = self.sqs[self.sqrot % len(self.sqs)]
        self.sqrot += 1
        S.add("act", lambda e: e.activation(out=sq[:parts, :T], in_=src_ap, func=AF.Square), reads=src_bufs, writes=[sqb])
        st, stb = self.statps
        S.add("pe", lambda e: e.matmul(st[:, :T], lhsT=self.ones_bf[:parts, :], rhs=sq[:parts, :T], start=first, stop=last),
              reads=[sqb, self.b_const], writes=[stb])

    def pre(self, l, k, tile):
        S = self.S
        t0, T, cond = tile
        big, hT, YT = self.big, self.hT, self.YT
        for c in range(KC):
            S.add("sp", lambda e, c=c: e.dma_start(out=big[:, c, :T], in_=YT[c, :, t0:t0 + T]),
                  reads=[self.bY(c, t0)], writes=[self.b_big[c]], dma_key=f"big{c % 4}")
            self.sumsq_acc(big[:, c, :T], [self.b_big[c]], c == 0, c == KC - 1, T)
        self.rstd_from(self.statps[0][:, :T], self.statps[1], D, self.rstd[:, :T], self.b_rstd)
        for c in range(KC):
            tmp, tb = self.tmps[self.tmprot % len(self.tmps)]
            self.tmprot += 1
            S.add("dve", lambda e, c=c, tmp=tmp: e.tensor_tensor(out=tmp[:, :T], in0=big[:, c, :T], in1=self.rstd[:, :T], op=ALU.mult),
                  reads=[self.b_big[c], self.b_rstd], writes=[tb])
            S.add("act", lambda e, c=c, tmp=tmp: e.activation(out=hT[:, c, :T], in_=tmp[:, :T], func=AF.Identity,
                                                               scale=self.modA[:, k, c, cond:cond + 1], bias=self.modS[:, k, c, cond:cond + 1]),
                  reads=[tb, self.b_mod], writes=[self.b_hT[c]])

    def post_consume(self, T):
        S = self.S
        big = self.big

        def consume(ci, ps, pb):
            S.add("act", lambda e: e.activation(out=big[:, ci, :T], in_=ps, func=AF.Copy), reads=[pb], writes=[self.b_big[ci]])
            self.sumsq_acc(big[:, ci, :T], [self.b_big[ci]], ci == 0, ci == KC - 1, T)
        return consume

    def post(self, l, k, tile, final):
        S = self.S
        t0, T, cond = tile
        big, YT = self.big, self.YT
        self.rstd_from(self.statps[0][:, :T], self.statps[1], D, self.rstd[:, :T], self.b_rstd)
        for c in range(KC):
            tmp, tb = self.tmps[self.tmprot % len(self.tmps)]
            self.tmprot += 1
            S.add("sp", lambda e, c=c, tmp=tmp: e.dma_start(out=tmp[:, :T], in_=YT[c, :, t0:t0 + T]), reads=[self.bY(c, t0)], writes=[tb], dma_key=tb.name)
            S.add("dve", lambda e, c=c: e.scalar_tensor_tensor(out=big[:, c, :T], in0=big[:, c, :T], scalar=self.modG[:, k, c, cond:cond + 1],
                                                                in1=self.rstd[:, :T], op0=ALU.mult, op1=ALU.mult),
                  reads=[self.b_big[c], self.b_rstd, self.b_mod], writes=[self.b_big[c]])
            S.add("pool", lambda e, c=c, tmp=tmp: e.tensor_tensor(out=tmp[:, :T], in0=tmp[:, :T], in1=big[:, c, :T], op=ALU.add),
                  reads=[tb, self.b_big[c]], writes=[tb])
            dst = self.OUTT if final else self.YS
            S.add("sp", lambda e, c=c, tmp=tmp, dst=dst: e.dma_start(out=dst[c, :, t0:t0 + T], in_=tmp[:, :T]), reads=[tb], writes=[self.bY(c, t0)],
                  dma_key=tb.name + "s")

    def adaln(self, l):
        S = self.S
        mr = self.modraw

        def x(kc):
            return self.scT[:, kc, :], [self.b_const]

        def consume(ci, ps, pb):
            S.add("dve", lambda e: e.tensor_scalar(out=mr[:, ci, :], in0=ps, scalar1=self.adab[:, l, ci:ci + 1], scalar2=None, op0=ALU.add),
                  reads=[pb, self.b_const], writes=[self.b_mod])
        self.linear_fm(self.din["ada_w"][l], KC, 96, 4, x, 2, consume)
        for k in range(2):
            sh = mr[:, (3 * k) * 16:(3 * k + 1) * 16, :]
            scl = mr[:, (3 * k + 1) * 16:(3 * k + 2) * 16, :]
            gt = mr[:, (3 * k + 2) * 16:(3 * k + 3) * 16, :]
            gpre = self.ngT[:, l, 2 * k, :].unsqueeze(2).broadcast_to([128, 16, 2])
            gpost = self.ngT[:, l, 2 * k + 1, :].unsqueeze(2).broadcast_to([128, 16, 2])
            S.add("dve", lambda e, k=k, scl=scl, gpre=gpre: e.scalar_tensor_tensor(out=self.modA[:, k, :, :], in0=scl, scalar=1.0, in1=gpre, op0=ALU.add, op1=ALU.mult),
                  reads=[self.b_mod, self.b_const], writes=[self.b_mod])
            S.add("dve", lambda e, k=k, gt=gt, gpost=gpost: e.tensor_tensor(out=self.modG[:, k, :, :], in0=gt, in1=gpost, op=ALU.mult),
                  reads=[self.b_mod, self.b_const], writes=[self.b_mod])
            S.add("dve", lambda e, k=k, sh=sh: e.tensor_copy(out=self.modS[:, k, :, :], in_=sh), reads=[self.b_mod], writes=[self.b_mod])

    def mlp(self, l, tile, final):
        S = self.S
        t0, T, cond = tile
        self.pre(l, 1, tile)
        hT, aT = self.hT, self.aT

        def x1(kc):
            return hT[:, kc, :T], [self.b_hT[kc]]

        def c1(ci, ps, pb):
            tmp, tb = self.tmps[self.tmprot % len(self.tmps)]
            self.tmprot += 1
            S.add("act", lambda e: e.activation(out=tmp[:, :T], in_=ps, func=AF.Relu), reads=[pb], writes=[tb])
            S.add("dve", lambda e: e.tensor_tensor(out=aT[:, ci, :T], in0=tmp[:, :T], in1=tmp[:, :T], op=ALU.mult), reads=[tb], writes=[self.b_aT[ci]])
        self.linear_fm(self.din["mlp_w_in"][l], KC, 64, 4, x1, T, c1)

        def x2(kc):
            return aT[:, kc, :T], [self.b_aT[kc]]
        self.linear_fm(self.din["mlp_w_out"][l], 64, 16, 1, x2, T, self.post_consume(T))
        self.post(l, 1, tile, final)


    def attend(self, pieces_q, nkb, kpieces, vblk, Tq, dv, scale, maskf, finish):
        S = self.S
        (o_ps, o_pb), (d_ps, d_pb) = self.accps[self.accrot % 2]
        self.accrot += 1
        for kb in range(nkb):
            ps, pb = self.linps[self.linrot % len(self.linps)]
            self.linrot += 1
            kp = kpieces(kb)
            for i, ((qa, qb), (ka, kbufs)) in enumerate(zip(pieces_q, kp)):
                S.add("pe", lambda e, ps=ps, ka=ka, qa=qa, i=i: e.matmul(ps[:, :Tq], lhsT=ka, rhs=qa, start=(i == 0), stop=(i == len(kp) - 1)),
                      reads=qb + kbufs, writes=[pb])
            pt, ptb = self.pts[self.ptrot % len(self.pts)]
            self.ptrot += 1
            S.add("act", lambda e, ps=ps, pt=pt: e.activation(out=pt[:, :Tq], in_=ps[:, :Tq], func=AF.Exp, scale=scale), reads=[pb], writes=[ptb])
            m = maskf(kb) if maskf is not None else None
            if m is not None:
                ma, mb = m
                S.add("pool", lambda e, pt=pt, ma=ma: e.tensor_tensor(out=pt[:, :Tq], in0=pt[:, :Tq], in1=ma, op=ALU.mult), reads=[ptb] + mb, writes=[ptb])
            va, vb = vblk(kb)
            S.add("pe", lambda e, pt=pt, va=va, kb=kb: e.matmul(o_ps[:dv, :Tq], lhsT=va, rhs=pt[:, :Tq], start=(kb == 0), stop=(kb == nkb - 1)),
                  reads=[ptb] + vb, writes=[o_pb])
            S.add("pe", lambda e, pt=pt, kb=kb: e.matmul(d_ps[:dv, :Tq], lhsT=self.ones_bf[:, :dv], rhs=pt[:, :Tq], start=(kb == 0), stop=(kb == nkb - 1)),
                  reads=[ptb, self.b_const], writes=[d_pb])
        finish(o_ps[:dv, :Tq], o_pb, d_ps[:dv, :Tq], d_pb)

    def attn_finish(self, out_ap, out_bufs, sink_ap=None, then=None):
        S = self

        def fin(o_ps, o_pb, d_ps, d_pb):
            rd, rdb = self.tmps[self.tmprot % len(self.tmps)]
            self.tmprot += 1
            p, T = o_ps.shape[0], o_ps.shape[-1]
            rda = rd[:p, :T]
            if sink_ap is not None:
                self.S.add("dve", lambda e: e.tensor_tensor(out=rda.rearrange("p (g t) -> p g t", g=4), in0=d_ps.rearrange("p (g t) -> p g t", g=4), in1=sink_ap, op=ALU.add),
                           reads=[d_pb, self.b_const], writes=[rdb])
                self.S.add("dve", lambda e: e.reciprocal(out=rda, in_=rda), reads=[rdb], writes=[rdb])
            else:
                self.S.add("dve", lambda e: e.reciprocal(out=rda, in_=d_ps), reads=[d_pb], writes=[rdb])
            self.S.add("dve", lambda e: e.tensor_tensor(out=out_ap, in0=o_ps, in1=rda, op=ALU.mult), reads=[o_pb, rdb], writes=out_bufs)
            if then is not None:
                then()
        return fin

    def rope_pair(self, t0, T, dst_fn):
        S = self.S
        st = {}
        if self.rope_t0 != t0:
            self.rope_t0 = t0
            S.add("sp", lambda e: e.dma_start(out=self.ropeC[:, :T], in_=self.din["ropeC"][:, t0:t0 + T]), writes=[self.b_rope], dma_key="c3")
            S.add("sp", lambda e: e.dma_start(out=self.ropeS[:, :T], in_=self.din["ropeS"][:, t0:t0 + T]), writes=[self.b_rope], dma_key="c4")

        def consume(ci, ps, pb):
            if ci % 2 == 0:
                tmp, tb = self.tmps[self.tmprot % len(self.tmps)]
                self.tmprot += 1
                st["a"] = (tmp, tb)
                S.add("dve", lambda e: e.tensor_tensor(out=tmp[:64, :T], in0=ps, in1=self.ropeC[:, :T], op=ALU.mult), reads=[pb, self.b_rope], writes=[tb])
            else:
                tmp, tb = st["a"]
                t2, tb2 = self.tmps[self.tmprot % len(self.tmps)]
                self.tmprot += 1
                S.add("dve", lambda e: e.tensor_tensor(out=t2[:64, :T], in0=ps, in1=self.ropeS[:, :T], op=ALU.mult), reads=[pb, self.b_rope], writes=[tb2])
                da, db = dst_fn(ci // 2)
                S.add("pool", lambda e: e.tensor_tensor(out=da, in0=tmp[:64, :T], in1=t2[:64, :T], op=ALU.add), reads=[tb, tb2], writes=db)
        return consume

    def swa_layer(self, l, j):
        nc, S = self.nc, self.S
        NT, NS, NP, SEQ = self.NT, self.NS, self.NP, self.SEQ
        QT = self.scratch(f"swaQ{l}", [64, 32, NT], BF16)
        KT = self.scratch(f"swaK{l}", [64, 8, NT], BF16)
        VV = self.scratch(f"swaV{l}", [NT, 512], BF16)
        OT = self.scratch(f"swaO{l}", [64, 32, NT], BF16)
        bQ, bK, bV, bO = {}, {}, {}, {}

        def gb(d, k):
            if k not in d:
                d[k] = Buf()
            return d[k]
        wq = self.din["swa_w_qk"][j]
        wv = self.din["swa_w_v"][j]
        wo = self.din["swa_w_out"][j]
        with ExitStack() as ph:
            self.big = self.sb(ph, [128, KC, TT], F32, "big")
            self.b_big = [Buf() for c in range(KC)]
            self.hT = self.sb(ph, [128, KC, TT], BF16, "hT")
            self.b_hT = [Buf() for c in range(KC)]
            hT = self.hT
            qst = self.sb(ph, [64, 40, TT], BF16, "qst")
            b_qst = [Buf() for _ in range(40)]
            vst = self.sb(ph, [128, 4, 512], BF16, "vst")
            b_vst = [Buf() for _ in range(4)]
            kf = self.sb(ph, [64, 8, TT], F32, "kf")
            vf = self.sb(ph, [128, 4, 512], F32, "vf")
            b_kf, b_vf = Buf(), Buf()
            for tile in self.tiles:
                t0, T, cond = tile
                self.pre(l, 0, tile)

                def x(kc):
                    return hT[:, kc, :T], [self.b_hT[kc]]
                rp = self.rope_pair(t0, T, lambda h: (qst[:, h, :T], [b_qst[h]]))

                def cons(ci, ps, pb, rp=rp, cond=cond, T=T):
                    rp(ci, ps, pb)
                    if cond == 1 and ci >= 64 and ci % 2 == 0:
                        S.add("act", lambda e: e.activation(out=kf[:, (ci - 64) // 2, :T], in_=ps, func=AF.Copy), reads=[pb], writes=[b_kf])
                self.linear_fm(wq, KC, 80, 8, x, T, cons, kp=128, cw=64)
                S.add("sp", lambda e, t0=t0, T=T: e.dma_start(out=QT[:, :, t0:t0 + T], in_=qst[:, 0:32, :T]), reads=b_qst[:32], writes=[gb(bQ, t0)], dma_key="swq")
                S.add("sp", lambda e, t0=t0, T=T: e.dma_start(out=KT[:, :, t0:t0 + T], in_=qst[:, 32:40, :T]), reads=b_qst[32:], writes=[gb(bK, t0)], dma_key="swk")

                def xv(kc, tb):
                    return hT[:, kc, tb * 128:(tb + 1) * 128], [self.b_hT[kc]]

                def consv(s_, tb, ps, pb, cond=cond):
                    S.add("act", lambda e: e.activation(out=vst[:, tb, :], in_=ps, func=AF.Copy), reads=[pb], writes=[b_vst[tb]])
                    if cond == 1:
                        S.add("dve", lambda e: e.tensor_copy(out=vf[:, tb, :], in_=ps), reads=[pb], writes=[b_vf])
                self.linear_tm(wv, KC, 1, xv, T, 128, consv)
                S.add("sp", lambda e, t0=t0, T=T: e.dma_start(out=VV[t0:t0 + T, :].rearrange("(b p) n -> p b n", p=128), in_=vst[:, :, :]), reads=b_vst, writes=[gb(bV, t0)], dma_key="swv")
                if cond == 1:
                    p0 = t0 - NS
                    S.add("sp", lambda e, p0=p0, T=T: e.dma_start(out=self.dout["swa_kout"][:, :, p0:p0 + T], in_=kf[:, :, :T]), reads=[b_kf], writes=[Buf()], dma_key="swko")
                    S.add("sp", lambda e, p0=p0, T=T: e.dma_start(out=self.dout["swa_vout"][p0:p0 + T, :].rearrange("(b p) n -> p b n", p=128), in_=vf[:, :, :]), reads=[b_vf], writes=[Buf()], dma_key="swvo")
            S.barrier()
        with ExitStack() as ph:
            self.pts = [(self.sb(ph, [128, TT], BF16, "pt"), Buf()) for _ in range(3)]
            self.ptrot = 0
            kctx = self.sb(ph, [64, 8, 256], BF16, "kctx")
            vctx = self.sb(ph, [128, 2, 512], BF16, "vctx")
            sinkr = self.sb(ph, [64, 32], F32, "sinkr")
            sinke = self.sb(ph, [64, 32], F32, "sinke")
            msk = self.sb(ph, [128, 2, 512], BF16, "msk")
            b_ctx = Buf()
            S.add("pool", lambda e: e.dma_start(out=kctx[:], in_=self.din["swa_kctxT"]), writes=[b_ctx], dma_key="sk1")
            S.add("pool", lambda e: e.dma_start(out=vctx[:], in_=self.din["swa_vctx"].rearrange("(b p) n -> p b n", p=128)), writes=[b_ctx], dma_key="sk2")
            S.add("pool", lambda e: e.dma_start(out=msk[:], in_=self.din["swa_masks"]), writes=[b_ctx], dma_key="sk3")
            S.add("sp", lambda e: e.dma_start(out=sinkr[:], in_=self.din["swa_sink_bc"][j]), writes=[b_ctx], dma_key="sk4")
            S.add("act", lambda e: e.activation(out=sinke[:], in_=sinkr[:], func=AF.Exp), reads=[b_ctx], writes=[b_ctx])
            NB = 2
            qb_ = [(self.sb(ph, [64, 32, 128], BF16, "qb"), Buf()) for _ in range(NB)]
            kl_ = [(self.sb(ph, [64, 8, 384], BF16, "kl"), Buf()) for _ in range(NB)]
            vl_ = [(self.sb(ph, [128, 3, 512], BF16, "vl"), Buf()) for _ in range(NB)]
            ob_ = [(self.sb(ph, [64, 32, 128], BF16, "ob"), Buf()) for _ in range(NB)]
            blocks = [(jb * 128, 0, NS, True) for jb in range(NS // 128)]
            for s_ in range(NP // SEQ):
                blocks += [(NS + s_ * SEQ + jb * 128, NS + s_ * SEQ, NS + (s_ + 1) * SEQ, False) for jb in range(SEQ // 128)]
            scale = 64 ** -0.5
            for bi, (q0, lo, hi, is_s) in enumerate(blocks):
                (qb, qbb), (kl, klb), (vl, vlb), (ob, obb) = qb_[bi % NB], kl_[bi % NB], vl_[bi % NB], ob_[bi % NB]
                tq = (q0 // TT) * TT
                S.add("sp", lambda e, qb=qb, q0=q0: e.dma_start(out=qb[:], in_=QT[:, :, q0:q0 + 128]), reads=[gb(bQ, tq)], writes=[qbb], dma_key="lq%d" % (bi % NB))
                if is_s:
                    kbs = [x_ for x_ in (q0 - 128, q0, q0 + 128) if lo <= x_ < hi]
                else:
                    kbs = list(range(lo, hi, 128))
                for i, k0 in enumerate(kbs):
                    tk = (k0 // TT) * TT
                    S.add("sp", lambda e, kl=kl, i=i, k0=k0: e.dma_start(out=kl[:, :, i * 128:(i + 1) * 128], in_=KT[:, :, k0:k0 + 128]), reads=[gb(bK, tk)], writes=[klb], dma_key="lk%d" % (bi % NB))
                    S.add("sp", lambda e, vl=vl, i=i, k0=k0: e.dma_start(out=vl[:, i, :], in_=VV[k0:k0 + 128, :]), reads=[gb(bV, tk)], writes=[vlb], dma_key="lv%d" % (bi % NB))
                nctx = 2 if is_s else 0
                for g in range(8):
                    def kpieces(kb, g=g, kl=kl, klb=klb):
                        if kb < nctx:
                            return [(kctx[:, g, kb * 128:(kb + 1) * 128], [b_ctx])]
                        i = kb - nctx
                        return [(kl[:, g, i * 128:(i + 1) * 128], [klb])]

                    def vblk(kb, g=g, vl=vl, vlb=vlb):
                        if kb < nctx:
                            return vctx[:, kb, g * 64:(g + 1) * 64], [b_ctx]
                        return vl[:, kb - nctx, g * 64:(g + 1) * 64], [vlb]

                    def maskf(kb, kbs=kbs, q0=q0):
                        if kb < nctx or not is_s:
                            return None
                        k0 = kbs[kb - nctx]
                        if k0 < q0:
                            return msk[:, 0, :], [b_ctx]
                        if k0 > q0:
                            return msk[:, 1, :], [b_ctx]
                        return None
                    qa = qb[:, 4 * g:4 * g + 4, :]
                    oa = ob[:, 4 * g:4 * g + 4, :].rearrange("p g t -> p (g t)")
                    sk = sinke[:, 4 * g:4 * g + 4].unsqueeze(2).broadcast_to([64, 4, 128])
                    self.attend([(qa, [qbb])], nctx + len(kbs), kpieces, vblk, 512, 64, scale, maskf, self.attn_finish(oa, [obb], sink_ap=sk))
                S.add("sp", lambda e, ob=ob, q0=q0: e.dma_start(out=OT[:, :, q0:q0 + 128], in_=ob[:]), reads=[obb], writes=[gb(bO, tq)], dma_key="so%d" % (bi % NB))
            S.barrier()
        with ExitStack() as ph:
            self.big = self.sb(ph, [128, KC, TT], F32, "big")
            self.b_big = [Buf() for c in range(KC)]
            oT = self.sb(ph, [64, 32, TT], BF16, "oTt")
            b_oT = Buf()
            for tile in self.tiles:
                t0, T, cond = tile
                S.add("sp", lambda e, t0=t0, T=T: e.dma_start(out=oT[:, :, :T], in_=OT[:, :, t0:t0 + T]), reads=[gb(bO, t0)], writes=[b_oT], dma_key="swo")

                def x(kc):
                    return oT[:, kc, :T], [b_oT]
                self.linear_fm(wo, 32, 16, 2, x, T, self.post_consume(T), kp=64)
                self.post(l, 0, tile, False)
            S.barrier()


    def mla_layer(self, l, j):
        nc, S = self.nc, self.S
        NT, NS, NP, SEQ = self.NT, self.NS, self.NP, self.SEQ
        NKEY = 256 + NT
        QN = self.scratch(f"mlaQN{l}", [16, 128, NT], BF16)
        QR = self.scratch(f"mlaQR{l}", [16, 64, NT], BF16)
        OT = self.scratch(f"mlaO{l}", [128, 16, NT], BF16)
        bQ, bO = {}, {}

        def gb(d, k):
            if k not in d:
                d[k] = Buf()
            return d[k]
        wdn = self.din["mla_w_down"][j]
        wuq = self.din["mla_w_uq"][j]
        wo = self.din["mla_w_out"][j]
        with ExitStack() as allph:
            ckv_all = self.sb(allph, [128, 4, NKEY], BF16, "ckvall")
            kpe_all = self.sb(allph, [64, NKEY], BF16, "kpeall")
            b_ckv = [Buf() for _ in range((NKEY + TT - 1) // TT + 1)]
            gq = self.sb(allph, [128, 2, 4], F32, "gq")
            b_g = Buf()
            S.add("sp", lambda e: e.dma_start(out=gq[:], in_=self.din["mla_gT"][j]), writes=[b_g], dma_key="mg")
            S.add("pool", lambda e: e.dma_start(out=ckv_all[:, :, 0:256], in_=self.din["mla_ckv_ctxT"]), writes=[b_ckv[0]], dma_key="mc1")
            S.add("pool", lambda e: e.dma_start(out=kpe_all[:, 0:256], in_=self.din["mla_kpe_ctxT"]), writes=[b_ckv[0]], dma_key="mc2")
            with ExitStack() as ph:
                self.big = self.sb(ph, [128, KC, TT], F32, "big")
                self.b_big = [Buf() for c in range(KC)]
                self.hT = self.sb(ph, [128, KC, TT], BF16, "hT")
                self.b_hT = [Buf() for c in range(KC)]
                hT = self.hT
                cf = self.sb(ph, [128, 8, TT], F32, "cf")
                b_cf = [Buf() for _ in range(8)]
                cqn = self.sb(ph, [128, 4, TT], BF16, "cqn")
                b_cqn = [Buf() for _ in range(4)]
                kpf = self.sb(ph, [64, TT], F32, "kpf")
                b_kpf = Buf()
                qn = self.sb(ph, [128, 16, TT], BF16, "qn")
                qr = self.sb(ph, [64, 16, TT], BF16, "qr")
                b_qn = [Buf() for _ in range(16)]
                b_qr = [Buf() for _ in range(16)]
                r2 = self.sb(ph, [128, TT], F32, "r2")
                b_r2 = Buf()
                for ti, tile in enumerate(self.tiles):
                    t0, T, cond = tile
                    k0 = 256 + t0
                    bk = b_ckv[1 + ti]
                    self.pre(l, 0, tile)

                    def x(kc):
                        return hT[:, kc, :T], [self.b_hT[kc]]
                    rp = self.rope_pair(t0, T, lambda h: (kpe_all[:, k0:k0 + T], [bk]))
                    st2 = self.miscps[0]

                    def cons(ci, ps, pb, T=T, rp=rp, cond=cond):
                        if ci < 8:
                            S.add("act", lambda e: e.activation(out=cf[:, ci, :T], in_=ps, func=AF.Copy), reads=[pb], writes=[b_cf[ci]])
                            sq, sqb = self.sqs[self.sqrot % len(self.sqs)]
                            self.sqrot += 1
                            S.add("act", lambda e: e.activation(out=sq[:, :T], in_=cf[:, ci, :T], func=AF.Square), reads=[b_cf[ci]], writes=[sqb])
                            st, stb = self.statps if ci < 4 else st2
                            S.add("pe", lambda e: e.matmul(st[:, :T], lhsT=self.ones_bf[:, :], rhs=sq[:, :T], start=(ci % 4 == 0), stop=(ci % 4 == 3)),
                                  reads=[sqb, self.b_const], writes=[stb])
                        else:
                            rp(ci - 8, ps, pb)
                            if ci == 8 and cond == 1:
                                S.add("act", lambda e: e.activation(out=kpf[:, :T], in_=ps, func=AF.Copy), reads=[pb], writes=[b_kpf])
                    self.linear_fm(wdn, KC, 10, 4, x, T, cons, widths=[128] * 8 + [64, 64])
                    self.rstd_from(self.statps[0][:, :T], self.statps[1], 512, self.rstd[:, :T], self.b_rstd)
                    self.rstd_from(st2[0][:, :T], st2[1], 512, r2[:, :T], b_r2)
                    for c in range(4):
                        S.add("dve", lambda e, c=c, T=T: e.scalar_tensor_tensor(out=cqn[:, c, :T], in0=cf[:, c, :T], scalar=gq[:, 0, c:c + 1], in1=self.rstd[:, :T], op0=ALU.mult, op1=ALU.mult),
                              reads=[b_cf[c], b_g, self.b_rstd], writes=[b_cqn[c]])
                        S.add("dve", lambda e, c=c, T=T: e.scalar_tensor_tensor(out=cf[:, 4 + c, :T], in0=cf[:, 4 + c, :T], scalar=gq[:, 1, c:c + 1], in1=r2[:, :T], op0=ALU.mult, op1=ALU.mult),
                              reads=[b_cf[4 + c], b_g, b_r2], writes=[b_cf[4 + c]])
                        S.add("pool", lambda e, c=c, k0=k0, T=T: e.tensor_copy(out=ckv_all[:, c, k0:k0 + T], in_=cf[:, 4 + c, :T]), reads=[b_cf[4 + c]], writes=[bk])
                    if cond == 1:
                        p0 = t0 - NS
                        S.add("sp", lambda e, p0=p0, T=T: e.dma_start(out=self.dout["mla_ckvout"][:, :, p0:p0 + T].rearrange("c p t -> p c t"), in_=cf[:, 4:8, :T]),
                              reads=b_cf[4:8], writes=[Buf()], dma_key="mco")
                        S.add("sp", lambda e, p0=p0, T=T: e.dma_start(out=self.dout["mla_kpeout"][:, p0:p0 + T], in_=kpf[:, :T]), reads=[b_kpf], writes=[Buf()], dma_key="mko")

                    def xq(kc):
                        return cqn[:, kc, :T], [b_cqn[kc]]
                    rq = self.rope_pair(t0, T, lambda h: (qr[:, h, :T], [b_qr[h]]))

                    def consq(ci, ps, pb, T=T, rq=rq):
                        if ci < 16:
                            S.add("act", lambda e: e.activation(out=qn[:, ci, :T], in_=ps, func=AF.Copy), reads=[pb], writes=[b_qn[ci]])
                        else:
                            rq(ci - 16, ps, pb)
                    self.linear_fm(wuq, 4, 48, 16, xq, T, consq, widths=[128] * 16 + [64] * 32)
                    S.add("sp", lambda e, t0=t0, T=T: e.dma_start(out=QN[:, :, t0:t0 + T].rearrange("h p t -> p h t"), in_=qn[:, :, :T]), reads=b_qn, writes=[gb(bQ, t0)], dma_key="mqn")
                    S.add("sp", lambda e, t0=t0, T=T: e.dma_start(out=QR[:, :, t0:t0 + T].rearrange("h p t -> p h t"), in_=qr[:, :, :T]), reads=b_qr, writes=[gb(bQ, t0)], dma_key="mqr")
                S.barrier()
            with ExitStack() as ph:
                self.pts = [(self.sb(ph, [128, TT], BF16, "pt"), Buf()) for _ in range(3)]
                self.ptrot = 0
                wkv = self.sb(ph, [128, 4, 4096], BF16, "wkv")
                b_wkv = Buf()
                S.add("pool", lambda e: e.dma_start(out=wkv[:], in_=self.din["mla_w_ukvT"][j]), writes=[b_wkv], dma_key="mwkv")
                NB = 2
                kth_ = [(self.sb(ph, [128, NKEY], BF16, "kth"), Buf()) for _ in range(NB)]
                vh_ = [(self.sb(ph, [128, NKEY // 128, 128], BF16, "vh"), Buf()) for _ in range(NB)]
                qnh_ = [(self.sb(ph, [128, NT], BF16, "qnh"), Buf())] * NB
                qrh_ = [(self.sb(ph, [64, NT], BF16, "qrh"), Buf())] * NB
                oh_ = [(self.sb(ph, [128, TT], BF16, "oh"), Buf()) for _ in range(3)]
                orot = 0
                allck = b_ckv
                scale = 192 ** -0.5
                for h in range(16):
                    (kth, kthb), (vh, vhb), (qnh, qnhb), (qrh, qrhb) = kth_[h % NB], vh_[h % NB], qnh_[h % NB], qrh_[h % NB]
                    S.add("sp", lambda e, qnh=qnh, h=h: e.dma_start(out=qnh[:], in_=QN[h]), reads=list(bQ.values()), writes=[qnhb], dma_key="mlq")
                    S.add("sp", lambda e, qrh=qrh, h=h: e.dma_start(out=qrh[:], in_=QR[h]), reads=list(bQ.values()), writes=[qrhb], dma_key="mlr")
                    for kt in range(0, NKEY, TT):
                        w = min(TT, NKEY - kt)
                        ps, pb = self.linps[self.linrot % len(self.linps)]
                        self.linrot += 1
                        for kc in range(4):
                            S.add("pe", lambda e, ps=ps, kc=kc, kt=kt, w=w, h=h: e.matmul(ps[:, :w], lhsT=wkv[:, kc, h * 256:h * 256 + 128], rhs=ckv_all[:, kc, kt:kt + w], start=(kc == 0), stop=(kc == 3)),
                                  reads=[b_wkv] + allck, writes=[pb])
                        S.add("act", lambda e, ps=ps, kt=kt, w=w, kth=kth: e.activation(out=kth[:, kt:kt + w], in_=ps[:, :w], func=AF.Copy), reads=[pb], writes=[kthb])
                    for kb4 in range(0, NKEY // 128, 4):
                        nb4 = min(4, NKEY // 128 - kb4)
                        ps, pb = self.linps[self.linrot % len(self.linps)]
                        self.linrot += 1
                        for i in range(nb4):
                            kb = kb4 + i
                            for kc in range(4):
                                S.add("pe", lambda e, ps=ps, kc=kc, kb=kb, i=i, h=h: e.matmul(ps[:, i * 128:(i + 1) * 128], lhsT=ckv_all[:, kc, kb * 128:(kb + 1) * 128], rhs=wkv[:, kc, h * 256 + 128:h * 256 + 256], start=(kc == 0), stop=(kc == 3)),
                                      reads=[b_wkv] + allck, writes=[pb])
                        S.add("dve", lambda e, ps=ps, kb4=kb4, nb4=nb4, vh=vh: e.tensor_copy(out=vh[:, kb4:kb4 + nb4, :].rearrange("p b d -> p (b d)"), in_=ps[:, :nb4 * 128]), reads=[pb], writes=[vhb])
                    qts = [(t0, TT, 0, (256 + NS) // 128) for t0 in range(0, NS, TT)]
                    for s_ in range(NP // SEQ):
                        qts.append((NS + s_ * SEQ, SEQ, (256 + NS + s_ * SEQ) // 128, SEQ // 128))
                    for (q0, Tq, kb0, nkb) in qts:
                        oh, ohb = oh_[orot % 3]
                        orot += 1

                        def kpieces(kb, kb0=kb0, kth=kth, kthb=kthb):
                            a = (kb0 + kb) * 128
                            return [(kth[:, a:a + 128], [kthb]), (kpe_all[:, a:a + 128], allck)]

                        def vblk(kb, kb0=kb0, vh=vh, vhb=vhb):
                            return vh[:, kb0 + kb, :], [vhb]
                        tq = (q0 // TT) * TT
                        self.attend([(qnh[:, q0:q0 + Tq], [qnhb]), (qrh[:, q0:q0 + Tq], [qrhb])], nkb, kpieces, vblk, Tq, 128, scale, None,
                                    self.attn_finish(oh[:, :Tq], [ohb]))
                        S.add("sp", lambda e, oh=oh, h=h, q0=q0, Tq=Tq: e.dma_start(out=OT[:, h, q0:q0 + Tq], in_=oh[:, :Tq]), reads=[ohb], writes=[gb(bO, (tq, h, q0))], dma_key="mo%d" % (orot % 3))
                S.barrier()
        with ExitStack() as ph:
            self.big = self.sb(ph, [128, KC, TT], F32, "big")
            self.b_big = [Buf() for c in range(KC)]
            oT = self.sb(ph, [128, 16, TT], BF16, "oTt")
            b_oT = Buf()
            for tile in self.tiles:
                t0, T, cond = tile
                S.add("sp", lambda e, t0=t0, T=T: e.dma_start(out=oT[:, :, :T], in_=OT[:, :, t0:t0 + T]), reads=[b for k_, b in bO.items() if k_[0] == t0], writes=[b_oT], dma_key="mlo")

                def x(kc):
                    return oT[:, kc, :T], [b_oT]
                self.linear_fm(wo, KC, 16, 4, x, T, self.post_consume(T))
                self.post(l, 0, tile, False)
            S.barrier()


    def hgrn_setup(self, g):
        S, L = self.S, self.L
        lg = self.sb(g, [128, 2, L, 16], F32, "lbl")
        self.lb = self.sb(g, [128, 2, L, 16], F32, "lb")
        self.oml = self.sb(g, [128, 2, L, 16], F32, "oml")
        sm = self.sb(g, [128, 2, 16], F32, "lbs")
        self.b_lb = Buf()
        b = self.b_lb
        S.add("sp", lambda e: e.dma_start(out=lg[:], in_=self.din["hg_lbT"]), writes=[b], dma_key="hlb")
        S.add("act", lambda e: e.activation(out=lg[:], in_=lg[:], func=AF.Exp), reads=[b], writes=[b])
        S.add("dve", lambda e: e.tensor_copy(out=sm[:], in_=lg[:, :, 0, :]), reads=[b], writes=[b])
        for i in range(1, L):
            S.add("dve", lambda e, i=i: e.tensor_tensor(out=sm[:], in0=sm[:], in1=lg[:, :, i, :], op=ALU.add), reads=[b], writes=[b])
        S.add("dve", lambda e: e.reciprocal(out=sm[:], in_=sm[:]), reads=[b], writes=[b])
        S.add("dve", lambda e: e.memset(self.lb[:, :, 0, :], 0.0), writes=[b])
        for i in range(1, L):
            S.add("dve", lambda e, i=i: e.tensor_tensor(out=lg[:, :, i, :], in0=lg[:, :, i, :], in1=sm[:], op=ALU.mult), reads=[b], writes=[b])
            S.add("dve", lambda e, i=i: e.tensor_tensor(out=self.lb[:, :, i, :], in0=self.lb[:, :, i - 1, :], in1=lg[:, :, i, :], op=ALU.add), reads=[b], writes=[b])
        S.add("dve", lambda e: e.tensor_scalar(out=self.oml[:], in0=self.lb[:], scalar1=-1.0, scalar2=1.0, op0=ALU.mult, op1=ALU.add), reads=[b], writes=[b])

    def hgrn_layer(self, l, j):
        nc, S = self.nc, self.S
        NT, NS, NP, SEQ = self.NT, self.NS, self.NP, self.SEQ
        NCH = NT // 64
        Q2 = self.scratch(f"hgQ2{l}", [16, 128, NT], BF16)
        K2 = self.scratch(f"hgK2{l}", [16, 128, NT], BF16)
        D2 = self.scratch(f"hgD2{l}", [16, 128, NCH, 3], F32)
        V64 = self.scratch(f"hgV{l}", [NT, 2048], BF16)
        GS = self.scratch(f"hgG{l}", [NT, 2048], BF16)
        O1 = self.scratch(f"hgO1{l}", [NT, 2048], F32)
        bsc = {}

        def gb(k):
            if k not in bsc:
                bsc[k] = Buf()
            return bsc[k]
        wqf = self.din["hg_w_qf"][j]
        wig = self.din["hg_w_ig"][j]
        wo = self.din["hg_w_out"][j]
        lb, oml = self.lb, self.oml
        with ExitStack() as allph:
            Sst = [self.sb(allph, [128, 16, 128], F32, "Sst") for _ in range(2)]
            b_S = [[Buf() for _ in range(16)] for _ in range(2)]
            ident = self.sb(allph, [128, 128], BF16, "ident")
            onesf = self.sb(allph, [128, TT], F32, "onesf")
            hgbc = self.sb(allph, [64, 2048], F32, "hgbc")
            b_hc = Buf()
            S.add("pool", lambda e: e.dma_start(out=ident[:], in_=self.din["ident"]), writes=[b_hc], dma_key="hm2")
            S.add("sp", lambda e: e.dma_start(out=hgbc[:], in_=self.din["hg_gbc"][j]), writes=[b_hc], dma_key="hm3")
            S.add("dve", lambda e: e.memset(onesf[:], 1.0), writes=[b_hc])
            sbf_ = [(self.sb(allph, [128, 128], BF16, "sbf"), Buf()) for _ in range(4)]
            t1_ = [(self.sb(allph, [128, 128], F32, "t1"), Buf()) for _ in range(3)]
            am8 = self.sb(allph, [64, 512], BF16, "am8")
            ktok8 = self.sb(allph, [64, 8, 128], BF16, "ktok8")
            b_am8, b_ktok8 = Buf(), Buf()
            hmaskf = self.sb(allph, [64, 2, 64], F32, "hmaskf")
            S.add("sp", lambda e: e.dma_start(out=hmaskf[:], in_=self.din["hg_masks"]), writes=[b_hc], dma_key="hm4")
            rot = {"sbf": 0, "t1": 0}
            par = [0] * 16
            pa8, b_pa8 = self.miscps[0]
            pk8 = [self.miscps[1], self.miscps[2]]

            def nxt(lst, k):
                r = lst[rot[k] % len(lst)]
                rot[k] += 1
                return r

            def mk_memset(h):
                def f():
                    st_, b_ = Sst[par[h]], b_S[par[h]][h]
                    S.add("pool", lambda e: e.memset(st_[:, h, :], 0.0), writes=[b_])
                return f

            def mk_stout(h, sq_, d):
                def f():
                    st_, b_ = Sst[par[h]], b_S[par[h]][h]
                    S.add("sp", lambda e: e.dma_start(out=self.dout["hg_stout"][j, sq_, d, :, h, :], in_=st_[:, h, :]), reads=[b_], writes=[Buf()], dma_key="hso%d" % (h % 4))
                return f

            def scan_group(d, items, po2, sink4):
                def amm(i, it):
                    S.add("pe", lambda e: e.matmul(pa8[:64, i * 64:(i + 1) * 64], lhsT=it["kt"], rhs=it["qt"], start=True, stop=True),
                          reads=it["qtb"] + it["ktb"], writes=[b_pa8])
                for i, it in enumerate(items):
                    amm(i, it)
                S.add("dve", lambda e: e.tensor_scalar(out=am8[:], in0=pa8[:64, :512], scalar1=1e30, scalar2=-1e30, op0=ALU.min, op1=ALU.max), reads=[b_pa8], writes=[b_am8])
                S.add("pool", lambda e: e.tensor_tensor(out=am8[:].rearrange("p (c t) -> p c t", t=64), in0=am8[:].rearrange("p (c t) -> p c t", t=64),
                                                         in1=hmaskf[:, d, :].unsqueeze(1).broadcast_to([64, 8, 64]), op=ALU.mult), reads=[b_am8, b_hc], writes=[b_am8])

                def tr(i, it):
                    S.add("pe", lambda e: e.transpose(self.pstr[:64, i * 128:(i + 1) * 128], it["kt"], ident[:, :]), reads=it["ktb"] + [b_hc], writes=[self.b_pstr])
                for i, it in enumerate(items):
                    tr(i, it)
                S.add("act", lambda e: e.activation(out=ktok8[:].rearrange("p c k -> p (c k)"), in_=self.pstr[:64, :1024], func=AF.Copy), reads=[self.b_pstr], writes=[b_ktok8])

                def kvmm(i, it):
                    pk, pkb = pk8[i // 4]
                    col = (i % 4) * 128
                    S.add("pe", lambda e: e.matmul(pk[:, col:col + 128], lhsT=ktok8[:, i, :], rhs=it["v"], start=True, stop=True), reads=[b_ktok8] + it["vb"], writes=[pkb])
                for i, it in enumerate(items):
                    kvmm(i, it)

                def step(i, it):
                    h, dv, dvb = it["h"], it["dv"], it["dvb"]
                    for f in it["pre"]:
                        f()
                    cur = par[h]
                    new = 1 - cur
                    Sc, Sn = Sst[cur], Sst[new]
                    bc, bn = b_S[cur][h], b_S[new][h]
                    pk, pkb = pk8[i // 4]
                    po, pob = po2[i // 4]
                    col = (i % 4) * 128
                    sbf, sbfb = nxt(sbf_, "sbf")
                    S.add("pool", lambda e: e.tensor_scalar(out=sbf[:], in0=Sc[:, h, :], scalar1=dv[:, 0:1], scalar2=None, op0=ALU.mult), reads=[bc] + dvb, writes=[sbfb])
                    S.add("pe", lambda e: e.matmul(po[:64, col:col + 128], lhsT=it["qt"], rhs=sbf[:], start=True, stop=False), reads=it["qtb"] + [sbfb], writes=[pob])
                    S.add("pe", lambda e: e.matmul(po[:64, col:col + 128], lhsT=am8[:, i * 64:(i + 1) * 64], rhs=it["v"], start=False, stop=True), reads=[b_am8] + it["vb"], writes=[pob])
                    t1, t1b = nxt(t1_, "t1")
                    S.add("dve", lambda e: e.tensor_scalar(out=t1[:], in0=Sc[:, h, :], scalar1=dv[:, 2:3], scalar2=None, op0=ALU.mult), reads=[bc] + dvb, writes=[t1b])
                    S.add("dve", lambda e: e.scalar_tensor_tensor(out=Sn[:, h, :], in0=pk[:, col:col + 128], scalar=dv[:, 1:2], in1=t1[:], op0=ALU.mult, op1=ALU.add),
                          reads=[pkb, t1b] + dvb, writes=[bn])
                    par[h] = new
                    for f in it["post"]:
                        f()
                    if i % 4 == 3:
                        sink4(i // 4, po[:64, :512], pob, items[i - 3:i + 1])
                for i, it in enumerate(items):
                    step(i, it)

            with ExitStack() as ph:
                self.big = self.sb(ph, [128, KC, TT], F32, "big")
                self.b_big = [Buf() for c in range(KC)]
                self.hT = self.sb(ph, [128, KC, TT], BF16, "hT")
                self.b_hT = [Buf() for c in range(KC)]
                hT = self.hT
                v64 = self.sb(ph, [64, 8, 2048], BF16, "v64")
                b_v64 = [Buf() for _ in range(8)]
                gst_ = [(self.sb(ph, [64, 512], BF16, "gst"), Buf()) for _ in range(2)]
                gtmp = self.sb(ph, [64, 512], F32, "gtmp")
                b_gtmp = Buf()
                qs_ = [(self.sb(ph, [128, TT], F32, "qs"), Buf()) for _ in range(2)]
                ft = {n: (self.sb(ph, [128, TT], F32, n), Buf()) for n in ["f", "g", "B", "X", "E", "eq", "ek"]}
                qk_ = [[(self.sb(ph, [128, TT], BF16, "qkt"), Buf()) for _ in range(2)] for _ in range(4)]
                dvt_ = [(self.sb(ph, [128, 8, 3], F32, "dvt"), Buf()) for _ in range(4)]
                o1h_ = [(self.sb(ph, [64, 8, 128], F32, "o1h"), Buf()) for _ in range(2)]
                S.add("sp", lambda e: e.dma_start(out=Sst[0][:], in_=self.din["hg_s0"][j, 0]), writes=b_S[0], dma_key="hs0")
                hcount = 0
                lin_saved = self.linps
                po2_p1 = [self.statps, lin_saved[2]]
                self.linps = lin_saved[:2]
                for ti, tile in enumerate(self.tiles):
                    t0, T, cond = tile
                    self.pre(l, 0, tile)

                    def xv(kc, tb):
                        return hT[:, kc, tb * 64:(tb + 1) * 64], [self.b_hT[kc]]

                    def consv(s_, tb, ps, pb, t0=t0):
                        if s_ < 4:
                            S.add("act", lambda e: e.activation(out=v64[:, tb, s_ * 512:(s_ + 1) * 512], in_=ps, func=AF.Copy), reads=[pb], writes=[b_v64[tb]])
                        else:
                            gst, gstb = gst_[(s_ * 8 + tb) % 2]
                            S.add("act", lambda e: e.activation(out=gtmp[:], in_=ps, func=AF.Silu), reads=[pb], writes=[b_gtmp])
                            S.add("dve", lambda e: e.tensor_tensor(out=gst[:], in0=gtmp[:], in1=hgbc[:, (s_ - 4) * 512:(s_ - 3) * 512], op=ALU.mult), reads=[b_gtmp, b_hc], writes=[gstb])
                            r0 = t0 + tb * 64
                            S.add("sp", lambda e: e.dma_start(out=GS[r0:r0 + 64, (s_ - 4) * 512:(s_ - 3) * 512], in_=gst[:]), reads=[gstb], writes=[gb(("G", t0))], dma_key="hg%d" % ((s_ * 8 + tb) % 2))
                    self.linear_tm(wig, KC, 8, xv, T, 64, consv)
                    S.add("sp", lambda e, t0=t0: e.dma_start(out=V64[t0:t0 + TT, :].rearrange("(c p) n -> p c n", p=64), in_=v64[:]), reads=b_v64, writes=[gb(("V", t0))], dma_key="hv")

                    def x(kc):
                        return hT[:, kc, :T], [self.b_hT[kc]]
                    stq = {}

                    def cons(ci, ps, pb, t0=t0, cond=cond, ti=ti):
                        h, kind = divmod(ci, 3)
                        if kind == 0:
                            qs, qsb = qs_[h % 2]
                            stq["qs"] = (qs, qsb)
                            S.add("act", lambda e: e.activation(out=qs[:], in_=ps, func=AF.Silu), reads=[pb], writes=[qsb])
                            return
                        d = kind - 1
                        qs, qsb = stq["qs"]
                        (f, fb), (g_, gb_), (B, Bb), (X, Xb), (E, Eb), (eq, eqb), (ek, ekb) = [ft[n] for n in ["f", "g", "B", "X", "E", "eq", "ek"]]
                        S.add("act", lambda e: e.activation(out=f[:], in_=ps, func=AF.Sigmoid), reads=[pb], writes=[fb])
                        S.add("dve", lambda e: e.tensor_scalar(out=f[:], in0=f[:], scalar1=oml[:, d, l, h:h + 1], scalar2=lb[:, d, l, h:h + 1], op0=ALU.mult, op1=ALU.add), reads=[fb, self.b_lb], writes=[fb])
                        S.add("act", lambda e: e.activation(out=g_[:], in_=f[:], func=AF.Ln), reads=[fb], writes=[gb_])
                        S.add("dve", lambda e: e.tensor_tensor_scan(out=B[:], data0=onesf[:], data1=g_[:], initial=0.0, op0=ALU.mult, op1=ALU.add), reads=[gb_, b_hc], writes=[Bb])
                        S.add("pool", lambda e: e.tensor_tensor(out=X[:], in0=B[:], in1=g_[:], op=ALU.subtract), reads=[Bb, gb_], writes=[Xb])
                        S.add("pool", lambda e: e.tensor_scalar(out=f[:], in0=f[:], scalar1=-1.0, scalar2=1.0, op0=ALU.mult, op1=ALU.add), reads=[fb], writes=[fb])
                        B3 = B[:].rearrange("p (c t) -> p c t", t=64)
                        X3 = X[:].rearrange("p (c t) -> p c t", t=64)
                        E3 = E[:].rearrange("p (c t) -> p c t", t=64)
                        if d == 0:
                            S.add("dve", lambda e: e.tensor_tensor(out=E3, in0=B3, in1=B3[:, :, 32:33].broadcast_to([128, 8, 64]), op=ALU.subtract), reads=[Bb], writes=[Eb])
                        else:
                            S.add("dve", lambda e: e.tensor_tensor(out=E3, in0=X3[:, :, 32:33].broadcast_to([128, 8, 64]), in1=X3, op=ALU.subtract), reads=[Xb], writes=[Eb])
                        S.add("act", lambda e: e.activation(out=eq[:], in_=E[:], func=AF.Exp), reads=[Eb], writes=[eqb])
                        S.add("act", lambda e: e.activation(out=ek[:], in_=E[:], func=AF.Exp, scale=-1.0), reads=[Eb], writes=[ekb])
                        (qt, qtb) = qk_[2 * d][hcount_ref[0] % 2]
                        (kt, ktb) = qk_[2 * d + 1][hcount_ref[0] % 2]
                        (dvt, dvb) = dvt_[2 * d + hcount_ref[0] % 2]
                        S.add("dve", lambda e: e.scalar_tensor_tensor(out=qt[:], in0=qs[:], scalar=128 ** -0.5, in1=eq[:], op0=ALU.mult, op1=ALU.mult), reads=[qsb, eqb], writes=[qtb])
                        S.add("pool", lambda e: e.tensor_tensor(out=kt[:], in0=f[:], in1=ek[:], op=ALU.mult), reads=[fb, ekb], writes=[ktb])
                        mid = (B3 if d == 0 else X3)[:, :, 32:33]
                        if d == 0:
                            S.add("pool", lambda e: e.tensor_tensor(out=dvt[:, :, 0:1], in0=mid, in1=X3[:, :, 0:1], op=ALU.subtract), reads=[Bb, Xb], writes=[dvb])
                            S.add("pool", lambda e: e.tensor_tensor(out=dvt[:, :, 1:2], in0=B3[:, :, 63:64], in1=mid, op=ALU.subtract), reads=[Bb, Xb], writes=[dvb])
                        else:
                            S.add("pool", lambda e: e.tensor_tensor(out=dvt[:, :, 0:1], in0=B3[:, :, 63:64], in1=mid, op=ALU.subtract), reads=[Bb, Xb], writes=[dvb])
                            S.add("pool", lambda e: e.tensor_tensor(out=dvt[:, :, 1:2], in0=mid, in1=X3[:, :, 0:1], op=ALU.subtract), reads=[Bb, Xb], writes=[dvb])
                        S.add("pool", lambda e: e.tensor_tensor(out=dvt[:, :, 2:3], in0=B3[:, :, 63:64], in1=X3[:, :, 0:1], op=ALU.subtract), reads=[Bb, Xb], writes=[dvb])
                        S.add("act", lambda e: e.activation(out=dvt[:], in_=dvt[:], func=AF.Exp), reads=[dvb], writes=[dvb])
                        if d == 0:
                            o1h, o1hb = o1h_[h % 2]
                            items = []
                            for c in range(8):
                                pre_, post_ = [], []
                                if cond == 1 and c % (SEQ // 64) == 0:
                                    pre_.append(mk_memset(h))
                                if cond == 1 and (c + 1) % (SEQ // 64) == 0:
                                    post_.append(mk_stout(h, c // (SEQ // 64), 0))
                                items.append(dict(h=h, qt=qt[:, c * 64:(c + 1) * 64], qtb=[qtb], kt=kt[:, c * 64:(c + 1) * 64], ktb=[ktb], dv=dvt[:, c, :], dvb=[dvb],
                                                  v=v64[:, c, h * 128:(h + 1) * 128], vb=[b_v64[c]], pre=pre_, post=post_))

                            def do_scan(items=items, o1h=o1h, o1hb=o1hb, h=h):
                                def sink4(half, po_ap, pob, its):
                                    S.add("act", lambda e: e.activation(out=o1h[:, half * 4:(half + 1) * 4, :], in_=po_ap.rearrange("p (c v) -> p c v", v=128), func=AF.Copy), reads=[pob], writes=[o1hb])
                                scan_group(0, items, po2_p1, sink4)
                                S.add("sp", lambda e: e.dma_start(out=O1[t0:t0 + TT, h * 128:(h + 1) * 128].rearrange("(c p) v -> p c v", p=64), in_=o1h[:]), reads=[o1hb], writes=[gb(("O", t0, h))], dma_key="ho%d" % (h % 2))
                            if pend_scan:
                                pend_scan.pop()()
                            pend_scan.append(do_scan)
                        else:
                            S.add("sp", lambda e: e.dma_start(out=Q2[h, :, t0:t0 + TT], in_=qt[:]), reads=[qtb], writes=[gb(("Q", t0, h))], dma_key="hq%d" % (hcount_ref[0] % 2))
                            S.add("sp", lambda e: e.dma_start(out=K2[h, :, t0:t0 + TT], in_=kt[:]), reads=[ktb], writes=[gb(("K", t0, h))], dma_key="hk%d" % (hcount_ref[0] % 2))
                            S.add("sp", lambda e: e.dma_start(out=D2[h, :, ti * 8:(ti + 1) * 8, :], in_=dvt[:]), reads=[dvb], writes=[gb(("D", t0, h))], dma_key="hd%d" % (hcount_ref[0] % 2))
                            hcount_ref[0] += 1
                    hcount_ref = [hcount]
                    pend_scan = []
                    self.linear_fm(wqf, KC, 48, 4, x, T, cons)
                    while pend_scan:
                        pend_scan.pop()()
                    hcount = hcount_ref[0]
                self.linps = lin_saved
                S.barrier()
            with ExitStack() as ph:
                self.big = self.sb(ph, [128, KC, TT], F32, "big")
                self.b_big = [Buf() for c in range(KC)]
                oT = self.sb(ph, [128, 16, TT], BF16, "oT")
                b_oT = [Buf() for _ in range(8)]
                NB = 2
                q2c_ = [(self.sb(ph, [128, 16, 64], BF16, "q2c"), Buf()) for _ in range(NB)]
                k2c_ = [(self.sb(ph, [128, 16, 64], BF16, "k2c"), Buf()) for _ in range(NB)]
                vc_ = [(self.sb(ph, [64, 2048], BF16, "vc"), Buf()) for _ in range(NB)]
                gc_ = [(self.sb(ph, [64, 2048], BF16, "gc"), Buf()) for _ in range(NB)]
                o1c_ = [(self.sb(ph, [64, 2048], F32, "o1c"), Buf()) for _ in range(NB)]
                d2t = self.sb(ph, [128, 16, 8, 3], F32, "d2t")
                b_d2t = Buf()
                osum = self.sb(ph, [64, 16, 128], F32, "osum")
                b_osum = [Buf() for _ in range(16)]
                sqt = self.sb(ph, [64, 2048], F32, "sqt")
                b_sqt = Buf()
                ssq = self.sb(ph, [64, 16], F32, "ssq")
                b_ssq = Buf()
                obf = self.sb(ph, [64, 2048], BF16, "obf")
                b_obf = Buf()
                order = [t for t in self.tiles if t[2] == 1] + [t for t in reversed(self.tiles) if t[2] == 0]
                po2_p2 = [self.linps[0], self.linps[1]]
                first_sample = True
                ci_ = 0
                for tile in order:
                    t0, T, cond = tile
                    ti = t0 // TT
                    if cond == 0 and first_sample:
                        first_sample = False
                        p0 = par[0]
                        assert all(p == p0 for p in par)
                        S.add("sp", lambda e, p0=p0: e.dma_start(out=Sst[p0][:], in_=self.din["hg_s0"][j, 1]), writes=b_S[p0], dma_key="hs1")
                    S.add("sp", lambda e, ti=ti: e.dma_start(out=d2t[:], in_=D2[:, :, ti * 8:(ti + 1) * 8, :].rearrange("h p c k -> p h c k")), reads=[gb(("D", t0, h)) for h in range(16)], writes=[b_d2t], dma_key="hd2")
                    for c in reversed(range(8)):
                        r0 = t0 + c * 64
                        (q2c, q2b), (k2c, k2b), (vc, vcb), (gc, gcb), (o1c, o1b) = q2c_[ci_ % NB], k2c_[ci_ % NB], vc_[ci_ % NB], gc_[ci_ % NB], o1c_[ci_ % NB]
                        kk = ci_ % NB
                        ci_ += 1
                        S.add("sp", lambda e, q2c=q2c, r0=r0: e.dma_start(out=q2c[:], in_=Q2[:, :, r0:r0 + 64].rearrange("h p t -> p h t")), reads=[gb(("Q", t0, h)) for h in range(16)], writes=[q2b], dma_key="p2q%d" % kk)
                        S.add("sp", lambda e, k2c=k2c, r0=r0: e.dma_start(out=k2c[:], in_=K2[:, :, r0:r0 + 64].rearrange("h p t -> p h t")), reads=[gb(("K", t0, h)) for h in range(16)], writes=[k2b], dma_key="p2k%d" % kk)
                        S.add("sp", lambda e, vc=vc, r0=r0: e.dma_start(out=vc[:], in_=V64[r0:r0 + 64, :]), reads=[gb(("V", t0))], writes=[vcb], dma_key="p2v%d" % kk)
                        S.add("sp", lambda e, gc=gc, r0=r0: e.dma_start(out=gc[:], in_=GS[r0:r0 + 64, :]), reads=[gb(("G", t0))], writes=[gcb], dma_key="p2g%d" % kk)
                        S.add("sp", lambda e, o1c=o1c, r0=r0: e.dma_start(out=o1c[:], in_=O1[r0:r0 + 64, :]), reads=[gb(("O", t0, h)) for h in range(16)], writes=[o1b], dma_key="p2o%d" % kk)
                        for g0 in (0, 8):
                            items = []
                            for h in range(g0, g0 + 8):
                                pre_, post_ = [], []
                                if cond == 1 and (c + 1) % (SEQ // 64) == 0:
                                    pre_.append(mk_memset(h))
                                if cond == 1 and c % (SEQ // 64) == 0:
                                    post_.append(mk_stout(h, c // (SEQ // 64), 1))
                                items.append(dict(h=h, qt=q2c[:, h, :], qtb=[q2b], kt=k2c[:, h, :], ktb=[k2b], dv=d2t[:, h, c, :], dvb=[b_d2t],
                                                  v=vc[:, h * 128:(h + 1) * 128], vb=[vcb], pre=pre_, post=post_))

                            def sink4(half, po_ap, pob, its, o1c=o1c, o1b=o1b):
                                h0 = its[0]["h"]
                                S.add("dve", lambda e: e.tensor_tensor(out=osum[:, h0:h0 + 4, :], in0=po_ap.rearrange("p (h v) -> p h v", v=128),
                                                                        in1=o1c[:, h0 * 128:(h0 + 4) * 128].rearrange("p (h v) -> p h v", v=128), op=ALU.add),
                                      reads=[pob, o1b], writes=[b_osum[hh] for hh in range(h0, h0 + 4)])
                            scan_group(1, items, po2_p2, sink4)
                        of = osum[:].rearrange("p h v -> p (h v)")
                        S.add("act", lambda e: e.activation(out=sqt[:], in_=of, func=AF.Square), reads=b_osum, writes=[b_sqt])
                        S.add("dve", lambda e: e.tensor_reduce(out=ssq[:], in_=sqt[:].rearrange("p (h v) -> p h v", v=128), axis=AX.X, op=ALU.add), reads=[b_sqt], writes=[b_ssq])
                        S.add("act", lambda e: e.activation(out=ssq[:], in_=ssq[:], func=AF.Ln, scale=1.0 / 128, bias=self.c_eps[:64, :]), reads=[b_ssq, self.b_const], writes=[b_ssq])
                        S.add("act", lambda e: e.activation(out=ssq[:], in_=ssq[:], func=AF.Exp, scale=-0.5), reads=[b_ssq], writes=[b_ssq])
                        S.add("dve", lambda e: e.tensor_tensor(out=sqt[:].rearrange("p (h v) -> p h v", v=128), in0=osum[:], in1=ssq[:].unsqueeze(2).broadcast_to([64, 16, 128]), op=ALU.mult), reads=b_osum + [b_ssq], writes=[b_sqt])
                        S.add("pool", lambda e, gc=gc: e.tensor_tensor(out=obf[:], in0=sqt[:], in1=gc[:], op=ALU.mult), reads=[b_sqt, gcb], writes=[b_obf])
                        for h in range(16):
                            S.add("pe", lambda e, h=h: e.transpose(self.pstr[:, h * 64:(h + 1) * 64], obf[:, h * 128:(h + 1) * 128], ident[:64, :64]), reads=[b_obf, b_hc], writes=[self.b_pstr])
                        S.add("act", lambda e, c=c: e.activation(out=oT[:, :, c * 64:(c + 1) * 64], in_=self.pstr[:, :].rearrange("p (h t) -> p h t", t=64), func=AF.Copy), reads=[self.b_pstr], writes=[b_oT[c]])

                    def x(kc):
                        return oT[:, kc, :T], b_oT
                    self.linear_fm(wo, KC, 16, 4, x, T, self.post_consume(T))
                    self.post(l, 0, tile, False)
                S.barrier()

    def build(self, kinds):
        nc, S = self.nc, self.S
        NT, L, NS, NP, SEQ = self.NT, self.L, self.NS, self.NP, self.SEQ
        NA = sum(1 for k in kinds if k == 0)
        NB_ = sum(1 for k in kinds if k == 1)
        NC_ = sum(1 for k in kinds if k == 2)
        self.inp("xT", [KC, 128, NT])
        self.inp("cT", [128, KC, 2])
        self.inp("ada_w", [L, 24, 128, KC * 512])
        self.inp("ada_bT", [128, L, 96])
        self.inp("norm_gT", [128, L, 4, KC])
        self.inp("mlp_w_in", [L, 16, 128, KC * 512])
        self.inp("mlp_w_out", [L, 16, 128, 64 * 128])
        self.inp("ropeC", [64, NT])
        self.inp("ropeS", [64, NT])
        self.inp("ident", [128, 128])
        if NA:
            self.inp("hg_w_qf", [NA, 12, 128, KC * 512])
            self.inp("hg_w_ig", [NA, 8, 128, KC * 512])
            self.inp("hg_w_out", [NA, 4, 128, KC * 512])
            self.inp("hg_lbT", [128, 2, L, 16])
            self.inp("hg_gbc", [NA, 64, 2048])
            self.inp("hg_s0", [NA, 2, 128, 16, 128])
            self.inp("hg_masks", [64, 2, 64])
            self.outp("hg_stout", [NA, 2, 2, 128, 16, 128])
        if NB_:
            self.inp("mla_w_down", [NB_, 3, 128, KC * 512])
            self.inp("mla_w_uq", [NB_, 3, 128, 4 * 2048])
            self.inp("mla_w_ukvT", [NB_, 128, 4, 4096])
            self.inp("mla_w_out", [NB_, 4, 128, KC * 512])
            self.inp("mla_gT", [NB_, 128, 2, 4])
            self.inp("mla_ckv_ctxT", [128, 4, 256])
            self.inp("mla_kpe_ctxT", [64, 256])
            self.outp("mla_ckvout", [4, 128, NP])
            self.outp("mla_kpeout", [64, NP])
        if NC_:
            self.inp("swa_w_qk", [NC_, 10, 128, KC * 512])
            self.inp("swa_w_v", [NC_, 1, 128, KC * 512])
            self.inp("swa_w_out", [NC_, 8, 64, 32 * 256])
            self.inp("swa_kctxT", [64, 8, 256])
            self.inp("swa_vctx", [256, 512])
            self.inp("swa_masks", [128, 2, 512])
            self.inp("swa_sink_bc", [NC_, 64, 32])
            self.outp("swa_kout", [64, 8, NP])
            self.outp("swa_vout", [NP, 512])
        self.OUTT = self.outp("yT", [KC, 128, NT])
        self.YT = self.din["xT"]
        YS = self.scratch("YS", [KC, 128, NT])
        self.YS = YS
        with ExitStack() as g:
            self.c_eps = self.sb(g, [128, 1], F32, "eps")
            self.ones_bf = self.sb(g, [128, 128], BF16, "ones")
            self.adab = self.sb(g, [128, L, 96], F32, "adab")
            self.ngT = self.sb(g, [128, L, 4, KC], F32, "ngT")
            self.scT = self.sb(g, [128, KC, 2], BF16, "scT")
            cTf = self.sb(g, [128, KC, 2], F32, "cTf")
            self.modraw = self.sb(g, [128, 96, 2], F32, "modraw")
            self.modA = self.sb(g, [128, 2, KC, 2], F32, "modA")
            self.modG = self.sb(g, [128, 2, KC, 2], F32, "modG")
            self.modS = self.sb(g, [128, 2, KC, 2], F32, "modS")
            self.rstd = self.sb(g, [128, TT], F32, "rstd")
            self.ropeC = self.sb(g, [64, TT], F32, "ropeC")
            self.ropeS = self.sb(g, [64, TT], F32, "ropeS")
            self.b_const, self.b_mod, self.b_rstd, self.b_rope = Buf("const"), Buf("mod"), Buf("rstd"), Buf("rope")
            self.wbufs = [(self.sb(g, [128, 8192], BF16, "wb"), Buf(f"wb{i}")) for i in range(2)]
            self.wrot = 0
            self.tmps = [(self.sb(g, [128, TT], F32, "tmp"), Buf(f"tmp{i}")) for i in range(4)]
            self.tmprot = 0
            self.sqs = [(self.sb(g, [128, TT], BF16, "sq"), Buf(f"sq{i}")) for i in range(2)]
            self.sqrot = 0
            psb = [g.enter_context(nc.psum_tensor(f"ps{i}", [128, 512], F32)) for i in range(7)]
            self.pstr = g.enter_context(nc.psum_tensor("pstr", [128, 1024], BF16))
            self.linps = [(psb[i], Buf(f"lin{i}", True)) for i in range(3)]
            self.linrot = 0
            self.statps = (psb[3], Buf("stat", True))
            self.miscps = [(psb[i], Buf(f"misc{i}", True)) for i in range(4, 7)]
            self.b_pstr = Buf("pstr", True)
            self.accps = [(self.miscps[0], self.miscps[1]), (self.miscps[2], self.statps)]
            self.accrot = 0
            S.add("dve", lambda e: e.memset(self.c_eps[:], EPS), writes=[self.b_const])
            S.add("dve", lambda e: e.memset(self.ones_bf[:], 1.0), writes=[self.b_const])
            S.add("sp", lambda e: e.dma_start(out=self.adab[:], in_=self.din["ada_bT"]), writes=[self.b_const], dma_key="c0")
            S.add("sp", lambda e: e.dma_start(out=self.ngT[:], in_=self.din["norm_gT"]), writes=[self.b_const], dma_key="c1")
            S.add("sp", lambda e: e.dma_start(out=cTf[:], in_=self.din["cT"]), writes=[self.b_const], dma_key="c2")
            S.add("act", lambda e: e.activation(out=self.scT[:], in_=cTf[:], func=AF.Silu), reads=[self.b_const], writes=[self.b_const])
            if NA:
                self.hgrn_setup(g)
            cnt = [0, 0, 0]
            for l in range(L):
                self.adaln(l)
                S.barrier()
                kind = kinds[l]
                if kind == 0:
                    self.hgrn_layer(l, cnt[0])
                elif kind == 1:
                    self.mla_layer(l, cnt[1])
                elif kind == 2:
                    self.swa_layer(l, cnt[2])
                if kind >= 0:
                    cnt[kind] += 1
                    self.YT = YS
                with ExitStack() as ph:
                    self.big = self.sb(ph, [128, KC, TT], F32, "big")
                    self.b_big = [Buf(f"big{c}") for c in range(KC)]
                    self.hT = self.sb(ph, [128, KC, TT], BF16, "hT")
                    self.b_hT = [Buf(f"hT{c}") for c in range(KC)]
                    self.aT = self.sb(ph, [128, 64, TT], BF16, "aT")
                    self.b_aT = [Buf(f"aT{c}") for c in range(64)]
                    for tile in self.tiles:
                        self.mlp(l, tile, l == L - 1)
                    S.barrier()
                self.YT = YS
            S.emit()
        return nc


ROPE_BASE = 10000.0


def _partner():
    d = np.arange(64)
    return np.where(d % 32 < 16, d + 16, d - 16)


def _rope_tables(NS, NP, GW):
    d = np.arange(64)
    inv = ROPE_BASE ** (-(d % 16).astype(np.float32) / 16.0)
    t = np.arange(NS)
    pos = np.where(d[:, None] < 32, (t // GW)[None, :], (t % GW)[None, :]).astype(np.float32)
    ang = pos * inv[:, None].astype(np.float32)
    c = np.cos(ang).astype(np.float32)
    sn = np.sin(ang).astype(np.float32)
    sn = np.where((d % 32 < 16)[:, None], -sn, sn)
    c = np.concatenate([c, np.ones((64, NP), np.float32)], axis=1)
    sn = np.concatenate([sn, np.zeros((64, NP), np.float32)], axis=1)
    return np.ascontiguousarray(c, np.float32), np.ascontiguousarray(sn, np.float32)


def _shared_inputs(inp, L, kinds, NS, NP, GW):
    m = {}
    m["ada_w"] = np.stack([tile_w(inp["ada_w"][l], plain_chunks(96), 4) for l in range(L)])
    m["ada_bT"] = np.ascontiguousarray(fm(inp["ada_b"][:L]))
    m["norm_gT"] = np.ascontiguousarray(fm(inp["norm_g"][:L]))
    m["mlp_w_in"] = np.stack([tile_w(inp["mlp_w_in"][l], plain_chunks(64), 4) for l in range(L)])
    m["mlp_w_out"] = np.stack([tile_w(inp["mlp_w_out"][l], plain_chunks(16), 1) for l in range(L)])
    m["ropeC"], m["ropeS"] = _rope_tables(NS, NP, GW)
    m["ident"] = np.eye(128, dtype=np.float32)
    par = _partner()
    NA = sum(1 for k in kinds if k == 0)
    NB_ = sum(1 for k in kinds if k == 1)
    NC_ = sum(1 for k in kinds if k == 2)
    if NA:
        qf, ig, wo, gbc = [], [], [], []
        for j in range(NA):
            W = inp["hgrn_w_in"][j]
            ch = []
            for h in range(16):
                ch += [np.arange(h * 128, (h + 1) * 128), np.arange(2048 + h * 128, 2048 + (h + 1) * 128), np.arange(4096 + h * 128, 4096 + (h + 1) * 128)]
            qf.append(tile_w(W, ch, 4))
            ig.append(tile_w(W, plain_chunks(32, start=6144), 4))
            wo.append(tile_w(inp["hgrn_w_out"][j], plain_chunks(16), 4))
            gbc.append(np.broadcast_to(np.tile(inp["hgrn_norm_g"][j], 16)[None, :], (64, 2048)))
        m["hg_w_qf"], m["hg_w_ig"], m["hg_w_out"] = np.stack(qf), np.stack(ig), np.stack(wo)
        m["hg_gbc"] = np.ascontiguousarray(np.stack(gbc), np.float32)
        lg = inp["hgrn_lb_logits"][:, :L]
        m["hg_lbT"] = np.ascontiguousarray(lg.reshape(2, L, 16, 128).transpose(3, 0, 1, 2), np.float32)
        s_, t_ = np.arange(64)[:, None], np.arange(64)[None, :]
        m["hg_masks"] = np.ascontiguousarray(np.stack([(s_ <= t_), (s_ >= t_)], axis=1).astype(np.float32))
    if NB_:
        wd, wq, wkv, wo, gT = [], [], [], [], []
        for j in range(NB_):
            W = inp["mla_w_down"][j]
            ch = plain_chunks(8) + [np.arange(1024, 1088), 1024 + par]
            wd.append(tile_w(W, ch, 4))
            U = inp["mla_w_uq"][j]
            ch = [np.arange(h * 192, h * 192 + 128) for h in range(16)]
            for h in range(16):
                ch += [h * 192 + 128 + np.arange(64), h * 192 + 128 + par]
            wq.append(tile_w(U, ch, 16))
            wkv.append(inp["mla_w_ukv"][j].reshape(4, 128, 4096).transpose(1, 0, 2))
            wo.append(tile_w(inp["mla_w_out"][j], plain_chunks(16), 4))
            gT.append(np.stack([fm(inp["mla_q_norm_g"][j]), fm(inp["mla_kv_norm_g"][j])], axis=1))
        m["mla_w_down"], m["mla_w_uq"], m["mla_w_out"] = np.stack(wd), np.stack(wq), np.stack(wo)
        m["mla_w_ukvT"] = np.ascontiguousarray(np.stack(wkv), np.float32)
        m["mla_gT"] = np.ascontiguousarray(np.stack(gT), np.float32)
    if NC_:
        wqk, wv, wo, sk = [], [], [], []
        for j in range(NC_):
            W = inp["swa_w_qkv"][j]
            ch = []
            for h in range(32):
                ch += [h * 64 + np.arange(64), h * 64 + par]
            for g_ in range(8):
                ch += [2048 + g_ * 64 + np.arange(64), 2048 + g_ * 64 + par]
            wqk.append(tile_w(W, ch, 8, cw=64))
            wv.append(tile_w(W, plain_chunks(4, start=2560), 4))
            wo.append(tile_w(inp["swa_w_out"][j], plain_chunks(16), 2, kp=64))
            sk.append(np.broadcast_to(inp["swa_sink"][j][None, :], (64, 32)))
        m["swa_w_qk"], m["swa_w_v"], m["swa_w_out"] = np.stack(wqk), np.stack(wv), np.stack(wo)
        m["swa_sink_bc"] = np.ascontiguousarray(np.stack(sk), np.float32)
        c_, a_ = np.arange(128)[:, None], np.arange(128)[None, :]
        m0 = np.tile((c_ >= a_).astype(np.float32), (1, 4))
        m1 = np.tile((c_ <= a_).astype(np.float32), (1, 4))
        m["swa_masks"] = np.ascontiguousarray(np.stack([m0, m1], axis=1))
    return m


def _host_inputs(inp, core, NS, NP, SEQ, kinds):
    b = core % inp["x_sample"].shape[0]
    nps = NP // SEQ
    xs = inp["x_sample"][b, :NS]
    xp = inp["x_prompt"][core * nps:(core + 1) * nps].reshape(NP, D)
    x = np.concatenate([xs, xp], axis=0)
    m = {}
    m["xT"] = np.ascontiguousarray(x.T.reshape(KC, 128, -1))
    cc = np.stack([inp["c"][b], inp["c_ctx"]], axis=-1)
    m["cT"] = np.ascontiguousarray(cc.reshape(KC, 128, 2).transpose(1, 0, 2))
    if 0 in kinds:
        m["hg_s0"] = np.ascontiguousarray(inp["state_hgrn"][b].transpose(0, 1, 3, 2, 4))
    if 1 in kinds:
        m["mla_ckv_ctxT"] = np.ascontiguousarray(inp["cache_mla_ckv"][b, 0].T.reshape(4, 128, -1).transpose(1, 0, 2))
        m["mla_kpe_ctxT"] = np.ascontiguousarray(inp["cache_mla_kpe"][b, 0].T)
    if 2 in kinds:
        m["swa_kctxT"] = np.ascontiguousarray(inp["cache_swa_k"][b, 0].transpose(2, 1, 0))
        m["swa_vctx"] = np.ascontiguousarray(inp["cache_swa_v"][b, 0].reshape(-1, 512))
    return m


def run(inp, NS, NP, SEQ, L, GRID_W, ncores, kinds):
    inp = {k: np.asarray(v) for k, v in inp.items()}
    p = Prog(NS, NP, SEQ, L, GRID_W)
    nc = p.build(kinds)
    shared = _shared_inputs(inp, L, kinds, NS, NP, GRID_W)
    maps = []
    for c in range(ncores):
        m = dict(shared)
        m.update(_host_inputs(inp, c, NS, NP, SEQ, kinds))
        maps.append(m)
    res = run_bass_kernel_spmd(nc, maps, core_ids=list(range(ncores)))
    return res.results


def assemble(r, NS, NP, SEQ, ncores, nsamp, kinds):
    nps = NP // SEQ
    out = {}
    out["ys"] = np.stack([r[c]["yT"].reshape(D, -1)[:, :NS].T for c in range(nsamp)])
    out["yp"] = np.concatenate([r[c]["yT"].reshape(D, -1)[:, NS:].T.reshape(nps, SEQ, D) for c in range(ncores)])
    if 0 in kinds:
        out["st"] = np.concatenate([r[c]["hg_stout"].transpose(1, 0, 2, 4, 3, 5) for c in range(ncores)])
    if 1 in kinds:
        out["ckv"] = np.concatenate([r[c]["mla_ckvout"].reshape(512, NP).T.reshape(nps, 1, SEQ, 512) for c in range(ncores)])
        out["kpe"] = np.concatenate([r[c]["mla_kpeout"].T.reshape(nps, 1, SEQ, 64) for c in range(ncores)])
    if 2 in kinds:
        out["k"] = np.concatenate([r[c]["swa_kout"].transpose(2, 1, 0).reshape(nps, 1, SEQ, 8, 64) for c in range(ncores)])
        out["v"] = np.concatenate([r[c]["swa_vout"].reshape(nps, 1, SEQ, 8, 64) for c in range(ncores)])
    return out


def kernel(**inputs):
    NS, NP, SEQ, L, GW = 4096, 512, 256, 4, 64
    kinds = [0, 1, 2, 0]
    r = run(inputs, NS, NP, SEQ, L, GW, 8, kinds)
    o = assemble(r, NS, NP, SEQ, 8, 4, kinds)
    f = lambda a: np.ascontiguousarray(a, dtype=np.float32)
    return (f(o["yp"]), f(o["ys"]), f(o["st"]), f(o["ckv"]), f(o["kpe"]), f(o["k"]), f(o["v"]))
```

```python
import numpy as np
from contextlib import ExitStack
import concourse.bass as bass
import concourse.mybir as mybir
from concourse.bass_utils import run_bass_kernel_spmd

F32 = mybir.dt.float32
BF16 = mybir.dt.bfloat16
AF = mybir.ActivationFunctionType
ALU = mybir.AluOpType
AX = mybir.AxisListType

SEM_CAP = 20000
D = 2048
KC = 16
TT = 512
EPS = 1e-6


class Buf:
    __slots__ = ("name", "last_w", "readers", "excl")

    def __init__(self, name="", excl=False):
        self.name = name
        self.last_w = None
        self.readers = []
        self.excl = excl


class Op:
    __slots__ = ("eng", "fn", "deps", "signal", "val", "semi", "dma", "dsem", "dval", "idx")

    def __init__(self, eng, fn, dma):
        self.eng = eng
        self.fn = fn
        self.deps = []
        self.signal = False
        self.val = None
        self.semi = None
        self.dma = dma
        self.dsem = None
        self.dval = None


class Sched:
    ENGS = ("pe", "act", "dve", "pool", "sp")

    def __init__(self, nc):
        self.nc = nc
        self.ops = {e: [] for e in self.ENGS}
        self.dma_cnt = {}
        self.dma_last = {}
        self.pending = {e: [] for e in self.ENGS}
        self.n = 0

    def add(self, eng, fn, reads=(), writes=(), dma_key=None):
        op = Op(eng, fn, dma_key is not None)
        op.idx = self.n
        self.n += 1
        deps = set(self.pending[eng])
        self.pending[eng] = []
        if dma_key is not None and dma_key in self.dma_last:
            deps.add(self.dma_last[dma_key])
        for b in reads:
            if b.last_w is not None:
                deps.add(b.last_w)
            if b.excl:
                for r in b.readers:
                    if r.eng != eng:
                        deps.add(r)
        for b in writes:
            if b.last_w is not None:
                deps.add(b.last_w)
            deps.update(b.readers)
        last = {}
        for d in deps:
            if d.dma:
                op.deps.append(d)
            elif d.eng not in last or last[d.eng].idx < d.idx:
                last[d.eng] = d
        for d in last.values():
            if d.eng == "pe" and eng == "pe" and not op.dma:
                continue
            d.signal = True
            op.deps.append(d)
        for b in reads:
            b.readers.append(op)
        for b in writes:
            b.last_w = op
            b.readers = []
        if dma_key is not None:
            c = self.dma_cnt.get(dma_key, 0) + 16
            self.dma_cnt[dma_key] = c
            op.dsem = dma_key
            op.dval = c
            self.dma_last[dma_key] = op
        self.ops[eng].append(op)
        return op

    def barrier(self):
        snap = []
        for e in self.ENGS:
            for op in reversed(self.ops[e]):
                if not op.dma:
                    op.signal = True
                    snap.append(op)
                    break
        snap.extend(self.dma_last.values())
        for e in self.ENGS:
            self.pending[e] = list(snap)

    def emit(self):
        nc = self.nc
        nsem = {}
        for e in self.ENGS:
            c = 0
            for op in self.ops[e]:
                if op.dma or not op.signal:
                    continue
                op.semi = c // SEM_CAP
                op.val = c % SEM_CAP + 1
                c += 1
            nsem[e] = max(1, (c + SEM_CAP - 1) // SEM_CAP)
        with ExitStack() as st:
            esem = {e: [st.enter_context(nc.semaphore(f"s_{e}_{i}")) for i in range(nsem[e])] for e in self.ENGS}
            dsem = {k: st.enter_context(nc.semaphore(f"d_{i}")) for i, k in enumerate(self.dma_cnt)}
            block = st.enter_context(nc.Block())
            handles = {"pe": block.tensor, "act": block.scalar, "dve": block.vector, "pool": block.gpsimd,
                       "sp": block.sync}

            def make(e):
                def body(eng):
                    waited = {}
                    for op in self.ops[e]:
                        need = {}
                        for d in op.deps:
                            if d.dma:
                                k, v = ("d", d.dsem), d.dval
                            else:
                                k, v = ("e", d.eng, d.semi), d.val
                            if need.get(k, 0) < v:
                                need[k] = v
                        for k, v in need.items():
                            if waited.get(k, 0) >= v:
                                continue
                            waited[k] = v
                            eng.wait_ge(dsem[k[1]] if k[0] == "d" else esem[k[1]][k[2]], v)
                        ins = op.fn(eng)
                        if op.dma:
                            ins.then_inc(dsem[op.dsem], 16)
                        elif op.signal:
                            ins.then_inc(esem[e][op.semi], 1)
                    if e == "sp":
                        for k, tot in self.dma_cnt.items():
                            if waited.get(("d", k), 0) < tot:
                                eng.wait_ge(dsem[k], tot)
                return body

            for e in self.ENGS:
                if self.ops[e] or e == "sp":
                    handles[e](make(e))


def tile_w(W, chunks, spc, kp=128, cw=128):
    K = W.shape[0]
    kc = K // kp
    ns = (len(chunks) + spc - 1) // spc
    out = np.zeros((ns, kp, kc, spc * cw), np.float32)
    Wr = W.reshape(kc, kp, W.shape[1])
    for i, cols in enumerate(chunks):
        s, j = divmod(i, spc)
        out[s, :, :, j * cw:j * cw + len(cols)] = Wr[:, :, cols].transpose(1, 0, 2)
    return out.reshape(ns, kp, kc * spc * cw)


def plain_chunks(n, w=128, start=0):
    return [np.arange(start + i * w, start + (i + 1) * w) for i in range(n)]


def fm(vec):
    v = np.asarray(vec, np.float32)
    v = v.reshape(v.shape[:-1] + (v.shape[-1] // 128, 128))
    return np.ascontiguousarray(np.moveaxis(v, -1, 0))


class Prog:
    def __init__(self, NS, NP, SEQ, DEPTH, GRID_W):
        self.NS, self.NP, self.SEQ, self.L, self.GW = NS, NP, SEQ, DEPTH, GRID_W
        self.NT = NS + NP
        self.tiles = [(t0, TT, 0) for t0 in range(0, NS, TT)] + [(NS + t0, TT, 1) for t0 in range(0, NP, TT)]
        self.nc = bass.Bass("TRN2", target_bir_lowering=False)
        self.S = Sched(self.nc)
        self.din = {}
        self.dout = {}
        self.uid = 0
        self.bYd = {}
        self.rope_t0 = None

    def inp(self, name, shape):
        self.din[name] = self.nc.dram_tensor(name, list(shape), F32, kind="ExternalInput").ap()
        return self.din[name]

    def outp(self, name, shape):
        self.dout[name] = self.nc.dram_tensor(name, list(shape), F32, kind="ExternalOutput").ap()
        return self.dout[name]

    def scratch(self, name, shape, dt=F32):
        return self.nc.dram_tensor(name, list(shape), dt).ap()

    def sb(self, st, shape, dt, name=None):
        self.uid += 1
        return st.enter_context(self.nc.sbuf_tensor(f"{name or 't'}_{self.uid}", list(shape), dt))

    def bY(self, c, t0):
        k = (c, t0)
        if k not in self.bYd:
            self.bYd[k] = Buf(f"Y{c}_{t0}")
        return self.bYd[k]

    def key(self, p="k"):
        self.uid += 1
        return f"{p}{self.uid}"

    def linear_fm(self, wd, KCn, nchunks, spc, x, T, consume, widths=None, kp=128, cw=128):
        S = self.S
        ns = (nchunks + spc - 1) // spc
        sc = spc * cw
        wb = self.wbufs

        def load(s):
            t, b = wb[self.wrot % len(wb)]
            self.wrot += 1
            S.add("pool", lambda e, t=t, s=s: e.dma_start(out=t[:kp, :KCn * sc], in_=wd[s]), writes=[b], dma_key=b.name)
            return t, b

        nxt = load(0)
        for s in range(ns):
            t, b = nxt
            if s + 1 < ns:
                nxt = load(s + 1)
            for j in range(min(spc, nchunks - s * spc)):
                ci = s * spc + j
                w = cw if widths is None else widths[ci]
                ps, pb = self.linps[self.linrot % len(self.linps)]
                self.linrot += 1
                for kc in range(KCn):
                    xa, xb = x(kc)
                    S.add("pe", lambda e, ps=ps, t=t, kc=kc, j=j, w=w, xa=xa: e.matmul(
                        ps[:w, :T], lhsT=t[:kp, kc * sc + j * cw: kc * sc + j * cw + w], rhs=xa,
                        start=(kc == 0), stop=(kc == KCn - 1)), reads=[b] + xb, writes=[pb])
                consume(ci, ps[:w, :T], pb)

    def linear_tm(self, wd, KCn, nslabs, x, T, blk, consume):
        S = self.S
        wb = self.wbufs

        def load(s):
            t, b = wb[self.wrot % len(wb)]
            self.wrot += 1
            S.add("pool", lambda e, t=t, s=s: e.dma_start(out=t[:, :KCn * 512], in_=wd[s]), writes=[b], dma_key=b.name)
            return t, b

        nxt = load(0)
        for s in range(nslabs):
            t, b = nxt
            if s + 1 < nslabs:
                nxt = load(s + 1)
            for tb in range(T // blk):
                ps, pb = self.linps[self.linrot % len(self.linps)]
                self.linrot += 1
                for kc in range(KCn):
                    xa, xb = x(kc, tb)
                    S.add("pe", lambda e, ps=ps, t=t, kc=kc, xa=xa: e.matmul(
                        ps[:blk, :512], lhsT=xa, rhs=t[:, kc * 512:(kc + 1) * 512],
                        start=(kc == 0), stop=(kc == KCn - 1)), reads=[b] + xb, writes=[pb])
                consume(s, tb, ps[:blk, :512], pb)

    def rstd_from(self, ps_ap, pb, n, out_t, out_b, parts=128):
        S = self.S
        T = ps_ap.shape[-1]
        S.add("act", lambda e: e.activation(out=out_t, in_=ps_ap, func=AF.Ln, scale=1.0 / n, bias=self.c_eps[:parts, :]),
              reads=[pb, self.b_const], writes=[out_b])
        S.add("act", lambda e: e.activation(out=out_t, in_=out_t, func=AF.Exp, scale=-0.5), reads=[out_b], writes=[out_b])

    def sumsq_acc(self, src_ap, src_bufs, first, last, T, parts=128):
        S = self.S
        sq, sqb = self.sqs[self.sqrot % len(self.sqs)]
        self.sqrot += 1
        S.add("act", lambda e: e.activation(out=sq[:parts, :T], in_=src_ap, func=AF.Square), reads=src_bufs, writes=[sqb])
        st, stb = self.statps
        S.add("pe", lambda e: e.matmul(st[:, :T], lhsT=self.ones_bf[:parts, :], rhs=sq[:parts, :T], start=first, stop=last),
              reads=[sqb, self.b_const], writes=[stb])

    def pre(self, l, k, tile):
        S = self.S
        t0, T, cond = tile
        big, hT, YT = self.big, self.hT, self.YT
        for c in range(KC):
            S.add("sp", lambda e, c=c: e.dma_start(out=big[:, c, :T], in_=YT[c, :, t0:t0 + T]),
                  reads=[self.bY(c, t0)], writes=[self.b_big[c]], dma_key=f"big{c % 4}")
            self.sumsq_acc(big[:, c, :T], [self.b_big[c]], c == 0, c == KC - 1, T)
        self.rstd_from(self.statps[0][:, :T], self.statps[1], D, self.rstd[:, :T], self.b_rstd)
        for c in range(KC):
            tmp, tb = self.tmps[self.tmprot % len(self.tmps)]
            self.tmprot += 1
            S.add("dve", lambda e, c=c, tmp=tmp: e.tensor_tensor(out=tmp[:, :T], in0=big[:, c, :T], in1=self.rstd[:, :T], op=ALU.mult),
                  reads=[self.b_big[c], self.b_rstd], writes=[tb])
            S.add("act", lambda e, c=c, tmp=tmp: e.activation(out=hT[:, c, :T], in_=tmp[:, :T], func=AF.Identity,
                                                               scale=self.modA[:, k, c, cond:cond + 1], bias=self.modS[:, k, c, cond:cond + 1]),
                  reads=[tb, self.b_mod], writes=[self.b_hT[c]])

    def post_consume(self, T):
        S = self.S
        big = self.big

        def consume(ci, ps, pb):
            S.add("act", lambda e: e.activation(out=big[:, ci, :T], in_=ps, func=AF.Copy), reads=[pb], writes=[self.b_big[ci]])
            self.sumsq_acc(big[:, ci, :T], [self.b_big[ci]], ci == 0, ci == KC - 1, T)
        return consume

    def post(self, l, k, tile, final):
        S = self.S
        t0, T, cond = tile
        big, YT = self.big, self.YT
        self.rstd_from(self.statps[0][:, :T], self.statps[1], D, self.rstd[:, :T], self.b_rstd)
        for c in range(KC):
            tmp, tb = self.tmps[self.tmprot % len(self.tmps)]
            self.tmprot += 1
            S.add("sp", lambda e, c=c, tmp=tmp: e.dma_start(out=tmp[:, :T], in_=YT[c, :, t0:t0 + T]), reads=[self.bY(c, t0)], writes=[tb], dma_key=tb.name)
            S.add("dve", lambda e, c=c: e.scalar_tensor_tensor(out=big[:, c, :T], in0=big[:, c, :T], scalar=self.modG[:, k, c, cond:cond + 1],
                                                                in1=self.rstd[:, :T], op0=ALU.mult, op1=ALU.mult),
                  reads=[self.b_big[c], self.b_rstd, self.b_mod], writes=[self.b_big[c]])
            S.add("dve", lambda e, c=c, tmp=tmp: e.tensor_tensor(out=tmp[:, :T], in0=tmp[:, :T], in1=big[:, c, :T], op=ALU.add),
                  reads=[tb, self.b_big[c]], writes=[tb])
            dst = self.OUTT if final else self.YS
            S.add("sp", lambda e, c=c, tmp=tmp, dst=dst: e.dma_start(out=dst[c, :, t0:t0 + T], in_=tmp[:, :T]), reads=[tb], writes=[self.bY(c, t0)],
                  dma_key=tb.name + "s")

    def adaln(self, l):
        S = self.S
        mr = self.modraw

        def x(kc):
            return self.scT[:, kc, :], [self.b_const]

        def consume(ci, ps, pb):
            S.add("dve", lambda e: e.tensor_scalar(out=mr[:, ci, :], in0=ps, scalar1=self.adab[:, l, ci:ci + 1], scalar2=None, op0=ALU.add),
                  reads=[pb, self.b_const], writes=[self.b_mod])
        self.linear_fm(self.din["ada_w"][l], KC, 96, 4, x, 2, consume)
        for k in range(2):
            sh = mr[:, (3 * k) * 16:(3 * k + 1) * 16, :]
            scl = mr[:, (3 * k + 1) * 16:(3 * k + 2) * 16, :]
            gt = mr[:, (3 * k + 2) * 16:(3 * k + 3) * 16, :]
            gpre = self.ngT[:, l, 2 * k, :].unsqueeze(2).broadcast_to([128, 16, 2])
            gpost = self.ngT[:, l, 2 * k + 1, :].unsqueeze(2).broadcast_to([128, 16, 2])
            S.add("dve", lambda e, k=k, scl=scl, gpre=gpre: e.scalar_tensor_tensor(out=self.modA[:, k, :, :], in0=scl, scalar=1.0, in1=gpre, op0=ALU.add, op1=ALU.mult),
                  reads=[self.b_mod, self.b_const], writes=[self.b_mod])
            S.add("dve", lambda e, k=k, gt=gt, gpost=gpost: e.tensor_tensor(out=self.modG[:, k, :, :], in0=gt, in1=gpost, op=ALU.mult),
                  reads=[self.b_mod, self.b_const], writes=[self.b_mod])
            S.add("dve", lambda e, k=k, sh=sh: e.tensor_copy(out=self.modS[:, k, :, :], in_=sh), reads=[self.b_mod], writes=[self.b_mod])

    def mlp(self, l, tile, final):
        S = self.S
        t0, T, cond = tile
        self.pre(l, 1, tile)
        hT, aT = self.hT, self.aT

        def x1(kc):
            return hT[:, kc, :T], [self.b_hT[kc]]

        def c1(ci, ps, pb):
            tmp, tb = self.tmps[self.tmprot % len(self.tmps)]
            self.tmprot += 1
            S.add("act", lambda e: e.activation(out=tmp[:, :T], in_=ps, func=AF.Relu), reads=[pb], writes=[tb])
            S.add("dve", lambda e: e.tensor_tensor(out=aT[:, ci, :T], in0=tmp[:, :T], in1=tmp[:, :T], op=ALU.mult), reads=[tb], writes=[self.b_aT[ci]])
        self.linear_fm(self.din["mlp_w_in"][l], KC, 64, 4, x1, T, c1)

        def x2(kc):
            return aT[:, kc, :T], [self.b_aT[kc]]
        self.linear_fm(self.din["mlp_w_out"][l], 64, 16, 1, x2, T, self.post_consume(T))
        self.post(l, 1, tile, final)


    def attend(self, pieces_q, nkb, kpieces, vblk, Tq, dv, scale, maskf, finish):
        S = self.S
        (o_ps, o_pb), (d_ps, d_pb) = self.accps[self.accrot % 2]
        self.accrot += 1
        for kb in range(nkb):
            ps, pb = self.linps[self.linrot % len(self.linps)]
            self.linrot += 1
            kp = kpieces(kb)
            for i, ((qa, qb), (ka, kbufs)) in enumerate(zip(pieces_q, kp)):
                S.add("pe", lambda e, ps=ps, ka=ka, qa=qa, i=i: e.matmul(ps[:, :Tq], lhsT=ka, rhs=qa, start=(i == 0), stop=(i == len(kp) - 1)),
                      reads=qb + kbufs, writes=[pb])
            pt, ptb = self.pts[self.ptrot % len(self.pts)]
            self.ptrot += 1
            S.add("act", lambda e, ps=ps, pt=pt: e.activation(out=pt[:, :Tq], in_=ps[:, :Tq], func=AF.Exp, scale=scale), reads=[pb], writes=[ptb])
            m = maskf(kb) if maskf is not None else None
            if m is not None:
                ma, mb = m
                S.add("pool", lambda e, pt=pt, ma=ma: e.tensor_tensor(out=pt[:, :Tq], in0=pt[:, :Tq], in1=ma, op=ALU.mult), reads=[ptb] + mb, writes=[ptb])
            va, vb = vblk(kb)
            S.add("pe", lambda e, pt=pt, va=va, kb=kb: e.matmul(o_ps[:dv, :Tq], lhsT=va, rhs=pt[:, :Tq], start=(kb == 0), stop=(kb == nkb - 1)),
                  reads=[ptb] + vb, writes=[o_pb])
            S.add("pe", lambda e, pt=pt, kb=kb: e.matmul(d_ps[:dv, :Tq], lhsT=self.ones_bf[:, :dv], rhs=pt[:, :Tq], start=(kb == 0), stop=(kb == nkb - 1)),
                  reads=[ptb, self.b_const], writes=[d_pb])
        finish(o_ps[:dv, :Tq], o_pb, d_ps[:dv, :Tq], d_pb)

    def attn_finish(self, out_ap, out_bufs, sink_ap=None, then=None):
        S = self

        def fin(o_ps, o_pb, d_ps, d_pb):
            rd, rdb = self.tmps[self.tmprot % len(self.tmps)]
            self.tmprot += 1
            p, T = o_ps.shape[0], o_ps.shape[-1]
            rda = rd[:p, :T]
            if sink_ap is not None:
                self.S.add("dve", lambda e: e.tensor_tensor(out=rda.rearrange("p (g t) -> p g t", g=4), in0=d_ps.rearrange("p (g t) -> p g t", g=4), in1=sink_ap, op=ALU.add),
                           reads=[d_pb, self.b_const], writes=[rdb])
                self.S.add("dve", lambda e: e.reciprocal(out=rda, in_=rda), reads=[rdb], writes=[rdb])
            else:
                self.S.add("dve", lambda e: e.reciprocal(out=rda, in_=d_ps), reads=[d_pb], writes=[rdb])
            self.S.add("dve", lambda e: e.tensor_tensor(out=out_ap, in0=o_ps, in1=rda, op=ALU.mult), reads=[o_pb, rdb], writes=out_bufs)
            if then is not None:
                then()
        return fin

    def rope_pair(self, t0, T, dst_fn):
        S = self.S
        st = {}
        if self.rope_t0 != t0:
            self.rope_t0 = t0
            S.add("sp", lambda e: e.dma_start(out=self.ropeC[:, :T], in_=self.din["ropeC"][:, t0:t0 + T]), writes=[self.b_rope], dma_key="c3")
            S.add("sp", lambda e: e.dma_start(out=self.ropeS[:, :T], in_=self.din["ropeS"][:, t0:t0 + T]), writes=[self.b_rope], dma_key="c4")

        def consume(ci, ps, pb):
            if ci % 2 == 0:
                tmp, tb = self.tmps[self.tmprot % len(self.tmps)]
                self.tmprot += 1
                st["a"] = (tmp, tb)
                S.add("dve", lambda e: e.tensor_tensor(out=tmp[:64, :T], in0=ps, in1=self.ropeC[:, :T], op=ALU.mult), reads=[pb, self.b_rope], writes=[tb])
            else:
                tmp, tb = st["a"]
                t2, tb2 = self.tmps[self.tmprot % len(self.tmps)]
                self.tmprot += 1
                S.add("dve", lambda e: e.tensor_tensor(out=t2[:64, :T], in0=ps, in1=self.ropeS[:, :T], op=ALU.mult), reads=[pb, self.b_rope], writes=[tb2])
                da, db = dst_fn(ci // 2)
                S.add("pool", lambda e: e.tensor_tensor(out=da, in0=tmp[:64, :T], in1=t2[:64, :T], op=ALU.add), reads=[tb, tb2], writes=db)
        return consume

    def swa_layer(self, l, j):
        nc, S = self.nc, self.S
        NT, NS, NP, SEQ = self.NT, self.NS, self.NP, self.SEQ
        QT = self.scratch(f"swaQ{l}", [64, 32, NT], BF16)
        KT = self.scratch(f"swaK{l}", [64, 8, NT], BF16)
        VV = self.scratch(f"swaV{l}", [NT, 512], BF16)
        OT = self.scratch(f"swaO{l}", [64, 32, NT], BF16)
        bQ, bK, bV, bO = {}, {}, {}, {}

        def gb(d, k):
            if k not in d:
                d[k] = Buf()
            return d[k]
        wq = self.din["swa_w_qk"][j]
        wv = self.din["swa_w_v"][j]
        wo = self.din["swa_w_out"][j]
        with ExitStack() as ph:
            self.big = self.sb(ph, [128, KC, TT], F32, "big")
            self.b_big = [Buf() for c in range(KC)]
            self.hT = self.sb(ph, [128, KC, TT], BF16, "hT")
            self.b_hT = [Buf() for c in range(KC)]
            hT = self.hT
            qst = self.sb(ph, [64, 40, TT], BF16, "qst")
            b_qst = [Buf() for _ in range(40)]
            vst = self.sb(ph, [128, 4, 512], BF16, "vst")
            b_vst = [Buf() for _ in range(4)]
            kf = self.sb(ph, [64, 8, TT], F32, "kf")
            vf = self.sb(ph, [128, 4, 512], F32, "vf")
            b_kf, b_vf = Buf(), Buf()
            for tile in self.tiles:
                t0, T, cond = tile
                self.pre(l, 0, tile)

                def x(kc):
                    return hT[:, kc, :T], [self.b_hT[kc]]
                rp = self.rope_pair(t0, T, lambda h: (qst[:, h, :T], [b_qst[h]]))

                def cons(ci, ps, pb, rp=rp, cond=cond, T=T):
                    rp(ci, ps, pb)
                    if cond == 1 and ci >= 64 and ci % 2 == 0:
                        S.add("act", lambda e: e.activation(out=kf[:, (ci - 64) // 2, :T], in_=ps, func=AF.Copy), reads=[pb], writes=[b_kf])
                self.linear_fm(wq, KC, 80, 8, x, T, cons, kp=128, cw=64)
                S.add("sp", lambda e, t0=t0, T=T: e.dma_start(out=QT[:, :, t0:t0 + T], in_=qst[:, 0:32, :T]), reads=b_qst[:32], writes=[gb(bQ, t0)], dma_key="swq")
                S.add("sp", lambda e, t0=t0, T=T: e.dma_start(out=KT[:, :, t0:t0 + T], in_=qst[:, 32:40, :T]), reads=b_qst[32:], writes=[gb(bK, t0)], dma_key="swk")

                def xv(kc, tb):
                    return hT[:, kc, tb * 128:(tb + 1) * 128], [self.b_hT[kc]]

                def consv(s_, tb, ps, pb, cond=cond):
                    S.add("act", lambda e: e.activation(out=vst[:, tb, :], in_=ps, func=AF.Copy), reads=[pb], writes=[b_vst[tb]])
                    if cond == 1:
                        S.add("dve", lambda e: e.tensor_copy(out=vf[:, tb, :], in_=ps), reads=[pb], writes=[b_vf])
                self.linear_tm(wv, KC, 1, xv, T, 128, consv)
                S.add("sp", lambda e, t0=t0, T=T: e.dma_start(out=VV[t0:t0 + T, :].rearrange("(b p) n -> p b n", p=128), in_=vst[:, :, :]), reads=b_vst, writes=[gb(bV, t0)], dma_key="swv")
                if cond == 1:
                    p0 = t0 - NS
                    S.add("sp", lambda e, p0=p0, T=T: e.dma_start(out=self.dout["swa_kout"][:, :, p0:p0 + T], in_=kf[:, :, :T]), reads=[b_kf], writes=[Buf()], dma_key="swko")
                    S.add("sp", lambda e, p0=p0, T=T: e.dma_start(out=self.dout["swa_vout"][p0:p0 + T, :].rearrange("(b p) n -> p b n", p=128), in_=vf[:, :, :]), reads=[b_vf], writes=[Buf()], dma_key="swvo")
            S.barrier()
        with ExitStack() as ph:
            self.pts = [(self.sb(ph, [128, TT], BF16, "pt"), Buf()) for _ in range(3)]
            self.ptrot = 0
            kctx = self.sb(ph, [64, 8, 256], BF16, "kctx")
            vctx = self.sb(ph, [128, 2, 512], BF16, "vctx")
            sinkr = self.sb(ph, [64, 32], F32, "sinkr")
            sinke = self.sb(ph, [64, 32], F32, "sinke")
            msk = self.sb(ph, [128, 2, 512], BF16, "msk")
            b_ctx = Buf()
            S.add("pool", lambda e: e.dma_start(out=kctx[:], in_=self.din["swa_kctxT"]), writes=[b_ctx], dma_key="sk1")
            S.add("pool", lambda e: e.dma_start(out=vctx[:], in_=self.din["swa_vctx"].rearrange("(b p) n -> p b n", p=128)), writes=[b_ctx], dma_key="sk2")
            S.add("pool", lambda e: e.dma_start(out=msk[:], in_=self.din["swa_masks"]), writes=[b_ctx], dma_key="sk3")
            S.add("sp", lambda e: e.dma_start(out=sinkr[:], in_=self.din["swa_sink_bc"][j]), writes=[b_ctx], dma_key="sk4")
            S.add("act", lambda e: e.activation(out=sinke[:], in_=sinkr[:], func=AF.Exp), reads=[b_ctx], writes=[b_ctx])
            NB = 2
            qb_ = [(self.sb(ph, [64, 32, 128], BF16, "qb"), Buf()) for _ in range(NB)]
            kl_ = [(self.sb(ph, [64, 8, 384], BF16, "kl"), Buf()) for _ in range(NB)]
            vl_ = [(self.sb(ph, [128, 3, 512], BF16, "vl"), Buf()) for _ in range(NB)]
            ob_ = [(self.sb(ph, [64, 32, 128], BF16, "ob"), Buf()) for _ in range(NB)]
            blocks = [(jb * 128, 0, NS, True) for jb in range(NS // 128)]
            for s_ in range(NP // SEQ):
                blocks += [(NS + s_ * SEQ + jb * 128, NS + s_ * SEQ, NS + (s_ + 1) * SEQ, False) for jb in range(SEQ // 128)]
            scale = 64 ** -0.5
            for bi, (q0, lo, hi, is_s) in enumerate(blocks):
                (qb, qbb), (kl, klb), (vl, vlb), (ob, obb) = qb_[bi % NB], kl_[bi % NB], vl_[bi % NB], ob_[bi % NB]
                tq = (q0 // TT) * TT
                S.add("sp", lambda e, qb=qb, q0=q0: e.dma_start(out=qb[:], in_=QT[:, :, q0:q0 + 128]), reads=[gb(bQ, tq)], writes=[qbb], dma_key="lq%d" % (bi % NB))
                if is_s:
                    kbs = [x_ for x_ in (q0 - 128, q0, q0 + 128) if lo <= x_ < hi]
                else:
                    kbs = list(range(lo, hi, 128))
                for i, k0 in enumerate(kbs):
                    tk = (k0 // TT) * TT
                    S.add("sp", lambda e, kl=kl, i=i, k0=k0: e.dma_start(out=kl[:, :, i * 128:(i + 1) * 128], in_=KT[:, :, k0:k0 + 128]), reads=[gb(bK, tk)], writes=[klb], dma_key="lk%d" % (bi % NB))
                    S.add("sp", lambda e, vl=vl, i=i, k0=k0: e.dma_start(out=vl[:, i, :], in_=VV[k0:k0 + 128, :]), reads=[gb(bV, tk)], writes=[vlb], dma_key="lv%d" % (bi % NB))
                nctx = 2 if is_s else 0
                for g in range(8):
                    def kpieces(kb, g=g, kl=kl, klb=klb):
                        if kb < nctx:
                            return [(kctx[:, g, kb * 128:(kb + 1) * 128], [b_ctx])]
                        i = kb - nctx
                        return [(kl[:, g, i * 128:(i + 1) * 128], [klb])]

                    def vblk(kb, g=g, vl=vl, vlb=vlb):
                        if kb < nctx:
                            return vctx[:, kb, g * 64:(g + 1) * 64], [b_ctx]
                        return vl[:, kb - nctx, g * 64:(g + 1) * 64], [vlb]

                    def maskf(kb, kbs=kbs, q0=q0):
                        if kb < nctx or not is_s:
                            return None
                        k0 = kbs[kb - nctx]
                        if k0 < q0:
                            return msk[:, 0, :], [b_ctx]
                        if k0 > q0:
                            return msk[:, 1, :], [b_ctx]
                        return None
                    qa = qb[:, 4 * g:4 * g + 4, :]
                    oa = ob[:, 4 * g:4 * g + 4, :].rearrange("p g t -> p (g t)")
                    sk = sinke[:, 4 * g:4 * g + 4].unsqueeze(2).broadcast_to([64, 4, 128])
                    self.attend([(qa, [qbb])], nctx + len(kbs), kpieces, vblk, 512, 64, scale, maskf, self.attn_finish(oa, [obb], sink_ap=sk))
                S.add("sp", lambda e, ob=ob, q0=q0: e.dma_start(out=OT[:, :, q0:q0 + 128], in_=ob[:]), reads=[obb], writes=[gb(bO, tq)], dma_key="so%d" % (bi % NB))
            S.barrier()
        with ExitStack() as ph:
            self.big = self.sb(ph, [128, KC, TT], F32, "big")
            self.b_big = [Buf() for c in range(KC)]
            oT = self.sb(ph, [64, 32, TT], BF16, "oTt")
            b_oT = Buf()
            for tile in self.tiles:
                t0, T, cond = tile
                S.add("sp", lambda e, t0=t0, T=T: e.dma_start(out=oT[:, :, :T], in_=OT[:, :, t0:t0 + T]), reads=[gb(bO, t0)], writes=[b_oT], dma_key="swo")

                def x(kc):
                    return oT[:, kc, :T], [b_oT]
                self.linear_fm(wo, 32, 16, 2, x, T, self.post_consume(T), kp=64)
                self.post(l, 0, tile, False)
            S.barrier()


    def mla_layer(self, l, j):
        nc, S = self.nc, self.S
        NT, NS, NP, SEQ = self.NT, self.NS, self.NP, self.SEQ
        NKEY = 256 + NT
        QN = self.scratch(f"mlaQN{l}", [16, 128, NT], BF16)
        QR = self.scratch(f"mlaQR{l}", [16, 64, NT], BF16)
        OT = self.scratch(f"mlaO{l}", [128, 16, NT], BF16)
        bQ, bO = {}, {}

        def gb(d, k):
            if k not in d:
                d[k] = Buf()
            return d[k]
        wdn = self.din["mla_w_down"][j]
        wuq = self.din["mla_w_uq"][j]
        wo = self.din["mla_w_out"][j]
        with ExitStack() as allph:
            ckv_all = self.sb(allph, [128, 4, NKEY], BF16, "ckvall")
            kpe_all = self.sb(allph, [64, NKEY], BF16, "kpeall")
            b_ckv = [Buf() for _ in range((NKEY + TT - 1) // TT + 1)]
            gq = self.sb(allph, [128, 2, 4], F32, "gq")
            b_g = Buf()
            S.add("sp", lambda e: e.dma_start(out=gq[:], in_=self.din["mla_gT"][j]), writes=[b_g], dma_key="mg")
            S.add("pool", lambda e: e.dma_start(out=ckv_all[:, :, 0:256], in_=self.din["mla_ckv_ctxT"]), writes=[b_ckv[0]], dma_key="mc1")
            S.add("pool", lambda e: e.dma_start(out=kpe_all[:, 0:256], in_=self.din["mla_kpe_ctxT"]), writes=[b_ckv[0]], dma_key="mc2")
            with ExitStack() as ph:
                self.big = self.sb(ph, [128, KC, TT], F32, "big")
                self.b_big = [Buf() for c in range(KC)]
                self.hT = self.sb(ph, [128, KC, TT], BF16, "hT")
                self.b_hT = [Buf() for c in range(KC)]
                hT = self.hT
                cf = self.sb(ph, [128, 8, TT], F32, "cf")
                b_cf = [Buf() for _ in range(8)]
                cqn = self.sb(ph, [128, 4, TT], BF16, "cqn")
                b_cqn = [Buf() for _ in range(4)]
                kpf = self.sb(ph, [64, TT], F32, "kpf")
                b_kpf = Buf()
                qn = self.sb(ph, [128, 16, TT], BF16, "qn")
                qr = self.sb(ph, [64, 16, TT], BF16, "qr")
                b_qn = [Buf() for _ in range(16)]
                b_qr = [Buf() for _ in range(16)]
                r2 = self.sb(ph, [128, TT], F32, "r2")
                b_r2 = Buf()
                for ti, tile in enumerate(self.tiles):
                    t0, T, cond = tile
                    k0 = 256 + t0
                    bk = b_ckv[1 + ti]
                    self.pre(l, 0, tile)

                    def x(kc):
                        return hT[:, kc, :T], [self.b_hT[kc]]
                    rp = self.rope_pair(t0, T, lambda h: (kpe_all[:, k0:k0 + T], [bk]))
                    st2 = self.miscps[0]

                    def cons(ci, ps, pb, T=T, rp=rp, cond=cond):
                        if ci < 8:
                            S.add("act", lambda e: e.activation(out=cf[:, ci, :T], in_=ps, func=AF.Copy), reads=[pb], writes=[b_cf[ci]])
                            sq, sqb = self.sqs[self.sqrot % len(self.sqs)]
                            self.sqrot += 1
                            S.add("act", lambda e: e.activation(out=sq[:, :T], in_=cf[:, ci, :T], func=AF.Square), reads=[b_cf[ci]], writes=[sqb])
                            st, stb = self.statps if ci < 4 else st2
                            S.add("pe", lambda e: e.matmul(st[:, :T], lhsT=self.ones_bf[:, :], rhs=sq[:, :T], start=(ci % 4 == 0), stop=(ci % 4 == 3)),
                                  reads=[sqb, self.b_const], writes=[stb])
                        else:
                            rp(ci - 8, ps, pb)
                            if ci == 8 and cond == 1:
                                S.add("act", lambda e: e.activation(out=kpf[:, :T], in_=ps, func=AF.Copy), reads=[pb], writes=[b_kpf])
                    self.linear_fm(wdn, KC, 10, 4, x, T, cons, widths=[128] * 8 + [64, 64])
                    self.rstd_from(self.statps[0][:, :T], self.statps[1], 512, self.rstd[:, :T], self.b_rstd)
                    self.rstd_from(st2[0][:, :T], st2[1], 512, r2[:, :T], b_r2)
                    for c in range(4):
                        S.add("dve", lambda e, c=c, T=T: e.scalar_tensor_tensor(out=cqn[:, c, :T], in0=cf[:, c, :T], scalar=gq[:, 0, c:c + 1], in1=self.rstd[:, :T], op0=ALU.mult, op1=ALU.mult),
                              reads=[b_cf[c], b_g, self.b_rstd], writes=[b_cqn[c]])
                        S.add("dve", lambda e, c=c, T=T: e.scalar_tensor_tensor(out=cf[:, 4 + c, :T], in0=cf[:, 4 + c, :T], scalar=gq[:, 1, c:c + 1], in1=r2[:, :T], op0=ALU.mult, op1=ALU.mult),
                              reads=[b_cf[4 + c], b_g, b_r2], writes=[b_cf[4 + c]])
                        S.add("pool", lambda e, c=c, k0=k0, T=T: e.tensor_copy(out=ckv_all[:, c, k0:k0 + T], in_=cf[:, 4 + c, :T]), reads=[b_cf[4 + c]], writes=[bk])
                    if cond == 1:
                        p0 = t0 - NS
                        S.add("sp", lambda e, p0=p0, T=T: e.dma_start(out=self.dout["mla_ckvout"][:, :, p0:p0 + T].rearrange("c p t -> p c t"), in_=cf[:, 4:8, :T]),
                              reads=b_cf[4:8], writes=[Buf()], dma_key="mco")
                        S.add("sp", lambda e, p0=p0, T=T: e.dma_start(out=self.dout["mla_kpeout"][:, p0:p0 + T], in_=kpf[:, :T]), reads=[b_kpf], writes=[Buf()], dma_key="mko")

                    def xq(kc):
                        return cqn[:, kc, :T], [b_cqn[kc]]
                    rq = self.rope_pair(t0, T, lambda h: (qr[:, h, :T], [b_qr[h]]))

                    def consq(ci, ps, pb, T=T, rq=rq):
                        if ci < 16:
                            S.add("act", lambda e: e.activation(out=qn[:, ci, :T], in_=ps, func=AF.Copy), reads=[pb], writes=[b_qn[ci]])
                        else:
                            rq(ci - 16, ps, pb)
                    self.linear_fm(wuq, 4, 48, 16, xq, T, consq, widths=[128] * 16 + [64] * 32)
                    S.add("sp", lambda e, t0=t0, T=T: e.dma_start(out=QN[:, :, t0:t0 + T].rearrange("h p t -> p h t"), in_=qn[:, :, :T]), reads=b_qn, writes=[gb(bQ, t0)], dma_key="mqn")
                    S.add("sp", lambda e, t0=t0, T=T: e.dma_start(out=QR[:, :, t0:t0 + T].rearrange("h p t -> p h t"), in_=qr[:, :, :T]), reads=b_qr, writes=[gb(bQ, t0)], dma_key="mqr")
                S.barrier()
            with ExitStack() as ph:
                self.pts = [(self.sb(ph, [128, TT], BF16, "pt"), Buf()) for _ in range(3)]
                self.ptrot = 0
                wkv = self.sb(ph, [128, 4, 4096], BF16, "wkv")
                b_wkv = Buf()
                S.add("pool", lambda e: e.dma_start(out=wkv[:], in_=self.din["mla_w_ukvT"][j]), writes=[b_wkv], dma_key="mwkv")
                NB = 2
                kth_ = [(self.sb(ph, [128, NKEY], BF16, "kth"), Buf()) for _ in range(NB)]
                vh_ = [(self.sb(ph, [128, NKEY // 128, 128], BF16, "vh"), Buf()) for _ in range(NB)]
                qnh_ = [(self.sb(ph, [128, NT], BF16, "qnh"), Buf())] * NB
                qrh_ = [(self.sb(ph, [64, NT], BF16, "qrh"), Buf())] * NB
                oh_ = [(self.sb(ph, [128, TT], BF16, "oh"), Buf()) for _ in range(3)]
                orot = 0
                allck = b_ckv
                scale = 192 ** -0.5
                for h in range(16):
                    (kth, kthb), (vh, vhb), (qnh, qnhb), (qrh, qrhb) = kth_[h % NB], vh_[h % NB], qnh_[h % NB], qrh_[h % NB]
                    S.add("sp", lambda e, qnh=qnh, h=h: e.dma_start(out=qnh[:], in_=QN[h]), reads=list(bQ.values()), writes=[qnhb], dma_key="mlq")
                    S.add("sp", lambda e, qrh=qrh, h=h: e.dma_start(out=qrh[:], in_=QR[h]), reads=list(bQ.values()), writes=[qrhb], dma_key="mlr")
                    for kt in range(0, NKEY, TT):
                        w = min(TT, NKEY - kt)
                        ps, pb = self.linps[self.linrot % len(self.linps)]
                        self.linrot += 1
                        for kc in range(4):
                            S.add("pe", lambda e, ps=ps, kc=kc, kt=kt, w=w, h=h: e.matmul(ps[:, :w], lhsT=wkv[:, kc, h * 256:h * 256 + 128], rhs=ckv_all[:, kc, kt:kt + w], start=(kc == 0), stop=(kc == 3)),
                                  reads=[b_wkv] + allck, writes=[pb])
                        S.add("act", lambda e, ps=ps, kt=kt, w=w, kth=kth: e.activation(out=kth[:, kt:kt + w], in_=ps[:, :w], func=AF.Copy), reads=[pb], writes=[kthb])
                    for kb4 in range(0, NKEY // 128, 4):
                        nb4 = min(4, NKEY // 128 - kb4)
                        ps, pb = self.linps[self.linrot % len(self.linps)]
                        self.linrot += 1
                        for i in range(nb4):
                            kb = kb4 + i
                            for kc in range(4):
                                S.add("pe", lambda e, ps=ps, kc=kc, kb=kb, i=i, h=h: e.matmul(ps[:, i * 128:(i + 1) * 128], lhsT=ckv_all[:, kc, kb * 128:(kb + 1) * 128], rhs=wkv[:, kc, h * 256 + 128:h * 256 + 256], start=(kc == 0), stop=(kc == 3)),
                                      reads=[b_wkv] + allck, writes=[pb])
                        S.add("dve", lambda e, ps=ps, kb4=kb4, nb4=nb4, vh=vh: e.tensor_copy(out=vh[:, kb4:kb4 + nb4, :].rearrange("p b d -> p (b d)"), in_=ps[:, :nb4 * 128]), reads=[pb], writes=[vhb])
                    qts = [(t0, TT, 0, (256 + NS) // 128) for t0 in range(0, NS, TT)]
                    for s_ in range(NP // SEQ):
                        qts.append((NS + s_ * SEQ, SEQ, (256 + NS + s_ * SEQ) // 128, SEQ // 128))
                    for (q0, Tq, kb0, nkb) in qts:
                        oh, ohb = oh_[orot % 3]
                        orot += 1

                        def kpieces(kb, kb0=kb0, kth=kth, kthb=kthb):
                            a = (kb0 + kb) * 128
                            return [(kth[:, a:a + 128], [kthb]), (kpe_all[:, a:a + 128], allck)]

                        def vblk(kb, kb0=kb0, vh=vh, vhb=vhb):
                            return vh[:, kb0 + kb, :], [vhb]
                        tq = (q0 // TT) * TT
                        self.attend([(qnh[:, q0:q0 + Tq], [qnhb]), (qrh[:, q0:q0 + Tq], [qrhb])], nkb, kpieces, vblk, Tq, 128, scale, None,
                                    self.attn_finish(oh[:, :Tq], [ohb]))
                        S.add("sp", lambda e, oh=oh, h=h, q0=q0, Tq=Tq: e.dma_start(out=OT[:, h, q0:q0 + Tq], in_=oh[:, :Tq]), reads=[ohb], writes=[gb(bO, (tq, h, q0))], dma_key="mo%d" % (orot % 3))
                S.barrier()
        with ExitStack() as ph:
            self.big = self.sb(ph, [128, KC, TT], F32, "big")
            self.b_big = [Buf() for c in range(KC)]
            oT = self.sb(ph, [128, 16, TT], BF16, "oTt")
            b_oT = Buf()
            for tile in self.tiles:
                t0, T, cond = tile
                S.add("sp", lambda e, t0=t0, T=T: e.dma_start(out=oT[:, :, :T], in_=OT[:, :, t0:t0 + T]), reads=[b for k_, b in bO.items() if k_[0] == t0], writes=[b_oT], dma_key="mlo")

                def x(kc):
                    return oT[:, kc, :T], [b_oT]
                self.linear_fm(wo, KC, 16, 4, x, T, self.post_consume(T))
                self.post(l, 0, tile, False)
            S.barrier()


    def hgrn_setup(self, g):
        S, L = self.S, self.L
        lg = self.sb(g, [128, 2, L, 16], F32, "lbl")
        self.lb = self.sb(g, [128, 2, L, 16], F32, "lb")
        self.oml = self.sb(g, [128, 2, L, 16], F32, "oml")
        sm = self.sb(g, [128, 2, 16], F32, "lbs")
        self.b_lb = Buf()
        b = self.b_lb
        S.add("sp", lambda e: e.dma_start(out=lg[:], in_=self.din["hg_lbT"]), writes=[b], dma_key="hlb")
        S.add("act", lambda e: e.activation(out=lg[:], in_=lg[:], func=AF.Exp), reads=[b], writes=[b])
        S.add("dve", lambda e: e.tensor_copy(out=sm[:], in_=lg[:, :, 0, :]), reads=[b], writes=[b])
        for i in range(1, L):
            S.add("dve", lambda e, i=i: e.tensor_tensor(out=sm[:], in0=sm[:], in1=lg[:, :, i, :], op=ALU.add), reads=[b], writes=[b])
        S.add("dve", lambda e: e.reciprocal(out=sm[:], in_=sm[:]), reads=[b], writes=[b])
        S.add("dve", lambda e: e.memset(self.lb[:, :, 0, :], 0.0), writes=[b])
        for i in range(1, L):
            S.add("dve", lambda e, i=i: e.tensor_tensor(out=lg[:, :, i, :], in0=lg[:, :, i, :], in1=sm[:], op=ALU.mult), reads=[b], writes=[b])
            S.add("dve", lambda e, i=i: e.tensor_tensor(out=self.lb[:, :, i, :], in0=self.lb[:, :, i - 1, :], in1=lg[:, :, i, :], op=ALU.add), reads=[b], writes=[b])
        S.add("dve", lambda e: e.tensor_scalar(out=self.oml[:], in0=self.lb[:], scalar1=-1.0, scalar2=1.0, op0=ALU.mult, op1=ALU.add), reads=[b], writes=[b])

    def hgrn_layer(self, l, j):
        nc, S = self.nc, self.S
        NT, NS, NP, SEQ = self.NT, self.NS, self.NP, self.SEQ
        NCH = NT // 64
        Q2 = self.scratch(f"hgQ2{l}", [16, 128, NT], BF16)
        K2 = self.scratch(f"hgK2{l}", [16, 128, NT], BF16)
        D2 = self.scratch(f"hgD2{l}", [16, 128, NCH, 3], F32)
        V64 = self.scratch(f"hgV{l}", [NT, 2048], BF16)
        GS = self.scratch(f"hgG{l}", [NT, 2048], BF16)
        O1 = self.scratch(f"hgO1{l}", [NT, 2048], F32)
        bsc = {}

        def gb(k):
            if k not in bsc:
                bsc[k] = Buf()
            return bsc[k]
        wqf = self.din["hg_w_qf"][j]
        wig = self.din["hg_w_ig"][j]
        wo = self.din["hg_w_out"][j]
        lb, oml = self.lb, self.oml
        with ExitStack() as allph:
            Sst = [self.sb(allph, [128, 16, 128], F32, "Sst") for _ in range(2)]
            b_S = [[Buf() for _ in range(16)] for _ in range(2)]
            ident = self.sb(allph, [128, 128], BF16, "ident")
            onesf = self.sb(allph, [128, TT], F32, "onesf")
            hgbc = self.sb(allph, [64, 2048], F32, "hgbc")
            b_hc = Buf()
            S.add("pool", lambda e: e.dma_start(out=ident[:], in_=self.din["ident"]), writes=[b_hc], dma_key="hm2")
            S.add("sp", lambda e: e.dma_start(out=hgbc[:], in_=self.din["hg_gbc"][j]), writes=[b_hc], dma_key="hm3")
            S.add("dve", lambda e: e.memset(onesf[:], 1.0), writes=[b_hc])
            sbf_ = [(self.sb(allph, [128, 128], BF16, "sbf"), Buf()) for _ in range(4)]
            t1_ = [(self.sb(allph, [128, 128], F32, "t1"), Buf()) for _ in range(3)]
            am8 = self.sb(allph, [64, 512], BF16, "am8")
            ktok8 = self.sb(allph, [64, 8, 128], BF16, "ktok8")
            b_am8, b_ktok8 = Buf(), Buf()
            hmaskf = self.sb(allph, [64, 2, 64], F32, "hmaskf")
            S.add("sp", lambda e: e.dma_start(out=hmaskf[:], in_=self.din["hg_masks"]), writes=[b_hc], dma_key="hm4")
            rot = {"sbf": 0, "t1": 0}
            par = [0] * 16
            pa8, b_pa8 = self.miscps[0]
            pk8 = [self.miscps[1], self.miscps[2]]

            def nxt(lst, k):
                r = lst[rot[k] % len(lst)]
                rot[k] += 1
                return r

            def mk_memset(h):
                def f():
                    st_, b_ = Sst[par[h]], b_S[par[h]][h]
                    S.add("pool", lambda e: e.memset(st_[:, h, :], 0.0), writes=[b_])
                return f

            def mk_stout(h, sq_, d):
                def f():
                    st_, b_ = Sst[par[h]], b_S[par[h]][h]
                    S.add("sp", lambda e: e.dma_start(out=self.dout["hg_stout"][j, sq_, d, :, h, :], in_=st_[:, h, :]), reads=[b_], writes=[Buf()], dma_key="hso%d" % (h % 4))
                return f

            def scan_group(d, items, po2, sink4):
                def amm(i, it):
                    S.add("pe", lambda e: e.matmul(pa8[:64, i * 64:(i + 1) * 64], lhsT=it["kt"], rhs=it["qt"], start=True, stop=True),
                          reads=it["qtb"] + it["ktb"], writes=[b_pa8])
                for i, it in enumerate(items):
                    amm(i, it)
                S.add("dve", lambda e: e.tensor_scalar(out=am8[:], in0=pa8[:64, :512], scalar1=1e30, scalar2=-1e30, op0=ALU.min, op1=ALU.max), reads=[b_pa8], writes=[b_am8])
                S.add("pool", lambda e: e.tensor_tensor(out=am8[:].rearrange("p (c t) -> p c t", t=64), in0=am8[:].rearrange("p (c t) -> p c t", t=64),
                                                         in1=hmaskf[:, d, :].unsqueeze(1).broadcast_to([64, 8, 64]), op=ALU.mult), reads=[b_am8, b_hc], writes=[b_am8])

                def tr(i, it):
                    S.add("pe", lambda e: e.transpose(self.pstr[:64, i * 128:(i + 1) * 128], it["kt"], ident[:, :]), reads=it["ktb"] + [b_hc], writes=[self.b_pstr])
                for i, it in enumerate(items):
                    tr(i, it)
                S.add("act", lambda e: e.activation(out=ktok8[:].rearrange("p c k -> p (c k)"), in_=self.pstr[:64, :1024], func=AF.Copy), reads=[self.b_pstr], writes=[b_ktok8])

                def kvmm(i, it):
                    pk, pkb = pk8[i // 4]
                    col = (i % 4) * 128
                    S.add("pe", lambda e: e.matmul(pk[:, col:col + 128], lhsT=ktok8[:, i, :], rhs=it["v"], start=True, stop=True), reads=[b_ktok8] + it["vb"], writes=[pkb])
                for i, it in enumerate(items):
                    kvmm(i, it)

                def step(i, it):
                    h, dv, dvb = it["h"], it["dv"], it["dvb"]
                    for f in it["pre"]:
                        f()
                    cur = par[h]
                    new = 1 - cur
                    Sc, Sn = Sst[cur], Sst[new]
                    bc, bn = b_S[cur][h], b_S[new][h]
                    pk, pkb = pk8[i // 4]
                    po, pob = po2[i // 4]
                    col = (i % 4) * 128
                    sbf, sbfb = nxt(sbf_, "sbf")
                    S.add("pool", lambda e: e.tensor_scalar(out=sbf[:], in0=Sc[:, h, :], scalar1=dv[:, 0:1], scalar2=None, op0=ALU.mult), reads=[bc] + dvb, writes=[sbfb])
                    S.add("pe", lambda e: e.matmul(po[:64, col:col + 128], lhsT=it["qt"], rhs=sbf[:], start=True, stop=False), reads=it["qtb"] + [sbfb], writes=[pob])
                    S.add("pe", lambda e: e.matmul(po[:64, col:col + 128], lhsT=am8[:, i * 64:(i + 1) * 64], rhs=it["v"], start=False, stop=True), reads=[b_am8] + it["vb"], writes=[pob])
                    t1, t1b = nxt(t1_, "t1")
                    S.add("dve", lambda e: e.tensor_scalar(out=t1[:], in0=Sc[:, h, :], scalar1=dv[:, 2:3], scalar2=None, op0=ALU.mult), reads=[bc] + dvb, writes=[t1b])
                    S.add("dve", lambda e: e.scalar_tensor_tensor(out=Sn[:, h, :], in0=pk[:, col:col + 128], scalar=dv[:, 1:2], in1=t1[:], op0=ALU.mult, op1=ALU.add),
                          reads=[pkb, t1b] + dvb, writes=[bn])
                    par[h] = new
                    for f in it["post"]:
                        f()
                    if i % 4 == 3:
                        sink4(i // 4, po[:64, :512], pob, items[i - 3:i + 1])
                for i, it in enumerate(items):
                    step(i, it)

            with ExitStack() as ph:
                self.big = self.sb(ph, [128, KC, TT], F32, "big")
                self.b_big = [Buf() for c in range(KC)]
                self.hT = self.sb(ph, [128, KC, TT], BF16, "hT")
                self.b_hT = [Buf() for c in range(KC)]
                hT = self.hT
                v64 = self.sb(ph, [64, 8, 2048], BF16, "v64")
                b_v64 = [Buf() for _ in range(8)]
                gst_ = [(self.sb(ph, [64, 512], BF16, "gst"), Buf()) for _ in range(2)]
                gtmp = self.sb(ph, [64, 512], F32, "gtmp")
                b_gtmp = Buf()
                qs_ = [(self.sb(ph, [128, TT], F32, "qs"), Buf()) for _ in range(2)]
                ft = {n: (self.sb(ph, [128, TT], F32, n), Buf()) for n in ["f", "g", "B", "X", "E", "eq", "ek"]}
                qk_ = [[(self.sb(ph, [128, TT], BF16, "qkt"), Buf()) for _ in range(2)] for _ in range(4)]
                dvt_ = [(self.sb(ph, [128, 8, 3], F32, "dvt"), Buf()) for _ in range(4)]
                o1h_ = [(self.sb(ph, [64, 8, 128], F32, "o1h"), Buf()) for _ in range(2)]
                S.add("sp", lambda e: e.dma_start(out=Sst[0][:], in_=self.din["hg_s0"][j, 0]), writes=b_S[0], dma_key="hs0")
                hcount = 0
                lin_saved = self.linps
                po2_p1 = [self.statps, lin_saved[2]]
                self.linps = lin_saved[:2]
                for ti, tile in enumerate(self.tiles):
                    t0, T, cond = tile
                    self.pre(l, 0, tile)

                    def xv(kc, tb):
                        return hT[:, kc, tb * 64:(tb + 1) * 64], [self.b_hT[kc]]

                    def consv(s_, tb, ps, pb, t0=t0):
                        if s_ < 4:
                            S.add("act", lambda e: e.activation(out=v64[:, tb, s_ * 512:(s_ + 1) * 512], in_=ps, func=AF.Copy), reads=[pb], writes=[b_v64[tb]])
                        else:
                            gst, gstb = gst_[(s_ * 8 + tb) % 2]
                            S.add("act", lambda e: e.activation(out=gtmp[:], in_=ps, func=AF.Silu), reads=[pb], writes=[b_gtmp])
                            S.add("dve", lambda e: e.tensor_tensor(out=gst[:], in0=gtmp[:], in1=hgbc[:, (s_ - 4) * 512:(s_ - 3) * 512], op=ALU.mult), reads=[b_gtmp, b_hc], writes=[gstb])
                            r0 = t0 + tb * 64
                            S.add("sp", lambda e: e.dma_start(out=GS[r0:r0 + 64, (s_ - 4) * 512:(s_ - 3) * 512], in_=gst[:]), reads=[gstb], writes=[gb(("G", t0))], dma_key="hg%d" % ((s_ * 8 + tb) % 2))
                    self.linear_tm(wig, KC, 8, xv, T, 64, consv)
                    S.add("sp", lambda e, t0=t0: e.dma_start(out=V64[t0:t0 + TT, :].rearrange("(c p) n -> p c n", p=64), in_=v64[:]), reads=b_v64, writes=[gb(("V", t0))], dma_key="hv")

                    def x(kc):
                        return hT[:, kc, :T], [self.b_hT[kc]]
                    stq = {}

                    def cons(ci, ps, pb, t0=t0, cond=cond, ti=ti):
                        h, kind = divmod(ci, 3)
                        if kind == 0:
                            qs, qsb = qs_[h % 2]
                            stq["qs"] = (qs, qsb)
                            S.add("act", lambda e: e.activation(out=qs[:], in_=ps, func=AF.Silu), reads=[pb], writes=[qsb])
                            return
                        d = kind - 1
                        qs, qsb = stq["qs"]
                        (f, fb), (g_, gb_), (B, Bb), (X, Xb), (E, Eb), (eq, eqb), (ek, ekb) = [ft[n] for n in ["f", "g", "B", "X", "E", "eq", "ek"]]
                        S.add("act", lambda e: e.activation(out=f[:], in_=ps, func=AF.Sigmoid), reads=[pb], writes=[fb])
                        S.add("dve", lambda e: e.tensor_scalar(out=f[:], in0=f[:], scalar1=oml[:, d, l, h:h + 1], scalar2=lb[:, d, l, h:h + 1], op0=ALU.mult, op1=ALU.add), reads=[fb, self.b_lb], writes=[fb])
                        S.add("act", lambda e: e.activation(out=g_[:], in_=f[:], func=AF.Ln), reads=[fb], writes=[gb_])
                        S.add("dve", lambda e: e.tensor_tensor_scan(out=B[:], data0=onesf[:], data1=g_[:], initial=0.0, op0=ALU.mult, op1=ALU.add), reads=[gb_, b_hc], writes=[Bb])
                        S.add("pool", lambda e: e.tensor_tensor(out=X[:], in0=B[:], in1=g_[:], op=ALU.subtract), reads=[Bb, gb_], writes=[Xb])
                        S.add("pool", lambda e: e.tensor_scalar(out=f[:], in0=f[:], scalar1=-1.0, scalar2=1.0, op0=ALU.mult, op1=ALU.add), reads=[fb], writes=[fb])
                        B3 = B[:].rearrange("p (c t) -> p c t", t=64)
                        X3 = X[:].rearrange("p (c t) -> p c t", t=64)
                        E3 = E[:].rearrange("p (c t) -> p c t", t=64)
                        if d == 0:
                            S.add("dve", lambda e: e.tensor_tensor(out=E3, in0=B3, in1=B3[:, :, 32:33].broadcast_to([128, 8, 64]), op=ALU.subtract), reads=[Bb], writes=[Eb])
                        else:
                            S.add("dve", lambda e: e.tensor_tensor(out=E3, in0=X3[:, :, 32:33].broadcast_to([128, 8, 64]), in1=X3, op=ALU.subtract), reads=[Xb], writes=[Eb])
                        S.add("act", lambda e: e.activation(out=eq[:], in_=E[:], func=AF.Exp), reads=[Eb], writes=[eqb])
                        S.add("act", lambda e: e.activation(out=ek[:], in_=E[:], func=AF.Exp, scale=-1.0), reads=[Eb], writes=[ekb])
                        (qt, qtb) = qk_[2 * d][hcount_ref[0] % 2]
                        (kt, ktb) = qk_[2 * d + 1][hcount_ref[0] % 2]
                        (dvt, dvb) = dvt_[2 * d + hcount_ref[0] % 2]
                        S.add("dve", lambda e: e.scalar_tensor_tensor(out=qt[:], in0=qs[:], scalar=128 ** -0.5, in1=eq[:], op0=ALU.mult, op1=ALU.mult), reads=[qsb, eqb], writes=[qtb])
                        S.add("pool", lambda e: e.tensor_tensor(out=kt[:], in0=f[:], in1=ek[:], op=ALU.mult), reads=[fb, ekb], writes=[ktb])
                        mid = (B3 if d == 0 else X3)[:, :, 32:33]
                        if d == 0:
                            S.add("pool", lambda e: e.tensor_tensor(out=dvt[:, :, 0:1], in0=mid, in1=X3[:, :, 0:1], op=ALU.subtract), reads=[Bb, Xb], writes=[dvb])
                            S.add("pool", lambda e: e.tensor_tensor(out=dvt[:, :, 1:2], in0=B3[:, :, 63:64], in1=mid, op=ALU.subtract), reads=[Bb, Xb], writes=[dvb])
                        else:
                            S.add("pool", lambda e: e.tensor_tensor(out=dvt[:, :, 0:1], in0=B3[:, :, 63:64], in1=mid, op=ALU.subtract), reads=[Bb, Xb], writes=[dvb])
                            S.add("pool", lambda e: e.tensor_tensor(out=dvt[:, :, 1:2], in0=mid, in1=X3[:, :, 0:1], op=ALU.subtract), reads=[Bb, Xb], writes=[dvb])
                        S.add("pool", lambda e: e.tensor_tensor(out=dvt[:, :, 2:3], in0=B3[:, :, 63:64], in1=X3[:, :, 0:1], op=ALU.subtract), reads=[Bb, Xb], writes=[dvb])
                        S.add("act", lambda e: e.activation(out=dvt[:], in_=dvt[:], func=AF.Exp), reads=[dvb], writes=[dvb])
                        if d == 0:
                            o1h, o1hb = o1h_[h % 2]
                            items = []
                            for c in range(8):
                                pre_, post_ = [], []
                                if cond == 1 and c % (SEQ // 64) == 0:
                                    pre_.append(mk_memset(h))
                                if cond == 1 and (c + 1) % (SEQ // 64) == 0:
                                    post_.append(mk_stout(h, c // (SEQ // 64), 0))
                                items.append(dict(h=h, qt=qt[:, c * 64:(c + 1) * 64], qtb=[qtb], kt=kt[:, c * 64:(c + 1) * 64], ktb=[ktb], dv=dvt[:, c, :], dvb=[dvb],
                                                  v=v64[:, c, h * 128:(h + 1) * 128], vb=[b_v64[c]], pre=pre_, post=post_))

                            def do_scan(items=items, o1h=o1h, o1hb=o1hb, h=h):
                                def sink4(half, po_ap, pob, its):
                                    S.add("act", lambda e: e.activation(out=o1h[:, half * 4:(half + 1) * 4, :], in_=po_ap.rearrange("p (c v) -> p c v", v=128), func=AF.Copy), reads=[pob], writes=[o1hb])
                                scan_group(0, items, po2_p1, sink4)
                                S.add("sp", lambda e: e.dma_start(out=O1[t0:t0 + TT, h * 128:(h + 1) * 128].rearrange("(c p) v -> p c v", p=64), in_=o1h[:]), reads=[o1hb], writes=[gb(("O", t0, h))], dma_key="ho%d" % (h % 2))
                            if pend_scan:
                                pend_scan.pop()()
                            pend_scan.append(do_scan)
                        else:
                            S.add("sp", lambda e: e.dma_start(out=Q2[h, :, t0:t0 + TT], in_=qt[:]), reads=[qtb], writes=[gb(("Q", t0, h))], dma_key="hq%d" % (hcount_ref[0] % 2))
                            S.add("sp", lambda e: e.dma_start(out=K2[h, :, t0:t0 + TT], in_=kt[:]), reads=[ktb], writes=[gb(("K", t0, h))], dma_key="hk%d" % (hcount_ref[0] % 2))
                            S.add("sp", lambda e: e.dma_start(out=D2[h, :, ti * 8:(ti + 1) * 8, :], in_=dvt[:]), reads=[dvb], writes=[gb(("D", t0, h))], dma_key="hd%d" % (hcount_ref[0] % 2))
                            hcount_ref[0] += 1
                    hcount_ref = [hcount]
                    pend_scan = []
                    self.linear_fm(wqf, KC, 48, 4, x, T, cons)
                    while pend_scan:
                        pend_scan.pop()()
                    hcount = hcount_ref[0]
                self.linps = lin_saved
                S.barrier()
            with ExitStack() as ph:
                self.big = self.sb(ph, [128, KC, TT], F32, "big")
                self.b_big = [Buf() for c in range(KC)]
                oT = self.sb(ph, [128, 16, TT], BF16, "oT")
                b_oT = [Buf() for _ in range(8)]
                NB = 2
                q2c_ = [(self.sb(ph, [128, 16, 64], BF16, "q2c"), Buf()) for _ in range(NB)]
                k2c_ = [(self.sb(ph, [128, 16, 64], BF16, "k2c"), Buf()) for _ in range(NB)]
                vc_ = [(self.sb(ph, [64, 2048], BF16, "vc"), Buf()) for _ in range(NB)]
                gc_ = [(self.sb(ph, [64, 2048], BF16, "gc"), Buf()) for _ in range(NB)]
                o1c_ = [(self.sb(ph, [64, 2048], F32, "o1c"), Buf()) for _ in range(NB)]
                d2t = self.sb(ph, [128, 16, 8, 3], F32, "d2t")
                b_d2t = Buf()
                osum = self.sb(ph, [64, 16, 128], F32, "osum")
                b_osum = [Buf() for _ in range(16)]
                sqt = self.sb(ph, [64, 2048], F32, "sqt")
                b_sqt = Buf()
                ssq = self.sb(ph, [64, 16], F32, "ssq")
                b_ssq = Buf()
                obf = self.sb(ph, [64, 2048], BF16, "obf")
                b_obf = Buf()
                order = [t for t in self.tiles if t[2] == 1] + [t for t in reversed(self.tiles) if t[2] == 0]
                po2_p2 = [self.linps[0], self.linps[1]]
                first_sample = True
                ci_ = 0
                for tile in order:
                    t0, T, cond = tile
                    ti = t0 // TT
                    if cond == 0 and first_sample:
                        first_sample = False
                        p0 = par[0]
                        assert all(p == p0 for p in par)
                        S.add("sp", lambda e, p0=p0: e.dma_start(out=Sst[p0][:], in_=self.din["hg_s0"][j, 1]), writes=b_S[p0], dma_key="hs1")
                    S.add("sp", lambda e, ti=ti: e.dma_start(out=d2t[:], in_=D2[:, :, ti * 8:(ti + 1) * 8, :].rearrange("h p c k -> p h c k")), reads=[gb(("D", t0, h)) for h in range(16)], writes=[b_d2t], dma_key="hd2")
                    for c in reversed(range(8)):
                        r0 = t0 + c * 64
                        (q2c, q2b), (k2c, k2b), (vc, vcb), (gc, gcb), (o1c, o1b) = q2c_[ci_ % NB], k2c_[ci_ % NB], vc_[ci_ % NB], gc_[ci_ % NB], o1c_[ci_ % NB]
                        kk = ci_ % NB
                        ci_ += 1
                        S.add("sp", lambda e, q2c=q2c, r0=r0: e.dma_start(out=q2c[:], in_=Q2[:, :, r0:r0 + 64].rearrange("h p t -> p h t")), reads=[gb(("Q", t0, h)) for h in range(16)], writes=[q2b], dma_key="p2q%d" % kk)
                        S.add("sp", lambda e, k2c=k2c, r0=r0: e.dma_start(out=k2c[:], in_=K2[:, :, r0:r0 + 64].rearrange("h p t -> p h t")), reads=[gb(("K", t0, h)) for h in range(16)], writes=[k2b], dma_key="p2k%d" % kk)
                        S.add("sp", lambda e, vc=vc, r0=r0: e.dma_start(out=vc[:], in_=V64[r0:r0 + 64, :]), reads=[gb(("V", t0))], writes=[vcb], dma_key="p2v%d" % kk)
                        S.add("sp", lambda e, gc=gc, r0=r0: e.dma_start(out=gc[:], in_=GS[r0:r0 + 64, :]), reads=[gb(("G", t0))], writes=[gcb], dma_key="p2g%d" % kk)
                        S.add("sp", lambda e, o1c=o1c, r0=r0: e.dma_start(out=o1c[:], in_=O1[r0:r0 + 64, :]), reads=[gb(("O", t0, h)) for h in range(16)], writes=[o1b], dma_key="p2o%d" % kk)
                        for g0 in (0, 8):
                            items = []
                            for h in range(g0, g0 + 8):
                                pre_, post_ = [], []
                                if cond == 1 and (c + 1) % (SEQ // 64) == 0:
                                    pre_.append(mk_memset(h))
                                if cond == 1 and c % (SEQ // 64) == 0:
                                    post_.append(mk_stout(h, c // (SEQ // 64), 1))
                                items.append(dict(h=h, qt=q2c[:, h, :], qtb=[q2b], kt=k2c[:, h, :], ktb=[k2b], dv=d2t[:, h, c, :], dvb=[b_d2t],
                                                  v=vc[:, h * 128:(h + 1) * 128], vb=[vcb], pre=pre_, post=post_))

                            def sink4(half, po_ap, pob, its, o1c=o1c, o1b=o1b):
                                h0 = its[0]["h"]
                                S.add("dve", lambda e: e.tensor_tensor(out=osum[:, h0:h0 + 4, :], in0=po_ap.rearrange("p (h v) -> p h v", v=128),
                                                                        in1=o1c[:, h0 * 128:(h0 + 4) * 128].rearrange("p (h v) -> p h v", v=128), op=ALU.add),
                                      reads=[pob, o1b], writes=[b_osum[hh] for hh in range(h0, h0 + 4)])
                            scan_group(1, items, po2_p2, sink4)
                        of = osum[:].rearrange("p h v -> p (h v)")
                        S.add("act", lambda e: e.activation(out=sqt[:], in_=of, func=AF.Square), reads=b_osum, writes=[b_sqt])
                        S.add("dve", lambda e: e.tensor_reduce(out=ssq[:], in_=sqt[:].rearrange("p (h v) -> p h v", v=128), axis=AX.X, op=ALU.add), reads=[b_sqt], writes=[b_ssq])
                        S.add("act", lambda e: e.activation(out=ssq[:], in_=ssq[:], func=AF.Ln, scale=1.0 / 128, bias=self.c_eps[:64, :]), reads=[b_ssq, self.b_const], writes=[b_ssq])
                        S.add("act", lambda e: e.activation(out=ssq[:], in_=ssq[:], func=AF.Exp, scale=-0.5), reads=[b_ssq], writes=[b_ssq])
                        S.add("dve", lambda e: e.tensor_tensor(out=sqt[:].rearrange("p (h v) -> p h v", v=128), in0=osum[:], in1=ssq[:].unsqueeze(2).broadcast_to([64, 16, 128]), op=ALU.mult), reads=b_osum + [b_ssq], writes=[b_sqt])
                        S.add("pool", lambda e, gc=gc: e.tensor_tensor(out=obf[:], in0=sqt[:], in1=gc[:], op=ALU.mult), reads=[b_sqt, gcb], writes=[b_obf])
                        for h in range(16):
                            S.add("pe", lambda e, h=h: e.transpose(self.pstr[:, h * 64:(h + 1) * 64], obf[:, h * 128:(h + 1) * 128], ident[:64, :64]), reads=[b_obf, b_hc], writes=[self.b_pstr])
                        S.add("act", lambda e, c=c: e.activation(out=oT[:, :, c * 64:(c + 1) * 64], in_=self.pstr[:, :].rearrange("p (h t) -> p h t", t=64), func=AF.Copy), reads=[self.b_pstr], writes=[b_oT[c]])

                    def x(kc):
                        return oT[:, kc, :T], b_oT
                    self.linear_fm(wo, KC, 16, 4, x, T, self.post_consume(T))
                    self.post(l, 0, tile, False)
                S.barrier()

    def build(self, kinds):
        nc, S = self.nc, self.S
        NT, L, NS, NP, SEQ = self.NT, self.L, self.NS, self.NP, self.SEQ
        NA = sum(1 for k in kinds if k == 0)
        NB_ = sum(1 for k in kinds if k == 1)
        NC_ = sum(1 for k in kinds if k == 2)
        self.inp("xT", [KC, 128, NT])
        self.inp("cT", [128, KC, 2])
        self.inp("ada_w", [L, 24, 128, KC * 512])
        self.inp("ada_bT", [128, L, 96])
        self.inp("norm_gT", [128, L, 4, KC])
        self.inp("mlp_w_in", [L, 16, 128, KC * 512])
        self.inp("mlp_w_out", [L, 16, 128, 64 * 128])
        self.inp("ropeC", [64, NT])
        self.inp("ropeS", [64, NT])
        self.inp("ident", [128, 128])
        if NA:
            self.inp("hg_w_qf", [NA, 12, 128, KC * 512])
            self.inp("hg_w_ig", [NA, 8, 128, KC * 512])
            self.inp("hg_w_out", [NA, 4, 128, KC * 512])
            self.inp("hg_lbT", [128, 2, L, 16])
            self.inp("hg_gbc", [NA, 64, 2048])
            self.inp("hg_s0", [NA, 2, 128, 16, 128])
            self.inp("hg_masks", [64, 2, 64])
            self.outp("hg_stout", [NA, 2, 2, 128, 16, 128])
        if NB_:
            self.inp("mla_w_down", [NB_, 3, 128, KC * 512])
            self.inp("mla_w_uq", [NB_, 3, 128, 4 * 2048])
            self.inp("mla_w_ukvT", [NB_, 128, 4, 4096])
            self.inp("mla_w_out", [NB_, 4, 128, KC * 512])
            self.inp("mla_gT", [NB_, 128, 2, 4])
            self.inp("mla_ckv_ctxT", [128, 4, 256])
            self.inp("mla_kpe_ctxT", [64, 256])
            self.outp("mla_ckvout", [4, 128, NP])
            self.outp("mla_kpeout", [64, NP])
        if NC_:
            self.inp("swa_w_qk", [NC_, 10, 128, KC * 512])
            self.inp("swa_w_v", [NC_, 1, 128, KC * 512])
            self.inp("swa_w_out", [NC_, 8, 64, 32 * 256])
            self.inp("swa_kctxT", [64, 8, 256])
            self.inp("swa_vctx", [256, 512])
            self.inp("swa_masks", [128, 2, 512])
            self.inp("swa_sink_bc", [NC_, 64, 32])
            self.outp("swa_kout", [64, 8, NP])
            self.outp("swa_vout", [NP, 512])
        self.OUTT = self.outp("yT", [KC, 128, NT])
        self.YT = self.din["xT"]
        YS = self.scratch("YS", [KC, 128, NT])
        self.YS = YS
        with ExitStack() as g:
            self.c_eps = self.sb(g, [128, 1], F32, "eps")
            self.ones_bf = self.sb(g, [128, 128], BF16, "ones")
            self.adab = self.sb(g, [128, L, 96], F32, "adab")
            self.ngT = self.sb(g, [128, L, 4, KC], F32, "ngT")
            self.scT = self.sb(g, [128, KC, 2], BF16, "scT")
            cTf = self.sb(g, [128, KC, 2], F32, "cTf")
            self.modraw = self.sb(g, [128, 96, 2], F32, "modraw")
            self.modA = self.sb(g, [128, 2, KC, 2], F32, "modA")
            self.modG = self.sb(g, [128, 2, KC, 2], F32, "modG")
            self.modS = self.sb(g, [128, 2, KC, 2], F32, "modS")
            self.rstd = self.sb(g, [128, TT], F32, "rstd")
            self.ropeC = self.sb(g, [64, TT], F32, "ropeC")
            self.ropeS = self.sb(g, [64, TT], F32, "ropeS")
            self.b_const, self.b_mod, self.b_rstd, self.b_rope = Buf("const"), Buf("mod"), Buf("rstd"), Buf("rope")
            self.wbufs = [(self.sb(g, [128, 8192], BF16, "wb"), Buf(f"wb{i}")) for i in range(2)]
            self.wrot = 0
            self.tmps = [(self.sb(g, [128, TT], F32, "tmp"), Buf(f"tmp{i}")) for i in range(4)]
            self.tmprot = 0
            self.sqs = [(self.sb(g, [128, TT], BF16, "sq"), Buf(f"sq{i}")) for i in range(2)]
            self.sqrot = 0
            psb = [g.enter_context(nc.psum_tensor(f"ps{i}", [128, 512], F32)) for i in range(7)]
            self.pstr = g.enter_context(nc.psum_tensor("pstr", [128, 1024], BF16))
            self.linps = [(psb[i], Buf(f"lin{i}", True)) for i in range(3)]
            self.linrot = 0
            self.statps = (psb[3], Buf("stat", True))
            self.miscps = [(psb[i], Buf(f"misc{i}", True)) for i in range(4, 7)]
            self.b_pstr = Buf("pstr", True)
            self.accps = [(self.miscps[0], self.miscps[1]), (self.miscps[2], self.statps)]
            self.accrot = 0
            S.add("dve", lambda e: e.memset(self.c_eps[:], EPS), writes=[self.b_const])
            S.add("dve", lambda e: e.memset(self.ones_bf[:], 1.0), writes=[self.b_const])
            S.add("sp", lambda e: e.dma_start(out=self.adab[:], in_=self.din["ada_bT"]), writes=[self.b_const], dma_key="c0")
            S.add("sp", lambda e: e.dma_start(out=self.ngT[:], in_=self.din["norm_gT"]), writes=[self.b_const], dma_key="c1")
            S.add("sp", lambda e: e.dma_start(out=cTf[:], in_=self.din["cT"]), writes=[self.b_const], dma_key="c2")
            S.add("act", lambda e: e.activation(out=self.scT[:], in_=cTf[:], func=AF.Silu), reads=[self.b_const], writes=[self.b_const])
            if NA:
                self.hgrn_setup(g)
            cnt = [0, 0, 0]
            for l in range(L):
                self.adaln(l)
                S.barrier()
                kind = kinds[l]
                if kind == 0:
                    self.hgrn_layer(l, cnt[0])
                elif kind == 1:
                    self.mla_layer(l, cnt[1])
                elif kind == 2:
                    self.swa_layer(l, cnt[2])
                if kind >= 0:
                    cnt[kind] += 1
                    self.YT = YS
                with ExitStack() as ph:
                    self.big = self.sb(ph, [128, KC, TT], F32, "big")
                    self.b_big = [Buf(f"big{c}") for c in range(KC)]
                    self.hT = self.sb(ph, [128, KC, TT], BF16, "hT")
                    self.b_hT = [Buf(f"hT{c}") for c in range(KC)]
                    self.aT = self.sb(ph, [128, 64, TT], BF16, "aT")
                    self.b_aT = [Buf(f"aT{c}") for c in range(64)]
                    for tile in self.tiles:
                        self.mlp(l, tile, l == L - 1)
                    S.barrier()
                self.YT = YS
            S.emit()
        return nc


ROPE_BASE = 10000.0


def _partner():
    d = np.arange(64)
    return np.where(d % 32 < 16, d + 16, d - 16)


def _rope_tables(NS, NP, GW):
    d = np.arange(64)
    inv = ROPE_BASE ** (-(d % 16).astype(np.float32) / 16.0)
    t = np.arange(NS)
    pos = np.where(d[:, None] < 32, (t // GW)[None, :], (t % GW)[None, :]).astype(np.float32)
    ang = pos * inv[:, None].astype(np.float32)
    c = np.cos(ang).astype(np.float32)
    sn = np.sin(ang).astype(np.float32)
    sn = np.where((d % 32 < 16)[:, None], -sn, sn)
    c = np.concatenate([c, np.ones((64, NP), np.float32)], axis=1)
    sn = np.concatenate([sn, np.zeros((64, NP), np.float32)], axis=1)
    return np.ascontiguousarray(c, np.float32), np.ascontiguousarray(sn, np.float32)


def _shared_inputs(inp, L, kinds, NS, NP, GW):
    m = {}
    m["ada_w"] = np.stack([tile_w(inp["ada_w"][l], plain_chunks(96), 4) for l in range(L)])
    m["ada_bT"] = np.ascontiguousarray(fm(inp["ada_b"][:L]))
    m["norm_gT"] = np.ascontiguousarray(fm(inp["norm_g"][:L]))
    m["mlp_w_in"] = np.stack([tile_w(inp["mlp_w_in"][l], plain_chunks(64), 4) for l in range(L)])
    m["mlp_w_out"] = np.stack([tile_w(inp["mlp_w_out"][l], plain_chunks(16), 1) for l in range(L)])
    m["ropeC"], m["ropeS"] = _rope_tables(NS, NP, GW)
    m["ident"] = np.eye(128, dtype=np.float32)
    par = _partner()
    NA = sum(1 for k in kinds if k == 0)
    NB_ = sum(1 for k in kinds if k == 1)
    NC_ = sum(1 for k in kinds if k == 2)
    if NA:
        qf, ig, wo, gbc = [], [], [], []
        for j in range(NA):
            W = inp["hgrn_w_in"][j]
            ch = []
            for h in range(16):
                ch += [np.arange(h * 128, (h + 1) * 128), np.arange(2048 + h * 128, 2048 + (h + 1) * 128), np.arange(4096 + h * 128, 4096 + (h + 1) * 128)]
            qf.append(tile_w(W, ch, 4))
            ig.append(tile_w(W, plain_chunks(32, start=6144), 4))
            wo.append(tile_w(inp["hgrn_w_out"][j], plain_chunks(16), 4))
            gbc.append(np.broadcast_to(np.tile(inp["hgrn_norm_g"][j], 16)[None, :], (64, 2048)))
        m["hg_w_qf"], m["hg_w_ig"], m["hg_w_out"] = np.stack(qf), np.stack(ig), np.stack(wo)
        m["hg_gbc"] = np.ascontiguousarray(np.stack(gbc), np.float32)
        lg = inp["hgrn_lb_logits"][:, :L]
        m["hg_lbT"] = np.ascontiguousarray(lg.reshape(2, L, 16, 128).transpose(3, 0, 1, 2), np.float32)
        s_, t_ = np.arange(64)[:, None], np.arange(64)[None, :]
        m["hg_masks"] = np.ascontiguousarray(np.stack([(s_ <= t_), (s_ >= t_)], axis=1).astype(np.float32))
    if NB_:
        wd, wq, wkv, wo, gT = [], [], [], [], []
        for j in range(NB_):
            W = inp["mla_w_down"][j]
            ch = plain_chunks(8) + [np.arange(1024, 1088), 1024 + par]
            wd.append(tile_w(W, ch, 4))
            U = inp["mla_w_uq"][j]
            ch = [np.arange(h * 192, h * 192 + 128) for h in range(16)]
            for h in range(16):
                ch += [h * 192 + 128 + np.arange(64), h * 192 + 128 + par]
            wq.append(tile_w(U, ch, 16))
            wkv.append(inp["mla_w_ukv"][j].reshape(4, 128, 4096).transpose(1, 0, 2))
            wo.append(tile_w(inp["mla_w_out"][j], plain_chunks(16), 4))
            gT.append(np.stack([fm(inp["mla_q_norm_g"][j]), fm(inp["mla_kv_norm_g"][j])], axis=1))
        m["mla_w_down"], m["mla_w_uq"], m["mla_w_out"] = np.stack(wd), np.stack(wq), np.stack(wo)
        m["mla_w_ukvT"] = np.ascontiguousarray(np.stack(wkv), np.float32)
        m["mla_gT"] = np.ascontiguousarray(np.stack(gT), np.float32)
    if NC_:
        wqk, wv, wo, sk = [], [], [], []
        for j in range(NC_):
            W = inp["swa_w_qkv"][j]
            ch = []
            for h in range(32):
                ch += [h * 64 + np.arange(64), h * 64 + par]
            for g_ in range(8):
                ch += [2048 + g_ * 64 + np.arange(64), 2048 + g_ * 64 + par]
            wqk.append(tile_w(W, ch, 8, cw=64))
            wv.append(tile_w(W, plain_chunks(4, start=2560), 4))
            wo.append(tile_w(inp["swa_w_out"][j], plain_chunks(16), 2, kp=64))
            sk.append(np.broadcast_to(inp["swa_sink"][j][None, :], (64, 32)))
        m["swa_w_qk"], m["swa_w_v"], m["swa_w_out"] = np.stack(wqk), np.stack(wv), np.stack(wo)
        m["swa_sink_bc"] = np.ascontiguousarray(np.stack(sk), np.float32)
        c_, a_ = np.arange(128)[:, None], np.arange(128)[None, :]
        m0 = np.tile((c_ >= a_).astype(np.float32), (1, 4))
        m1 = np.tile((c_ <= a_).astype(np.float32), (1, 4))
        m["swa_masks"] = np.ascontiguousarray(np.stack([m0, m1], axis=1))
    return m


def _host_inputs(inp, core, NS, NP, SEQ, kinds):
    b = core % inp["x_sample"].shape[0]
    nps = NP // SEQ
    xs = inp["x_sample"][b, :NS]
    xp = inp["x_prompt"][core * nps:(core + 1) * nps].reshape(NP, D)
    x = np.concatenate([xs, xp], axis=0)
    m = {}
    m["xT"] = np.ascontiguousarray(x.T.reshape(KC, 128, -1))
    cc = np.stack([inp["c"][b], inp["c_ctx"]], axis=-1)
    m["cT"] = np.ascontiguousarray(cc.reshape(KC, 128, 2).transpose(1, 0, 2))
    if 0 in kinds:
        m["hg_s0"] = np.ascontiguousarray(inp["state_hgrn"][b].transpose(0, 1, 3, 2, 4))
    if 1 in kinds:
        m["mla_ckv_ctxT"] = np.ascontiguousarray(inp["cache_mla_ckv"][b, 0].T.reshape(4, 128, -1).transpose(1, 0, 2))
        m["mla_kpe_ctxT"] = np.ascontiguousarray(inp["cache_mla_kpe"][b, 0].T)
    if 2 in kinds:
        m["swa_kctxT"] = np.ascontiguousarray(inp["cache_swa_k"][b, 0].transpose(2, 1, 0))
        m["swa_vctx"] = np.ascontiguousarray(inp["cache_swa_v"][b, 0].reshape(-1, 512))
    return m


def run(inp, NS, NP, SEQ, L, GRID_W, ncores, kinds):
    inp = {k: np.asarray(v) for k, v in inp.items()}
    p = Prog(NS, NP, SEQ, L, GRID_W)
    nc = p.build(kinds)
    shared = _shared_inputs(inp, L, kinds, NS, NP, GRID_W)
    maps = []
    for c in range(ncores):
        m = dict(shared)
        m.update(_host_inputs(inp, c, NS, NP, SEQ, kinds))
        maps.append(m)
    res = run_bass_kernel_spmd(nc, maps, core_ids=list(range(ncores)))
    return res.results


def assemble(r, NS, NP, SEQ, ncores, nsamp, kinds):
    nps = NP // SEQ
    out = {}
    out["ys"] = np.stack([r[c]["yT"].reshape(D, -1)[:, :NS].T for c in range(nsamp)])
    out["yp"] = np.concatenate([r[c]["yT"].reshape(D, -1)[:, NS:].T.reshape(nps, SEQ, D) for c in range(ncores)])
    if 0 in kinds:
        out["st"] = np.concatenate([r[c]["hg_stout"].transpose(1, 0, 2, 4, 3, 5) for c in range(ncores)])
    if 1 in kinds:
        out["ckv"] = np.concatenate([r[c]["mla_ckvout"].reshape(512, NP).T.reshape(nps, 1, SEQ, 512) for c in range(ncores)])
        out["kpe"] = np.concatenate([r[c]["mla_kpeout"].T.reshape(nps, 1, SEQ, 64) for c in range(ncores)])
    if 2 in kinds:
        out["k"] = np.concatenate([r[c]["swa_kout"].transpose(2, 1, 0).reshape(nps, 1, SEQ, 8, 64) for c in range(ncores)])
        out["v"] = np.concatenate([r[c]["swa_vout"].reshape(nps, 1, SEQ, 8, 64) for c in range(ncores)])
    return out


def kernel(**inputs):
    NS, NP, SEQ, L, GW = 4096, 512, 256, 4, 64
    kinds = [0, 1, 2, 0]
    r = run(inputs, NS, NP, SEQ, L, GW, 8, kinds)
    o = assemble(r, NS, NP, SEQ, 8, 4, kinds)
    f = lambda a: np.ascontiguousarray(a, dtype=np.float32)
    return (f(o["yp"]), f(o["ys"]), f(o["st"]), f(o["ckv"]), f(o["kpe"]), f(o["k"]), f(o["v"]))
```
